# Optimizing a Trainium2 kernel written in Bass

```python
import math
import jax, jax.numpy as jnp
from jax import lax
import numpy as np

D_MODEL = 1024
BATCH = 8
SEQ = 4096
DEPTH = 2
DEC_BATCH = 8
DEC_SEQ = 64
PAST_LEN = 2048

CHUNK = 64
N_META = 16
EPS = 1e-6
F32 = jnp.float32

SSD_HEADS = 8
SSD_HEAD_DIM = 64
SSD_INNER = SSD_HEADS * SSD_HEAD_DIM
SSD_GROUPS = 2
SSD_STATE = 128
SSD_CONV = 4
SSD_CONV_DIM = SSD_INNER + 2 * SSD_GROUPS * SSD_STATE
SSD_IN = SSD_INNER + SSD_CONV_DIM + SSD_HEADS

RW_HEADS = 8
RW_HEAD_DIM = 64
RW_DIM = RW_HEADS * RW_HEAD_DIM
RW_DECAY_LORA = 64
RW_A_LORA = 64
RW_GATE_LORA = 128
RW_IN = 3 * RW_DIM + RW_DECAY_LORA + RW_A_LORA + RW_GATE_LORA
RW_GN_EPS = 64e-5

HG_HEADS = 4
HG_KEY = 128
HG_VAL = 128
HG_KDIM = HG_HEADS * HG_KEY
HG_VDIM = HG_HEADS * HG_VAL
HG_IN = 2 * HG_KDIM + 2 * HG_VDIM

D_MIX = SSD_INNER + RW_DIM + HG_VDIM
N_IN = SSD_IN + RW_IN + HG_IN
D_FF = -(-8 * D_MODEL // (3 * 256)) * 256
EXP_CLAMP = 60.0

kernel_name = 'hymba_ssd_rwkv7_hgrn2_stream_step'


def _split(a, sizes):
    return jnp.split(a, np.cumsum(sizes)[:-1].tolist(), axis=-1)


def _rmsnorm(x, w):
    xf = x.astype(F32)
    y = xf * lax.rsqrt(jnp.mean(jnp.square(xf), -1, keepdims=True) + EPS)
    return (y * w.astype(F32)).astype(x.dtype)


def _group_rmsnorm(x, w, n_groups):
    shp = x.shape
    xf = x.astype(F32).reshape(shp[:-1] + (n_groups, shp[-1] // n_groups))
    y = xf * lax.rsqrt(jnp.mean(jnp.square(xf), -1, keepdims=True) + EPS)
    return y.reshape(shp) * w.astype(F32)


def _group_layernorm(x, w, b, n_groups, eps):
    shp = x.shape
    xf = x.astype(F32).reshape(shp[:-1] + (n_groups, shp[-1] // n_groups))
    mu = jnp.mean(xf, -1, keepdims=True)
    var = jnp.mean(jnp.square(xf - mu), -1, keepdims=True)
    y = (xf - mu) * lax.rsqrt(var + eps)
    return y.reshape(shp) * w.astype(F32) + b.astype(F32)


def _masked_exp(diff, mask):
    return jnp.where(mask, jnp.exp(jnp.where(mask, diff, 0.0)), 0.0)


def _causal_conv(u, buf, w, b):
    up = jnp.concatenate([buf.astype(u.dtype), u], axis=1)
    T = u.shape[1]
    y = sum(up[:, j:j + T] * w[j] for j in range(SSD_CONV)) + b
    return jax.nn.silu(y), up[:, -(SSD_CONV - 1):]


def _token_shift(u, prev, mu):
    shifted = jnp.concatenate([prev[:, None].astype(u.dtype), u[:, :-1]], axis=1)
    return u + (shifted - u) * mu, u[:, -1]


def _ssd_chunk_scan(x, loga, Bm, Cm, h0, chunk):
    b, T = x.shape[:2]
    n = T // chunk
    R = SSD_HEADS // SSD_GROUPS
    mask = jnp.tril(jnp.ones((chunk, chunk), bool))[None, :, :, None, None]

    def blocks(a):
        return jnp.moveaxis(a.astype(F32).reshape((b, n, chunk) + a.shape[2:]), 1, 0)

    def step(h, inp):
        xc, ac, bc, cc = inp
        xg = xc.reshape(b, chunk, SSD_GROUPS, R, SSD_HEAD_DIM)
        cum = jnp.cumsum(ac, axis=1).reshape(b, chunk, SSD_GROUPS, R)
        L = _masked_exp(cum[:, :, None] - cum[:, None], mask)
        cb = jnp.einsum('bign,bjgn->bijg', cc, bc)
        y = jnp.einsum('bijgr,bjgrp->bigrp', L * cb[..., None], xg)
        y = y + jnp.einsum('bign,bgrpn->bigrp', cc, h) * jnp.exp(cum)[..., None]
        h = h * jnp.exp(cum[:, -1])[..., None, None] + jnp.einsum(
            'bjgrp,bjgn->bgrpn', xg * jnp.exp(cum[:, -1:] - cum)[..., None], bc)
        return h, y

    hg0 = h0.astype(F32).reshape(b, SSD_GROUPS, R, SSD_HEAD_DIM, SSD_STATE)
    hT, ys = lax.scan(step, hg0, (blocks(x), blocks(loga), blocks(Bm), blocks(Cm)))
    y = jnp.moveaxis(ys, 0, 1).reshape(b, T, SSD_HEADS, SSD_HEAD_DIM)
    return y, hT.reshape(b, SSD_HEADS, SSD_HEAD_DIM, SSD_STATE)


def _gla_chunk_scan(q, k, v, logf, S0, chunk):
    b, T, H, K = q.shape
    n = T // chunk
    mask = jnp.tril(jnp.ones((chunk, chunk), bool))[None, :, :, None, None]

    def blocks(a):
        return jnp.moveaxis(a.astype(F32).reshape((b, n, chunk) + a.shape[2:]), 1, 0)

    def step(S, inp):
        qc, kc, vc, fc = inp
        cum = jnp.cumsum(fc, axis=1)
        dec = _masked_exp(cum[:, :, None] - cum[:, None], mask)
        A = jnp.einsum('bijhk,bjhk->bijh', dec * qc[:, :, None], kc)
        o = jnp.einsum('bijh,bjhv->bihv', A, vc) + jnp.einsum('bihk,bhkv->bihv', qc * jnp.exp(cum), S)
        S = S * jnp.exp(cum[:, -1])[..., None] + jnp.einsum(
            'bjhk,bjhv->bhkv', kc * jnp.exp(cum[:, -1:] - cum), vc)
        return S, o

    ST, os_ = lax.scan(step, S0.astype(F32), tuple(blocks(a) for a in (q, k, v, logf)))
    return jnp.moveaxis(os_, 0, 1).reshape(b, T, H, v.shape[-1]), ST


def _rwkv7_scan(r, w, k, v, a, bb, S0):
    def step(S, inp):
        rt, wt, kt, vt, at, bt = inp
        sa = jnp.einsum('bhvk,bhk->bhv', S, at)
        S = S * wt[:, :, None, :] + sa[..., None] * bt[:, :, None, :] + vt[..., None] * kt[:, :, None, :]
        return S, jnp.einsum('bhvk,bhk->bhv', S, rt)

    xs = tuple(jnp.moveaxis(t.astype(F32), 1, 0) for t in (r, w, k, v, a, bb))
    ST, outs = lax.scan(step, S0.astype(F32), xs)
    return jnp.moveaxis(outs, 0, 1), ST


def _run_segments(scan_fn, arrays, state, segs):
    outs, start = [], 0
    for length, chunk in segs:
        y, state = scan_fn(*[a[:, start:start + length] for a in arrays], state, chunk)
        outs.append(y)
        start += length
    return jnp.concatenate(outs, axis=1), state


def _lower_bounds(logits):
    s = jax.nn.softmax(logits.astype(F32), axis=0)
    return jnp.cumsum(s, axis=0) - s[0]


def _mixer(h, l, ssm0, conv0, rw0, shift0, hg0, segs, p):
    b, T, _ = h.shape
    proj = jnp.einsum('btd,dn->btn', h, p['w_in'][l])
    z, xbc, dt_raw, rw_in, hg_in = _split(proj, [SSD_INNER, SSD_CONV_DIM, SSD_HEADS, RW_IN, HG_IN])

    xbc, conv_new = _causal_conv(xbc, conv0, p['conv_w'][l], p['conv_b'][l])
    xs, Bm, Cm = _split(xbc.astype(F32), [SSD_INNER, SSD_GROUPS * SSD_STATE, SSD_GROUPS * SSD_STATE])
    dt = jax.nn.softplus(dt_raw.astype(F32) + p['dt_bias'][l])
    loga = -jnp.exp(p['a_log'][l].astype(F32)) * dt
    xh = xs.reshape(b, T, SSD_HEADS, SSD_HEAD_DIM)
    y_ssd, ssm_new = _run_segments(
        _ssd_chunk_scan,
        (xh * dt[..., None], loga, Bm.reshape(b, T, SSD_GROUPS, SSD_STATE), Cm.reshape(b, T, SSD_GROUPS, SSD_STATE)),
        ssm0, segs)
    y_ssd = (y_ssd + p['d_skip'][l][:, None] * xh).reshape(b, T, SSD_INNER)
    y_ssd = _group_rmsnorm(y_ssd * jax.nn.silu(z.astype(F32)), p['ssd_norm_w'][l], SSD_GROUPS)

    rw_in, shift_new = _token_shift(rw_in, shift0, p['rw_mu'][l])
    r, k, v, xw, xa, xg = _split(rw_in.astype(F32), [RW_DIM] * 3 + [RW_DECAY_LORA, RW_A_LORA, RW_GATE_LORA])
    logw = -jax.nn.softplus(-(p['rw_w0'][l] + jnp.tanh(xw) @ p['rw_w2'][l])) - 0.5
    decay = jnp.exp(-jnp.exp(logw))
    a = jax.nn.sigmoid(p['rw_a0'][l] + xa @ p['rw_a2'][l])
    g = jax.nn.sigmoid(xg) @ p['rw_g2'][l]
    heads = lambda t: t.reshape(b, T, RW_HEADS, RW_HEAD_DIM)
    kk = heads(k * p['rw_kk'][l])
    kk = kk * lax.rsqrt(jnp.sum(kk * kk, -1, keepdims=True) + 1e-12)
    k = k * (1.0 + (a - 1.0) * p['rw_ka'][l])
    rh, kh, vh, ah = heads(r), heads(k), heads(v), heads(a)
    o_rw, rw_new = _rwkv7_scan(rh, heads(decay), kh, vh, -kk, kk * ah, rw0)
    o_rw = _group_layernorm(o_rw.reshape(b, T, RW_DIM), p['rw_lnx_w'][l], p['rw_lnx_b'][l], RW_HEADS, RW_GN_EPS)
    bonus = jnp.sum(rh * kh * p['rw_rk'][l].reshape(RW_HEADS, RW_HEAD_DIM), -1, keepdims=True) * vh
    y_rw = (o_rw + bonus.reshape(b, T, RW_DIM)) * g

    q, fz, inp, gg = _split(hg_in.astype(F32), [HG_KDIM, HG_KDIM, HG_VDIM, HG_VDIM])
    lb = _lower_bounds(p['hg_lb_logits'])[l]
    logf = jax.nn.log_sigmoid(fz) + jnp.log1p(lb * jnp.exp(jnp.minimum(-fz, EXP_CLAMP)))
    kg = (1.0 - lb) * jax.nn.sigmoid(-fz)
    hk = lambda t: t.reshape(b, T, HG_HEADS, HG_KEY)
    o_hg, hg_new = _run_segments(
        _gla_chunk_scan, (hk(q), hk(kg), inp.reshape(b, T, HG_HEADS, HG_VAL), hk(logf)), hg0, segs)
    y_hg = _group_rmsnorm(o_hg.reshape(b, T, HG_VDIM), p['hg_norm_w'][l], HG_HEADS) * jax.nn.silu(gg)

    y = jnp.concatenate([y_ssd, y_rw, y_hg], axis=-1).astype(h.dtype)
    out = jnp.einsum('btm,md->btd', y, p['w_out'][l])
    dt_ = h.dtype
    return out, (ssm_new.astype(dt_), conv_new.astype(dt_), rw_new.astype(dt_),
                 shift_new.astype(dt_), hg_new.astype(dt_))


def _swiglu(h, wg, wu, wd):
    return (jax.nn.silu(h @ wg) * (h @ wu)) @ wd


def _trunk(x, states, segs, p):
    new = [[] for _ in range(5)]
    for l in range(DEPTH):
        st = [s[l] for s in states]
        m, ns = _mixer(_rmsnorm(x, p['norm1_w'][l]), l, *st, segs, p)
        x = x + m
        x = x + _swiglu(_rmsnorm(x, p['norm2_w'][l]), p['w_gate'][l], p['w_up'][l], p['w_down'][l])
        for lst, s in zip(new, ns):
            lst.append(s)
    return _rmsnorm(x, p['final_norm_w']), [jnp.stack(lst) for lst in new]


def setup_inputs(seed: int = 0) -> dict:
    k = jax.random.split(jax.random.key(seed), 40)

    def nrm(i, shape, scale):
        return jax.random.normal(k[i], shape, jnp.float32) * scale

    L = DEPTH
    dt0 = jnp.exp(jax.random.uniform(k[12], (L, SSD_HEADS), jnp.float32, math.log(1e-3), math.log(1e-1)))
    return {
        'x_prompt': nrm(0, (BATCH, SEQ, D_MODEL), 1.0),
        'x_sample': nrm(1, (DEC_BATCH, DEC_SEQ, D_MODEL), 1.0),
        'state_ssm': nrm(2, (L, DEC_BATCH, SSD_HEADS, SSD_HEAD_DIM, SSD_STATE), 0.1),
        'state_conv': nrm(3, (L, DEC_BATCH, SSD_CONV - 1, SSD_CONV_DIM), 1.0),
        'state_rwkv': nrm(4, (L, DEC_BATCH, RW_HEADS, RW_HEAD_DIM, RW_HEAD_DIM), 0.1),
        'state_shift': nrm(5, (L, DEC_BATCH, RW_IN), 1.0),
        'state_hgrn': nrm(6, (L, DEC_BATCH, HG_HEADS, HG_KEY, HG_VAL), 0.1),
        'meta_tokens': nrm(7, (N_META, D_MODEL), 1.0),
        'norm1_w': 1.0 + nrm(8, (L, D_MODEL), 0.05),
        'w_in': nrm(9, (L, D_MODEL, N_IN), D_MODEL ** -0.5),
        'conv_w': nrm(10, (L, SSD_CONV, SSD_CONV_DIM), 0.5),
        'conv_b': nrm(11, (L, SSD_CONV_DIM), 0.05),
        'dt_bias': dt0 + jnp.log(-jnp.expm1(-dt0)),
        'a_log': jnp.log(jax.random.uniform(k[13], (L, SSD_HEADS), jnp.float32, 1.0, 16.0)),
        'd_skip': 1.0 + nrm(14, (L, SSD_HEADS), 0.1),
        'ssd_norm_w': 1.0 + nrm(15, (L, SSD_INNER), 0.05),
        'rw_mu': jax.random.uniform(k[16], (L, RW_IN), jnp.float32),
        'rw_w0': jax.random.uniform(k[17], (L, RW_DIM), jnp.float32, -6.0, 1.0),
        'rw_w2': nrm(18, (L, RW_DECAY_LORA, RW_DIM), 0.1),
        'rw_a0': nrm(19, (L, RW_DIM), 0.1),
        'rw_a2': nrm(20, (L, RW_A_LORA, RW_DIM), 0.1),
        'rw_g2': nrm(21, (L, RW_GATE_LORA, RW_DIM), RW_GATE_LORA ** -0.5),
        'rw_kk': 1.0 + nrm(22, (L, RW_DIM), 0.1),
        'rw_ka': 1.0 + nrm(23, (L, RW_DIM), 0.1),
        'rw_rk': nrm(24, (L, RW_DIM), 0.1),
        'rw_lnx_w': 1.0 + nrm(25, (L, RW_DIM), 0.05),
        'rw_lnx_b': nrm(26, (L, RW_DIM), 0.02),
        'hg_lb_logits': nrm(27, (L, HG_KDIM), 0.5),
        'hg_norm_w': 1.0 + nrm(28, (L, HG_VDIM), 0.05),
        'w_out': nrm(29, (L, D_MIX, D_MODEL), D_MIX ** -0.5),
        'norm2_w': 1.0 + nrm(30, (L, D_MODEL), 0.05),
        'w_gate': nrm(31, (L, D_MODEL, D_FF), D_MODEL ** -0.5),
        'w_up': nrm(32, (L, D_MODEL, D_FF), D_MODEL ** -0.5),
        'w_down': nrm(33, (L, D_FF, D_MODEL), D_FF ** -0.5),
        'final_norm_w': 1.0 + nrm(34, (D_MODEL,), 0.05),
    }


def reference(x_prompt, x_sample, state_ssm, state_conv, state_rwkv, state_shift, state_hgrn,
              meta_tokens, norm1_w, w_in, conv_w, conv_b, dt_bias, a_log, d_skip, ssd_norm_w,
              rw_mu, rw_w0, rw_w2, rw_a0, rw_a2, rw_g2, rw_kk, rw_ka, rw_rk, rw_lnx_w, rw_lnx_b,
              hg_lb_logits, hg_norm_w, w_out, norm2_w, w_gate, w_up, w_down, final_norm_w):
    p = {'norm1_w': norm1_w, 'w_in': w_in, 'conv_w': conv_w, 'conv_b': conv_b, 'dt_bias': dt_bias,
         'a_log': a_log, 'd_skip': d_skip, 'ssd_norm_w': ssd_norm_w, 'rw_mu': rw_mu, 'rw_w0': rw_w0,
         'rw_w2': rw_w2, 'rw_a0': rw_a0, 'rw_a2': rw_a2, 'rw_g2': rw_g2, 'rw_kk': rw_kk, 'rw_ka': rw_ka,
         'rw_rk': rw_rk, 'rw_lnx_w': rw_lnx_w, 'rw_lnx_b': rw_lnx_b, 'hg_lb_logits': hg_lb_logits,
         'hg_norm_w': hg_norm_w, 'w_out': w_out, 'norm2_w': norm2_w, 'w_gate': w_gate, 'w_up': w_up,
         'w_down': w_down, 'final_norm_w': final_norm_w}

    b, T = x_prompt.shape[:2]
    dt_ = x_prompt.dtype
    meta = jnp.broadcast_to(meta_tokens[None].astype(dt_), (b, N_META, D_MODEL))
    xp = jnp.concatenate([meta, x_prompt], axis=1)
    zero_states = [jnp.zeros((DEPTH, b) + s.shape[2:], dt_)
                   for s in (state_ssm, state_conv, state_rwkv, state_shift, state_hgrn)]
    yp, ps = _trunk(xp, zero_states, ((N_META, N_META), (T, CHUNK)), p)
    y_prompt = yp[:, N_META:]
    p_ssm, p_conv, p_rwkv, p_shift, p_hgrn = ps

    ds = x_sample.shape[1]
    y_sample, ss = _trunk(x_sample, (state_ssm, state_conv, state_rwkv, state_shift, state_hgrn),
                          ((ds, ds),), p)
    s_ssm, s_conv, s_rwkv, s_shift, s_hgrn = ss
    return (y_prompt, y_sample, p_ssm, p_conv, p_rwkv, p_shift, p_hgrn,
            s_ssm, s_conv, s_rwkv, s_shift, s_hgrn)
```

```python
import numpy as np
from contextlib import ExitStack
import concourse.bass as bass
import concourse.mybir as mybir
from concourse.bass_utils import run_bass_kernel_spmd

F32 = mybir.dt.float32
BF16 = mybir.dt.bfloat16
AF = mybir.ActivationFunctionType
ALU = mybir.AluOpType

D = 1024
DFF = 2816
NIN = 5384
NCORES = 8
DQ = 'sp'
TRUNK_BF16 = True
import os
KDBG = os.environ.get('KDBG', '')
STAGES = {'states', 'proj', 'ssd', 'rwkv', 'hgrn', 'oproj', 'ffn'}
ENGS = ['pe', 'act', 'dve', 'pool', 'sp']


class Dep:
    __slots__ = ('lw', 'rd')

    def __init__(self):
        self.lw = None
        self.rd = []


class Prog:
    def __init__(self, nc, es):
        self.nc, self.es = nc, es
        self.q = {e: [] for e in ENGS}
        self.deps = {}
        self.dsem = {}
        self.sem = {e: es.enter_context(nc.semaphore("q_" + e)) for e in ENGS}
        self.tensors = {}
        self.psum_names = set()

    def sb(self, name, shape, dtype=F32):
        name = "s_" + name
        t = self.es.enter_context(self.nc.sbuf_tensor(name, list(shape), dtype))
        self.deps[name] = Dep()
        return t

    def psum(self, name, shape):
        t = self.es.enter_context(self.nc.psum_tensor(name, list(shape), F32))
        self.deps[name] = Dep()
        self.psum_names.add(name)
        return t

    def _dep(self, ap):
        nm = ap.tensor.name
        return self.deps.get(nm)

    def _collect(self, eng, outs, ins):
        w = []
        for ap in ins:
            d = self._dep(ap)
            if d is not None and d.lw is not None:
                w.append(d.lw)
            if d is not None and ap.tensor.name in self.psum_names:
                for r in d.rd:
                    if not (r[0] == 'e' and r[1] == eng):
                        w.append(r)
        for ap in outs:
            d = self._dep(ap)
            if d is None:
                continue
            if d.lw is not None:
                w.append(d.lw)
            for r in d.rd:
                w.append(r)
        if eng == 'pe':
            w = [x for x in w if not (x[0] == 'e' and x[1] == 'pe')]
        return w

    def _commit(self, ev, outs, ins):
        for ap in ins:
            d = self._dep(ap)
            if d is not None:
                d.rd.append(ev)
        for ap in outs:
            d = self._dep(ap)
            if d is not None:
                d.lw = ev
                d.rd = []

    def op(self, eng, fn, outs, ins):
        w = self._collect(eng, outs, ins)
        idx = len(self.q[eng])
        self.q[eng].append(dict(fn=fn, waits=w, kind='c'))
        self._commit(('e', eng, idx), outs, ins)

    def dma(self, eng, out, in_, semname=None, **kw):
        outs, ins = [out], [in_]
        w = self._collect(eng, outs, ins)
        if semname is None:
            d = self._dep(out)
            semname = out.tensor.name if d is not None else in_.tensor.name
        if semname not in self.dsem:
            self.dsem[semname] = [self.es.enter_context(self.nc.semaphore("d_" + semname)), 0]
        s = self.dsem[semname]
        s[1] += 16
        self.q[eng].append(dict(fn=lambda e: e.dma_start(out=out, in_=in_, **kw), waits=w, kind='d', sem=semname))
        self._commit(('d', semname, s[1]), outs, ins)

    def emit(self):
        nc = self.nc
        targets = {e: set() for e in ENGS}
        for e in ENGS:
            for ins in self.q[e]:
                for d in ins['waits']:
                    if d[0] == 'e':
                        targets[d[1]].add(d[2])
        semval = {}
        for e in ENGS:
            c = 0
            vals = []
            for i in range(len(self.q[e])):
                if i in targets[e]:
                    c += 1
                vals.append(c)
            semval[e] = vals
        final = [(self.dsem[k][0], self.dsem[k][1]) for k in self.dsem]

        def body_for(e):
            def body(eng):
                waited = {}
                for i, ins in enumerate(self.q[e]):
                    need = {}
                    for d in ins['waits']:
                        if d[0] == 'e':
                            key = ('e', d[1])
                            val = semval[d[1]][d[2]]
                        else:
                            key = ('d', d[1])
                            val = d[2]
                        if val > need.get(key, 0):
                            need[key] = val
                    for key, val in need.items():
                        if waited.get(key, 0) >= val:
                            continue
                        waited[key] = val
                        h = self.sem[key[1]] if key[0] == 'e' else self.dsem[key[1]][0]
                        eng.wait_ge(h, val)
                    r = ins['fn'](eng)
                    if ins['kind'] == 'c':
                        if i in targets[e]:
                            r.then_inc(self.sem[e], 1)
                    else:
                        r.then_inc(self.dsem[ins['sem']][0], 16)
                if e == 'sp':
                    for h, v in final:
                        eng.wait_ge(h, v)
            return body

        with nc.Block() as block:
            block.sync(body_for('sp'))
            block.tensor(body_for('pe'))
            block.scalar(body_for('act'))
            block.vector(body_for('dve'))
            block.gpsimd(body_for('pool'))


def _cols(v):
    v = np.asarray(v, np.float32)
    return v.reshape(-1, 128).T


PP_LAYOUT = {}


def _pack_params(inp, l):
    parts = []
    off = 0

    def add(name, arr):
        nonlocal off
        arr = np.ascontiguousarray(arr, dtype=np.float32)
        PP_LAYOUT[name] = (off, arr.shape[1])
        off += arr.shape[1]
        parts.append(arr)

    add('n1', _cols(inp['norm1_w'][l]))
    add('n2', _cols(inp['norm2_w'][l]))
    cw = inp['conv_w'][l]
    add('cw', np.stack([cw[j].reshape(8, 128).T for j in range(4)], axis=2).reshape(128, 32))
    add('cb', _cols(inp['conv_b'][l]))
    add('dsk', _cols(np.repeat(inp['d_skip'][l], 64)))
    add('snw', _cols(inp['ssd_norm_w'][l]))
    add('dtb', np.tile(inp['dt_bias'][l][None, :], (128, 1)))
    add('alog', np.tile(inp['a_log'][l][None, :], (128, 1)))
    add('mu', _cols(inp['rw_mu'][l]))
    add('w0', _cols(inp['rw_w0'][l]))
    add('a0', _cols(inp['rw_a0'][l]))
    add('kk', _cols(inp['rw_kk'][l]))
    add('ka', _cols(inp['rw_ka'][l]))
    add('rk', _cols(inp['rw_rk'][l]))
    add('lnw', _cols(inp['rw_lnx_w'][l]))
    add('lnb', _cols(inp['rw_lnx_b'][l]))
    add('lg0', _cols(inp['hg_lb_logits'][0]))
    add('lg1', _cols(inp['hg_lb_logits'][1]))
    add('hnw', _cols(inp['hg_norm_w'][l]))
    add('fin', _cols(inp['final_norm_w']))
    PP_LAYOUT['_n'] = off
    return np.concatenate(parts, axis=1)


CC = {}


def _consts():
    parts = []
    off = 0

    def add(name, arr):
        nonlocal off
        arr = np.ascontiguousarray(arr, dtype=np.float32)
        assert arr.shape[0] == 128
        CC[name] = (off, arr.shape[1])
        off += arr.shape[1]
        parts.append(arr)

    i = np.arange(128)
    add('ident', np.eye(128))
    add('ones', np.ones((128, 128)))
    bo = np.zeros((128, 128))
    bo[:64, :64] = 1
    bo[64:, 64:] = 1
    add('bones', bo)
    u1 = (i[:, None] <= i[None, :]).astype(np.float32)
    u0 = (i[:, None] < i[None, :]).astype(np.float32)
    add('u1', u1)
    add('u0', u0)
    add('l0', u0.T)
    add('mneg', (u1 - 1.0) * 30000.0)
    t = np.arange(256)
    add('rm64', np.tile(((t % 64) != 0).astype(np.float32)[None, :], (128, 1)))
    add('rm32', np.tile(((t % 32) != 0).astype(np.float32)[None, :], (128, 1)))
    sc = np.zeros((128, 8), np.float32)
    sc[:, 0] = 1e-6
    sc[:, 1] = 1.0
    sc[:, 2] = 64e-5
    sc[:, 3] = 1e-12
    sc[:, 4] = -0.5
    add('sc', sc)
    CC['_n'] = off
    return np.concatenate(parts, axis=1)


def build(SEQ, NT=int(os.environ.get('KNT', '256')), DEPTH=2):
    nc = bass.Bass("TRN2", target_bir_lowering=False)
    TDT = BF16 if TRUNK_BF16 else F32
    if TRUNK_BF16:
        nc.allow_low_precision("trunk projections use bf16 operands with fp32 accumulation")
    es = ExitStack()
    P = Prog(nc, es)
    NPP = PP_LAYOUT['_n']
    NCC = CC['_n']

    def din(name, shape):
        return nc.dram_tensor("i_" + name, list(shape), F32, kind="ExternalInput").ap()

    def dout(name, shape):
        return nc.dram_tensor("r_" + name, list(shape), F32, kind="ExternalOutput").ap()

    xp = din("xp", [SEQ, D])
    xs_in = din("xs", [64, D])
    meta = din("meta", [16, D])
    st_ssm = din("st_ssm", [DEPTH, 8, 64, 128])
    st_conv = din("st_conv", [DEPTH, 3, 1024])
    st_rwkv = din("st_rwkv", [DEPTH, 8, 64, 64])
    st_shift = din("st_shift", [DEPTH, 1792])
    st_hgrn = din("st_hgrn", [DEPTH, 4, 128, 128])
    w_in = din("w_in", [DEPTH, D, NIN])
    w_out = din("w_out", [DEPTH, 1536, D])
    w_gate = din("w_gate", [DEPTH, D, DFF])
    w_up = din("w_up", [DEPTH, D, DFF])
    w_down = din("w_down", [DEPTH, DFF, D])
    wa2_d = din("wa2", [DEPTH, 2, 128, 512])
    g2_d = din("g2", [DEPTH, 128, 512])
    pp_d = din("pp", [DEPTH, 128, NPP])
    cc_d = din("cc", [128, NCC])

    y_p = dout("y_p", [SEQ, D])
    y_s = dout("y_s", [64, D])
    o_ssm = dout("o_ssm", [2, DEPTH, 8, 64, 128])
    o_conv = dout("o_conv", [2, DEPTH, 3, 1024])
    o_rwkv = dout("o_rwkv", [2, DEPTH, 8, 64, 64])
    o_shift = dout("o_shift", [2, DEPTH, 1792])
    o_hgrn = dout("o_hgrn", [2, DEPTH, 4, 128, 128])

    cc = P.sb("cc", [128, NCC])
    pp = [P.sb("pp%d" % l, [128, NPP]) for l in range(DEPTH)]
    dv = [P.sb("dv%d" % l, [128, 40]) for l in range(DEPTH)]
    wa2 = [P.sb("wa2_%d" % l, [128, 2, 512]) for l in range(DEPTH)]
    g2 = [P.sb("g2_%d" % l, [128, 512]) for l in range(DEPTH)]
    wdt32 = [P.sb("wdt%d" % l, [128, 8, 8]) for l in range(DEPTH)]
    wdt = [P.sb("wdtb%d" % l, [128, 8, 8], TDT) for l in range(DEPTH)] if TRUNK_BF16 else wdt32

    xT = [P.sb("xT%d" % k, [128, NT]) for k in range(8)]
    hT = [P.sb("hT%d" % k, [128, NT], TDT) for k in range(8)]
    zT = P.sb("zT", [128, 4, NT])
    xsT = P.sb("xsT", [128, 4, NT + 3])
    bcT = P.sb("bcT", [128, 4, NT + 3])
    rT = P.sb("rT", [128, 4, NT + 1])
    kT = P.sb("kT", [128, 4, NT + 1])
    vT = P.sb("vT", [128, 4, NT + 1])
    x12 = P.sb("x12", [128, NT + 1])
    x13 = P.sb("x13", [128, NT + 1])
    qT = P.sb("qT", [128, 4, NT])
    fzT = P.sb("fzT", [128, 4, NT])
    ivT = P.sb("ivT", [128, 4, NT])
    ggT = P.sb("ggT", [128, 4, NT])
    Y = [P.sb("Y%d" % j, [128, NT], TDT) for j in range(12)]
    A16 = P.sb("A16", [128, 22, NT], TDT) if TRUNK_BF16 else None
    NSL = 12
    SL = [P.sb("SL%d" % j, [128, 4, NT]) for j in range(NSL)]
    sc1 = [P.sb("sc1_%d" % j, [128, NT]) for j in range(3)]
    NW = 4
    wsl = [P.sb("wsl%d" % j, [128, 512]) for j in range(NW)]
    NWB = 4
    wbl = [P.sb("wbl%d" % j, [128, 512], TDT) for j in range(NWB)] if TRUNK_BF16 else None
    xin = [P.sb("xin%d" % j, [128, D]) for j in range(1)]

    hstg = [[P.sb("hst%d_%d" % (l, g), [128, 256]) for g in range(2)] for l in range(DEPTH)]
    hist = [P.sb("hist%d" % l, [128, 8, 3]) for l in range(DEPTH)]
    prev = [P.sb("prev%d" % l, [128, 14]) for l in range(DEPTH)]
    rstg = [[P.sb("rst%d_%d" % (l, g), [128, 2, 64]) for g in range(2)] for l in range(DEPTH)]
    gstg = [[P.sb("gst%d_%d" % (l, g), [128, 2, 128]) for g in range(2)] for l in range(DEPTH)]
    stmp = P.sb("stmp", [128, 4, 128])
    rtmp = P.sb("rtmp", [64, 4, 2, 64])

    tk = {}
    for nm, shp in [('wlc', [128, 4, 4]), ('slc', [128, 4, 8]), ('rs', [128, NT])]:
        tk[nm] = P.sb("tk_" + nm, shp)
    tk['rs2'] = tk['rs']
    tkg = []
    for g in range(2):
        T = {}
        for nm, shp in [('t1', [64, 4]), ('dt', [64, 4]), ('loga', [64, 4]), ('cum', [64, 8]), ('ecum', [64, 4]),
                        ('dec', [64, 4]), ('et128', [128, 4]), ('Btok', [64, 128]),
                        ('AkT', [64, 4, 64]), ('RbT', [64, 4, 64]), ('RkT', [64, 4, 64]),
                        ('Pa', [64, 4, 64]), ('PaT', [64, 4, 64]), ('Pb', [64, 4, 64]), ('PbT', [64, 4, 64]),
                        ('Xa', [64, 4, 64]), ('Xb', [64, 4, 64]), ('Vtok', [64, 4, 64]), ('Otok', [64, 4, 64]),
                        ('bbt', [64, 2, 128]), ('kbt', [64, 2, 128]), ('stm', [128, 2, 64]),
                        ('hMT', [32, 2, 32]), ('hvt', [32, 2, 128]), ('hkt', [32, 2, 128])]:
            T[nm] = P.sb("tk%d_%s" % (g, nm), shp)
        for a_, b_ in [('lgB', 'Pa'), ('Dm', 'PaT'), ('LT', 'Pb'), ('MT', 'PbT'), ('xdt', 'Xa'), ('xdec', 'Xb'),
                       ('ytok', 'Otok')]:
            T[a_] = T[b_]
        tkg.append(T)

    PSB = [P.psum("ps%d" % j, [128, 512]) for j in range(8)]
    psi = [0]

    def ps():
        t = PSB[psi[0] % 8]
        psi[0] += 1
        return t

    wi = [0]
    wbi = [0]

    def wslot():
        t = wsl[wi[0] % NW]
        wi[0] += 1
        return t

    def isap(x):
        return not isinstance(x, (int, float))

    def tt(eng, out, a, b, op):
        P.op(eng, lambda e: e.tensor_tensor(out=out, in0=a, in1=b, op=op), [out], [a, b])

    def ts(eng, out, a, s1, op0, s2=None, op1=None):
        ins = [a] + [s for s in (s1, s2) if s is not None and isap(s)]
        if op1 is None:
            P.op(eng, lambda e: e.tensor_scalar(out=out, in0=a, scalar1=s1, scalar2=None, op0=op0), [out], ins)
        else:
            P.op(eng, lambda e: e.tensor_scalar(out=out, in0=a, scalar1=s1, scalar2=s2, op0=op0, op1=op1), [out], ins)

    def stt(eng, out, a, s, b, op0, op1):
        eng = 'dve'
        ins = [a, b] + ([s] if isap(s) else [])
        P.op(eng, lambda e: e.scalar_tensor_tensor(out=out, in0=a, scalar=s, in1=b, op0=op0, op1=op1), [out], ins)

    def act(out, in_, func, bias=None, scale=1.0):
        ins = [in_] + ([bias] if bias is not None else []) + ([scale] if isap(scale) else [])
        if bias is None:
            P.op('act', lambda e: e.activation(out=out, in_=in_, func=func, scale=scale), [out], ins)
        else:
            P.op('act', lambda e: e.activation(out=out, in_=in_, func=func, bias=bias, scale=scale), [out], ins)

    def cp(eng, out, in_):
        if eng == 'act':
            P.op('act', lambda e: e.copy(out=out, in_=in_), [out], [in_])
        else:
            P.op(eng, lambda e: e.tensor_copy(out=out, in_=in_), [out], [in_])

    def recip(out, in_):
        P.op('dve', lambda e: e.reciprocal(out=out, in_=in_), [out], [in_])

    def mm(out, lhsT, rhs, start=True, stop=True):
        P.op('pe', lambda e: e.matmul(out, lhsT, rhs, start=start, stop=stop), [out], [lhsT, rhs])

    def tr(out, in_, n_in_part):
        idn = cc[0:n_in_part, CC['ident'][0]:CC['ident'][0] + n_in_part]
        P.op('pe', lambda e: e.transpose(out, in_, idn), [out], [in_, idn])

    def scan(out, d0, d1):
        P.op('dve', lambda e: e.tensor_tensor_scan(out=out, data0=d0, data1=d1, initial=0.0, op0=ALU.mult, op1=ALU.add),
             [out], [d0, d1])

    def memset(eng, out, val):
        P.op(eng, lambda e: e.memset(out, val), [out], [])

    def C(name, rows=128, c0=0, c1=None):
        o, n = CC[name]
        if c1 is None:
            c1 = n
        return cc[0:rows, o + c0:o + c1]

    def SC(i, rows=128):
        o = CC['sc'][0]
        return cc[0:rows, o + i:o + i + 1]

    def PPc(l, name, i=0, n=1, rows=128):
        o = PP_LAYOUT[name][0]
        return pp[l][0:rows, o + i:o + i + n]

    def bc(ap, shape, axis):
        return ap.unsqueeze(axis).broadcast_to(list(shape))

    eng_rr = [0]

    def ve():
        eng_rr[0] += 1
        return 'dve' if eng_rr[0] % 2 else 'pool'

    def rstd(out, in_, scale, eps_ap):
        act(out, in_, AF.Ln, bias=eps_ap, scale=scale)
        act(out, out, AF.Exp, scale=-0.5)

    def sigmoid(out, in_, scale=1.0, nbias=None):
        act(out, in_, AF.Exp, bias=nbias, scale=-scale)
        ts('pool' if True else 'dve', out, out, 1.0, ALU.add)
        recip(out, out)

    P.dma('sp', cc[:, :], cc_d[:, :])
    for l in range(DEPTH):
        P.dma('sp', pp[l][:, :], pp_d[l, :, :])
        P.dma('sp', wa2[l][:, :, :], wa2_d[l].rearrange("t p c -> p t c"))
        P.dma('sp', g2[l][:, :], g2_d[l, :, :])
        if 'a' not in KDBG:
            P.dma('sp', wdt32[l][:, :, :], w_in[l, :, 1536:1544].rearrange("(k p) c -> p k c", p=128),
                  allow_slow_non_contiguous=True)
            if TRUNK_BF16:
                P.op('dve', lambda e, l=l: e.tensor_copy(out=wdt[l][:, :, :], in_=wdt32[l][:, :, :]),
                     [wdt[l][:, :, :]], [wdt32[l][:, :, :]])
    for l in range(DEPTH if 'd' not in KDBG else 0):
        ts('dve', dv[l][:, 0:4], PPc(l, 'w0', 0, 4), -1.0, ALU.mult)
        ts('dve', dv[l][:, 4:8], PPc(l, 'a0', 0, 4), -1.0, ALU.mult)
        act(dv[l][:, 8:16], PPc(l, 'alog', 0, 8), AF.Exp)
        ts('dve', dv[l][:, 8:16], dv[l][:, 8:16], -1.0, ALU.mult)
        if l == 0:
            memset('dve', dv[l][:, 16:20], 0.0)
        else:
            tt('dve', dv[l][:, 16:20], PPc(l, 'lg0', 0, 4), PPc(l, 'lg1', 0, 4), ALU.subtract)
            act(dv[l][:, 16:20], dv[l][:, 16:20], AF.Exp)
            ts('dve', dv[l][:, 16:20], dv[l][:, 16:20], 1.0, ALU.add)
            recip(dv[l][:, 16:20], dv[l][:, 16:20])
        ts('dve', dv[l][:, 20:24], dv[l][:, 16:20], -1.0, ALU.mult, 1.0, ALU.add)

    def big_proj(wsrc, nk, c0, ntile, rhs_list, n, evac, width=None):
        banks = [ps() for _ in range(ntile)]
        wcols = ntile * 128 if width is None else width
        for k in range(nk):
            wt = wslot()
            P.dma('sp', wt[:, 0:wcols], wsrc[k * 128:(k + 1) * 128, c0:c0 + wcols])
            if TRUNK_BF16:
                wb = wbl[wbi[0] % NWB]
                wbi[0] += 1
                cp('act' if wbi[0] % 2 else 'pool', wb[:, 0:wcols], wt[:, 0:wcols])
                wt = wb
            for j in range(ntile):
                mm(banks[j][:, 0:n], wt[:, j * 128:(j + 1) * 128], rhs_list[k], start=(k == 0), stop=(k == nk - 1))
        for j in range(ntile):
            evac(j, banks[j][:, 0:n])

    def rmsnorm(l, pname, n, src, dst):
        pss = ps()
        for k in range(8):
            s = sc1[k % 2]
            tt(ve(), s[:, 0:n], src[k][:, 0:n], src[k][:, 0:n], ALU.mult)
            mm(pss[:, 0:n], C('ones'), s[:, 0:n], start=(k == 0), stop=(k == 7))
        rstd(tk['rs'][:, 0:n], pss[:, 0:n], 1.0 / D, SC(0))
        for k in range(8):
            stt(ve(), dst[k][:, 0:n], src[k][:, 0:n], PPc(l, pname, k), tk['rs'][:, 0:n], ALU.mult, ALU.mult)

    def layer(l, n, L, Lh):
        nch = n // L
        nchh = n // Lh
        W_in = w_in[l]
        rmsnorm(l, 'n1', n, xT, hT)
        hl = [hT[k][:, 0:n] for k in range(8)]

        evc = [0]

        def ev_to(dst_fn):
            def f(j, psap):
                evc[0] += 1
                cp('act' if evc[0] % 2 else 'dve', dst_fn(j), psap)
            return f
        big_proj(W_in, 8, 0, 4, hl, n, ev_to(lambda j: zT[:, j, 0:n]))
        big_proj(W_in, 8, 512, 4, hl, n, ev_to(lambda j: xsT[:, j, 3:3 + n]))
        big_proj(W_in, 8, 1024, 4, hl, n, ev_to(lambda j: bcT[:, j, 3:3 + n]))
        big_proj(W_in, 8, 1544, 4, hl, n, ev_to(lambda j: rT[:, j, 1:1 + n]))
        big_proj(W_in, 8, 1544 + 512, 4, hl, n, ev_to(lambda j: kT[:, j, 1:1 + n]))
        big_proj(W_in, 8, 1544 + 1024, 4, hl, n, ev_to(lambda j: vT[:, j, 1:1 + n]))
        big_proj(W_in, 8, 1544 + 1536, 2, hl, n, ev_to(lambda j: (x12 if j == 0 else x13)[:, 1:1 + n]))
        big_proj(W_in, 8, 3336, 4, hl, n, ev_to(lambda j: qT[:, j, 0:n]))
        big_proj(W_in, 8, 3336 + 512, 4, hl, n, ev_to(lambda j: fzT[:, j, 0:n]))
        big_proj(W_in, 8, 3336 + 1024, 4, hl, n, ev_to(lambda j: ivT[:, j, 0:n]))
        big_proj(W_in, 8, 3336 + 1536, 4, hl, n, ev_to(lambda j: ggT[:, j, 0:n]))

        if 'proj' not in STAGES:
            return
        ssd_pro(l, n, L, nch)
        hgrn_pro(l, n, Lh, nchh)
        interleave(ssd_chain(l, n, L, nch, 0), hgrn_chain(l, n, Lh, nchh, 0),
                   ssd_chain(l, n, L, nch, 1), hgrn_chain(l, n, Lh, nchh, 1))
        ssd_epi(l, n, L, nch)
        hgrn_epi(l, n, Lh, nchh)
        rwkv_pro(l, n, L, nch)
        interleave(rwkv_chain(l, n, L, nch, 0), rwkv_chain(l, n, L, nch, 1))
        rwkv_epi(l, n, L, nch)
        if 'oproj' not in STAGES:
            return

        yl = [Y[j][:, 0:n] for j in range(12)]
        for g in range(2):
            big_proj(w_out[l], 12, g * 512, 4, yl, n,
                     lambda j, psap, g=g: tt('dve', xT[4 * g + j][:, 0:n], xT[4 * g + j][:, 0:n], psap, ALU.add))

        if 'ffn' not in STAGES:
            return
        rmsnorm(l, 'n2', n, xT, hT)
        hl = [hT[k][:, 0:n] for k in range(8)]
        f0 = 0
        while f0 < 22:
            nt_ = min(4, 22 - f0)
            gb = {}

            def ev_gate(j, psap, gb=gb):
                gb[j] = psap
            big_proj(w_gate[l], 8, f0 * 128, nt_, hl, n, ev_gate)

            def ev_up(j, psap, gb=gb, f0=f0):
                f = f0 + j
                a_ = A16[:, f, 0:n] if TRUNK_BF16 else SL[f // 4][:, f % 4, 0:n]
                s = sc1[j % 2]
                sigmoid(s[:, 0:n], gb[j])
                tt('dve', s[:, 0:n], s[:, 0:n], gb[j], ALU.mult)
                tt('dve', a_, s[:, 0:n], psap, ALU.mult)
            big_proj(w_up[l], 8, f0 * 128, nt_, hl, n, ev_up)
            f0 += nt_
        al = [(A16[:, f, 0:n] if TRUNK_BF16 else SL[f // 4][:, f % 4, 0:n]) for f in range(22)]
        for g in range(2):
            big_proj(w_down[l], 22, g * 512, 4, al, n,
                     lambda j, psap, g=g: tt('dve', xT[4 * g + j][:, 0:n], xT[4 * g + j][:, 0:n], psap, ALU.add))

    def ssd_pro(l, n, L, nch):
        cp('pool', xsT[:, :, 0:3], hist[l][:, 0:4, :])
        cp('pool', bcT[:, :, 0:3], hist[l][:, 4:8, :])
        cwo = PP_LAYOUT['cw'][0]
        for i in range(8):
            src = xsT if i < 4 else bcT
            ii = i % 4
            a_ = sc1[i % 2]
            e1 = ve()
            ts(e1, a_[:, 0:n], src[:, ii, 0:n], pp[l][:, cwo + i * 4:cwo + i * 4 + 1], ALU.mult)
            for j in range(1, 4):
                stt(e1, a_[:, 0:n], src[:, ii, j:j + n], pp[l][:, cwo + i * 4 + j:cwo + i * 4 + j + 1], a_[:, 0:n],
                    ALU.mult, ALU.add)
            cp('pool', hist[l][:, i, :], src[:, ii, n:n + 3])
            ts('dve', a_[:, 0:n], a_[:, 0:n], PPc(l, 'cb', i), ALU.add)
            s2 = sc1[2]
            sigmoid(s2[:, 0:n], a_[:, 0:n])
            tt('dve', src[:, ii, 3:3 + n], a_[:, 0:n], s2[:, 0:n], ALU.mult)
        for i in range(4):
            s2 = sc1[i % 2]
            sigmoid(s2[:, 0:n], zT[:, i, 0:n])
            tt('dve', zT[:, i, 0:n], zT[:, i, 0:n], s2[:, 0:n], ALU.mult)

    def ssd_chain(l, n, L, nch, g):
        T = tkg[g]
        yss = SL[8 + g]
        hs4 = slice(4 * g, 4 * g + 4)
        for c in range(nch):
            cs = slice(c * L, (c + 1) * L)
            cs3 = slice(3 + c * L, 3 + (c + 1) * L)
            pdt = ps()
            for k in range(8):
                mm(pdt[0:L, 0:4], hT[k][:, cs], wdt[l][:, k, hs4], start=(k == 0), stop=(k == 7))
            tt('dve', T['t1'][0:L, :], pdt[0:L, 0:4], PPc(l, 'dtb', 4 * g, 4, rows=L), ALU.add)
            act(T['t1'][0:L, :], T['t1'][0:L, :], AF.Exp)
            act(T['dt'][0:L, :], T['t1'][0:L, :], AF.Ln, bias=SC(1, L))
            tt('dve', T['loga'][0:L, :], T['dt'][0:L, :], dv[l][0:L, 8 + 4 * g:12 + 4 * g], ALU.mult)
            yield
            pc = ps()
            mm(pc[0:L, 0:4], C('u1', L, 0, L), T['loga'][0:L, :])
            mm(pc[0:L, 4:8], C('ones', L, 0, L), T['loga'][0:L, :])
            mm(pc[:, 8:12], C('ones', L, 0, 128), T['loga'][0:L, :])
            cp('pool', T['lgB'][0:L, :, 0:L], bc(T['loga'][0:L, :], [L, 4, L], 2))
            cp('dve', T['cum'][0:L, :], pc[0:L, 0:8])
            act(T['et128'][:, :], pc[:, 8:12], AF.Exp)
            act(T['ecum'][0:L, :], T['cum'][0:L, 0:4], AF.Exp)
            tt('dve', T['dec'][0:L, :], T['cum'][0:L, 4:8], T['cum'][0:L, 0:4], ALU.subtract)
            act(T['dec'][0:L, :], T['dec'][0:L, :], AF.Exp)
            yield
            pcb = ps()
            for h in range(4):
                mm(pcb[0:L, h * L:(h + 1) * L], T['lgB'][0:L, h, 0:L], C('u1', L, 0, L))
            pg = ps()
            mm(pg[0:L, 0:L], bcT[:, g, cs3], bcT[:, 2 + g, cs3])
            px = ps()
            for i in range(2):
                tr(px[0:L, i * 128:(i + 1) * 128], xsT[:, 2 * g + i, cs3], 128)
            pb = ps()
            tr(pb[0:L, 0:128], bcT[:, g, cs3], 128)
            pcb3 = pcb[0:L, 0:4 * L].rearrange("p (h i) -> p h i", h=4)
            tt('dve', T['Dm'][0:L, :, 0:L], pcb3, bc(C('mneg', L, 0, L), [L, 4, L], 1), ALU.add)
            tt('pool', T['Dm'][0:L, :, 0:L], T['Dm'][0:L, :, 0:L], bc(T['cum'][0:L, 0:4], [L, 4, L], 2), ALU.subtract)
            act(T['LT'][0:L, :, 0:L], T['Dm'][0:L, :, 0:L], AF.Exp)
            px3 = px[0:L, 0:256].rearrange("p (h d) -> p h d", h=4)
            tt('dve', T['xdt'][0:L, :, :], px3, bc(T['dt'][0:L, :], [L, 4, 64], 2), ALU.mult)
            tt('pool', T['xdec'][0:L, :, :], T['xdt'][0:L, :, :], bc(T['dec'][0:L, :], [L, 4, 64], 2), ALU.mult)
            cp('act', T['Btok'][0:L, :], pb[0:L, 0:128])
            tt('dve', T['MT'][0:L, :, 0:L], T['LT'][0:L, :, 0:L], bc(pg[0:L, 0:L], [L, 4, L], 1), ALU.mult)
            yield
            py = ps()
            for h in range(4):
                mm(py[0:L, h * 64:(h + 1) * 64], T['MT'][0:L, h, 0:L], T['xdt'][0:L, h, :])
            pyi = ps()
            mm(pyi[0:L, 0:256], bcT[:, 2 + g, cs3], hstg[l][g][:, :])
            ph = ps()
            mm(ph[:, 0:256], T['Btok'][0:L, :], T['xdec'][0:L, :, :].rearrange("p h d -> p (h d)"))
            pyi3 = pyi[0:L, 0:256].rearrange("p (h d) -> p h d", h=4)
            py3 = py[0:L, 0:256].rearrange("p (h d) -> p h d", h=4)
            tt('dve', T['ytok'][0:L, :, :], pyi3, bc(T['ecum'][0:L, :], [L, 4, 64], 2), ALU.mult)
            tt('dve', T['ytok'][0:L, :, :], T['ytok'][0:L, :, :], py3, ALU.add)
            h3 = hstg[l][g][:, :].rearrange("p (h d) -> p h d", h=4)
            tt('dve', h3, h3, bc(T['et128'][:, :], [128, 4, 64], 2), ALU.mult)
            tt('dve', hstg[l][g][:, :], hstg[l][g][:, :], ph[:, 0:256], ALU.add)
            yield
            pyt = ps()
            for i in range(2):
                tr(pyt[:, i * L:(i + 1) * L], T['ytok'][0:L, 2 * i:2 * i + 2, :].rearrange("p h d -> p (h d)"), L)
            for i in range(2):
                stt('dve', yss[:, i, cs], xsT[:, 2 * g + i, cs3], PPc(l, 'dsk', 2 * g + i), pyt[:, i * L:(i + 1) * L],
                    ALU.mult, ALU.add)
            yield

    def ssd_epi(l, n, L, nch):
        for g in range(2):
            yss = SL[8 + g]
            for t_ in range(2):
                tt(ve(), yss[:, t_, 0:n], yss[:, t_, 0:n], zT[:, 2 * g + t_, 0:n], ALU.mult)
            pss = ps()
            for t_ in range(2):
                s = sc1[t_]
                tt(ve(), s[:, 0:n], yss[:, t_, 0:n], yss[:, t_, 0:n], ALU.mult)
                mm(pss[:, 0:n], C('ones'), s[:, 0:n], start=(t_ == 0), stop=(t_ == 1))
            rstd(tk['rs2'][:, 0:n], pss[:, 0:n], 1.0 / 256, SC(0))
            for t_ in range(2):
                i = 2 * g + t_
                stt(ve(), Y[i][:, 0:n], yss[:, t_, 0:n], PPc(l, 'snw', i), tk['rs2'][:, 0:n], ALU.mult, ALU.mult)

    def rwkv_pro(l, n, L, nch):
        c1 = slice(1, 1 + n)
        muo = PP_LAYOUT['mu'][0]
        tl = [(rT, 0), (rT, 1), (rT, 2), (rT, 3), (kT, 0), (kT, 1), (kT, 2), (kT, 3), (vT, 0), (vT, 1), (vT, 2), (vT, 3)]
        for g, t3 in enumerate((rT, kT, vT)):
            cp('pool', t3[:, :, 0:1], prev[l][:, 4 * g:4 * g + 4].unsqueeze(2))
        cp('pool', x12[:, 0:1], prev[l][:, 12:13])
        cp('pool', x13[:, 0:1], prev[l][:, 13:14])

        def shift(cur, prv, lastcol, prevdst, mucol):
            d = sc1[mucol % 2]
            e1 = ve()
            tt(e1, d[:, 0:n], prv, cur, ALU.subtract)
            cp('pool', prevdst, lastcol)
            stt(e1, cur, d[:, 0:n], pp[l][:, muo + mucol:muo + mucol + 1], cur, ALU.mult, ALU.add)
        for idx, (t3, i) in enumerate(tl):
            shift(t3[:, i, 1:1 + n], t3[:, i, 0:n], t3[:, i, n:n + 1], prev[l][:, idx:idx + 1], idx)
        shift(x12[:, 1:1 + n], x12[:, 0:n], x12[:, n:n + 1], prev[l][:, 12:13], 12)
        shift(x13[:, 1:1 + n], x13[:, 0:n], x13[:, n:n + 1], prev[l][:, 13:14], 13)
        sigmoid(x12[0:64, c1], x12[0:64, c1], scale=2.0)
        ts('dve', x12[0:64, c1], x12[0:64, c1], 2.0, ALU.mult, -1.0, ALU.add)
        sigmoid(x13[:, c1], x13[:, c1])

        LW, AS, G, CW, KKN, KM, BON, BV, RT_, AT_, BT_, KT_ = SL
        for m in range(4):
            p1 = ps()
            mm(p1[:, 0:n], wa2[l][:, 0, m * 128:(m + 1) * 128], x12[:, c1])
            sigmoid(LW[:, m, 0:n], p1[:, 0:n], nbias=dv[l][:, m:m + 1])
            ts('pool', LW[:, m, 0:n], LW[:, m, 0:n], -0.6065306597126334, ALU.mult)
            p2 = ps()
            mm(p2[:, 0:n], wa2[l][:, 1, m * 128:(m + 1) * 128], x12[:, c1])
            sigmoid(AS[:, m, 0:n], p2[:, 0:n], nbias=dv[l][:, 4 + m:5 + m])
            p3 = ps()
            mm(p3[:, 0:n], g2[l][:, m * 128:(m + 1) * 128], x13[:, c1])
            cp('act', G[:, m, 0:n], p3[:, 0:n])
            scan(CW[:, m, 0:n], C('rm64', 128, 0, n), LW[:, m, 0:n])
            ts(ve(), KKN[:, m, 0:n], kT[:, m, c1], PPc(l, 'kk', m), ALU.mult)
            s = sc1[m % 2]
            tt(ve(), s[:, 0:n], KKN[:, m, 0:n], KKN[:, m, 0:n], ALU.mult)
            p4 = ps()
            mm(p4[:, 0:n], C('bones'), s[:, 0:n])
            rstd(s[:, 0:n], p4[:, 0:n], 1.0, SC(3))
            tt(ve(), KKN[:, m, 0:n], KKN[:, m, 0:n], s[:, 0:n], ALU.mult)
            ts(ve(), KM[:, m, 0:n], AS[:, m, 0:n], PPc(l, 'ka', m), ALU.mult, PPc(l, 'ka', m), ALU.subtract)
            stt(ve(), KM[:, m, 0:n], KM[:, m, 0:n], 1.0, kT[:, m, c1], ALU.add, ALU.mult)
            s = sc1[2]
            stt(ve(), s[:, 0:n], rT[:, m, c1], PPc(l, 'rk', m), KM[:, m, 0:n], ALU.mult, ALU.mult)
            p5 = ps()
            mm(p5[:, 0:n], C('bones'), s[:, 0:n])
            tt('dve', BON[:, m, 0:n], p5[:, 0:n], vT[:, m, c1], ALU.mult)
            tt(ve(), BV[:, m, 0:n], KKN[:, m, 0:n], AS[:, m, 0:n], ALU.mult)
        Wt = AS
        act(Wt[:, :, 0:n], CW[:, :, 0:n], AF.Exp)
        tt(ve(), RT_[:, :, 0:n], rT[:, :, c1], Wt[:, :, 0:n], ALU.mult)
        cp('pool', tk['wlc'][:, :, 0:nch], Wt[:, :, L - 1:n:L])
        tt(ve(), AT_[:, :, 0:n], CW[:, :, 0:n], LW[:, :, 0:n], ALU.subtract)
        act(AT_[:, :, 0:n], AT_[:, :, 0:n], AF.Exp)
        stt(ve(), AT_[:, :, 0:n], KKN[:, :, 0:n], -1.0, AT_[:, :, 0:n], ALU.mult, ALU.mult)
        En = LW
        act(En[:, :, 0:n], CW[:, :, 0:n], AF.Exp, scale=-1.0)
        tt(ve(), BT_[:, :, 0:n], BV[:, :, 0:n], En[:, :, 0:n], ALU.mult)
        tt(ve(), KT_[:, :, 0:n], KM[:, :, 0:n], En[:, :, 0:n], ALU.mult)
        EB = LW
        for m in range(4):
            cw3 = CW[:, m, 0:n].rearrange("p (c l) -> p c l", l=L)
            eb3 = EB[:, m, 0:n].rearrange("p (c l) -> p c l", l=L)
            tt(ve(), eb3, bc(CW[:, m, L - 1:n:L], [128, nch, L], 2), cw3, ALU.subtract)
        act(EB[:, :, 0:n], EB[:, :, 0:n], AF.Exp)
        BB, KB = KKN, AS
        tt(ve(), BB[:, :, 0:n], BV[:, :, 0:n], EB[:, :, 0:n], ALU.mult)
        tt(ve(), KB[:, :, 0:n], KM[:, :, 0:n], EB[:, :, 0:n], ALU.mult)
        ATo, RTo = KM, BV
        cp(ve(), ATo[:, :, 0:n], AT_[:, :, 0:n])
        cp(ve(), RTo[:, :, 0:n], RT_[:, :, 0:n])
        memset('pool', ATo[0:64, :, 0:n], 0.0)
        memset('pool', RTo[0:64, :, 0:n], 0.0)
        memset('pool', AT_[64:128, :, 0:n], 0.0)
        memset('pool', RT_[64:128, :, 0:n], 0.0)

    def rwkv_chain(l, n, L, nch, g):
        LW, AS, G, CW, KKN, KM, BON, BV, RT_, AT_, BT_, KT_ = SL
        BB, KB = KKN, AS
        ATm, RTm = (AT_, KM), (RT_, BV)
        OT = (CW, LW)[g]
        T = tkg[g]
        ST = rstg[l][g]
        nst = int(np.log2(L))

        def v3(p_):
            return p_[0:L, 0:4 * L].rearrange("p (h i) -> p h i", h=4)

        def x3(p_):
            return p_[0:L, 0:256].rearrange("p (h d) -> p h d", h=4)
        u0b = bc(C('u0', L, 0, L), [L, 4, L], 1)
        u1b = bc(C('u1', L, 0, L), [L, 4, L], 1)
        l0b = bc(C('l0', L, 0, L), [L, 4, L], 1)
        for c in range(nch):
            cs = slice(c * L, (c + 1) * L)
            cs1 = slice(1 + c * L, 1 + (c + 1) * L)
            pN, pNT, pAk, pRb, pRk = ps(), ps(), ps(), ps(), ps()
            for hh in range(4):
                h = 4 * g + hh
                q = h // 2
                hsl = slice(hh * L, (hh + 1) * L)
                am, rm_ = ATm[h % 2], RTm[h % 2]
                mm(pN[0:L, hsl], am[:, q, cs], BT_[:, q, cs])
                mm(pNT[0:L, hsl], BT_[:, q, cs], am[:, q, cs])
                mm(pAk[0:L, hsl], KT_[:, q, cs], am[:, q, cs])
                mm(pRb[0:L, hsl], BT_[:, q, cs], rm_[:, q, cs])
                mm(pRk[0:L, hsl], KT_[:, q, cs], rm_[:, q, cs])
            pv = ps()
            for i in range(2):
                tr(pv[0:L, i * 128:(i + 1) * 128], vT[:, 2 * g + i, cs1], 128)
            pbb, pkb = ps(), ps()
            for i in range(2):
                tr(pbb[0:L, i * 128:(i + 1) * 128], BB[:, 2 * g + i, cs], 128)
                tr(pkb[0:L, i * 128:(i + 1) * 128], KB[:, 2 * g + i, cs], 128)
            Pa, PaT = T['Pa'], T['PaT']
            tt('dve', Pa[0:L, :, 0:L], v3(pN), l0b, ALU.mult)
            tt('dve', PaT[0:L, :, 0:L], v3(pNT), u0b, ALU.mult)
            tt('dve', T['AkT'][0:L, :, 0:L], v3(pAk), u0b, ALU.mult)
            cp('act', T['Vtok'][0:L, :, :], x3(pv))
            tt('dve', T['RbT'][0:L, :, 0:L], v3(pRb), u1b, ALU.mult)
            tt('dve', T['RkT'][0:L, :, 0:L], v3(pRk), u1b, ALU.mult)
            cp('act', T['bbt'][0:L, :, :], pbb[0:L, 0:256].rearrange("p (m d) -> p m d", m=2))
            cp('act', T['kbt'][0:L, :, :], pkb[0:L, 0:256].rearrange("p (m d) -> p m d", m=2))
            yield
            pX = ps()
            for hh in range(4):
                h = 4 * g + hh
                mm(pX[0:L, hh * 64:(hh + 1) * 64], ATm[h % 2][:, h // 2, cs], ST[:, hh // 2, :], start=True, stop=False)
                mm(pX[0:L, hh * 64:(hh + 1) * 64], T['AkT'][0:L, hh, 0:L], T['Vtok'][0:L, hh, :], start=False, stop=True)
            Xc, Xn = T['Xa'], T['Xb']
            cp('act', Xc[0:L, :, :], x3(pX))
            yield
            Pc, PcT, Pn, PnT = Pa, PaT, T['Pb'], T['PbT']
            for st_ in range(nst):
                pU = ps()
                for hh in range(4):
                    mm(pU[0:L, hh * 64:(hh + 1) * 64], PcT[0:L, hh, 0:L], Xc[0:L, hh, :])
                if st_ < nst - 1:
                    pS, pST = ps(), ps()
                    for hh in range(4):
                        hsl = slice(hh * L, (hh + 1) * L)
                        mm(pS[0:L, hsl], PcT[0:L, hh, 0:L], Pc[0:L, hh, 0:L])
                        mm(pST[0:L, hsl], Pc[0:L, hh, 0:L], PcT[0:L, hh, 0:L])
                tt('dve', Xn[0:L, :, :], Xc[0:L, :, :], x3(pU), ALU.add)
                Xc, Xn = Xn, Xc
                if st_ < nst - 1:
                    cp('act', Pn[0:L, :, 0:L], v3(pS))
                    cp('dve', PnT[0:L, :, 0:L], v3(pST))
                    Pc, PcT, Pn, PnT = Pn, PnT, Pc, PcT
                yield
            SA = Xc
            pO = ps()
            for hh in range(4):
                h = 4 * g + hh
                o_ = pO[0:L, hh * 64:(hh + 1) * 64]
                mm(o_, RTm[h % 2][:, h // 2, cs], ST[:, hh // 2, :], start=True, stop=False)
                mm(o_, T['RbT'][0:L, hh, 0:L], SA[0:L, hh, :], start=False, stop=False)
                mm(o_, T['RkT'][0:L, hh, 0:L], T['Vtok'][0:L, hh, :], start=False, stop=True)
            pSt = ps()
            for i in range(2):
                mm(pSt[:, i * 128:(i + 1) * 128], T['bbt'][0:L, i, :],
                   SA[0:L, 2 * i:2 * i + 2, :].rearrange("p h d -> p (h d)"), start=True, stop=False)
                mm(pSt[:, i * 128:(i + 1) * 128], T['kbt'][0:L, i, :],
                   T['Vtok'][0:L, 2 * i:2 * i + 2, :].rearrange("p h d -> p (h d)"), start=False, stop=True)
            cp('act', T['Otok'][0:L, :, :], x3(pO))
            pSt3 = pSt[:, 0:256].rearrange("p (q d) -> p q d", q=2)
            for hh2 in range(2):
                prr = slice(hh2 * 64, hh2 * 64 + 64)
                tt('dve', T['stm'][prr, :, :], ST[prr, :, :], bc(tk['wlc'][prr, 2 * g:2 * g + 2, c], [64, 2, 64], 2),
                   ALU.mult)
                tt('dve', ST[prr, :, :], T['stm'][prr, :, :], pSt3[prr, :, hh2 * 64:(hh2 + 1) * 64], ALU.add)
            yield
            pot = ps()
            for i in range(2):
                tr(pot[:, i * L:(i + 1) * L], T['Otok'][0:L, 2 * i:2 * i + 2, :].rearrange("p h d -> p (h d)"), L)
            cp('act', OT[:, 2 * g:2 * g + 2, cs], pot[:, 0:2 * L].rearrange("p (q i) -> p q i", q=2))
            yield

    def rwkv_epi(l, n, L, nch):
        LW, AS, G, CW, KKN, KM, BON, BV, RT_, AT_, BT_, KT_ = SL
        for m in range(4):
            OT = (CW, LW)[m // 2]
            pm = ps()
            mm(pm[:, 0:n], C('bones'), OT[:, m, 0:n])
            cen = sc1[m % 2]
            stt('dve', cen[:, 0:n], pm[:, 0:n], -1.0 / 64, OT[:, m, 0:n], ALU.mult, ALU.add)
            sq = sc1[2]
            tt(ve(), sq[:, 0:n], cen[:, 0:n], cen[:, 0:n], ALU.mult)
            pvv = ps()
            mm(pvv[:, 0:n], C('bones'), sq[:, 0:n])
            rstd(sq[:, 0:n], pvv[:, 0:n], 1.0 / 64, SC(2))
            tt(ve(), cen[:, 0:n], cen[:, 0:n], sq[:, 0:n], ALU.mult)
            ts(ve(), cen[:, 0:n], cen[:, 0:n], PPc(l, 'lnw', m), ALU.mult, PPc(l, 'lnb', m), ALU.add)
            tt(ve(), cen[:, 0:n], cen[:, 0:n], BON[:, m, 0:n], ALU.add)
            tt(ve(), Y[4 + m][:, 0:n], cen[:, 0:n], G[:, m, 0:n], ALU.mult)

    def hgrn_pro(l, n, Lh, nchh):
        E_, L1, KG, CU, TM, QT, KT2, QH = SL[0:8]
        mid = Lh // 2
        rmn = 'rm32' if Lh == 32 else 'rm64'
        for m in range(4):
            ts(ve(), fzT[:, m, 0:n], fzT[:, m, 0:n], -60.0, ALU.max)
        act(E_[:, :, 0:n], fzT[:, :, 0:n], AF.Exp, scale=-1.0)
        for m in range(4):
            act(L1[:, m, 0:n], E_[:, m, 0:n], AF.Ln, bias=SC(1), scale=dv[l][:, 16 + m:17 + m])
        act(TM[:, :, 0:n], E_[:, :, 0:n], AF.Ln, bias=SC(1))
        tt(ve(), L1[:, :, 0:n], L1[:, :, 0:n], TM[:, :, 0:n], ALU.subtract)
        ts('pool', TM[:, :, 0:n], E_[:, :, 0:n], 1.0, ALU.add)
        recip(TM[:, :, 0:n], TM[:, :, 0:n])
        for m in range(4):
            stt(ve(), KG[:, m, 0:n], E_[:, m, 0:n], dv[l][:, 20 + m:21 + m], TM[:, m, 0:n], ALU.mult, ALU.mult)
            scan(CU[:, m, 0:n], C(rmn, 128, 0, n), L1[:, m, 0:n])
        for m in range(4):
            cu3 = CU[:, m, 0:n].rearrange("p (c l) -> p c l", l=Lh)
            tm3 = TM[:, m, 0:n].rearrange("p (c l) -> p c l", l=Lh)
            tt(ve(), tm3, cu3, bc(CU[:, m, mid:n:Lh], [128, nchh, Lh], 2), ALU.subtract)
        ts('dve', TM[:, :, 0:n], TM[:, :, 0:n], 38.0, ALU.min, -38.0, ALU.max)
        act(QT[:, :, 0:n], TM[:, :, 0:n], AF.Exp)
        act(KT2[:, :, 0:n], TM[:, :, 0:n], AF.Exp, scale=-1.0)
        tt(ve(), QT[:, :, 0:n], QT[:, :, 0:n], qT[:, :, 0:n], ALU.mult)
        tt(ve(), KT2[:, :, 0:n], KT2[:, :, 0:n], KG[:, :, 0:n], ALU.mult)
        act(QH[:, :, 0:n], CU[:, :, 0:n], AF.Exp)
        cp('pool', tk['slc'][:, :, 0:nchh], QH[:, :, Lh - 1:n:Lh])
        tt(ve(), QH[:, :, 0:n], QH[:, :, 0:n], qT[:, :, 0:n], ALU.mult)
        KH = E_
        for m in range(4):
            cu3 = CU[:, m, 0:n].rearrange("p (c l) -> p c l", l=Lh)
            kh3 = KH[:, m, 0:n].rearrange("p (c l) -> p c l", l=Lh)
            tt(ve(), kh3, bc(CU[:, m, Lh - 1:n:Lh], [128, nchh, Lh], 2), cu3, ALU.subtract)
        act(KH[:, :, 0:n], KH[:, :, 0:n], AF.Exp)
        tt(ve(), KH[:, :, 0:n], KH[:, :, 0:n], KG[:, :, 0:n], ALU.mult)

    def hgrn_chain(l, n, Lh, nchh, g):
        E_, L1, KG, CU, TM, QT, KT2, QH = SL[0:8]
        KH = E_
        OT = (L1, TM)[g]
        T = tkg[g]
        S = gstg[l][g]
        for c in range(nchh):
            cs = slice(c * Lh, (c + 1) * Lh)
            pA = ps()
            for hh in range(2):
                h = 2 * g + hh
                mm(pA[0:Lh, hh * Lh:(hh + 1) * Lh], KT2[:, h, cs], QT[:, h, cs])
            pv, pk = ps(), ps()
            for hh in range(2):
                h = 2 * g + hh
                tr(pv[0:Lh, hh * 128:(hh + 1) * 128], ivT[:, h, cs], 128)
                tr(pk[0:Lh, hh * 128:(hh + 1) * 128], KH[:, h, cs], 128)
            tt('dve', T['hMT'][0:Lh, :, 0:Lh], pA[0:Lh, 0:2 * Lh].rearrange("p (h i) -> p h i", h=2),
               bc(C('u1', Lh, 0, Lh), [Lh, 2, Lh], 1), ALU.mult)
            cp('act', T['hvt'][0:Lh, :, :], pv[0:Lh, 0:256].rearrange("p (h d) -> p h d", h=2))
            cp('dve', T['hkt'][0:Lh, :, :], pk[0:Lh, 0:256].rearrange("p (h d) -> p h d", h=2))
            yield
            po = ps()
            for hh in range(2):
                h = 2 * g + hh
                o_ = po[:, hh * Lh:(hh + 1) * Lh]
                mm(o_, T['hvt'][0:Lh, hh, :], T['hMT'][0:Lh, hh, 0:Lh], start=True, stop=False)
                mm(o_, S[:, hh, :], QH[:, h, cs], start=False, stop=True)
            pS = ps()
            for hh in range(2):
                mm(pS[:, hh * 128:(hh + 1) * 128], T['hkt'][0:Lh, hh, :], T['hvt'][0:Lh, hh, :])
            cp('act', OT[:, 2 * g:2 * g + 2, cs], po[:, 0:2 * Lh].rearrange("p (h i) -> p h i", h=2))
            for hh in range(2):
                h = 2 * g + hh
                stt('dve', S[:, hh, :], S[:, hh, :], tk['slc'][:, h, c:c + 1], pS[:, hh * 128:(hh + 1) * 128],
                    ALU.mult, ALU.add)
            yield

    def hgrn_epi(l, n, Lh, nchh):
        E_, L1, KG, CU, TM, QT, KT2, QH = SL[0:8]
        for h in range(4):
            OT = (L1, TM)[h // 2]
            sq = sc1[h % 2]
            tt(ve(), sq[:, 0:n], OT[:, h, 0:n], OT[:, h, 0:n], ALU.mult)
            pss = ps()
            mm(pss[:, 0:n], C('ones'), sq[:, 0:n])
            rstd(sq[:, 0:n], pss[:, 0:n], 1.0 / 128, SC(0))
            stt(ve(), OT[:, h, 0:n], OT[:, h, 0:n], PPc(l, 'hnw', h), sq[:, 0:n], ALU.mult, ALU.mult)
            s2 = sc1[2]
            sigmoid(s2[:, 0:n], ggT[:, h, 0:n])
            tt(ve(), s2[:, 0:n], s2[:, 0:n], ggT[:, h, 0:n], ALU.mult)
            tt(ve(), Y[8 + h][:, 0:n], OT[:, h, 0:n], s2[:, 0:n], ALU.mult)

    def interleave(*gens):
        gens = list(gens)
        while gens:
            for g_ in list(gens):
                try:
                    next(g_)
                except StopIteration:
                    gens.remove(g_)


    def run_tile(src, n, L, Lh, dst):
        nb = (n + 127) // 128
        for b in range(nb):
            tb = min(128, n - b * 128)
            xi = xin[0]
            P.dma(DQ, xi[0:tb, :], src[b * 128:b * 128 + tb, :])
            for g in range(2 if 'i' not in KDBG else 0):
                pt = ps()
                for k in range(4):
                    tr(pt[:, k * 128:k * 128 + tb], xi[0:tb, (4 * g + k) * 128:(4 * g + k + 1) * 128], tb)
                for k in range(4):
                    cp('act' if (k % 2 and 'j' not in KDBG) else 'dve', xT[4 * g + k][:, b * 128:b * 128 + tb], pt[:, k * 128:k * 128 + tb])
        for l in range(DEPTH):
            layer(l, n, L, Lh)
        if dst is None or 'g' in KDBG:
            return
        hF = [SL[k // 4][:, k % 4, :] for k in range(8)]
        if 'h' not in KDBG:
            rmsnorm(DEPTH - 1, 'fin', n, xT, hF)
        for b in range(nb):
            tb = min(128, n - b * 128)
            xo = xin[0]
            for g in range(2):
                pt = ps()
                for k in range(4):
                    tr(pt[0:tb, k * 128:(k + 1) * 128], hF[4 * g + k][:, b * 128:b * 128 + tb], 128)
                cp('act' if g else 'dve', xo[0:tb, g * 512:(g + 1) * 512], pt[0:tb, :])
            P.dma(DQ, dst[b * 128:b * 128 + tb, :], xo[0:tb, :])

    def store_states(si):
        for l in range(DEPTH):
            for i in range(4):
                pt = ps()
                tr(pt[:, 0:128], hstg[l][i // 2][:, (i % 2) * 128:(i % 2 + 1) * 128], 128)
                cp('act', stmp[:, i, :], pt[:, 0:128])
            for hh in range(2):
                P.dma(DQ, o_ssm[si, l].rearrange("(i hh) p n -> hh p i n", hh=2)[hh],
                      stmp[hh * 64:(hh + 1) * 64, :, :])
            for i in range(8):
                P.dma(DQ, o_conv[si, l][:, i * 128:(i + 1) * 128].rearrange("j p -> p j"), hist[l][:, i, :],
                      allow_slow_non_contiguous=True)
            P.dma(DQ, o_shift[si, l].rearrange("(i p) -> p i", p=128), prev[l][:, :],
                  allow_slow_non_contiguous=True)
            for q in range(4):
                pt = ps()
                tr(pt[0:64, 0:128], rstg[l][q // 2][:, q % 2, :], 128)
                cp('act', rtmp[:, q, :, :], pt[0:64, 0:128].rearrange("p (hh n) -> p hh n", hh=2))
            P.dma(DQ, o_rwkv[si, l].rearrange("(q hh) v n -> v q hh n", hh=2), rtmp[:, :, :, :])
            for g in range(2):
                P.dma(DQ, o_hgrn[si, l, 2 * g:2 * g + 2].rearrange("h k v -> k h v"), gstg[l][g][:, :, :])

    def load_states():
        for l in range(DEPTH):
            for hh in range(2):
                P.dma(DQ, stmp[hh * 64:(hh + 1) * 64, :, :],
                      st_ssm[l].rearrange("(i hh) p n -> hh p i n", hh=2)[hh])
            for i in range(4):
                pt = ps()
                tr(pt[:, 0:128], stmp[:, i, :], 128)
                cp('act', hstg[l][i // 2][:, (i % 2) * 128:(i % 2 + 1) * 128], pt[:, 0:128])
            for i in range(8):
                P.dma(DQ, hist[l][:, i, :], st_conv[l][:, i * 128:(i + 1) * 128].rearrange("j p -> p j"),
                      allow_slow_non_contiguous=True)
            P.dma(DQ, prev[l][:, :], st_shift[l].rearrange("(i p) -> p i", p=128),
                  allow_slow_non_contiguous=True)
            P.dma(DQ, rtmp[:, :, :, :], st_rwkv[l].rearrange("(q hh) v n -> v q hh n", hh=2))
            for q in range(4):
                pt = ps()
                tr(pt[:, 0:64], rtmp[:, q, :, :].rearrange("p hh n -> p (hh n)"), 64)
                cp('act', rstg[l][q // 2][:, q % 2, :], pt[:, 0:64])
            for g in range(2):
                P.dma(DQ, gstg[l][g][:, :, :], st_hgrn[l, 2 * g:2 * g + 2].rearrange("h k v -> k h v"))

    def zero_states():
        for l in range(DEPTH):
            for g in range(2):
                memset('pool', hstg[l][g][:, :], 0.0)
                memset('pool', rstg[l][g][:, :, :], 0.0)
                memset('pool', gstg[l][g][:, :, :], 0.0)
            memset('pool', hist[l][:, :, :], 0.0)
            memset('pool', prev[l][:, :], 0.0)

    if 'states' in STAGES:
        load_states()
    else:
        zero_states()
    if 'f' not in KDBG:
        run_tile(xs_in, 64, 64, 32, y_s)
    if 'states' in STAGES:
        store_states(1)
    zero_states()
    if 'b' not in KDBG:
        run_tile(meta, 16, 16, 16, None)
    t0 = 0 if 'c' not in KDBG else SEQ
    while t0 < SEQ:
        n = min(NT, SEQ - t0)
        run_tile(xp[t0:t0 + n, :], n, 64, 32, y_p[t0:t0 + n, :])
        t0 += n
    if 'states' in STAGES:
        store_states(0)

    P.emit()
    return nc, es, P


_CACHE = {}


def kernel(**inp):
    inp = {k: np.asarray(v, dtype=np.float32) for k, v in inp.items()}
    B, SEQ, _ = inp['x_prompt'].shape
    DEPTH = inp['w_in'].shape[0]
    pps = np.stack([_pack_params(inp, l) for l in range(DEPTH)], axis=0)
    ccs = _consts()
    zpad = np.zeros_like(inp['rw_w2'])
    wa2 = np.ascontiguousarray(np.stack([np.concatenate([inp['rw_w2'], zpad], axis=1),
                                         np.concatenate([zpad, inp['rw_a2']], axis=1)], axis=1))
    key = (SEQ, DEPTH)
    if key not in _CACHE:
        _CACHE[key] = build(SEQ, DEPTH=DEPTH)
    nc = _CACHE[key][0]
    in_maps = []
    for c in range(NCORES):
        in_maps.append({
            "xp": np.ascontiguousarray(inp['x_prompt'][c]),
            "xs": np.ascontiguousarray(inp['x_sample'][c]),
            "meta": inp['meta_tokens'],
            "st_ssm": np.ascontiguousarray(inp['state_ssm'][:, c]),
            "st_conv": np.ascontiguousarray(inp['state_conv'][:, c]),
            "st_rwkv": np.ascontiguousarray(inp['state_rwkv'][:, c]),
            "st_shift": np.ascontiguousarray(inp['state_shift'][:, c]),
            "st_hgrn": np.ascontiguousarray(inp['state_hgrn'][:, c]),
            "w_in": inp['w_in'], "w_out": inp['w_out'], "w_gate": inp['w_gate'], "w_up": inp['w_up'],
            "w_down": inp['w_down'], "wa2": wa2, "g2": inp['rw_g2'], "pp": pps, "cc": ccs,
        })
    in_maps = [{"i_" + k: v for k, v in m.items()} for m in in_maps]
    if os.environ.get('KONE'):
        res = run_bass_kernel_spmd(nc, in_maps[:1], core_ids=[0])
        R = [{k[2:]: v for k, v in res.results[0].items()}] * NCORES
    else:
        res = run_bass_kernel_spmd(nc, in_maps, core_ids=list(range(NCORES)))
        R = [{k[2:]: v for k, v in r.items()} for r in res.results]
    y_prompt = np.stack([R[c]["y_p"] for c in range(NCORES)], axis=0)
    y_sample = np.stack([R[c]["y_s"] for c in range(NCORES)], axis=0)

    def st(name, si):
        return np.stack([R[c][name][si] for c in range(NCORES)], axis=1)
    outs = [y_prompt, y_sample]
    for si in (0, 1):
        for name in ("o_ssm", "o_conv", "o_rwkv", "o_shift", "o_hgrn"):
            outs.append(st(name, si))
    return tuple(np.ascontiguousarray(o, dtype=np.float32) for o in outs)
```

```python
import numpy as np
from contextlib import ExitStack
import concourse.bass as bass
import concourse.mybir as mybir
from concourse.bass_utils import run_bass_kernel_spmd

F32 = mybir.dt.float32
BF16 = mybir.dt.bfloat16
AF = mybir.ActivationFunctionType
ALU = mybir.AluOpType

D = 1024
DFF = 2816
NIN = 5384
NCORES = 8
DQ = 'sp'
TRUNK_BF16 = True
import os
KDBG = os.environ.get('KDBG', '')
STAGES = {'states', 'proj', 'ssd', 'rwkv', 'hgrn', 'oproj', 'ffn'}
ENGS = ['pe', 'act', 'dve', 'pool', 'sp']


class Dep:
    __slots__ = ('lw', 'rd')

    def __init__(self):
        self.lw = None
        self.rd = []


class Prog:
    def __init__(self, nc, es):
        self.nc, self.es = nc, es
        self.q = {e: [] for e in ENGS}
        self.deps = {}
        self.dsem = {}
        self.sem = {e: es.enter_context(nc.semaphore("q_" + e)) for e in ENGS}
        self.tensors = {}
        self.psum_names = set()

    def sb(self, name, shape, dtype=F32):
        name = "s_" + name
        t = self.es.enter_context(self.nc.sbuf_tensor(name, list(shape), dtype))
        self.deps[name] = Dep()
        return t

    def psum(self, name, shape):
        t = self.es.enter_context(self.nc.psum_tensor(name, list(shape), F32))
        self.deps[name] = Dep()
        self.psum_names.add(name)
        return t

    def _dep(self, ap):
        nm = ap.tensor.name
        return self.deps.get(nm)

    def _collect(self, eng, outs, ins):
        w = []
        for ap in ins:
            d = self._dep(ap)
            if d is not None and d.lw is not None:
                w.append(d.lw)
            if d is not None and ap.tensor.name in self.psum_names:
                for r in d.rd:
                    if not (r[0] == 'e' and r[1] == eng):
                        w.append(r)
        for ap in outs:
            d = self._dep(ap)
            if d is None:
                continue
            if d.lw is not None:
                w.append(d.lw)
            for r in d.rd:
                w.append(r)
        if eng == 'pe':
            w = [x for x in w if not (x[0] == 'e' and x[1] == 'pe')]
        return w

    def _commit(self, ev, outs, ins):
        for ap in ins:
            d = self._dep(ap)
            if d is not None:
                d.rd.append(ev)
        for ap in outs:
            d = self._dep(ap)
            if d is not None:
                d.lw = ev
                d.rd = []

    def op(self, eng, fn, outs, ins):
        w = self._collect(eng, outs, ins)
        idx = len(self.q[eng])
        self.q[eng].append(dict(fn=fn, waits=w, kind='c'))
        self._commit(('e', eng, idx), outs, ins)

    def dma(self, eng, out, in_, semname=None, **kw):
        outs, ins = [out], [in_]
        w = self._collect(eng, outs, ins)
        if semname is None:
            d = self._dep(out)
            semname = out.tensor.name if d is not None else in_.tensor.name
        if semname not in self.dsem:
            self.dsem[semname] = [self.es.enter_context(self.nc.semaphore("d_" + semname)), 0]
        s = self.dsem[semname]
        s[1] += 16
        self.q[eng].append(dict(fn=lambda e: e.dma_start(out=out, in_=in_, **kw), waits=w, kind='d', sem=semname))
        self._commit(('d', semname, s[1]), outs, ins)

    def emit(self):
        nc = self.nc
        targets = {e: set() for e in ENGS}
        for e in ENGS:
            for ins in self.q[e]:
                for d in ins['waits']:
                    if d[0] == 'e':
                        targets[d[1]].add(d[2])
        semval = {}
        for e in ENGS:
            c = 0
            vals = []
            for i in range(len(self.q[e])):
                if i in targets[e]:
                    c += 1
                vals.append(c)
            semval[e] = vals
        final = [(self.dsem[k][0], self.dsem[k][1]) for k in self.dsem]

        def body_for(e):
            def body(eng):
                waited = {}
                for i, ins in enumerate(self.q[e]):
                    need = {}
                    for d in ins['waits']:
                        if d[0] == 'e':
                            key = ('e', d[1])
                            val = semval[d[1]][d[2]]
                        else:
                            key = ('d', d[1])
                            val = d[2]
                        if val > need.get(key, 0):
                            need[key] = val
                    for key, val in need.items():
                        if waited.get(key, 0) >= val:
                            continue
                        waited[key] = val
                        h = self.sem[key[1]] if key[0] == 'e' else self.dsem[key[1]][0]
                        eng.wait_ge(h, val)
                    r = ins['fn'](eng)
                    if ins['kind'] == 'c':
                        if i in targets[e]:
                            r.then_inc(self.sem[e], 1)
                    else:
                        r.then_inc(self.dsem[ins['sem']][0], 16)
                if e == 'sp':
                    for h, v in final:
                        eng.wait_ge(h, v)
            return body

        with nc.Block() as block:
            block.sync(body_for('sp'))
            block.tensor(body_for('pe'))
            block.scalar(body_for('act'))
            block.vector(body_for('dve'))
            block.gpsimd(body_for('pool'))


def _cols(v):
    v = np.asarray(v, np.float32)
    return v.reshape(-1, 128).T


PP_LAYOUT = {}


def _pack_params(inp, l):
    parts = []
    off = 0

    def add(name, arr):
        nonlocal off
        arr = np.ascontiguousarray(arr, dtype=np.float32)
        PP_LAYOUT[name] = (off, arr.shape[1])
        off += arr.shape[1]
        parts.append(arr)

    add('n1', _cols(inp['norm1_w'][l]))
    add('n2', _cols(inp['norm2_w'][l]))
    cw = inp['conv_w'][l]
    add('cw', np.stack([cw[j].reshape(8, 128).T for j in range(4)], axis=2).reshape(128, 32))
    add('cb', _cols(inp['conv_b'][l]))
    add('dsk', _cols(np.repeat(inp['d_skip'][l], 64)))
    add('snw', _cols(inp['ssd_norm_w'][l]))
    add('dtb', np.tile(inp['dt_bias'][l][None, :], (128, 1)))
    add('alog', np.tile(inp['a_log'][l][None, :], (128, 1)))
    add('mu', _cols(inp['rw_mu'][l]))
    add('w0', _cols(inp['rw_w0'][l]))
    add('a0', _cols(inp['rw_a0'][l]))
    add('kk', _cols(inp['rw_kk'][l]))
    add('ka', _cols(inp['rw_ka'][l]))
    add('rk', _cols(inp['rw_rk'][l]))
    add('lnw', _cols(inp['rw_lnx_w'][l]))
    add('lnb', _cols(inp['rw_lnx_b'][l]))
    add('lg0', _cols(inp['hg_lb_logits'][0]))
    add('lg1', _cols(inp['hg_lb_logits'][1]))
    add('hnw', _cols(inp['hg_norm_w'][l]))
    add('fin', _cols(inp['final_norm_w']))
    PP_LAYOUT['_n'] = off
    return np.concatenate(parts, axis=1)


CC = {}


def _consts():
    parts = []
    off = 0

    def add(name, arr):
        nonlocal off
        arr = np.ascontiguousarray(arr, dtype=np.float32)
        assert arr.shape[0] == 128
        CC[name] = (off, arr.shape[1])
        off += arr.shape[1]
        parts.append(arr)

    i = np.arange(128)
    add('ident', np.eye(128))
    add('ones', np.ones((128, 128)))
    bo = np.zeros((128, 128))
    bo[:64, :64] = 1
    bo[64:, 64:] = 1
    add('bones', bo)
    u1 = (i[:, None] <= i[None, :]).astype(np.float32)
    u0 = (i[:, None] < i[None, :]).astype(np.float32)
    add('u1', u1)
    add('u0', u0)
    add('l0', u0.T)
    add('mneg', (u1 - 1.0) * 30000.0)
    t = np.arange(256)
    add('rm64', np.tile(((t % 64) != 0).astype(np.float32)[None, :], (128, 1)))
    add('rm32', np.tile(((t % 32) != 0).astype(np.float32)[None, :], (128, 1)))
    sc = np.zeros((128, 8), np.float32)
    sc[:, 0] = 1e-6
    sc[:, 1] = 1.0
    sc[:, 2] = 64e-5
    sc[:, 3] = 1e-12
    sc[:, 4] = -0.5
    add('sc', sc)
    CC['_n'] = off
    return np.concatenate(parts, axis=1)


def build(SEQ, NT=int(os.environ.get('KNT', '256')), DEPTH=2):
    nc = bass.Bass("TRN2", target_bir_lowering=False)
    TDT = BF16 if TRUNK_BF16 else F32
    if TRUNK_BF16:
        nc.allow_low_precision("trunk projections use bf16 operands with fp32 accumulation")
    es = ExitStack()
    P = Prog(nc, es)
    NPP = PP_LAYOUT['_n']
    NCC = CC['_n']

    def din(name, shape):
        return nc.dram_tensor("i_" + name, list(shape), F32, kind="ExternalInput").ap()

    def dout(name, shape):
        return nc.dram_tensor("r_" + name, list(shape), F32, kind="ExternalOutput").ap()

    xp = din("xp", [SEQ, D])
    xs_in = din("xs", [64, D])
    meta = din("meta", [16, D])
    st_ssm = din("st_ssm", [DEPTH, 8, 64, 128])
    st_conv = din("st_conv", [DEPTH, 3, 1024])
    st_rwkv = din("st_rwkv", [DEPTH, 8, 64, 64])
    st_shift = din("st_shift", [DEPTH, 1792])
    st_hgrn = din("st_hgrn", [DEPTH, 4, 128, 128])
    w_in = din("w_in", [DEPTH, D, NIN])
    w_out = din("w_out", [DEPTH, 1536, D])
    w_gate = din("w_gate", [DEPTH, D, DFF])
    w_up = din("w_up", [DEPTH, D, DFF])
    w_down = din("w_down", [DEPTH, DFF, D])
    wa2_d = din("wa2", [DEPTH, 2, 128, 512])
    g2_d = din("g2", [DEPTH, 128, 512])
    pp_d = din("pp", [DEPTH, 128, NPP])
    cc_d = din("cc", [128, NCC])

    y_p = dout("y_p", [SEQ, D])
    y_s = dout("y_s", [64, D])
    o_ssm = dout("o_ssm", [2, DEPTH, 8, 64, 128])
    o_conv = dout("o_conv", [2, DEPTH, 3, 1024])
    o_rwkv = dout("o_rwkv", [2, DEPTH, 8, 64, 64])
    o_shift = dout("o_shift", [2, DEPTH, 1792])
    o_hgrn = dout("o_hgrn", [2, DEPTH, 4, 128, 128])

    cc = P.sb("cc", [128, NCC])
    pp = [P.sb("pp%d" % l, [128, NPP]) for l in range(DEPTH)]
    dv = [P.sb("dv%d" % l, [128, 40]) for l in range(DEPTH)]
    wa2 = [P.sb("wa2_%d" % l, [128, 2, 512]) for l in range(DEPTH)]
    g2 = [P.sb("g2_%d" % l, [128, 512]) for l in range(DEPTH)]
    wdt32 = [P.sb("wdt%d" % l, [128, 8, 8]) for l in range(DEPTH)]
    wdt = [P.sb("wdtb%d" % l, [128, 8, 8], TDT) for l in range(DEPTH)] if TRUNK_BF16 else wdt32

    xT = [P.sb("xT%d" % k, [128, NT]) for k in range(8)]
    hT = [P.sb("hT%d" % k, [128, NT], TDT) for k in range(8)]
    zT = P.sb("zT", [128, 4, NT])
    xsT = P.sb("xsT", [128, 4, NT + 3])
    bcT = P.sb("bcT", [128, 4, NT + 3])
    rT = P.sb("rT", [128, 4, NT + 1])
    kT = P.sb("kT", [128, 4, NT + 1])
    vT = P.sb("vT", [128, 4, NT + 1])
    x12 = P.sb("x12", [128, NT + 1])
    x13 = P.sb("x13", [128, NT + 1])
    qT = P.sb("qT", [128, 4, NT])
    fzT = P.sb("fzT", [128, 4, NT])
    ivT = P.sb("ivT", [128, 4, NT])
    ggT = P.sb("ggT", [128, 4, NT])
    Y = [P.sb("Y%d" % j, [128, NT], TDT) for j in range(12)]

    NSL = 12
    SL = [P.sb("SL%d" % j, [128, 4, NT]) for j in range(NSL)]
    sc1 = [P.sb("sc1_%d" % j, [128, NT]) for j in range(3)]

    def a16(f, n):
        v = SL[f // 8][:, :, :].rearrange("p a b -> p (a b)").bitcast(BF16)
        t = f % 8
        return v[:, t * NT:t * NT + n]
    NW = 9
    wsl = [P.sb("wsl%d" % j, [128, 512]) for j in range(NW)]
    NWB = 4
    wbl = [P.sb("wbl%d" % j, [128, 512], TDT) for j in range(NWB)] if TRUNK_BF16 else None
    xin = [P.sb("xin%d" % j, [128, D]) for j in range(1)]

    hstg = [[P.sb("hst%d_%d" % (l, g), [128, 256]) for g in range(2)] for l in range(DEPTH)]
    hist = [P.sb("hist%d" % l, [128, 8, 3]) for l in range(DEPTH)]
    prev = [P.sb("prev%d" % l, [128, 14]) for l in range(DEPTH)]
    rstg = [[P.sb("rst%d_%d" % (l, g), [128, 2, 64]) for g in range(2)] for l in range(DEPTH)]
    gstg = [[P.sb("gst%d_%d" % (l, g), [128, 2, 128]) for g in range(2)] for l in range(DEPTH)]
    stmp = P.sb("stmp", [128, 4, 128])
    rtmp = P.sb("rtmp", [64, 4, 2, 64])

    tk = {}
    for nm, shp in [('wlc', [128, 4, 4]), ('slc', [128, 4, 8]), ('rs', [128, NT])]:
        tk[nm] = P.sb("tk_" + nm, shp)
    tk['rs2'] = tk['rs']
    tkg = []
    for g in range(2):
        T = {}
        for nm, shp in [('t1', [64, 4]), ('dt', [64, 4]), ('loga', [64, 4]), ('cum', [64, 8]), ('ecum', [64, 4]),
                        ('dec', [64, 4]), ('et128', [128, 4]), ('Btok', [64, 128]),
                        ('AkT', [64, 4, 64]), ('RbT', [64, 4, 64]), ('RkT', [64, 4, 64]),
                        ('Pa', [64, 4, 64]), ('PaT', [64, 4, 64]), ('Pb', [64, 4, 64]), ('PbT', [64, 4, 64]),
                        ('Xa', [64, 4, 64]), ('Xb', [64, 4, 64]), ('Vtok', [64, 4, 64]), ('Otok', [64, 4, 64]),
                        ('bbt', [64, 2, 128]), ('kbt', [64, 2, 128]), ('stm', [128, 2, 64]),
                        ('hMT', [32, 2, 32]), ('hvt', [32, 2, 128]), ('hkt', [32, 2, 128])]:
            T[nm] = P.sb("tk%d_%s" % (g, nm), shp)
        for a_, b_ in [('lgB', 'Pa'), ('Dm', 'PaT'), ('LT', 'Pb'), ('MT', 'PbT'), ('xdt', 'Xa'), ('xdec', 'Xb'),
                       ('ytok', 'Otok')]:
            T[a_] = T[b_]
        tkg.append(T)

    PSB = [P.psum("ps%d" % j, [128, 512]) for j in range(8)]
    psi = [0]

    def ps():
        t = PSB[psi[0] % 8]
        psi[0] += 1
        return t

    wi = [0]
    wbi = [0]

    def wslot():
        t = wsl[wi[0] % NW]
        wi[0] += 1
        return t

    def isap(x):
        return not isinstance(x, (int, float))

    def tt(eng, out, a, b, op):
        P.op(eng, lambda e: e.tensor_tensor(out=out, in0=a, in1=b, op=op), [out], [a, b])

    def ts(eng, out, a, s1, op0, s2=None, op1=None):
        eng = 'dve'
        ins = [a] + [s for s in (s1, s2) if s is not None and isap(s)]
        if op1 is None:
            P.op(eng, lambda e: e.tensor_scalar(out=out, in0=a, scalar1=s1, scalar2=None, op0=op0), [out], ins)
        else:
            P.op(eng, lambda e: e.tensor_scalar(out=out, in0=a, scalar1=s1, scalar2=s2, op0=op0, op1=op1), [out], ins)

    def stt(eng, out, a, s, b, op0, op1):
        eng = 'dve'
        ins = [a, b] + ([s] if isap(s) else [])
        P.op(eng, lambda e: e.scalar_tensor_tensor(out=out, in0=a, scalar=s, in1=b, op0=op0, op1=op1), [out], ins)

    def act(out, in_, func, bias=None, scale=1.0):
        ins = [in_] + ([bias] if bias is not None else []) + ([scale] if isap(scale) else [])
        if bias is None:
            P.op('act', lambda e: e.activation(out=out, in_=in_, func=func, scale=scale), [out], ins)
        else:
            P.op('act', lambda e: e.activation(out=out, in_=in_, func=func, bias=bias, scale=scale), [out], ins)

    def cp(eng, out, in_):
        if eng == 'act':
            P.op('act', lambda e: e.copy(out=out, in_=in_), [out], [in_])
        else:
            P.op(eng, lambda e: e.tensor_copy(out=out, in_=in_), [out], [in_])

    def recip(out, in_):
        P.op('dve', lambda e: e.reciprocal(out=out, in_=in_), [out], [in_])

    def mm(out, lhsT, rhs, start=True, stop=True):
        P.op('pe', lambda e: e.matmul(out, lhsT, rhs, start=start, stop=stop), [out], [lhsT, rhs])

    def tr(out, in_, n_in_part):
        idn = cc[0:n_in_part, CC['ident'][0]:CC['ident'][0] + n_in_part]
        P.op('pe', lambda e: e.transpose(out, in_, idn), [out], [in_, idn])

    def scan(out, d0, d1):
        P.op('dve', lambda e: e.tensor_tensor_scan(out=out, data0=d0, data1=d1, initial=0.0, op0=ALU.mult, op1=ALU.add),
             [out], [d0, d1])

    def memset(eng, out, val):
        P.op(eng, lambda e: e.memset(out, val), [out], [])

    def C(name, rows=128, c0=0, c1=None):
        o, n = CC[name]
        if c1 is None:
            c1 = n
        return cc[0:rows, o + c0:o + c1]

    def SC(i, rows=128):
        o = CC['sc'][0]
        return cc[0:rows, o + i:o + i + 1]

    def PPc(l, name, i=0, n=1, rows=128):
        o = PP_LAYOUT[name][0]
        return pp[l][0:rows, o + i:o + i + n]

    def bc(ap, shape, axis):
        return ap.unsqueeze(axis).broadcast_to(list(shape))

    eng_rr = [0]

    def ve():
        eng_rr[0] += 1
        return 'dve' if eng_rr[0] % 2 else 'pool'

    def rstd(out, in_, scale, eps_ap):
        act(out, in_, AF.Ln, bias=eps_ap, scale=scale)
        act(out, out, AF.Exp, scale=-0.5)

    def sigmoid(out, in_, scale=1.0, nbias=None):
        act(out, in_, AF.Exp, bias=nbias, scale=-scale)
        act(out, out, AF.Ln, bias=SC(1, out.shape[0]))
        act(out, out, AF.Exp, scale=-1.0)

    P.dma('sp', cc[:, :], cc_d[:, :])
    for l in range(DEPTH):
        P.dma('sp', pp[l][:, :], pp_d[l, :, :])
        P.dma('sp', wa2[l][:, :, :], wa2_d[l].rearrange("t p c -> p t c"))
        P.dma('sp', g2[l][:, :], g2_d[l, :, :])
        if 'a' not in KDBG:
            P.dma('sp', wdt32[l][:, :, :], w_in[l, :, 1536:1544].rearrange("(k p) c -> p k c", p=128),
                  allow_slow_non_contiguous=True)
            if TRUNK_BF16:
                P.op('dve', lambda e, l=l: e.tensor_copy(out=wdt[l][:, :, :], in_=wdt32[l][:, :, :]),
                     [wdt[l][:, :, :]], [wdt32[l][:, :, :]])
    for l in range(DEPTH if 'd' not in KDBG else 0):
        ts('dve', dv[l][:, 0:4], PPc(l, 'w0', 0, 4), -1.0, ALU.mult)
        ts('dve', dv[l][:, 4:8], PPc(l, 'a0', 0, 4), -1.0, ALU.mult)
        act(dv[l][:, 8:16], PPc(l, 'alog', 0, 8), AF.Exp)
        ts('dve', dv[l][:, 8:16], dv[l][:, 8:16], -1.0, ALU.mult)
        if l == 0:
            memset('dve', dv[l][:, 16:20], 0.0)
        else:
            tt('dve', dv[l][:, 16:20], PPc(l, 'lg0', 0, 4), PPc(l, 'lg1', 0, 4), ALU.subtract)
            act(dv[l][:, 16:20], dv[l][:, 16:20], AF.Exp)
            ts('dve', dv[l][:, 16:20], dv[l][:, 16:20], 1.0, ALU.add)
            recip(dv[l][:, 16:20], dv[l][:, 16:20])
        ts('dve', dv[l][:, 20:24], dv[l][:, 16:20], -1.0, ALU.mult, 1.0, ALU.add)

    def big_proj(wsrc, nk, c0, ntile, rhs_list, n, evac, width=None):
        banks = [ps() for _ in range(ntile)]
        wcols = ntile * 128 if width is None else width
        for k in range(nk):
            wt = wslot()
            P.dma('sp', wt[:, 0:wcols], wsrc[k * 128:(k + 1) * 128, c0:c0 + wcols])
            if TRUNK_BF16:
                wb = wbl[wbi[0] % NWB]
                wbi[0] += 1
                cp('act' if wbi[0] % 2 else 'dve', wb[:, 0:wcols], wt[:, 0:wcols])
                wt = wb
            for j in range(ntile):
                mm(banks[j][:, 0:n], wt[:, j * 128:(j + 1) * 128], rhs_list[k], start=(k == 0), stop=(k == nk - 1))
        for j in range(ntile):
            evac(j, banks[j][:, 0:n])

    def rmsnorm(l, pname, n, src, dst):
        pss = ps()
        for k in range(8):
            s = sc1[k % 2]
            tt(ve(), s[:, 0:n], src[k][:, 0:n], src[k][:, 0:n], ALU.mult)
            mm(pss[:, 0:n], C('ones'), s[:, 0:n], start=(k == 0), stop=(k == 7))
        rstd(tk['rs'][:, 0:n], pss[:, 0:n], 1.0 / D, SC(0))
        for k in range(8):
            stt(ve(), dst[k][:, 0:n], src[k][:, 0:n], PPc(l, pname, k), tk['rs'][:, 0:n], ALU.mult, ALU.mult)

    def layer(l, n, L, Lh):
        nch = n // L
        nchh = n // Lh
        W_in = w_in[l]
        rmsnorm(l, 'n1', n, xT, hT)
        hl = [hT[k][:, 0:n] for k in range(8)]

        evc = [0]

        def ev_to(dst_fn):
            def f(j, psap):
                evc[0] += 1
                cp('act' if evc[0] % 2 else 'dve', dst_fn(j), psap)
            return f
        big_proj(W_in, 8, 0, 4, hl, n, ev_to(lambda j: zT[:, j, 0:n]))
        big_proj(W_in, 8, 512, 4, hl, n, ev_to(lambda j: xsT[:, j, 3:3 + n]))
        big_proj(W_in, 8, 1024, 4, hl, n, ev_to(lambda j: bcT[:, j, 3:3 + n]))
        big_proj(W_in, 8, 1544, 4, hl, n, ev_to(lambda j: rT[:, j, 1:1 + n]))
        big_proj(W_in, 8, 1544 + 512, 4, hl, n, ev_to(lambda j: kT[:, j, 1:1 + n]))
        big_proj(W_in, 8, 1544 + 1024, 4, hl, n, ev_to(lambda j: vT[:, j, 1:1 + n]))
        big_proj(W_in, 8, 1544 + 1536, 2, hl, n, ev_to(lambda j: (x12 if j == 0 else x13)[:, 1:1 + n]))
        big_proj(W_in, 8, 3336, 4, hl, n, ev_to(lambda j: qT[:, j, 0:n]))
        big_proj(W_in, 8, 3336 + 512, 4, hl, n, ev_to(lambda j: fzT[:, j, 0:n]))
        big_proj(W_in, 8, 3336 + 1024, 4, hl, n, ev_to(lambda j: ivT[:, j, 0:n]))
        big_proj(W_in, 8, 3336 + 1536, 4, hl, n, ev_to(lambda j: ggT[:, j, 0:n]))

        if 'proj' not in STAGES:
            return
        ssd_pro(l, n, L, nch)
        hgrn_pro(l, n, Lh, nchh)
        interleave(ssd_chain(l, n, L, nch, 0), hgrn_chain(l, n, Lh, nchh, 0),
                   ssd_chain(l, n, L, nch, 1), hgrn_chain(l, n, Lh, nchh, 1))
        ssd_epi(l, n, L, nch)
        hgrn_epi(l, n, Lh, nchh)
        rwkv_pro(l, n, L, nch)
        interleave(rwkv_chain(l, n, L, nch, 0), rwkv_chain(l, n, L, nch, 1))
        rwkv_epi(l, n, L, nch)
        if 'oproj' not in STAGES:
            return

        yl = [Y[j][:, 0:n] for j in range(12)]
        for g in range(2):
            big_proj(w_out[l], 12, g * 512, 4, yl, n,
                     lambda j, psap, g=g: tt('dve', xT[4 * g + j][:, 0:n], xT[4 * g + j][:, 0:n], psap, ALU.add))

        if 'ffn' not in STAGES:
            return
        rmsnorm(l, 'n2', n, xT, hT)
        hl = [hT[k][:, 0:n] for k in range(8)]
        f0 = 0
        while f0 < 22:
            nt_ = min(4, 22 - f0)
            gb = {}

            def ev_gate(j, psap, gb=gb):
                gb[j] = psap
            big_proj(w_gate[l], 8, f0 * 128, nt_, hl, n, ev_gate)

            def ev_up(j, psap, gb=gb, f0=f0):
                f = f0 + j
                a_ = a16(f, n) if TRUNK_BF16 else SL[f // 4][:, f % 4, 0:n]
                s = sc1[j % 2]
                sigmoid(s[:, 0:n], gb[j])
                tt('dve', s[:, 0:n], s[:, 0:n], gb[j], ALU.mult)
                tt('dve', a_, s[:, 0:n], psap, ALU.mult)
            big_proj(w_up[l], 8, f0 * 128, nt_, hl, n, ev_up)
            f0 += nt_
        al = [(a16(f, n) if TRUNK_BF16 else SL[f // 4][:, f % 4, 0:n]) for f in range(22)]
        for g in range(2):
            big_proj(w_down[l], 22, g * 512, 4, al, n,
                     lambda j, psap, g=g: tt('dve', xT[4 * g + j][:, 0:n], xT[4 * g + j][:, 0:n], psap, ALU.add))

    def ssd_pro(l, n, L, nch):
        cp('pool', xsT[:, :, 0:3], hist[l][:, 0:4, :])
        cp('pool', bcT[:, :, 0:3], hist[l][:, 4:8, :])
        cwo = PP_LAYOUT['cw'][0]
        for i in range(8):
            src = xsT if i < 4 else bcT
            ii = i % 4
            a_ = sc1[i % 2]
            e1 = ve()
            ts(e1, a_[:, 0:n], src[:, ii, 0:n], pp[l][:, cwo + i * 4:cwo + i * 4 + 1], ALU.mult)
            for j in range(1, 4):
                stt(e1, a_[:, 0:n], src[:, ii, j:j + n], pp[l][:, cwo + i * 4 + j:cwo + i * 4 + j + 1], a_[:, 0:n],
                    ALU.mult, ALU.add)
            cp('pool', hist[l][:, i, :], src[:, ii, n:n + 3])
            ts('dve', a_[:, 0:n], a_[:, 0:n], PPc(l, 'cb', i), ALU.add)
            s2 = sc1[2]
            sigmoid(s2[:, 0:n], a_[:, 0:n])
            tt('dve', src[:, ii, 3:3 + n], a_[:, 0:n], s2[:, 0:n], ALU.mult)
        for i in range(4):
            s2 = sc1[i % 2]
            sigmoid(s2[:, 0:n], zT[:, i, 0:n])
            tt('dve', zT[:, i, 0:n], zT[:, i, 0:n], s2[:, 0:n], ALU.mult)

    def ssd_chain(l, n, L, nch, g):
        T = tkg[g]
        yss = SL[8 + g]
        hs4 = slice(4 * g, 4 * g + 4)
        for c in range(nch):
            cs = slice(c * L, (c + 1) * L)
            cs3 = slice(3 + c * L, 3 + (c + 1) * L)
            pdt = ps()
            for k in range(8):
                mm(pdt[0:L, 0:4], hT[k][:, cs], wdt[l][:, k, hs4], start=(k == 0), stop=(k == 7))
            tt('dve', T['t1'][0:L, :], pdt[0:L, 0:4], PPc(l, 'dtb', 4 * g, 4, rows=L), ALU.add)
            act(T['t1'][0:L, :], T['t1'][0:L, :], AF.Exp)
            act(T['dt'][0:L, :], T['t1'][0:L, :], AF.Ln, bias=SC(1, L))
            tt('dve', T['loga'][0:L, :], T['dt'][0:L, :], dv[l][0:L, 8 + 4 * g:12 + 4 * g], ALU.mult)
            yield
            pc = ps()
            mm(pc[0:L, 0:4], C('u1', L, 0, L), T['loga'][0:L, :])
            mm(pc[0:L, 4:8], C('ones', L, 0, L), T['loga'][0:L, :])
            mm(pc[:, 8:12], C('ones', L, 0, 128), T['loga'][0:L, :])
            cp('pool', T['lgB'][0:L, :, 0:L], bc(T['loga'][0:L, :], [L, 4, L], 2))
            cp('dve', T['cum'][0:L, :], pc[0:L, 0:8])
            act(T['et128'][:, :], pc[:, 8:12], AF.Exp)
            act(T['ecum'][0:L, :], T['cum'][0:L, 0:4], AF.Exp)
            tt('dve', T['dec'][0:L, :], T['cum'][0:L, 4:8], T['cum'][0:L, 0:4], ALU.subtract)
            act(T['dec'][0:L, :], T['dec'][0:L, :], AF.Exp)
            yield
            pcb = ps()
            for h in range(4):
                mm(pcb[0:L, h * L:(h + 1) * L], T['lgB'][0:L, h, 0:L], C('u1', L, 0, L))
            pg = ps()
            mm(pg[0:L, 0:L], bcT[:, g, cs3], bcT[:, 2 + g, cs3])
            px = ps()
            for i in range(2):
                tr(px[0:L, i * 128:(i + 1) * 128], xsT[:, 2 * g + i, cs3], 128)
            pb = ps()
            tr(pb[0:L, 0:128], bcT[:, g, cs3], 128)
            pcb3 = pcb[0:L, 0:4 * L].rearrange("p (h i) -> p h i", h=4)
            tt('dve', T['Dm'][0:L, :, 0:L], pcb3, bc(C('mneg', L, 0, L), [L, 4, L], 1), ALU.add)
            tt('pool', T['Dm'][0:L, :, 0:L], T['Dm'][0:L, :, 0:L], bc(T['cum'][0:L, 0:4], [L, 4, L], 2), ALU.subtract)
            act(T['LT'][0:L, :, 0:L], T['Dm'][0:L, :, 0:L], AF.Exp)
            px3 = px[0:L, 0:256].rearrange("p (h d) -> p h d", h=4)
            tt('dve', T['xdt'][0:L, :, :], px3, bc(T['dt'][0:L, :], [L, 4, 64], 2), ALU.mult)
            tt('pool', T['xdec'][0:L, :, :], T['xdt'][0:L, :, :], bc(T['dec'][0:L, :], [L, 4, 64], 2), ALU.mult)
            cp('act', T['Btok'][0:L, :], pb[0:L, 0:128])
            tt('dve', T['MT'][0:L, :, 0:L], T['LT'][0:L, :, 0:L], bc(pg[0:L, 0:L], [L, 4, L], 1), ALU.mult)
            yield
            py = ps()
            for h in range(4):
                mm(py[0:L, h * 64:(h + 1) * 64], T['MT'][0:L, h, 0:L], T['xdt'][0:L, h, :])
            pyi = ps()
            mm(pyi[0:L, 0:256], bcT[:, 2 + g, cs3], hstg[l][g][:, :])
            ph = ps()
            mm(ph[:, 0:256], T['Btok'][0:L, :], T['xdec'][0:L, :, :].rearrange("p h d -> p (h d)"))
            pyi3 = pyi[0:L, 0:256].rearrange("p (h d) -> p h d", h=4)
            py3 = py[0:L, 0:256].rearrange("p (h d) -> p h d", h=4)
            tt('dve', T['ytok'][0:L, :, :], pyi3, bc(T['ecum'][0:L, :], [L, 4, 64], 2), ALU.mult)
            tt('dve', T['ytok'][0:L, :, :], T['ytok'][0:L, :, :], py3, ALU.add)
            h3 = hstg[l][g][:, :].rearrange("p (h d) -> p h d", h=4)
            tt('dve', h3, h3, bc(T['et128'][:, :], [128, 4, 64], 2), ALU.mult)
            tt('dve', hstg[l][g][:, :], hstg[l][g][:, :], ph[:, 0:256], ALU.add)
            yield
            pyt = ps()
            for i in range(2):
                tr(pyt[:, i * L:(i + 1) * L], T['ytok'][0:L, 2 * i:2 * i + 2, :].rearrange("p h d -> p (h d)"), L)
            for i in range(2):
                stt('dve', yss[:, i, cs], xsT[:, 2 * g + i, cs3], PPc(l, 'dsk', 2 * g + i), pyt[:, i * L:(i + 1) * L],
                    ALU.mult, ALU.add)
            yield

    def ssd_epi(l, n, L, nch):
        for g in range(2):
            yss = SL[8 + g]
            for t_ in range(2):
                tt(ve(), yss[:, t_, 0:n], yss[:, t_, 0:n], zT[:, 2 * g + t_, 0:n], ALU.mult)
            pss = ps()
            for t_ in range(2):
                s = sc1[t_]
                tt(ve(), s[:, 0:n], yss[:, t_, 0:n], yss[:, t_, 0:n], ALU.mult)
                mm(pss[:, 0:n], C('ones'), s[:, 0:n], start=(t_ == 0), stop=(t_ == 1))
            rstd(tk['rs2'][:, 0:n], pss[:, 0:n], 1.0 / 256, SC(0))
            for t_ in range(2):
                i = 2 * g + t_
                stt(ve(), Y[i][:, 0:n], yss[:, t_, 0:n], PPc(l, 'snw', i), tk['rs2'][:, 0:n], ALU.mult, ALU.mult)

    def rwkv_pro(l, n, L, nch):
        c1 = slice(1, 1 + n)
        muo = PP_LAYOUT['mu'][0]
        tl = [(rT, 0), (rT, 1), (rT, 2), (rT, 3), (kT, 0), (kT, 1), (kT, 2), (kT, 3), (vT, 0), (vT, 1), (vT, 2), (vT, 3)]
        for g, t3 in enumerate((rT, kT, vT)):
            cp('pool', t3[:, :, 0:1], prev[l][:, 4 * g:4 * g + 4].unsqueeze(2))
        cp('pool', x12[:, 0:1], prev[l][:, 12:13])
        cp('pool', x13[:, 0:1], prev[l][:, 13:14])

        def shift(cur, prv, lastcol, prevdst, mucol):
            d = sc1[mucol % 2]
            e1 = ve()
            tt(e1, d[:, 0:n], prv, cur, ALU.subtract)
            cp('pool', prevdst, lastcol)
            stt(e1, cur, d[:, 0:n], pp[l][:, muo + mucol:muo + mucol + 1], cur, ALU.mult, ALU.add)
        for idx, (t3, i) in enumerate(tl):
            shift(t3[:, i, 1:1 + n], t3[:, i, 0:n], t3[:, i, n:n + 1], prev[l][:, idx:idx + 1], idx)
        shift(x12[:, 1:1 + n], x12[:, 0:n], x12[:, n:n + 1], prev[l][:, 12:13], 12)
        shift(x13[:, 1:1 + n], x13[:, 0:n], x13[:, n:n + 1], prev[l][:, 13:14], 13)
        sigmoid(x12[0:64, c1], x12[0:64, c1], scale=2.0)
        ts('dve', x12[0:64, c1], x12[0:64, c1], 2.0, ALU.mult, -1.0, ALU.add)
        sigmoid(x13[:, c1], x13[:, c1])

        LW, AS, G, CW, KKN, KM, BON, BV, RT_, AT_, BT_, KT_ = SL
        for m in range(4):
            p1 = ps()
            mm(p1[:, 0:n], wa2[l][:, 0, m * 128:(m + 1) * 128], x12[:, c1])
            sigmoid(LW[:, m, 0:n], p1[:, 0:n], nbias=dv[l][:, m:m + 1])
            ts('pool', LW[:, m, 0:n], LW[:, m, 0:n], -0.6065306597126334, ALU.mult)
            p2 = ps()
            mm(p2[:, 0:n], wa2[l][:, 1, m * 128:(m + 1) * 128], x12[:, c1])
            sigmoid(AS[:, m, 0:n], p2[:, 0:n], nbias=dv[l][:, 4 + m:5 + m])
            p3 = ps()
            mm(p3[:, 0:n], g2[l][:, m * 128:(m + 1) * 128], x13[:, c1])
            cp('act', G[:, m, 0:n], p3[:, 0:n])
            scan(CW[:, m, 0:n], C('rm64', 128, 0, n), LW[:, m, 0:n])
            ts(ve(), KKN[:, m, 0:n], kT[:, m, c1], PPc(l, 'kk', m), ALU.mult)
            s = sc1[m % 2]
            tt(ve(), s[:, 0:n], KKN[:, m, 0:n], KKN[:, m, 0:n], ALU.mult)
            p4 = ps()
            mm(p4[:, 0:n], C('bones'), s[:, 0:n])
            rstd(s[:, 0:n], p4[:, 0:n], 1.0, SC(3))
            tt(ve(), KKN[:, m, 0:n], KKN[:, m, 0:n], s[:, 0:n], ALU.mult)
            ts(ve(), KM[:, m, 0:n], AS[:, m, 0:n], PPc(l, 'ka', m), ALU.mult, PPc(l, 'ka', m), ALU.subtract)
            stt(ve(), KM[:, m, 0:n], KM[:, m, 0:n], 1.0, kT[:, m, c1], ALU.add, ALU.mult)
            s = sc1[2]
            stt(ve(), s[:, 0:n], rT[:, m, c1], PPc(l, 'rk', m), KM[:, m, 0:n], ALU.mult, ALU.mult)
            p5 = ps()
            mm(p5[:, 0:n], C('bones'), s[:, 0:n])
            tt('dve', BON[:, m, 0:n], p5[:, 0:n], vT[:, m, c1], ALU.mult)
            tt(ve(), BV[:, m, 0:n], KKN[:, m, 0:n], AS[:, m, 0:n], ALU.mult)
        Wt = AS
        act(Wt[:, :, 0:n], CW[:, :, 0:n], AF.Exp)
        tt(ve(), RT_[:, :, 0:n], rT[:, :, c1], Wt[:, :, 0:n], ALU.mult)
        cp('pool', tk['wlc'][:, :, 0:nch], Wt[:, :, L - 1:n:L])
        tt(ve(), AT_[:, :, 0:n], CW[:, :, 0:n], LW[:, :, 0:n], ALU.subtract)
        act(AT_[:, :, 0:n], AT_[:, :, 0:n], AF.Exp)
        stt(ve(), AT_[:, :, 0:n], KKN[:, :, 0:n], -1.0, AT_[:, :, 0:n], ALU.mult, ALU.mult)
        En = LW
        act(En[:, :, 0:n], CW[:, :, 0:n], AF.Exp, scale=-1.0)
        tt(ve(), BT_[:, :, 0:n], BV[:, :, 0:n], En[:, :, 0:n], ALU.mult)
        tt(ve(), KT_[:, :, 0:n], KM[:, :, 0:n], En[:, :, 0:n], ALU.mult)
        EB = LW
        for m in range(4):
            cw3 = CW[:, m, 0:n].rearrange("p (c l) -> p c l", l=L)
            eb3 = EB[:, m, 0:n].rearrange("p (c l) -> p c l", l=L)
            tt(ve(), eb3, bc(CW[:, m, L - 1:n:L], [128, nch, L], 2), cw3, ALU.subtract)
        act(EB[:, :, 0:n], EB[:, :, 0:n], AF.Exp)
        BB, KB = KKN, AS
        tt(ve(), BB[:, :, 0:n], BV[:, :, 0:n], EB[:, :, 0:n], ALU.mult)
        tt(ve(), KB[:, :, 0:n], KM[:, :, 0:n], EB[:, :, 0:n], ALU.mult)
        ATo, RTo = KM, BV
        cp(ve(), ATo[:, :, 0:n], AT_[:, :, 0:n])
        cp(ve(), RTo[:, :, 0:n], RT_[:, :, 0:n])
        memset('pool', ATo[0:64, :, 0:n], 0.0)
        memset('pool', RTo[0:64, :, 0:n], 0.0)
        memset('pool', AT_[64:128, :, 0:n], 0.0)
        memset('pool', RT_[64:128, :, 0:n], 0.0)

    def rwkv_chain(l, n, L, nch, g):
        LW, AS, G, CW, KKN, KM, BON, BV, RT_, AT_, BT_, KT_ = SL
        BB, KB = KKN, AS
        ATm, RTm = (AT_, KM), (RT_, BV)
        OT = (CW, LW)[g]
        T = tkg[g]
        ST = rstg[l][g]
        nst = int(np.log2(L))

        def v3(p_):
            return p_[0:L, 0:4 * L].rearrange("p (h i) -> p h i", h=4)

        def x3(p_):
            return p_[0:L, 0:256].rearrange("p (h d) -> p h d", h=4)
        u0b = bc(C('u0', L, 0, L), [L, 4, L], 1)
        u1b = bc(C('u1', L, 0, L), [L, 4, L], 1)
        l0b = bc(C('l0', L, 0, L), [L, 4, L], 1)
        for c in range(nch):
            cs = slice(c * L, (c + 1) * L)
            cs1 = slice(1 + c * L, 1 + (c + 1) * L)
            pN, pNT, pAk, pRb, pRk = ps(), ps(), ps(), ps(), ps()
            for hh in range(4):
                h = 4 * g + hh
                q = h // 2
                hsl = slice(hh * L, (hh + 1) * L)
                am, rm_ = ATm[h % 2], RTm[h % 2]
                mm(pN[0:L, hsl], am[:, q, cs], BT_[:, q, cs])
                mm(pNT[0:L, hsl], BT_[:, q, cs], am[:, q, cs])
                mm(pAk[0:L, hsl], KT_[:, q, cs], am[:, q, cs])
                mm(pRb[0:L, hsl], BT_[:, q, cs], rm_[:, q, cs])
                mm(pRk[0:L, hsl], KT_[:, q, cs], rm_[:, q, cs])
            pv = ps()
            for i in range(2):
                tr(pv[0:L, i * 128:(i + 1) * 128], vT[:, 2 * g + i, cs1], 128)
            pbb, pkb = ps(), ps()
            for i in range(2):
                tr(pbb[0:L, i * 128:(i + 1) * 128], BB[:, 2 * g + i, cs], 128)
                tr(pkb[0:L, i * 128:(i + 1) * 128], KB[:, 2 * g + i, cs], 128)
            Pa, PaT = T['Pa'], T['PaT']
            tt('dve', Pa[0:L, :, 0:L], v3(pN), l0b, ALU.mult)
            tt('dve', PaT[0:L, :, 0:L], v3(pNT), u0b, ALU.mult)
            tt('dve', T['AkT'][0:L, :, 0:L], v3(pAk), u0b, ALU.mult)
            cp('act', T['Vtok'][0:L, :, :], x3(pv))
            tt('dve', T['RbT'][0:L, :, 0:L], v3(pRb), u1b, ALU.mult)
            tt('dve', T['RkT'][0:L, :, 0:L], v3(pRk), u1b, ALU.mult)
            cp('act', T['bbt'][0:L, :, :], pbb[0:L, 0:256].rearrange("p (m d) -> p m d", m=2))
            cp('act', T['kbt'][0:L, :, :], pkb[0:L, 0:256].rearrange("p (m d) -> p m d", m=2))
            yield
            pX = ps()
            for hh in range(4):
                h = 4 * g + hh
                mm(pX[0:L, hh * 64:(hh + 1) * 64], ATm[h % 2][:, h // 2, cs], ST[:, hh // 2, :], start=True, stop=False)
                mm(pX[0:L, hh * 64:(hh + 1) * 64], T['AkT'][0:L, hh, 0:L], T['Vtok'][0:L, hh, :], start=False, stop=True)
            Xc, Xn = T['Xa'], T['Xb']
            cp('act', Xc[0:L, :, :], x3(pX))
            yield
            Pc, PcT, Pn, PnT = Pa, PaT, T['Pb'], T['PbT']
            for st_ in range(nst):
                pU = ps()
                for hh in range(4):
                    mm(pU[0:L, hh * 64:(hh + 1) * 64], PcT[0:L, hh, 0:L], Xc[0:L, hh, :])
                if st_ < nst - 1:
                    pS, pST = ps(), ps()
                    for hh in range(4):
                        hsl = slice(hh * L, (hh + 1) * L)
                        mm(pS[0:L, hsl], PcT[0:L, hh, 0:L], Pc[0:L, hh, 0:L])
                        mm(pST[0:L, hsl], Pc[0:L, hh, 0:L], PcT[0:L, hh, 0:L])
                tt('dve', Xn[0:L, :, :], Xc[0:L, :, :], x3(pU), ALU.add)
                Xc, Xn = Xn, Xc
                if st_ < nst - 1:
                    cp('act', Pn[0:L, :, 0:L], v3(pS))
                    cp('dve', PnT[0:L, :, 0:L], v3(pST))
                    Pc, PcT, Pn, PnT = Pn, PnT, Pc, PcT
                yield
            SA = Xc
            pO = ps()
            for hh in range(4):
                h = 4 * g + hh
                o_ = pO[0:L, hh * 64:(hh + 1) * 64]
                mm(o_, RTm[h % 2][:, h // 2, cs], ST[:, hh // 2, :], start=True, stop=False)
                mm(o_, T['RbT'][0:L, hh, 0:L], SA[0:L, hh, :], start=False, stop=False)
                mm(o_, T['RkT'][0:L, hh, 0:L], T['Vtok'][0:L, hh, :], start=False, stop=True)
            pSt = ps()
            for i in range(2):
                mm(pSt[:, i * 128:(i + 1) * 128], T['bbt'][0:L, i, :],
                   SA[0:L, 2 * i:2 * i + 2, :].rearrange("p h d -> p (h d)"), start=True, stop=False)
                mm(pSt[:, i * 128:(i + 1) * 128], T['kbt'][0:L, i, :],
                   T['Vtok'][0:L, 2 * i:2 * i + 2, :].rearrange("p h d -> p (h d)"), start=False, stop=True)
            cp('act', T['Otok'][0:L, :, :], x3(pO))
            pSt3 = pSt[:, 0:256].rearrange("p (q d) -> p q d", q=2)
            for hh2 in range(2):
                prr = slice(hh2 * 64, hh2 * 64 + 64)
                tt('dve', T['stm'][prr, :, :], ST[prr, :, :], bc(tk['wlc'][prr, 2 * g:2 * g + 2, c], [64, 2, 64], 2),
                   ALU.mult)
                tt('dve', ST[prr, :, :], T['stm'][prr, :, :], pSt3[prr, :, hh2 * 64:(hh2 + 1) * 64], ALU.add)
            yield
            pot = ps()
            for i in range(2):
                tr(pot[:, i * L:(i + 1) * L], T['Otok'][0:L, 2 * i:2 * i + 2, :].rearrange("p h d -> p (h d)"), L)
            cp('act', OT[:, 2 * g:2 * g + 2, cs], pot[:, 0:2 * L].rearrange("p (q i) -> p q i", q=2))
            yield

    def rwkv_epi(l, n, L, nch):
        LW, AS, G, CW, KKN, KM, BON, BV, RT_, AT_, BT_, KT_ = SL
        for m in range(4):
            OT = (CW, LW)[m // 2]
            pm = ps()
            mm(pm[:, 0:n], C('bones'), OT[:, m, 0:n])
            cen = sc1[m % 2]
            stt('dve', cen[:, 0:n], pm[:, 0:n], -1.0 / 64, OT[:, m, 0:n], ALU.mult, ALU.add)
            sq = sc1[2]
            tt(ve(), sq[:, 0:n], cen[:, 0:n], cen[:, 0:n], ALU.mult)
            pvv = ps()
            mm(pvv[:, 0:n], C('bones'), sq[:, 0:n])
            rstd(sq[:, 0:n], pvv[:, 0:n], 1.0 / 64, SC(2))
            tt(ve(), cen[:, 0:n], cen[:, 0:n], sq[:, 0:n], ALU.mult)
            ts(ve(), cen[:, 0:n], cen[:, 0:n], PPc(l, 'lnw', m), ALU.mult, PPc(l, 'lnb', m), ALU.add)
            tt(ve(), cen[:, 0:n], cen[:, 0:n], BON[:, m, 0:n], ALU.add)
            tt(ve(), Y[4 + m][:, 0:n], cen[:, 0:n], G[:, m, 0:n], ALU.mult)

    def hgrn_pro(l, n, Lh, nchh):
        E_, L1, KG, CU, TM, QT, KT2, QH = SL[0:8]
        mid = Lh // 2
        rmn = 'rm32' if Lh == 32 else 'rm64'
        for m in range(4):
            ts(ve(), fzT[:, m, 0:n], fzT[:, m, 0:n], -60.0, ALU.max)
        act(E_[:, :, 0:n], fzT[:, :, 0:n], AF.Exp, scale=-1.0)
        for m in range(4):
            act(L1[:, m, 0:n], E_[:, m, 0:n], AF.Ln, bias=SC(1), scale=dv[l][:, 16 + m:17 + m])
        act(TM[:, :, 0:n], E_[:, :, 0:n], AF.Ln, bias=SC(1))
        tt(ve(), L1[:, :, 0:n], L1[:, :, 0:n], TM[:, :, 0:n], ALU.subtract)
        act(TM[:, :, 0:n], TM[:, :, 0:n], AF.Exp, scale=-1.0)
        for m in range(4):
            stt(ve(), KG[:, m, 0:n], E_[:, m, 0:n], dv[l][:, 20 + m:21 + m], TM[:, m, 0:n], ALU.mult, ALU.mult)
            scan(CU[:, m, 0:n], C(rmn, 128, 0, n), L1[:, m, 0:n])
        for m in range(4):
            cu3 = CU[:, m, 0:n].rearrange("p (c l) -> p c l", l=Lh)
            tm3 = TM[:, m, 0:n].rearrange("p (c l) -> p c l", l=Lh)
            tt(ve(), tm3, cu3, bc(CU[:, m, mid:n:Lh], [128, nchh, Lh], 2), ALU.subtract)
        ts('dve', TM[:, :, 0:n], TM[:, :, 0:n], 38.0, ALU.min, -38.0, ALU.max)
        act(QT[:, :, 0:n], TM[:, :, 0:n], AF.Exp)
        act(KT2[:, :, 0:n], TM[:, :, 0:n], AF.Exp, scale=-1.0)
        tt(ve(), QT[:, :, 0:n], QT[:, :, 0:n], qT[:, :, 0:n], ALU.mult)
        tt(ve(), KT2[:, :, 0:n], KT2[:, :, 0:n], KG[:, :, 0:n], ALU.mult)
        act(QH[:, :, 0:n], CU[:, :, 0:n], AF.Exp)
        cp('pool', tk['slc'][:, :, 0:nchh], QH[:, :, Lh - 1:n:Lh])
        tt(ve(), QH[:, :, 0:n], QH[:, :, 0:n], qT[:, :, 0:n], ALU.mult)
        KH = E_
        for m in range(4):
            cu3 = CU[:, m, 0:n].rearrange("p (c l) -> p c l", l=Lh)
            kh3 = KH[:, m, 0:n].rearrange("p (c l) -> p c l", l=Lh)
            tt(ve(), kh3, bc(CU[:, m, Lh - 1:n:Lh], [128, nchh, Lh], 2), cu3, ALU.subtract)
        act(KH[:, :, 0:n], KH[:, :, 0:n], AF.Exp)
        tt(ve(), KH[:, :, 0:n], KH[:, :, 0:n], KG[:, :, 0:n], ALU.mult)

    def hgrn_chain(l, n, Lh, nchh, g):
        E_, L1, KG, CU, TM, QT, KT2, QH = SL[0:8]
        KH = E_
        OT = (L1, TM)[g]
        T = tkg[g]
        S = gstg[l][g]
        for c in range(nchh):
            cs = slice(c * Lh, (c + 1) * Lh)
            pA = ps()
            for hh in range(2):
                h = 2 * g + hh
                mm(pA[0:Lh, hh * Lh:(hh + 1) * Lh], KT2[:, h, cs], QT[:, h, cs])
            pv, pk = ps(), ps()
            for hh in range(2):
                h = 2 * g + hh
                tr(pv[0:Lh, hh * 128:(hh + 1) * 128], ivT[:, h, cs], 128)
                tr(pk[0:Lh, hh * 128:(hh + 1) * 128], KH[:, h, cs], 128)
            tt('dve', T['hMT'][0:Lh, :, 0:Lh], pA[0:Lh, 0:2 * Lh].rearrange("p (h i) -> p h i", h=2),
               bc(C('u1', Lh, 0, Lh), [Lh, 2, Lh], 1), ALU.mult)
            cp('act', T['hvt'][0:Lh, :, :], pv[0:Lh, 0:256].rearrange("p (h d) -> p h d", h=2))
            cp('dve', T['hkt'][0:Lh, :, :], pk[0:Lh, 0:256].rearrange("p (h d) -> p h d", h=2))
            yield
            po = ps()
            for hh in range(2):
                h = 2 * g + hh
                o_ = po[:, hh * Lh:(hh + 1) * Lh]
                mm(o_, T['hvt'][0:Lh, hh, :], T['hMT'][0:Lh, hh, 0:Lh], start=True, stop=False)
                mm(o_, S[:, hh, :], QH[:, h, cs], start=False, stop=True)
            pS = ps()
            for hh in range(2):
                mm(pS[:, hh * 128:(hh + 1) * 128], T['hkt'][0:Lh, hh, :], T['hvt'][0:Lh, hh, :])
            cp('act', OT[:, 2 * g:2 * g + 2, cs], po[:, 0:2 * Lh].rearrange("p (h i) -> p h i", h=2))
            for hh in range(2):
                h = 2 * g + hh
                stt('dve', S[:, hh, :], S[:, hh, :], tk['slc'][:, h, c:c + 1], pS[:, hh * 128:(hh + 1) * 128],
                    ALU.mult, ALU.add)
            yield

    def hgrn_epi(l, n, Lh, nchh):
        E_, L1, KG, CU, TM, QT, KT2, QH = SL[0:8]
        for h in range(4):
            OT = (L1, TM)[h // 2]
            sq = sc1[h % 2]
            tt(ve(), sq[:, 0:n], OT[:, h, 0:n], OT[:, h, 0:n], ALU.mult)
            pss = ps()
            mm(pss[:, 0:n], C('ones'), sq[:, 0:n])
            rstd(sq[:, 0:n], pss[:, 0:n], 1.0 / 128, SC(0))
            stt(ve(), OT[:, h, 0:n], OT[:, h, 0:n], PPc(l, 'hnw', h), sq[:, 0:n], ALU.mult, ALU.mult)
            s2 = sc1[2]
            sigmoid(s2[:, 0:n], ggT[:, h, 0:n])
            tt(ve(), s2[:, 0:n], s2[:, 0:n], ggT[:, h, 0:n], ALU.mult)
            tt(ve(), Y[8 + h][:, 0:n], OT[:, h, 0:n], s2[:, 0:n], ALU.mult)

    def interleave(*gens):
        gens = list(gens)
        while gens:
            for g_ in list(gens):
                try:
                    next(g_)
                except StopIteration:
                    gens.remove(g_)


    def run_tile(src, n, L, Lh, dst):
        nb = (n + 127) // 128
        for b in range(nb):
            tb = min(128, n - b * 128)
            xi = xin[0]
            P.dma(DQ, xi[0:tb, :], src[b * 128:b * 128 + tb, :])
            for g in range(2 if 'i' not in KDBG else 0):
                pt = ps()
                for k in range(4):
                    tr(pt[:, k * 128:k * 128 + tb], xi[0:tb, (4 * g + k) * 128:(4 * g + k + 1) * 128], tb)
                for k in range(4):
                    cp('act' if (k % 2 and 'j' not in KDBG) else 'dve', xT[4 * g + k][:, b * 128:b * 128 + tb], pt[:, k * 128:k * 128 + tb])
        for l in range(DEPTH):
            layer(l, n, L, Lh)
        if dst is None or 'g' in KDBG:
            return
        hF = [SL[k // 4][:, k % 4, :] for k in range(8)]
        if 'h' not in KDBG:
            rmsnorm(DEPTH - 1, 'fin', n, xT, hF)
        for b in range(nb):
            tb = min(128, n - b * 128)
            xo = xin[0]
            for g in range(2):
                pt = ps()
                for k in range(4):
                    tr(pt[0:tb, k * 128:(k + 1) * 128], hF[4 * g + k][:, b * 128:b * 128 + tb], 128)
                cp('act' if g else 'dve', xo[0:tb, g * 512:(g + 1) * 512], pt[0:tb, :])
            P.dma(DQ, dst[b * 128:b * 128 + tb, :], xo[0:tb, :])

    def store_states(si):
        for l in range(DEPTH):
            for i in range(4):
                pt = ps()
                tr(pt[:, 0:128], hstg[l][i // 2][:, (i % 2) * 128:(i % 2 + 1) * 128], 128)
                cp('act', stmp[:, i, :], pt[:, 0:128])
            for hh in range(2):
                P.dma(DQ, o_ssm[si, l].rearrange("(i hh) p n -> hh p i n", hh=2)[hh],
                      stmp[hh * 64:(hh + 1) * 64, :, :])
            for i in range(8):
                P.dma(DQ, o_conv[si, l][:, i * 128:(i + 1) * 128].rearrange("j p -> p j"), hist[l][:, i, :],
                      allow_slow_non_contiguous=True)
            P.dma(DQ, o_shift[si, l].rearrange("(i p) -> p i", p=128), prev[l][:, :],
                  allow_slow_non_contiguous=True)
            for q in range(4):
                pt = ps()
                tr(pt[0:64, 0:128], rstg[l][q // 2][:, q % 2, :], 128)
                cp('act', rtmp[:, q, :, :], pt[0:64, 0:128].rearrange("p (hh n) -> p hh n", hh=2))
            P.dma(DQ, o_rwkv[si, l].rearrange("(q hh) v n -> v q hh n", hh=2), rtmp[:, :, :, :])
            for g in range(2):
                P.dma(DQ, o_hgrn[si, l, 2 * g:2 * g + 2].rearrange("h k v -> k h v"), gstg[l][g][:, :, :])

    def load_states():
        for l in range(DEPTH):
            for hh in range(2):
                P.dma(DQ, stmp[hh * 64:(hh + 1) * 64, :, :],
                      st_ssm[l].rearrange("(i hh) p n -> hh p i n", hh=2)[hh])
            for i in range(4):
                pt = ps()
                tr(pt[:, 0:128], stmp[:, i, :], 128)
                cp('act', hstg[l][i // 2][:, (i % 2) * 128:(i % 2 + 1) * 128], pt[:, 0:128])
            for i in range(8):
                P.dma(DQ, hist[l][:, i, :], st_conv[l][:, i * 128:(i + 1) * 128].rearrange("j p -> p j"),
                      allow_slow_non_contiguous=True)
            P.dma(DQ, prev[l][:, :], st_shift[l].rearrange("(i p) -> p i", p=128),
                  allow_slow_non_contiguous=True)
            P.dma(DQ, rtmp[:, :, :, :], st_rwkv[l].rearrange("(q hh) v n -> v q hh n", hh=2))
            for q in range(4):
                pt = ps()
                tr(pt[:, 0:64], rtmp[:, q, :, :].rearrange("p hh n -> p (hh n)"), 64)
                cp('act', rstg[l][q // 2][:, q % 2, :], pt[:, 0:64])
            for g in range(2):
                P.dma(DQ, gstg[l][g][:, :, :], st_hgrn[l, 2 * g:2 * g + 2].rearrange("h k v -> k h v"))

    def zero_states():
        for l in range(DEPTH):
            for g in range(2):
                memset('pool', hstg[l][g][:, :], 0.0)
                memset('pool', rstg[l][g][:, :, :], 0.0)
                memset('pool', gstg[l][g][:, :, :], 0.0)
            memset('pool', hist[l][:, :, :], 0.0)
            memset('pool', prev[l][:, :], 0.0)

    if 'states' in STAGES:
        load_states()
    else:
        zero_states()
    if 'f' not in KDBG:
        run_tile(xs_in, 64, 64, 32, y_s)
    if 'states' in STAGES:
        store_states(1)
    zero_states()
    if 'b' not in KDBG:
        run_tile(meta, 16, 16, 16, None)
    t0 = 0 if 'c' not in KDBG else SEQ
    while t0 < SEQ:
        n = min(NT, SEQ - t0)
        run_tile(xp[t0:t0 + n, :], n, 64, 32, y_p[t0:t0 + n, :])
        t0 += n
    if 'states' in STAGES:
        store_states(0)

    P.emit()
    return nc, es, P


_CACHE = {}


def kernel(**inp):
    inp = {k: np.asarray(v, dtype=np.float32) for k, v in inp.items()}
    B, SEQ, _ = inp['x_prompt'].shape
    DEPTH = inp['w_in'].shape[0]
    pps = np.stack([_pack_params(inp, l) for l in range(DEPTH)], axis=0)
    ccs = _consts()
    zpad = np.zeros_like(inp['rw_w2'])
    wa2 = np.ascontiguousarray(np.stack([np.concatenate([inp['rw_w2'], zpad], axis=1),
                                         np.concatenate([zpad, inp['rw_a2']], axis=1)], axis=1))
    key = (SEQ, DEPTH)
    if key not in _CACHE:
        _CACHE[key] = build(SEQ, DEPTH=DEPTH)
    nc = _CACHE[key][0]
    in_maps = []
    for c in range(NCORES):
        in_maps.append({
            "xp": np.ascontiguousarray(inp['x_prompt'][c]),
            "xs": np.ascontiguousarray(inp['x_sample'][c]),
            "meta": inp['meta_tokens'],
            "st_ssm": np.ascontiguousarray(inp['state_ssm'][:, c]),
            "st_conv": np.ascontiguousarray(inp['state_conv'][:, c]),
            "st_rwkv": np.ascontiguousarray(inp['state_rwkv'][:, c]),
            "st_shift": np.ascontiguousarray(inp['state_shift'][:, c]),
            "st_hgrn": np.ascontiguousarray(inp['state_hgrn'][:, c]),
            "w_in": inp['w_in'], "w_out": inp['w_out'], "w_gate": inp['w_gate'], "w_up": inp['w_up'],
            "w_down": inp['w_down'], "wa2": wa2, "g2": inp['rw_g2'], "pp": pps, "cc": ccs,
        })
    in_maps = [{"i_" + k: v for k, v in m.items()} for m in in_maps]
    if os.environ.get('KONE'):
        res = run_bass_kernel_spmd(nc, in_maps[:1], core_ids=[0])
        R = [{k[2:]: v for k, v in res.results[0].items()}] * NCORES
    else:
        res = run_bass_kernel_spmd(nc, in_maps, core_ids=list(range(NCORES)))
        R = [{k[2:]: v for k, v in r.items()} for r in res.results]
    y_prompt = np.stack([R[c]["y_p"] for c in range(NCORES)], axis=0)
    y_sample = np.stack([R[c]["y_s"] for c in range(NCORES)], axis=0)

    def st(name, si):
        return np.stack([R[c][name][si] for c in range(NCORES)], axis=1)
    outs = [y_prompt, y_sample]
    for si in (0, 1):
        for name in ("o_ssm", "o_conv", "o_rwkv", "o_shift", "o_hgrn"):
            outs.append(st(name, si))
    return tuple(np.ascontiguousarray(o, dtype=np.float32) for o in outs)
```

```python
import numpy as np
from contextlib import ExitStack
import concourse.bass as bass
import concourse.mybir as mybir
from concourse.bass_utils import run_bass_kernel_spmd

F32 = mybir.dt.float32
BF16 = mybir.dt.bfloat16
AF = mybir.ActivationFunctionType
ALU = mybir.AluOpType

D = 1024
DFF = 2816
NIN = 5384
NCORES = 8
DQ = 'sp'
TRUNK_BF16 = True
import os
KDBG = os.environ.get('KDBG', '')
STAGES = {'states', 'proj', 'ssd', 'rwkv', 'hgrn', 'oproj', 'ffn'}
ENGS = ['pe', 'act', 'dve', 'pool', 'sp']


class Dep:
    __slots__ = ('lw', 'rd')

    def __init__(self):
        self.lw = None
        self.rd = []


class Prog:
    def __init__(self, nc, es):
        self.nc, self.es = nc, es
        self.q = {e: [] for e in ENGS}
        self.deps = {}
        self.dsem = {}
        self.sem = {e: es.enter_context(nc.semaphore("q_" + e)) for e in ENGS}
        self.tensors = {}
        self.psum_names = set()

    def sb(self, name, shape, dtype=F32):
        name = "s_" + name
        t = self.es.enter_context(self.nc.sbuf_tensor(name, list(shape), dtype))
        self.deps[name] = Dep()
        return t

    def psum(self, name, shape):
        t = self.es.enter_context(self.nc.psum_tensor(name, list(shape), F32))
        self.deps[name] = Dep()
        self.psum_names.add(name)
        return t

    def _dep(self, ap):
        nm = ap.tensor.name
        return self.deps.get(nm)

    def _collect(self, eng, outs, ins):
        w = []
        for ap in ins:
            d = self._dep(ap)
            if d is not None and d.lw is not None:
                w.append(d.lw)
            if d is not None and ap.tensor.name in self.psum_names:
                for r in d.rd:
                    if not (r[0] == 'e' and r[1] == eng):
                        w.append(r)
        for ap in outs:
            d = self._dep(ap)
            if d is None:
                continue
            if d.lw is not None:
                w.append(d.lw)
            for r in d.rd:
                w.append(r)
        if eng == 'pe':
            w = [x for x in w if not (x[0] == 'e' and x[1] == 'pe')]
        return w

    def _commit(self, ev, outs, ins):
        for ap in ins:
            d = self._dep(ap)
            if d is not None:
                d.rd.append(ev)
        for ap in outs:
            d = self._dep(ap)
            if d is not None:
                d.lw = ev
                d.rd = []

    def op(self, eng, fn, outs, ins):
        w = self._collect(eng, outs, ins)
        idx = len(self.q[eng])
        self.q[eng].append(dict(fn=fn, waits=w, kind='c'))
        self._commit(('e', eng, idx), outs, ins)

    def dma(self, eng, out, in_, semname=None, **kw):
        outs, ins = [out], [in_]
        w = self._collect(eng, outs, ins)
        if semname is None:
            d = self._dep(out)
            semname = out.tensor.name if d is not None else in_.tensor.name
        if semname not in self.dsem:
            self.dsem[semname] = [self.es.enter_context(self.nc.semaphore("d_" + semname)), 0]
        s = self.dsem[semname]
        s[1] += 16
        self.q[eng].append(dict(fn=lambda e: e.dma_start(out=out, in_=in_, **kw), waits=w, kind='d', sem=semname))
        self._commit(('d', semname, s[1]), outs, ins)

    def emit(self):
        nc = self.nc
        targets = {e: set() for e in ENGS}
        for e in ENGS:
            for ins in self.q[e]:
                for d in ins['waits']:
                    if d[0] == 'e':
                        targets[d[1]].add(d[2])
        semval = {}
        for e in ENGS:
            c = 0
            vals = []
            for i in range(len(self.q[e])):
                if i in targets[e]:
                    c += 1
                vals.append(c)
            semval[e] = vals
        final = [(self.dsem[k][0], self.dsem[k][1]) for k in self.dsem]

        def body_for(e):
            def body(eng):
                waited = {}
                for i, ins in enumerate(self.q[e]):
                    need = {}
                    for d in ins['waits']:
                        if d[0] == 'e':
                            key = ('e', d[1])
                            val = semval[d[1]][d[2]]
                        else:
                            key = ('d', d[1])
                            val = d[2]
                        if val > need.get(key, 0):
                            need[key] = val
                    for key, val in need.items():
                        if waited.get(key, 0) >= val:
                            continue
                        waited[key] = val
                        h = self.sem[key[1]] if key[0] == 'e' else self.dsem[key[1]][0]
                        eng.wait_ge(h, val)
                    r = ins['fn'](eng)
                    if ins['kind'] == 'c':
                        if i in targets[e]:
                            r.then_inc(self.sem[e], 1)
                    else:
                        r.then_inc(self.dsem[ins['sem']][0], 16)
                if e == 'sp':
                    for h, v in final:
                        eng.wait_ge(h, v)
            return body

        with nc.Block() as block:
            block.sync(body_for('sp'))
            block.tensor(body_for('pe'))
            block.scalar(body_for('act'))
            block.vector(body_for('dve'))
            block.gpsimd(body_for('pool'))


def _cols(v):
    v = np.asarray(v, np.float32)
    return v.reshape(-1, 128).T


PP_LAYOUT = {}


def _pack_params(inp, l):
    parts = []
    off = 0

    def add(name, arr):
        nonlocal off
        arr = np.ascontiguousarray(arr, dtype=np.float32)
        PP_LAYOUT[name] = (off, arr.shape[1])
        off += arr.shape[1]
        parts.append(arr)

    add('n1', _cols(inp['norm1_w'][l]))
    add('n2', _cols(inp['norm2_w'][l]))
    cw = inp['conv_w'][l]
    add('cw', np.stack([cw[j].reshape(8, 128).T for j in range(4)], axis=2).reshape(128, 32))
    add('cb', _cols(inp['conv_b'][l]))
    add('dsk', _cols(np.repeat(inp['d_skip'][l], 64)))
    add('snw', _cols(inp['ssd_norm_w'][l]))
    add('dtb', np.tile(inp['dt_bias'][l][None, :], (128, 1)))
    add('alog', np.tile(inp['a_log'][l][None, :], (128, 1)))
    add('mu', _cols(inp['rw_mu'][l]))
    add('w0', _cols(inp['rw_w0'][l]))
    add('a0', _cols(inp['rw_a0'][l]))
    add('kk', _cols(inp['rw_kk'][l]))
    add('ka', _cols(inp['rw_ka'][l]))
    add('rk', _cols(inp['rw_rk'][l]))
    add('lnw', _cols(inp['rw_lnx_w'][l]))
    add('lnb', _cols(inp['rw_lnx_b'][l]))
    add('lg0', _cols(inp['hg_lb_logits'][0]))
    add('lg1', _cols(inp['hg_lb_logits'][1]))
    add('hnw', _cols(inp['hg_norm_w'][l]))
    add('fin', _cols(inp['final_norm_w']))
    PP_LAYOUT['_n'] = off
    return np.concatenate(parts, axis=1)


CC = {}


def _consts():
    parts = []
    off = 0

    def add(name, arr):
        nonlocal off
        arr = np.ascontiguousarray(arr, dtype=np.float32)
        assert arr.shape[0] == 128
        CC[name] = (off, arr.shape[1])
        off += arr.shape[1]
        parts.append(arr)

    i = np.arange(128)
    add('ident', np.eye(128))
    add('ones', np.ones((128, 128)))
    bo = np.zeros((128, 128))
    bo[:64, :64] = 1
    bo[64:, 64:] = 1
    add('bones', bo)
    u1 = (i[:, None] <= i[None, :]).astype(np.float32)
    u0 = (i[:, None] < i[None, :]).astype(np.float32)
    add('u1', u1)
    add('u0', u0)
    add('l0', u0.T)
    add('mneg', (u1 - 1.0) * 30000.0)
    t = np.arange(256)
    add('rm64', np.tile(((t % 64) != 0).astype(np.float32)[None, :], (128, 1)))
    add('rm32', np.tile(((t % 32) != 0).astype(np.float32)[None, :], (128, 1)))
    sc = np.zeros((128, 8), np.float32)
    sc[:, 0] = 1e-6
    sc[:, 1] = 1.0
    sc[:, 2] = 64e-5
    sc[:, 3] = 1e-12
    sc[:, 4] = -0.5
    add('sc', sc)
    CC['_n'] = off
    return np.concatenate(parts, axis=1)


def build(SEQ, NT=int(os.environ.get('KNT', '256')), DEPTH=2):
    nc = bass.Bass("TRN2", target_bir_lowering=False)
    TDT = BF16 if TRUNK_BF16 else F32
    if TRUNK_BF16:
        nc.allow_low_precision("trunk projections use bf16 operands with fp32 accumulation")
    es = ExitStack()
    P = Prog(nc, es)
    NPP = PP_LAYOUT['_n']
    NCC = CC['_n']

    def din(name, shape):
        return nc.dram_tensor("i_" + name, list(shape), F32, kind="ExternalInput").ap()

    def dout(name, shape):
        return nc.dram_tensor("r_" + name, list(shape), F32, kind="ExternalOutput").ap()

    xp = din("xp", [SEQ, D])
    xs_in = din("xs", [64, D])
    meta = din("meta", [16, D])
    st_ssm = din("st_ssm", [DEPTH, 8, 64, 128])
    st_conv = din("st_conv", [DEPTH, 3, 1024])
    st_rwkv = din("st_rwkv", [DEPTH, 8, 64, 64])
    st_shift = din("st_shift", [DEPTH, 1792])
    st_hgrn = din("st_hgrn", [DEPTH, 4, 128, 128])
    w_in = din("w_in", [DEPTH, D, NIN])
    w_out = din("w_out", [DEPTH, 1536, D])
    w_gate = din("w_gate", [DEPTH, D, DFF])
    w_up = din("w_up", [DEPTH, D, DFF])
    w_down = din("w_down", [DEPTH, DFF, D])
    wa2_d = din("wa2", [DEPTH, 2, 128, 512])
    g2_d = din("g2", [DEPTH, 128, 512])
    pp_d = din("pp", [DEPTH, 128, NPP])
    cc_d = din("cc", [128, NCC])

    y_p = dout("y_p", [SEQ, D])
    y_s = dout("y_s", [64, D])
    o_ssm = dout("o_ssm", [2, DEPTH, 8, 64, 128])
    o_conv = dout("o_conv", [2, DEPTH, 3, 1024])
    o_rwkv = dout("o_rwkv", [2, DEPTH, 8, 64, 64])
    o_shift = dout("o_shift", [2, DEPTH, 1792])
    o_hgrn = dout("o_hgrn", [2, DEPTH, 4, 128, 128])

    cc = P.sb("cc", [128, NCC])
    pp = [P.sb("pp%d" % l, [128, NPP]) for l in range(DEPTH)]
    dv = [P.sb("dv%d" % l, [128, 40]) for l in range(DEPTH)]
    wa2 = [P.sb("wa2_%d" % l, [128, 2, 512]) for l in range(DEPTH)]
    g2 = [P.sb("g2_%d" % l, [128, 512]) for l in range(DEPTH)]
    wdt32 = [P.sb("wdt%d" % l, [128, 8, 8]) for l in range(DEPTH)]
    wdt = [P.sb("wdtb%d" % l, [128, 8, 8], TDT) for l in range(DEPTH)] if TRUNK_BF16 else wdt32

    xT = [P.sb("xT%d" % k, [128, NT]) for k in range(8)]
    hT = [P.sb("hT%d" % k, [128, NT], TDT) for k in range(8)]
    zT = P.sb("zT", [128, 4, NT])
    xsT = P.sb("xsT", [128, 4, NT + 3])
    bcT = P.sb("bcT", [128, 4, NT + 3])
    rT = P.sb("rT", [128, 4, NT + 1])
    kT = P.sb("kT", [128, 4, NT + 1])
    vT = P.sb("vT", [128, 4, NT + 1])
    x12 = P.sb("x12", [128, NT + 1])
    x13 = P.sb("x13", [128, NT + 1])
    qT = P.sb("qT", [128, 4, NT])
    fzT = P.sb("fzT", [128, 4, NT])
    ivT = P.sb("ivT", [128, 4, NT])
    ggT = P.sb("ggT", [128, 4, NT])
    Y = [P.sb("Y%d" % j, [128, NT], TDT) for j in range(12)]

    NSL = 12
    SL = [P.sb("SL%d" % j, [128, 4, NT]) for j in range(NSL)]
    sc1 = [P.sb("sc1_%d" % j, [128, NT]) for j in range(3)]

    def a16(f, n):
        v = SL[f // 8][:, :, :].rearrange("p a b -> p (a b)").bitcast(BF16)
        t = f % 8
        return v[:, t * NT:t * NT + n]
    NW = 9
    wsl = [P.sb("wsl%d" % j, [128, 512]) for j in range(NW)]
    NWB = 4
    wbl = [P.sb("wbl%d" % j, [128, 512], TDT) for j in range(NWB)] if TRUNK_BF16 else None
    xin = [P.sb("xin%d" % j, [128, D]) for j in range(1)]

    hstg = [[P.sb("hst%d_%d" % (l, g), [128, 256]) for g in range(2)] for l in range(DEPTH)]
    hist = [P.sb("hist%d" % l, [128, 8, 3]) for l in range(DEPTH)]
    prev = [P.sb("prev%d" % l, [128, 14]) for l in range(DEPTH)]
    rstg = [[P.sb("rst%d_%d" % (l, g), [128, 2, 64]) for g in range(2)] for l in range(DEPTH)]
    gstg = [[P.sb("gst%d_%d" % (l, g), [128, 2, 128]) for g in range(2)] for l in range(DEPTH)]
    stmp = P.sb("stmp", [128, 4, 128])
    rtmp = P.sb("rtmp", [64, 4, 2, 64])

    tk = {}
    for nm, shp in [('wlc', [128, 4, 4]), ('slc', [128, 4, 8]), ('rs', [128, NT])]:
        tk[nm] = P.sb("tk_" + nm, shp)
    tk['rs2'] = tk['rs']
    tkg = []
    for g in range(2):
        T = {}
        for nm, shp in [('t1', [64, 4]), ('dt', [64, 4]), ('loga', [64, 4]), ('cum', [64, 8]), ('ecum', [64, 4]),
                        ('dec', [64, 4]), ('et128', [128, 4]), ('Btok', [64, 128]),
                        ('AkT', [64, 4, 64]), ('RbT', [64, 4, 64]), ('RkT', [64, 4, 64]),
                        ('Pa', [64, 4, 64]), ('PaT', [64, 4, 64]), ('Pb', [64, 4, 64]), ('PbT', [64, 4, 64]),
                        ('Xa', [64, 4, 64]), ('Xb', [64, 4, 64]), ('Vtok', [64, 4, 64]), ('Otok', [64, 4, 64]),
                        ('bbt', [64, 2, 128]), ('kbt', [64, 2, 128]), ('stm', [128, 2, 64]),
                        ('hMT', [32, 2, 32]), ('hvt', [32, 2, 128]), ('hkt', [32, 2, 128])]:
            T[nm] = P.sb("tk%d_%s" % (g, nm), shp)
        for a_, b_ in [('lgB', 'Pa'), ('Dm', 'PaT'), ('LT', 'Pb'), ('MT', 'PbT'), ('xdt', 'Xa'), ('xdec', 'Xb'),
                       ('ytok', 'Otok')]:
            T[a_] = T[b_]
        tkg.append(T)

    PSB = [P.psum("ps%d" % j, [128, 512]) for j in range(8)]
    psi = [0]

    def ps():
        t = PSB[psi[0] % 8]
        psi[0] += 1
        return t

    wi = [0]
    wbi = [0]

    def wslot():
        t = wsl[wi[0] % NW]
        wi[0] += 1
        return t

    def isap(x):
        return not isinstance(x, (int, float))

    def tt(eng, out, a, b, op):
        P.op(eng, lambda e: e.tensor_tensor(out=out, in0=a, in1=b, op=op), [out], [a, b])

    def ts(eng, out, a, s1, op0, s2=None, op1=None):
        eng = 'dve'
        ins = [a] + [s for s in (s1, s2) if s is not None and isap(s)]
        if op1 is None:
            P.op(eng, lambda e: e.tensor_scalar(out=out, in0=a, scalar1=s1, scalar2=None, op0=op0), [out], ins)
        else:
            P.op(eng, lambda e: e.tensor_scalar(out=out, in0=a, scalar1=s1, scalar2=s2, op0=op0, op1=op1), [out], ins)

    def stt(eng, out, a, s, b, op0, op1):
        eng = 'dve'
        ins = [a, b] + ([s] if isap(s) else [])
        P.op(eng, lambda e: e.scalar_tensor_tensor(out=out, in0=a, scalar=s, in1=b, op0=op0, op1=op1), [out], ins)

    def act(out, in_, func, bias=None, scale=1.0):
        ins = [in_] + ([bias] if bias is not None else []) + ([scale] if isap(scale) else [])
        if bias is None:
            P.op('act', lambda e: e.activation(out=out, in_=in_, func=func, scale=scale), [out], ins)
        else:
            P.op('act', lambda e: e.activation(out=out, in_=in_, func=func, bias=bias, scale=scale), [out], ins)

    def cp(eng, out, in_):
        if eng == 'act':
            P.op('act', lambda e: e.copy(out=out, in_=in_), [out], [in_])
        else:
            P.op(eng, lambda e: e.tensor_copy(out=out, in_=in_), [out], [in_])

    def recip(out, in_):
        P.op('dve', lambda e: e.reciprocal(out=out, in_=in_), [out], [in_])

    def mm(out, lhsT, rhs, start=True, stop=True):
        P.op('pe', lambda e: e.matmul(out, lhsT, rhs, start=start, stop=stop), [out], [lhsT, rhs])

    def tr(out, in_, n_in_part):
        idn = cc[0:n_in_part, CC['ident'][0]:CC['ident'][0] + n_in_part]
        P.op('pe', lambda e: e.transpose(out, in_, idn), [out], [in_, idn])

    def scan(out, d0, d1):
        P.op('dve', lambda e: e.tensor_tensor_scan(out=out, data0=d0, data1=d1, initial=0.0, op0=ALU.mult, op1=ALU.add),
             [out], [d0, d1])

    def memset(eng, out, val):
        P.op(eng, lambda e: e.memset(out, val), [out], [])

    def C(name, rows=128, c0=0, c1=None):
        o, n = CC[name]
        if c1 is None:
            c1 = n
        return cc[0:rows, o + c0:o + c1]

    def SC(i, rows=128):
        o = CC['sc'][0]
        return cc[0:rows, o + i:o + i + 1]

    def PPc(l, name, i=0, n=1, rows=128):
        o = PP_LAYOUT[name][0]
        return pp[l][0:rows, o + i:o + i + n]

    def bc(ap, shape, axis):
        return ap.unsqueeze(axis).broadcast_to(list(shape))

    eng_rr = [0]

    def ve():
        eng_rr[0] += 1
        return 'dve' if eng_rr[0] % 2 else 'pool'

    def rstd(out, in_, scale, eps_ap):
        act(out, in_, AF.Ln, bias=eps_ap, scale=scale)
        act(out, out, AF.Exp, scale=-0.5)

    def sigmoid(out, in_, scale=1.0, nbias=None):
        act(out, in_, AF.Exp, bias=nbias, scale=-scale)
        act(out, out, AF.Ln, bias=SC(1, out.shape[0]))
        act(out, out, AF.Exp, scale=-1.0)

    P.dma('sp', cc[:, :], cc_d[:, :])
    for l in range(DEPTH):
        P.dma('sp', pp[l][:, :], pp_d[l, :, :])
        P.dma('sp', wa2[l][:, :, :], wa2_d[l].rearrange("t p c -> p t c"))
        P.dma('sp', g2[l][:, :], g2_d[l, :, :])
        if 'a' not in KDBG:
            P.dma('sp', wdt32[l][:, :, :], w_in[l, :, 1536:1544].rearrange("(k p) c -> p k c", p=128),
                  allow_slow_non_contiguous=True)
            if TRUNK_BF16:
                P.op('dve', lambda e, l=l: e.tensor_copy(out=wdt[l][:, :, :], in_=wdt32[l][:, :, :]),
                     [wdt[l][:, :, :]], [wdt32[l][:, :, :]])
    for l in range(DEPTH if 'd' not in KDBG else 0):
        ts('dve', dv[l][:, 0:4], PPc(l, 'w0', 0, 4), -1.0, ALU.mult)
        ts('dve', dv[l][:, 4:8], PPc(l, 'a0', 0, 4), -1.0, ALU.mult)
        act(dv[l][:, 8:16], PPc(l, 'alog', 0, 8), AF.Exp)
        ts('dve', dv[l][:, 8:16], dv[l][:, 8:16], -1.0, ALU.mult)
        if l == 0:
            memset('dve', dv[l][:, 16:20], 0.0)
        else:
            tt('dve', dv[l][:, 16:20], PPc(l, 'lg0', 0, 4), PPc(l, 'lg1', 0, 4), ALU.subtract)
            act(dv[l][:, 16:20], dv[l][:, 16:20], AF.Exp)
            ts('dve', dv[l][:, 16:20], dv[l][:, 16:20], 1.0, ALU.add)
            recip(dv[l][:, 16:20], dv[l][:, 16:20])
        ts('dve', dv[l][:, 20:24], dv[l][:, 16:20], -1.0, ALU.mult, 1.0, ALU.add)

    def big_proj(wsrc, nk, c0, ntile, rhs_list, n, evac, width=None):
        banks = [ps() for _ in range(ntile)]
        wcols = ntile * 128 if width is None else width
        for k in range(nk):
            wt = wslot()
            P.dma('sp', wt[:, 0:wcols], wsrc[k * 128:(k + 1) * 128, c0:c0 + wcols])
            if TRUNK_BF16:
                wb = wbl[wbi[0] % NWB]
                wbi[0] += 1
                cp('act' if wbi[0] % 2 else 'dve', wb[:, 0:wcols], wt[:, 0:wcols])
                wt = wb
            for j in range(ntile):
                mm(banks[j][:, 0:n], wt[:, j * 128:(j + 1) * 128], rhs_list[k], start=(k == 0), stop=(k == nk - 1))
        for j in range(ntile):
            evac(j, banks[j][:, 0:n])

    def rmsnorm(l, pname, n, src, dst):
        pss = ps()
        for k in range(8):
            s = sc1[k % 2]
            tt(ve(), s[:, 0:n], src[k][:, 0:n], src[k][:, 0:n], ALU.mult)
            mm(pss[:, 0:n], C('ones'), s[:, 0:n], start=(k == 0), stop=(k == 7))
        rstd(tk['rs'][:, 0:n], pss[:, 0:n], 1.0 / D, SC(0))
        for k in range(8):
            stt(ve(), dst[k][:, 0:n], src[k][:, 0:n], PPc(l, pname, k), tk['rs'][:, 0:n], ALU.mult, ALU.mult)

    def layer(l, n, L, Lh):
        nch = n // L
        nchh = n // Lh
        W_in = w_in[l]
        rmsnorm(l, 'n1', n, xT, hT)
        hl = [hT[k][:, 0:n] for k in range(8)]

        evc = [0]

        def ev_to(dst_fn):
            def f(j, psap):
                evc[0] += 1
                cp('act' if evc[0] % 2 else 'dve', dst_fn(j), psap)
            return f
        big_proj(W_in, 8, 0, 4, hl, n, ev_to(lambda j: zT[:, j, 0:n]))
        big_proj(W_in, 8, 512, 4, hl, n, ev_to(lambda j: xsT[:, j, 3:3 + n]))
        big_proj(W_in, 8, 1024, 4, hl, n, ev_to(lambda j: bcT[:, j, 3:3 + n]))
        ssd_pro(l, n, L, nch)
        big_proj(W_in, 8, 3336, 4, hl, n, ev_to(lambda j: qT[:, j, 0:n]))
        big_proj(W_in, 8, 3336 + 512, 4, hl, n, ev_to(lambda j: fzT[:, j, 0:n]))
        big_proj(W_in, 8, 3336 + 1024, 4, hl, n, ev_to(lambda j: ivT[:, j, 0:n]))
        big_proj(W_in, 8, 3336 + 1536, 4, hl, n, ev_to(lambda j: ggT[:, j, 0:n]))
        hgrn_pro(l, n, Lh, nchh)
        big_proj(W_in, 8, 1544, 4, hl, n, ev_to(lambda j: rT[:, j, 1:1 + n]))
        big_proj(W_in, 8, 1544 + 512, 4, hl, n, ev_to(lambda j: kT[:, j, 1:1 + n]))
        big_proj(W_in, 8, 1544 + 1024, 4, hl, n, ev_to(lambda j: vT[:, j, 1:1 + n]))
        big_proj(W_in, 8, 1544 + 1536, 2, hl, n, ev_to(lambda j: (x12 if j == 0 else x13)[:, 1:1 + n]))

        if 'proj' not in STAGES:
            return
        interleave(ssd_chain(l, n, L, nch, 0), hgrn_chain(l, n, Lh, nchh, 0),
                   ssd_chain(l, n, L, nch, 1), hgrn_chain(l, n, Lh, nchh, 1))
        ssd_epi(l, n, L, nch)
        hgrn_epi(l, n, Lh, nchh)
        rwkv_pro(l, n, L, nch)
        interleave(rwkv_chain(l, n, L, nch, 0), rwkv_chain(l, n, L, nch, 1))
        rwkv_epi(l, n, L, nch)
        if 'oproj' not in STAGES:
            return

        yl = [Y[j][:, 0:n] for j in range(12)]
        for g in range(2):
            big_proj(w_out[l], 12, g * 512, 4, yl, n,
                     lambda j, psap, g=g: tt('dve', xT[4 * g + j][:, 0:n], xT[4 * g + j][:, 0:n], psap, ALU.add))

        if 'ffn' not in STAGES:
            return
        rmsnorm(l, 'n2', n, xT, hT)
        hl = [hT[k][:, 0:n] for k in range(8)]
        f0 = 0
        while f0 < 22:
            nt_ = min(4, 22 - f0)
            gb = {}

            def ev_gate(j, psap, gb=gb):
                gb[j] = psap
            big_proj(w_gate[l], 8, f0 * 128, nt_, hl, n, ev_gate)

            def ev_up(j, psap, gb=gb, f0=f0):
                f = f0 + j
                a_ = a16(f, n) if TRUNK_BF16 else SL[f // 4][:, f % 4, 0:n]
                s = sc1[j % 2]
                sigmoid(s[:, 0:n], gb[j])
                tt('dve', s[:, 0:n], s[:, 0:n], gb[j], ALU.mult)
                tt('dve', a_, s[:, 0:n], psap, ALU.mult)
            big_proj(w_up[l], 8, f0 * 128, nt_, hl, n, ev_up)
            f0 += nt_
        al = [(a16(f, n) if TRUNK_BF16 else SL[f // 4][:, f % 4, 0:n]) for f in range(22)]
        for g in range(2):
            big_proj(w_down[l], 22, g * 512, 4, al, n,
                     lambda j, psap, g=g: tt('dve', xT[4 * g + j][:, 0:n], xT[4 * g + j][:, 0:n], psap, ALU.add))

    def ssd_pro(l, n, L, nch):
        cp('pool', xsT[:, :, 0:3], hist[l][:, 0:4, :])
        cp('pool', bcT[:, :, 0:3], hist[l][:, 4:8, :])
        cwo = PP_LAYOUT['cw'][0]
        for i in range(8):
            src = xsT if i < 4 else bcT
            ii = i % 4
            a_ = sc1[i % 2]
            e1 = ve()
            ts(e1, a_[:, 0:n], src[:, ii, 0:n], pp[l][:, cwo + i * 4:cwo + i * 4 + 1], ALU.mult)
            for j in range(1, 4):
                stt(e1, a_[:, 0:n], src[:, ii, j:j + n], pp[l][:, cwo + i * 4 + j:cwo + i * 4 + j + 1], a_[:, 0:n],
                    ALU.mult, ALU.add)
            cp('pool', hist[l][:, i, :], src[:, ii, n:n + 3])
            ts('dve', a_[:, 0:n], a_[:, 0:n], PPc(l, 'cb', i), ALU.add)
            s2 = sc1[2]
            sigmoid(s2[:, 0:n], a_[:, 0:n])
            tt('dve', src[:, ii, 3:3 + n], a_[:, 0:n], s2[:, 0:n], ALU.mult)
        for i in range(4):
            s2 = sc1[i % 2]
            sigmoid(s2[:, 0:n], zT[:, i, 0:n])
            tt('dve', zT[:, i, 0:n], zT[:, i, 0:n], s2[:, 0:n], ALU.mult)

    def ssd_chain(l, n, L, nch, g):
        T = tkg[g]
        yss = SL[8 + g]
        hs4 = slice(4 * g, 4 * g + 4)
        for c in range(nch):
            cs = slice(c * L, (c + 1) * L)
            cs3 = slice(3 + c * L, 3 + (c + 1) * L)
            pdt = ps()
            for k in range(8):
                mm(pdt[0:L, 0:4], hT[k][:, cs], wdt[l][:, k, hs4], start=(k == 0), stop=(k == 7))
            tt('dve', T['t1'][0:L, :], pdt[0:L, 0:4], PPc(l, 'dtb', 4 * g, 4, rows=L), ALU.add)
            act(T['t1'][0:L, :], T['t1'][0:L, :], AF.Exp)
            act(T['dt'][0:L, :], T['t1'][0:L, :], AF.Ln, bias=SC(1, L))
            tt('dve', T['loga'][0:L, :], T['dt'][0:L, :], dv[l][0:L, 8 + 4 * g:12 + 4 * g], ALU.mult)
            yield
            pc = ps()
            mm(pc[0:L, 0:4], C('u1', L, 0, L), T['loga'][0:L, :])
            mm(pc[0:L, 4:8], C('ones', L, 0, L), T['loga'][0:L, :])
            mm(pc[:, 8:12], C('ones', L, 0, 128), T['loga'][0:L, :])
            cp('pool', T['lgB'][0:L, :, 0:L], bc(T['loga'][0:L, :], [L, 4, L], 2))
            cp('dve', T['cum'][0:L, :], pc[0:L, 0:8])
            act(T['et128'][:, :], pc[:, 8:12], AF.Exp)
            act(T['ecum'][0:L, :], T['cum'][0:L, 0:4], AF.Exp)
            tt('dve', T['dec'][0:L, :], T['cum'][0:L, 4:8], T['cum'][0:L, 0:4], ALU.subtract)
            act(T['dec'][0:L, :], T['dec'][0:L, :], AF.Exp)
            yield
            pcb = ps()
            for h in range(4):
                mm(pcb[0:L, h * L:(h + 1) * L], T['lgB'][0:L, h, 0:L], C('u1', L, 0, L))
            pg = ps()
            mm(pg[0:L, 0:L], bcT[:, g, cs3], bcT[:, 2 + g, cs3])
            px = ps()
            for i in range(2):
                tr(px[0:L, i * 128:(i + 1) * 128], xsT[:, 2 * g + i, cs3], 128)
            pb = ps()
            tr(pb[0:L, 0:128], bcT[:, g, cs3], 128)
            pcb3 = pcb[0:L, 0:4 * L].rearrange("p (h i) -> p h i", h=4)
            tt('dve', T['Dm'][0:L, :, 0:L], pcb3, bc(C('mneg', L, 0, L), [L, 4, L], 1), ALU.add)
            tt('pool', T['Dm'][0:L, :, 0:L], T['Dm'][0:L, :, 0:L], bc(T['cum'][0:L, 0:4], [L, 4, L], 2), ALU.subtract)
            act(T['LT'][0:L, :, 0:L], T['Dm'][0:L, :, 0:L], AF.Exp)
            px3 = px[0:L, 0:256].rearrange("p (h d) -> p h d", h=4)
            tt('dve', T['xdt'][0:L, :, :], px3, bc(T['dt'][0:L, :], [L, 4, 64], 2), ALU.mult)
            tt('pool', T['xdec'][0:L, :, :], T['xdt'][0:L, :, :], bc(T['dec'][0:L, :], [L, 4, 64], 2), ALU.mult)
            cp('act', T['Btok'][0:L, :], pb[0:L, 0:128])
            tt('dve', T['MT'][0:L, :, 0:L], T['LT'][0:L, :, 0:L], bc(pg[0:L, 0:L], [L, 4, L], 1), ALU.mult)
            yield
            py = ps()
            for h in range(4):
                mm(py[0:L, h * 64:(h + 1) * 64], T['MT'][0:L, h, 0:L], T['xdt'][0:L, h, :])
            pyi = ps()
            mm(pyi[0:L, 0:256], bcT[:, 2 + g, cs3], hstg[l][g][:, :])
            ph = ps()
            mm(ph[:, 0:256], T['Btok'][0:L, :], T['xdec'][0:L, :, :].rearrange("p h d -> p (h d)"))
            pyi3 = pyi[0:L, 0:256].rearrange("p (h d) -> p h d", h=4)
            py3 = py[0:L, 0:256].rearrange("p (h d) -> p h d", h=4)
            tt('dve', T['ytok'][0:L, :, :], pyi3, bc(T['ecum'][0:L, :], [L, 4, 64], 2), ALU.mult)
            tt('dve', T['ytok'][0:L, :, :], T['ytok'][0:L, :, :], py3, ALU.add)
            h3 = hstg[l][g][:, :].rearrange("p (h d) -> p h d", h=4)
            tt('dve', h3, h3, bc(T['et128'][:, :], [128, 4, 64], 2), ALU.mult)
            tt('dve', hstg[l][g][:, :], hstg[l][g][:, :], ph[:, 0:256], ALU.add)
            yield
            pyt = ps()
            for i in range(2):
                tr(pyt[:, i * L:(i + 1) * L], T['ytok'][0:L, 2 * i:2 * i + 2, :].rearrange("p h d -> p (h d)"), L)
            for i in range(2):
                stt('dve', yss[:, i, cs], xsT[:, 2 * g + i, cs3], PPc(l, 'dsk', 2 * g + i), pyt[:, i * L:(i + 1) * L],
                    ALU.mult, ALU.add)
            yield

    def ssd_epi(l, n, L, nch):
        for g in range(2):
            yss = SL[8 + g]
            for t_ in range(2):
                tt(ve(), yss[:, t_, 0:n], yss[:, t_, 0:n], zT[:, 2 * g + t_, 0:n], ALU.mult)
            pss = ps()
            for t_ in range(2):
                s = sc1[t_]
                tt(ve(), s[:, 0:n], yss[:, t_, 0:n], yss[:, t_, 0:n], ALU.mult)
                mm(pss[:, 0:n], C('ones'), s[:, 0:n], start=(t_ == 0), stop=(t_ == 1))
            rstd(tk['rs2'][:, 0:n], pss[:, 0:n], 1.0 / 256, SC(0))
            for t_ in range(2):
                i = 2 * g + t_
                stt(ve(), Y[i][:, 0:n], yss[:, t_, 0:n], PPc(l, 'snw', i), tk['rs2'][:, 0:n], ALU.mult, ALU.mult)

    def rwkv_pro(l, n, L, nch):
        c1 = slice(1, 1 + n)
        muo = PP_LAYOUT['mu'][0]
        tl = [(rT, 0), (rT, 1), (rT, 2), (rT, 3), (kT, 0), (kT, 1), (kT, 2), (kT, 3), (vT, 0), (vT, 1), (vT, 2), (vT, 3)]
        for g, t3 in enumerate((rT, kT, vT)):
            cp('pool', t3[:, :, 0:1], prev[l][:, 4 * g:4 * g + 4].unsqueeze(2))
        cp('pool', x12[:, 0:1], prev[l][:, 12:13])
        cp('pool', x13[:, 0:1], prev[l][:, 13:14])

        def shift(cur, prv, lastcol, prevdst, mucol):
            d = sc1[mucol % 2]
            e1 = ve()
            tt(e1, d[:, 0:n], prv, cur, ALU.subtract)
            cp('pool', prevdst, lastcol)
            stt(e1, cur, d[:, 0:n], pp[l][:, muo + mucol:muo + mucol + 1], cur, ALU.mult, ALU.add)
        for idx, (t3, i) in enumerate(tl):
            shift(t3[:, i, 1:1 + n], t3[:, i, 0:n], t3[:, i, n:n + 1], prev[l][:, idx:idx + 1], idx)
        shift(x12[:, 1:1 + n], x12[:, 0:n], x12[:, n:n + 1], prev[l][:, 12:13], 12)
        shift(x13[:, 1:1 + n], x13[:, 0:n], x13[:, n:n + 1], prev[l][:, 13:14], 13)
        sigmoid(x12[0:64, c1], x12[0:64, c1], scale=2.0)
        ts('dve', x12[0:64, c1], x12[0:64, c1], 2.0, ALU.mult, -1.0, ALU.add)
        sigmoid(x13[:, c1], x13[:, c1])

        LW, AS, G, CW, KKN, KM, BON, BV, RT_, AT_, BT_, KT_ = SL
        S1, S2 = RT_, AT_
        A4 = slice(0, n)
        for m in range(4):
            p1 = ps()
            mm(p1[:, 0:n], wa2[l][:, 0, m * 128:(m + 1) * 128], x12[:, c1])
            act(LW[:, m, A4], p1[:, 0:n], AF.Exp, bias=dv[l][:, m:m + 1], scale=-1.0)
            p2 = ps()
            mm(p2[:, 0:n], wa2[l][:, 1, m * 128:(m + 1) * 128], x12[:, c1])
            act(AS[:, m, A4], p2[:, 0:n], AF.Exp, bias=dv[l][:, 4 + m:5 + m], scale=-1.0)
            p3 = ps()
            mm(p3[:, 0:n], g2[l][:, m * 128:(m + 1) * 128], x13[:, c1])
            cp('act', G[:, m, A4], p3[:, 0:n])
            ts('dve', KKN[:, m, A4], kT[:, m, c1], PPc(l, 'kk', m), ALU.mult)
        act(LW[:, :, A4], LW[:, :, A4], AF.Ln, bias=SC(1))
        act(LW[:, :, A4], LW[:, :, A4], AF.Exp, scale=-1.0)
        act(AS[:, :, A4], AS[:, :, A4], AF.Ln, bias=SC(1))
        act(AS[:, :, A4], AS[:, :, A4], AF.Exp, scale=-1.0)
        ts('dve', LW[:, :, A4], LW[:, :, A4], -0.6065306597126334, ALU.mult)
        for m in range(4):
            scan(CW[:, m, A4], C('rm64', 128, 0, n), LW[:, m, A4])
        tt('pool', S1[:, :, A4], KKN[:, :, A4], KKN[:, :, A4], ALU.mult)
        for m in range(4):
            p4 = ps()
            mm(p4[:, 0:n], C('bones'), S1[:, m, A4])
            act(S2[:, m, A4], p4[:, 0:n], AF.Ln, bias=SC(3))
        act(S2[:, :, A4], S2[:, :, A4], AF.Exp, scale=-0.5)
        tt('dve', KKN[:, :, A4], KKN[:, :, A4], S2[:, :, A4], ALU.mult)
        for m in range(4):
            ts('dve', KM[:, m, A4], AS[:, m, A4], PPc(l, 'ka', m), ALU.mult, PPc(l, 'ka', m), ALU.subtract)
        stt('dve', KM[:, :, A4], KM[:, :, A4], 1.0, kT[:, :, c1], ALU.add, ALU.mult)
        for m in range(4):
            stt('dve', S1[:, m, A4], rT[:, m, c1], PPc(l, 'rk', m), KM[:, m, A4], ALU.mult, ALU.mult)
            p5 = ps()
            mm(p5[:, 0:n], C('bones'), S1[:, m, A4])
            tt('dve', BON[:, m, A4], p5[:, 0:n], vT[:, m, c1], ALU.mult)
        tt('pool', BV[:, :, A4], KKN[:, :, A4], AS[:, :, A4], ALU.mult)
        Wt = AS
        act(Wt[:, :, 0:n], CW[:, :, 0:n], AF.Exp)
        tt(ve(), RT_[:, :, 0:n], rT[:, :, c1], Wt[:, :, 0:n], ALU.mult)
        cp('pool', tk['wlc'][:, :, 0:nch], Wt[:, :, L - 1:n:L])
        tt(ve(), AT_[:, :, 0:n], CW[:, :, 0:n], LW[:, :, 0:n], ALU.subtract)
        act(AT_[:, :, 0:n], AT_[:, :, 0:n], AF.Exp)
        stt(ve(), AT_[:, :, 0:n], KKN[:, :, 0:n], -1.0, AT_[:, :, 0:n], ALU.mult, ALU.mult)
        En = LW
        act(En[:, :, 0:n], CW[:, :, 0:n], AF.Exp, scale=-1.0)
        tt(ve(), BT_[:, :, 0:n], BV[:, :, 0:n], En[:, :, 0:n], ALU.mult)
        tt(ve(), KT_[:, :, 0:n], KM[:, :, 0:n], En[:, :, 0:n], ALU.mult)
        EB = LW
        for m in range(4):
            cw3 = CW[:, m, 0:n].rearrange("p (c l) -> p c l", l=L)
            eb3 = EB[:, m, 0:n].rearrange("p (c l) -> p c l", l=L)
            tt(ve(), eb3, bc(CW[:, m, L - 1:n:L], [128, nch, L], 2), cw3, ALU.subtract)
        act(EB[:, :, 0:n], EB[:, :, 0:n], AF.Exp)
        BB, KB = KKN, AS
        tt(ve(), BB[:, :, 0:n], BV[:, :, 0:n], EB[:, :, 0:n], ALU.mult)
        tt(ve(), KB[:, :, 0:n], KM[:, :, 0:n], EB[:, :, 0:n], ALU.mult)
        ATo, RTo = KM, BV
        cp(ve(), ATo[:, :, 0:n], AT_[:, :, 0:n])
        cp(ve(), RTo[:, :, 0:n], RT_[:, :, 0:n])
        memset('pool', ATo[0:64, :, 0:n], 0.0)
        memset('pool', RTo[0:64, :, 0:n], 0.0)
        memset('pool', AT_[64:128, :, 0:n], 0.0)
        memset('pool', RT_[64:128, :, 0:n], 0.0)

    def rwkv_chain(l, n, L, nch, g):
        LW, AS, G, CW, KKN, KM, BON, BV, RT_, AT_, BT_, KT_ = SL
        BB, KB = KKN, AS
        ATm, RTm = (AT_, KM), (RT_, BV)
        OT = (CW, LW)[g]
        T = tkg[g]
        ST = rstg[l][g]
        nst = int(np.log2(L))

        def v3(p_):
            return p_[0:L, 0:4 * L].rearrange("p (h i) -> p h i", h=4)

        def x3(p_):
            return p_[0:L, 0:256].rearrange("p (h d) -> p h d", h=4)
        u0b = bc(C('u0', L, 0, L), [L, 4, L], 1)
        u1b = bc(C('u1', L, 0, L), [L, 4, L], 1)
        l0b = bc(C('l0', L, 0, L), [L, 4, L], 1)
        for c in range(nch):
            cs = slice(c * L, (c + 1) * L)
            cs1 = slice(1 + c * L, 1 + (c + 1) * L)
            pN, pNT, pAk, pRb, pRk = ps(), ps(), ps(), ps(), ps()
            for hh in range(4):
                h = 4 * g + hh
                q = h // 2
                hsl = slice(hh * L, (hh + 1) * L)
                am, rm_ = ATm[h % 2], RTm[h % 2]
                mm(pN[0:L, hsl], am[:, q, cs], BT_[:, q, cs])
                mm(pNT[0:L, hsl], BT_[:, q, cs], am[:, q, cs])
                mm(pAk[0:L, hsl], KT_[:, q, cs], am[:, q, cs])
                mm(pRb[0:L, hsl], BT_[:, q, cs], rm_[:, q, cs])
                mm(pRk[0:L, hsl], KT_[:, q, cs], rm_[:, q, cs])
            pv = ps()
            for i in range(2):
                tr(pv[0:L, i * 128:(i + 1) * 128], vT[:, 2 * g + i, cs1], 128)
            pbb, pkb = ps(), ps()
            for i in range(2):
                tr(pbb[0:L, i * 128:(i + 1) * 128], BB[:, 2 * g + i, cs], 128)
                tr(pkb[0:L, i * 128:(i + 1) * 128], KB[:, 2 * g + i, cs], 128)
            Pa, PaT = T['Pa'], T['PaT']
            tt('dve', Pa[0:L, :, 0:L], v3(pN), l0b, ALU.mult)
            tt('dve', PaT[0:L, :, 0:L], v3(pNT), u0b, ALU.mult)
            tt('dve', T['AkT'][0:L, :, 0:L], v3(pAk), u0b, ALU.mult)
            cp('act', T['Vtok'][0:L, :, :], x3(pv))
            tt('dve', T['RbT'][0:L, :, 0:L], v3(pRb), u1b, ALU.mult)
            tt('dve', T['RkT'][0:L, :, 0:L], v3(pRk), u1b, ALU.mult)
            cp('act', T['bbt'][0:L, :, :], pbb[0:L, 0:256].rearrange("p (m d) -> p m d", m=2))
            cp('act', T['kbt'][0:L, :, :], pkb[0:L, 0:256].rearrange("p (m d) -> p m d", m=2))
            yield
            pX = ps()
            for hh in range(4):
                h = 4 * g + hh
                mm(pX[0:L, hh * 64:(hh + 1) * 64], ATm[h % 2][:, h // 2, cs], ST[:, hh // 2, :], start=True, stop=False)
                mm(pX[0:L, hh * 64:(hh + 1) * 64], T['AkT'][0:L, hh, 0:L], T['Vtok'][0:L, hh, :], start=False, stop=True)
            Xc, Xn = T['Xa'], T['Xb']
            cp('act', Xc[0:L, :, :], x3(pX))
            yield
            Pc, PcT, Pn, PnT = Pa, PaT, T['Pb'], T['PbT']
            for st_ in range(nst):
                pU = ps()
                for hh in range(4):
                    mm(pU[0:L, hh * 64:(hh + 1) * 64], PcT[0:L, hh, 0:L], Xc[0:L, hh, :])
                if st_ < nst - 1:
                    pS, pST = ps(), ps()
                    for hh in range(4):
                        hsl = slice(hh * L, (hh + 1) * L)
                        mm(pS[0:L, hsl], PcT[0:L, hh, 0:L], Pc[0:L, hh, 0:L])
                        mm(pST[0:L, hsl], Pc[0:L, hh, 0:L], PcT[0:L, hh, 0:L])
                tt('dve', Xn[0:L, :, :], Xc[0:L, :, :], x3(pU), ALU.add)
                Xc, Xn = Xn, Xc
                if st_ < nst - 1:
                    cp('act', Pn[0:L, :, 0:L], v3(pS))
                    cp('dve', PnT[0:L, :, 0:L], v3(pST))
                    Pc, PcT, Pn, PnT = Pn, PnT, Pc, PcT
                yield
            SA = Xc
            pO = ps()
            for hh in range(4):
                h = 4 * g + hh
                o_ = pO[0:L, hh * 64:(hh + 1) * 64]
                mm(o_, RTm[h % 2][:, h // 2, cs], ST[:, hh // 2, :], start=True, stop=False)
                mm(o_, T['RbT'][0:L, hh, 0:L], SA[0:L, hh, :], start=False, stop=False)
                mm(o_, T['RkT'][0:L, hh, 0:L], T['Vtok'][0:L, hh, :], start=False, stop=True)
            pSt = ps()
            for i in range(2):
                mm(pSt[:, i * 128:(i + 1) * 128], T['bbt'][0:L, i, :],
                   SA[0:L, 2 * i:2 * i + 2, :].rearrange("p h d -> p (h d)"), start=True, stop=False)
                mm(pSt[:, i * 128:(i + 1) * 128], T['kbt'][0:L, i, :],
                   T['Vtok'][0:L, 2 * i:2 * i + 2, :].rearrange("p h d -> p (h d)"), start=False, stop=True)
            cp('act', T['Otok'][0:L, :, :], x3(pO))
            pSt3 = pSt[:, 0:256].rearrange("p (q d) -> p q d", q=2)
            for hh2 in range(2):
                prr = slice(hh2 * 64, hh2 * 64 + 64)
                tt('dve', T['stm'][prr, :, :], ST[prr, :, :], bc(tk['wlc'][prr, 2 * g:2 * g + 2, c], [64, 2, 64], 2),
                   ALU.mult)
                tt('dve', ST[prr, :, :], T['stm'][prr, :, :], pSt3[prr, :, hh2 * 64:(hh2 + 1) * 64], ALU.add)
            yield
            pot = ps()
            for i in range(2):
                tr(pot[:, i * L:(i + 1) * L], T['Otok'][0:L, 2 * i:2 * i + 2, :].rearrange("p h d -> p (h d)"), L)
            cp('act', OT[:, 2 * g:2 * g + 2, cs], pot[:, 0:2 * L].rearrange("p (q i) -> p q i", q=2))
            yield

    def rwkv_epi(l, n, L, nch):
        LW, AS, G, CW, KKN, KM, BON, BV, RT_, AT_, BT_, KT_ = SL
        CEN, SQ = BT_, KT_
        A4 = slice(0, n)
        for m in range(4):
            OT = (CW, LW)[m // 2]
            pm = ps()
            mm(pm[:, 0:n], C('bones'), OT[:, m, A4])
            stt('dve', CEN[:, m, A4], pm[:, 0:n], -1.0 / 64, OT[:, m, A4], ALU.mult, ALU.add)
        tt('pool', SQ[:, :, A4], CEN[:, :, A4], CEN[:, :, A4], ALU.mult)
        for m in range(4):
            pvv = ps()
            mm(pvv[:, 0:n], C('bones'), SQ[:, m, A4])
            act(SQ[:, m, A4], pvv[:, 0:n], AF.Ln, bias=SC(2), scale=1.0 / 64)
        act(SQ[:, :, A4], SQ[:, :, A4], AF.Exp, scale=-0.5)
        tt('dve', CEN[:, :, A4], CEN[:, :, A4], SQ[:, :, A4], ALU.mult)
        for m in range(4):
            ts('dve', CEN[:, m, A4], CEN[:, m, A4], PPc(l, 'lnw', m), ALU.mult, PPc(l, 'lnb', m), ALU.add)
        tt('pool', CEN[:, :, A4], CEN[:, :, A4], BON[:, :, A4], ALU.add)
        for m in range(4):
            tt(ve(), Y[4 + m][:, 0:n], CEN[:, m, A4], G[:, m, A4], ALU.mult)

    def hgrn_pro(l, n, Lh, nchh):
        E_, L1, KG, CU, TM, QT, KT2, QH = SL[0:8]
        mid = Lh // 2
        rmn = 'rm32' if Lh == 32 else 'rm64'
        for m in range(4):
            ts(ve(), fzT[:, m, 0:n], fzT[:, m, 0:n], -60.0, ALU.max)
        act(E_[:, :, 0:n], fzT[:, :, 0:n], AF.Exp, scale=-1.0)
        for m in range(4):
            act(L1[:, m, 0:n], E_[:, m, 0:n], AF.Ln, bias=SC(1), scale=dv[l][:, 16 + m:17 + m])
        act(TM[:, :, 0:n], E_[:, :, 0:n], AF.Ln, bias=SC(1))
        tt(ve(), L1[:, :, 0:n], L1[:, :, 0:n], TM[:, :, 0:n], ALU.subtract)
        act(TM[:, :, 0:n], TM[:, :, 0:n], AF.Exp, scale=-1.0)
        for m in range(4):
            stt(ve(), KG[:, m, 0:n], E_[:, m, 0:n], dv[l][:, 20 + m:21 + m], TM[:, m, 0:n], ALU.mult, ALU.mult)
            scan(CU[:, m, 0:n], C(rmn, 128, 0, n), L1[:, m, 0:n])
        for m in range(4):
            cu3 = CU[:, m, 0:n].rearrange("p (c l) -> p c l", l=Lh)
            tm3 = TM[:, m, 0:n].rearrange("p (c l) -> p c l", l=Lh)
            tt(ve(), tm3, cu3, bc(CU[:, m, mid:n:Lh], [128, nchh, Lh], 2), ALU.subtract)
        ts('dve', TM[:, :, 0:n], TM[:, :, 0:n], 38.0, ALU.min, -38.0, ALU.max)
        act(QT[:, :, 0:n], TM[:, :, 0:n], AF.Exp)
        act(KT2[:, :, 0:n], TM[:, :, 0:n], AF.Exp, scale=-1.0)
        tt(ve(), QT[:, :, 0:n], QT[:, :, 0:n], qT[:, :, 0:n], ALU.mult)
        tt(ve(), KT2[:, :, 0:n], KT2[:, :, 0:n], KG[:, :, 0:n], ALU.mult)
        act(QH[:, :, 0:n], CU[:, :, 0:n], AF.Exp)
        cp('pool', tk['slc'][:, :, 0:nchh], QH[:, :, Lh - 1:n:Lh])
        tt(ve(), QH[:, :, 0:n], QH[:, :, 0:n], qT[:, :, 0:n], ALU.mult)
        KH = E_
        for m in range(4):
            cu3 = CU[:, m, 0:n].rearrange("p (c l) -> p c l", l=Lh)
            kh3 = KH[:, m, 0:n].rearrange("p (c l) -> p c l", l=Lh)
            tt(ve(), kh3, bc(CU[:, m, Lh - 1:n:Lh], [128, nchh, Lh], 2), cu3, ALU.subtract)
        act(KH[:, :, 0:n], KH[:, :, 0:n], AF.Exp)
        tt(ve(), KH[:, :, 0:n], KH[:, :, 0:n], KG[:, :, 0:n], ALU.mult)

    def hgrn_chain(l, n, Lh, nchh, g):
        E_, L1, KG, CU, TM, QT, KT2, QH = SL[0:8]
        KH = E_
        OT = (L1, TM)[g]
        T = tkg[g]
        S = gstg[l][g]
        for c in range(nchh):
            cs = slice(c * Lh, (c + 1) * Lh)
            pA = ps()
            for hh in range(2):
                h = 2 * g + hh
                mm(pA[0:Lh, hh * Lh:(hh + 1) * Lh], KT2[:, h, cs], QT[:, h, cs])
            pv, pk = ps(), ps()
            for hh in range(2):
                h = 2 * g + hh
                tr(pv[0:Lh, hh * 128:(hh + 1) * 128], ivT[:, h, cs], 128)
                tr(pk[0:Lh, hh * 128:(hh + 1) * 128], KH[:, h, cs], 128)
            tt('dve', T['hMT'][0:Lh, :, 0:Lh], pA[0:Lh, 0:2 * Lh].rearrange("p (h i) -> p h i", h=2),
               bc(C('u1', Lh, 0, Lh), [Lh, 2, Lh], 1), ALU.mult)
            cp('act', T['hvt'][0:Lh, :, :], pv[0:Lh, 0:256].rearrange("p (h d) -> p h d", h=2))
            cp('dve', T['hkt'][0:Lh, :, :], pk[0:Lh, 0:256].rearrange("p (h d) -> p h d", h=2))
            yield
            po = ps()
            for hh in range(2):
                h = 2 * g + hh
                o_ = po[:, hh * Lh:(hh + 1) * Lh]
                mm(o_, T['hvt'][0:Lh, hh, :], T['hMT'][0:Lh, hh, 0:Lh], start=True, stop=False)
                mm(o_, S[:, hh, :], QH[:, h, cs], start=False, stop=True)
            pS = ps()
            for hh in range(2):
                mm(pS[:, hh * 128:(hh + 1) * 128], T['hkt'][0:Lh, hh, :], T['hvt'][0:Lh, hh, :])
            cp('act', OT[:, 2 * g:2 * g + 2, cs], po[:, 0:2 * Lh].rearrange("p (h i) -> p h i", h=2))
            for hh in range(2):
                h = 2 * g + hh
                stt('dve', S[:, hh, :], S[:, hh, :], tk['slc'][:, h, c:c + 1], pS[:, hh * 128:(hh + 1) * 128],
                    ALU.mult, ALU.add)
            yield

    def hgrn_epi(l, n, Lh, nchh):
        E_, L1, KG, CU, TM, QT, KT2, QH = SL[0:8]
        for h in range(4):
            OT = (L1, TM)[h // 2]
            sq = sc1[h % 2]
            tt(ve(), sq[:, 0:n], OT[:, h, 0:n], OT[:, h, 0:n], ALU.mult)
            pss = ps()
            mm(pss[:, 0:n], C('ones'), sq[:, 0:n])
            rstd(sq[:, 0:n], pss[:, 0:n], 1.0 / 128, SC(0))
            stt(ve(), OT[:, h, 0:n], OT[:, h, 0:n], PPc(l, 'hnw', h), sq[:, 0:n], ALU.mult, ALU.mult)
            s2 = sc1[2]
            sigmoid(s2[:, 0:n], ggT[:, h, 0:n])
            tt(ve(), s2[:, 0:n], s2[:, 0:n], ggT[:, h, 0:n], ALU.mult)
            tt(ve(), Y[8 + h][:, 0:n], OT[:, h, 0:n], s2[:, 0:n], ALU.mult)

    def interleave(*gens):
        gens = list(gens)
        while gens:
            for g_ in list(gens):
                try:
                    next(g_)
                except StopIteration:
                    gens.remove(g_)


    def run_tile(src, n, L, Lh, dst):
        nb = (n + 127) // 128
        for b in range(nb):
            tb = min(128, n - b * 128)
            xi = xin[0]
            P.dma(DQ, xi[0:tb, :], src[b * 128:b * 128 + tb, :])
            for g in range(2 if 'i' not in KDBG else 0):
                pt = ps()
                for k in range(4):
                    tr(pt[:, k * 128:k * 128 + tb], xi[0:tb, (4 * g + k) * 128:(4 * g + k + 1) * 128], tb)
                for k in range(4):
                    cp('act' if (k % 2 and 'j' not in KDBG) else 'dve', xT[4 * g + k][:, b * 128:b * 128 + tb], pt[:, k * 128:k * 128 + tb])
        for l in range(DEPTH):
            layer(l, n, L, Lh)
        if dst is None or 'g' in KDBG:
            return
        hF = [SL[k // 4][:, k % 4, :] for k in range(8)]
        if 'h' not in KDBG:
            rmsnorm(DEPTH - 1, 'fin', n, xT, hF)
        for b in range(nb):
            tb = min(128, n - b * 128)
            xo = xin[0]
            for g in range(2):
                pt = ps()
                for k in range(4):
                    tr(pt[0:tb, k * 128:(k + 1) * 128], hF[4 * g + k][:, b * 128:b * 128 + tb], 128)
                cp('act' if g else 'dve', xo[0:tb, g * 512:(g + 1) * 512], pt[0:tb, :])
            P.dma(DQ, dst[b * 128:b * 128 + tb, :], xo[0:tb, :])

    def store_states(si):
        for l in range(DEPTH):
            for i in range(4):
                pt = ps()
                tr(pt[:, 0:128], hstg[l][i // 2][:, (i % 2) * 128:(i % 2 + 1) * 128], 128)
                cp('act', stmp[:, i, :], pt[:, 0:128])
            for hh in range(2):
                P.dma(DQ, o_ssm[si, l].rearrange("(i hh) p n -> hh p i n", hh=2)[hh],
                      stmp[hh * 64:(hh + 1) * 64, :, :])
            for i in range(8):
                P.dma(DQ, o_conv[si, l][:, i * 128:(i + 1) * 128].rearrange("j p -> p j"), hist[l][:, i, :],
                      allow_slow_non_contiguous=True)
            P.dma(DQ, o_shift[si, l].rearrange("(i p) -> p i", p=128), prev[l][:, :],
                  allow_slow_non_contiguous=True)
            for q in range(4):
                pt = ps()
                tr(pt[0:64, 0:128], rstg[l][q // 2][:, q % 2, :], 128)
                cp('act', rtmp[:, q, :, :], pt[0:64, 0:128].rearrange("p (hh n) -> p hh n", hh=2))
            P.dma(DQ, o_rwkv[si, l].rearrange("(q hh) v n -> v q hh n", hh=2), rtmp[:, :, :, :])
            for g in range(2):
                P.dma(DQ, o_hgrn[si, l, 2 * g:2 * g + 2].rearrange("h k v -> k h v"), gstg[l][g][:, :, :])

    def load_states():
        for l in range(DEPTH):
            for hh in range(2):
                P.dma(DQ, stmp[hh * 64:(hh + 1) * 64, :, :],
                      st_ssm[l].rearrange("(i hh) p n -> hh p i n", hh=2)[hh])
            for i in range(4):
                pt = ps()
                tr(pt[:, 0:128], stmp[:, i, :], 128)
                cp('act', hstg[l][i // 2][:, (i % 2) * 128:(i % 2 + 1) * 128], pt[:, 0:128])
            for i in range(8):
                P.dma(DQ, hist[l][:, i, :], st_conv[l][:, i * 128:(i + 1) * 128].rearrange("j p -> p j"),
                      allow_slow_non_contiguous=True)
            P.dma(DQ, prev[l][:, :], st_shift[l].rearrange("(i p) -> p i", p=128),
                  allow_slow_non_contiguous=True)
            P.dma(DQ, rtmp[:, :, :, :], st_rwkv[l].rearrange("(q hh) v n -> v q hh n", hh=2))
            for q in range(4):
                pt = ps()
                tr(pt[:, 0:64], rtmp[:, q, :, :].rearrange("p hh n -> p (hh n)"), 64)
                cp('act', rstg[l][q // 2][:, q % 2, :], pt[:, 0:64])
            for g in range(2):
                P.dma(DQ, gstg[l][g][:, :, :], st_hgrn[l, 2 * g:2 * g + 2].rearrange("h k v -> k h v"))

    def zero_states():
        for l in range(DEPTH):
            for g in range(2):
                memset('pool', hstg[l][g][:, :], 0.0)
                memset('pool', rstg[l][g][:, :, :], 0.0)
                memset('pool', gstg[l][g][:, :, :], 0.0)
            memset('pool', hist[l][:, :, :], 0.0)
            memset('pool', prev[l][:, :], 0.0)

    if 'states' in STAGES:
        load_states()
    else:
        zero_states()
    if 'f' not in KDBG:
        run_tile(xs_in, 64, 64, 32, y_s)
    if 'states' in STAGES:
        store_states(1)
    zero_states()
    if 'b' not in KDBG:
        run_tile(meta, 16, 16, 16, None)
    t0 = 0 if 'c' not in KDBG else SEQ
    while t0 < SEQ:
        n = min(NT, SEQ - t0)
        run_tile(xp[t0:t0 + n, :], n, 64, 32, y_p[t0:t0 + n, :])
        t0 += n
    if 'states' in STAGES:
        store_states(0)

    P.emit()
    return nc, es, P


_CACHE = {}


def kernel(**inp):
    inp = {k: np.asarray(v, dtype=np.float32) for k, v in inp.items()}
    B, SEQ, _ = inp['x_prompt'].shape
    DEPTH = inp['w_in'].shape[0]
    pps = np.stack([_pack_params(inp, l) for l in range(DEPTH)], axis=0)
    ccs = _consts()
    zpad = np.zeros_like(inp['rw_w2'])
    wa2 = np.ascontiguousarray(np.stack([np.concatenate([inp['rw_w2'], zpad], axis=1),
                                         np.concatenate([zpad, inp['rw_a2']], axis=1)], axis=1))
    key = (SEQ, DEPTH)
    if key not in _CACHE:
        _CACHE[key] = build(SEQ, DEPTH=DEPTH)
    nc = _CACHE[key][0]
    in_maps = []
    for c in range(NCORES):
        in_maps.append({
            "xp": np.ascontiguousarray(inp['x_prompt'][c]),
            "xs": np.ascontiguousarray(inp['x_sample'][c]),
            "meta": inp['meta_tokens'],
            "st_ssm": np.ascontiguousarray(inp['state_ssm'][:, c]),
            "st_conv": np.ascontiguousarray(inp['state_conv'][:, c]),
            "st_rwkv": np.ascontiguousarray(inp['state_rwkv'][:, c]),
            "st_shift": np.ascontiguousarray(inp['state_shift'][:, c]),
            "st_hgrn": np.ascontiguousarray(inp['state_hgrn'][:, c]),
            "w_in": inp['w_in'], "w_out": inp['w_out'], "w_gate": inp['w_gate'], "w_up": inp['w_up'],
            "w_down": inp['w_down'], "wa2": wa2, "g2": inp['rw_g2'], "pp": pps, "cc": ccs,
        })
    in_maps = [{"i_" + k: v for k, v in m.items()} for m in in_maps]
    if os.environ.get('KONE'):
        res = run_bass_kernel_spmd(nc, in_maps[:1], core_ids=[0])
        R = [{k[2:]: v for k, v in res.results[0].items()}] * NCORES
    else:
        res = run_bass_kernel_spmd(nc, in_maps, core_ids=list(range(NCORES)))
        R = [{k[2:]: v for k, v in r.items()} for r in res.results]
    y_prompt = np.stack([R[c]["y_p"] for c in range(NCORES)], axis=0)
    y_sample = np.stack([R[c]["y_s"] for c in range(NCORES)], axis=0)

    def st(name, si):
        return np.stack([R[c][name][si] for c in range(NCORES)], axis=1)
    outs = [y_prompt, y_sample]
    for si in (0, 1):
        for name in ("o_ssm", "o_conv", "o_rwkv", "o_shift", "o_hgrn"):
            outs.append(st(name, si))
    return tuple(np.ascontiguousarray(o, dtype=np.float32) for o in outs)
```

```python
import numpy as np
from contextlib import ExitStack
import concourse.bass as bass
import concourse.mybir as mybir
from concourse.bass_utils import run_bass_kernel_spmd

F32 = mybir.dt.float32
BF16 = mybir.dt.bfloat16
AF = mybir.ActivationFunctionType
ALU = mybir.AluOpType

D = 1024
DFF = 2816
NIN = 5384
NCORES = 8
DQ = 'sp'
TRUNK_BF16 = True
import os
KDBG = os.environ.get('KDBG', '')
STAGES = {'states', 'proj', 'ssd', 'rwkv', 'hgrn', 'oproj', 'ffn'}
ENGS = ['pe', 'act', 'dve', 'pool', 'sp']


class Dep:
    __slots__ = ('lw', 'rd')

    def __init__(self):
        self.lw = None
        self.rd = []


class Prog:
    def __init__(self, nc, es):
        self.nc, self.es = nc, es
        self.q = {e: [] for e in ENGS}
        self.deps = {}
        self.dsem = {}
        self.sem = {e: es.enter_context(nc.semaphore("q_" + e)) for e in ENGS}
        self.tensors = {}
        self.psum_names = set()

    def sb(self, name, shape, dtype=F32):
        name = "s_" + name
        t = self.es.enter_context(self.nc.sbuf_tensor(name, list(shape), dtype))
        self.deps[name] = Dep()
        return t

    def psum(self, name, shape):
        t = self.es.enter_context(self.nc.psum_tensor(name, list(shape), F32))
        self.deps[name] = Dep()
        self.psum_names.add(name)
        return t

    def _dep(self, ap):
        nm = ap.tensor.name
        return self.deps.get(nm)

    def _collect(self, eng, outs, ins):
        w = []
        for ap in ins:
            d = self._dep(ap)
            if d is not None and d.lw is not None:
                w.append(d.lw)
            if d is not None and ap.tensor.name in self.psum_names:
                for r in d.rd:
                    if not (r[0] == 'e' and r[1] == eng):
                        w.append(r)
        for ap in outs:
            d = self._dep(ap)
            if d is None:
                continue
            if d.lw is not None:
                w.append(d.lw)
            for r in d.rd:
                w.append(r)
        if eng == 'pe':
            w = [x for x in w if not (x[0] == 'e' and x[1] == 'pe')]
        return w

    def _commit(self, ev, outs, ins):
        for ap in ins:
            d = self._dep(ap)
            if d is not None:
                d.rd.append(ev)
        for ap in outs:
            d = self._dep(ap)
            if d is not None:
                d.lw = ev
                d.rd = []

    def op(self, eng, fn, outs, ins):
        w = self._collect(eng, outs, ins)
        idx = len(self.q[eng])
        self.q[eng].append(dict(fn=fn, waits=w, kind='c'))
        self._commit(('e', eng, idx), outs, ins)

    def dma(self, eng, out, in_, semname=None, **kw):
        outs, ins = [out], [in_]
        w = self._collect(eng, outs, ins)
        if semname is None:
            d = self._dep(out)
            semname = out.tensor.name if d is not None else in_.tensor.name
        if semname not in self.dsem:
            self.dsem[semname] = [self.es.enter_context(self.nc.semaphore("d_" + semname)), 0]
        s = self.dsem[semname]
        s[1] += 16
        self.q[eng].append(dict(fn=lambda e: e.dma_start(out=out, in_=in_, **kw), waits=w, kind='d', sem=semname))
        self._commit(('d', semname, s[1]), outs, ins)

    def emit(self):
        nc = self.nc
        targets = {e: set() for e in ENGS}
        for e in ENGS:
            for ins in self.q[e]:
                for d in ins['waits']:
                    if d[0] == 'e':
                        targets[d[1]].add(d[2])
        semval = {}
        for e in ENGS:
            c = 0
            vals = []
            for i in range(len(self.q[e])):
                if i in targets[e]:
                    c += 1
                vals.append(c)
            semval[e] = vals
        final = [(self.dsem[k][0], self.dsem[k][1]) for k in self.dsem]

        def body_for(e):
            def body(eng):
                waited = {}
                for i, ins in enumerate(self.q[e]):
                    need = {}
                    for d in ins['waits']:
                        if d[0] == 'e':
                            key = ('e', d[1])
                            val = semval[d[1]][d[2]]
                        else:
                            key = ('d', d[1])
                            val = d[2]
                        if val > need.get(key, 0):
                            need[key] = val
                    for key, val in need.items():
                        if waited.get(key, 0) >= val:
                            continue
                        waited[key] = val
                        h = self.sem[key[1]] if key[0] == 'e' else self.dsem[key[1]][0]
                        eng.wait_ge(h, val)
                    r = ins['fn'](eng)
                    if ins['kind'] == 'c':
                        if i in targets[e]:
                            r.then_inc(self.sem[e], 1)
                    else:
                        r.then_inc(self.dsem[ins['sem']][0], 16)
                if e == 'sp':
                    for h, v in final:
                        eng.wait_ge(h, v)
            return body

        with nc.Block() as block:
            block.sync(body_for('sp'))
            block.tensor(body_for('pe'))
            block.scalar(body_for('act'))
            block.vector(body_for('dve'))
            block.gpsimd(body_for('pool'))


def _cols(v):
    v = np.asarray(v, np.float32)
    return v.reshape(-1, 128).T


PP_LAYOUT = {}


def _pack_params(inp, l):
    parts = []
    off = 0

    def add(name, arr):
        nonlocal off
        arr = np.ascontiguousarray(arr, dtype=np.float32)
        PP_LAYOUT[name] = (off, arr.shape[1])
        off += arr.shape[1]
        parts.append(arr)

    add('n1', _cols(inp['norm1_w'][l]))
    add('n2', _cols(inp['norm2_w'][l]))
    cw = inp['conv_w'][l]
    add('cw', np.stack([cw[j].reshape(8, 128).T for j in range(4)], axis=2).reshape(128, 32))
    add('cb', _cols(inp['conv_b'][l]))
    add('dsk', _cols(np.repeat(inp['d_skip'][l], 64)))
    add('snw', _cols(inp['ssd_norm_w'][l]))
    add('dtb', np.tile(inp['dt_bias'][l][None, :], (128, 1)))
    add('alog', np.tile(inp['a_log'][l][None, :], (128, 1)))
    add('mu', _cols(inp['rw_mu'][l]))
    add('w0', _cols(inp['rw_w0'][l]))
    add('a0', _cols(inp['rw_a0'][l]))
    add('kk', _cols(inp['rw_kk'][l]))
    add('ka', _cols(inp['rw_ka'][l]))
    add('rk', _cols(inp['rw_rk'][l]))
    add('lnw', _cols(inp['rw_lnx_w'][l]))
    add('lnb', _cols(inp['rw_lnx_b'][l]))
    add('lg0', _cols(inp['hg_lb_logits'][0]))
    add('lg1', _cols(inp['hg_lb_logits'][1]))
    add('hnw', _cols(inp['hg_norm_w'][l]))
    add('fin', _cols(inp['final_norm_w']))
    PP_LAYOUT['_n'] = off
    return np.concatenate(parts, axis=1)


CC = {}


def _consts():
    parts = []
    off = 0

    def add(name, arr):
        nonlocal off
        arr = np.ascontiguousarray(arr, dtype=np.float32)
        assert arr.shape[0] == 128
        CC[name] = (off, arr.shape[1])
        off += arr.shape[1]
        parts.append(arr)

    i = np.arange(128)
    add('ident', np.eye(128))
    add('ones', np.ones((128, 128)))
    bo = np.zeros((128, 128))
    bo[:64, :64] = 1
    bo[64:, 64:] = 1
    add('bones', bo)
    u1 = (i[:, None] <= i[None, :]).astype(np.float32)
    u0 = (i[:, None] < i[None, :]).astype(np.float32)
    add('u1', u1)
    add('u0', u0)
    add('l0', u0.T)
    add('mneg', (u1 - 1.0) * 30000.0)
    t = np.arange(256)
    add('rm64', np.tile(((t % 64) != 0).astype(np.float32)[None, :], (128, 1)))
    add('rm32', np.tile(((t % 32) != 0).astype(np.float32)[None, :], (128, 1)))
    sc = np.zeros((128, 8), np.float32)
    sc[:, 0] = 1e-6
    sc[:, 1] = 1.0
    sc[:, 2] = 64e-5
    sc[:, 3] = 1e-12
    sc[:, 4] = -0.5
    add('sc', sc)
    CC['_n'] = off
    return np.concatenate(parts, axis=1)


def build(SEQ, NT=int(os.environ.get('KNT', '256')), DEPTH=2):
    nc = bass.Bass("TRN2", target_bir_lowering=False)
    TDT = BF16 if TRUNK_BF16 else F32
    if TRUNK_BF16:
        nc.allow_low_precision("trunk projections use bf16 operands with fp32 accumulation")
    es = ExitStack()
    P = Prog(nc, es)
    NPP = PP_LAYOUT['_n']
    NCC = CC['_n']

    def din(name, shape):
        return nc.dram_tensor("i_" + name, list(shape), F32, kind="ExternalInput").ap()

    def dout(name, shape):
        return nc.dram_tensor("r_" + name, list(shape), F32, kind="ExternalOutput").ap()

    xp = din("xp", [SEQ, D])
    xs_in = din("xs", [64, D])
    meta = din("meta", [16, D])
    st_ssm = din("st_ssm", [DEPTH, 8, 64, 128])
    st_conv = din("st_conv", [DEPTH, 3, 1024])
    st_rwkv = din("st_rwkv", [DEPTH, 8, 64, 64])
    st_shift = din("st_shift", [DEPTH, 1792])
    st_hgrn = din("st_hgrn", [DEPTH, 4, 128, 128])
    w_in = din("w_in", [DEPTH, D, NIN])
    w_out = din("w_out", [DEPTH, 1536, D])
    w_gate = din("w_gate", [DEPTH, D, DFF])
    w_up = din("w_up", [DEPTH, D, DFF])
    w_down = din("w_down", [DEPTH, DFF, D])
    wa2_d = din("wa2", [DEPTH, 2, 128, 512])
    g2_d = din("g2", [DEPTH, 128, 512])
    pp_d = din("pp", [DEPTH, 128, NPP])
    cc_d = din("cc", [128, NCC])

    WCACHE = TRUNK_BF16
    wc = {}
    if WCACHE:
        ffg = [(f * 128, min(4, 22 - f)) for f in range(0, 22, 4)]
        WG = {"in": [(0, 4), (512, 4), (1024, 4), (1544, 4), (2056, 4), (2568, 4), (3080, 2), (3336, 4), (3848, 4),
                     (4360, 4), (4872, 4)],
              "out": [(0, 4), (512, 4)], "gate": ffg, "up": ffg, "down": [(0, 4), (512, 4)]}
        WNK = {"in": 8, "out": 12, "gate": 8, "up": 8, "down": 22}
        for nm_ in WG:
            for l_ in range(DEPTH):
                nm2 = "wc_%s%d" % (nm_, l_)
                wc[(nm_, l_)] = nc.dram_tensor(nm2, [len(WG[nm_]), 128, WNK[nm_], 512], BF16, kind="Internal").ap()
                P.deps[nm2] = Dep()

    y_p = dout("y_p", [SEQ, D])
    y_s = dout("y_s", [64, D])
    o_ssm = dout("o_ssm", [2, DEPTH, 8, 64, 128])
    o_conv = dout("o_conv", [2, DEPTH, 3, 1024])
    o_rwkv = dout("o_rwkv", [2, DEPTH, 8, 64, 64])
    o_shift = dout("o_shift", [2, DEPTH, 1792])
    o_hgrn = dout("o_hgrn", [2, DEPTH, 4, 128, 128])

    cc = P.sb("cc", [128, NCC])
    pp = [P.sb("pp%d" % l, [128, NPP]) for l in range(DEPTH)]
    dv = [P.sb("dv%d" % l, [128, 40]) for l in range(DEPTH)]
    wa2 = [P.sb("wa2_%d" % l, [128, 2, 512]) for l in range(DEPTH)]
    g2 = [P.sb("g2_%d" % l, [128, 512]) for l in range(DEPTH)]
    wdt32 = [P.sb("wdt%d" % l, [128, 8, 8]) for l in range(DEPTH)]
    wdt = [P.sb("wdtb%d" % l, [128, 8, 8], TDT) for l in range(DEPTH)] if TRUNK_BF16 else wdt32

    xT = [P.sb("xT%d" % k, [128, NT]) for k in range(8)]
    hT = [P.sb("hT%d" % k, [128, NT], TDT) for k in range(8)]
    zT = P.sb("zT", [128, 4, NT])
    xsT = P.sb("xsT", [128, 4, NT + 3])
    bcT = P.sb("bcT", [128, 4, NT + 3])
    rT = P.sb("rT", [128, 4, NT + 1])
    kT = P.sb("kT", [128, 4, NT + 1])
    vT = P.sb("vT", [128, 4, NT + 1])
    x12 = P.sb("x12", [128, NT + 1])
    x13 = P.sb("x13", [128, NT + 1])
    qT = P.sb("qT", [128, 4, NT])
    fzT = P.sb("fzT", [128, 4, NT])
    ivT = P.sb("ivT", [128, 4, NT])
    ggT = P.sb("ggT", [128, 4, NT])
    Y = [P.sb("Y%d" % j, [128, NT], TDT) for j in range(12)]

    NSL = 12
    SL = [P.sb("SL%d" % j, [128, 4, NT]) for j in range(NSL)]
    sc1 = [P.sb("sc1_%d" % j, [128, NT]) for j in range(3)]

    def a16(f, n):
        v = SL[f // 8][:, :, :].rearrange("p a b -> p (a b)").bitcast(BF16)
        t = f % 8
        return v[:, t * NT:t * NT + n]
    KB = 4
    NW = 5 if WCACHE else 9
    wsl = [P.sb("wsl%d" % j, [128, KB, 512] if WCACHE else [128, 512], TDT if WCACHE else F32) for j in range(NW)]
    NWB = 4
    wbl = [P.sb("wbl%d" % j, [128, 512], TDT) for j in range(NWB)] if (TRUNK_BF16 and not WCACHE) else None
    xin = [P.sb("xin%d" % j, [128, D]) for j in range(1)]

    hstg = [[P.sb("hst%d_%d" % (l, g), [128, 256]) for g in range(2)] for l in range(DEPTH)]
    hist = [P.sb("hist%d" % l, [128, 8, 3]) for l in range(DEPTH)]
    prev = [P.sb("prev%d" % l, [128, 14]) for l in range(DEPTH)]
    rstg = [[P.sb("rst%d_%d" % (l, g), [128, 2, 64]) for g in range(2)] for l in range(DEPTH)]
    gstg = [[P.sb("gst%d_%d" % (l, g), [128, 2, 128]) for g in range(2)] for l in range(DEPTH)]
    stmp = P.sb("stmp", [128, 4, 128])
    rtmp = P.sb("rtmp", [64, 4, 2, 64])

    tk = {}
    for nm, shp in [('wlc', [128, 4, 4]), ('slc', [128, 4, 8]), ('rs', [128, NT])]:
        tk[nm] = P.sb("tk_" + nm, shp)
    tk['rs2'] = tk['rs']
    tkg = []
    for g in range(2):
        T = {}
        for nm, shp in [('t1', [64, 4]), ('dt', [64, 4]), ('loga', [64, 4]), ('cum', [64, 8]), ('ecum', [64, 4]),
                        ('dec', [64, 4]), ('et128', [128, 4]), ('Btok', [64, 128]),
                        ('AkT', [64, 4, 64]), ('RbT', [64, 4, 64]), ('RkT', [64, 4, 64]),
                        ('Pa', [64, 4, 64]), ('PaT', [64, 4, 64]), ('Pb', [64, 4, 64]), ('PbT', [64, 4, 64]),
                        ('Xa', [64, 4, 64]), ('Xb', [64, 4, 64]), ('Vtok', [64, 4, 64]), ('Otok', [64, 4, 64]),
                        ('bbt', [64, 2, 128]), ('kbt', [64, 2, 128]), ('stm', [128, 2, 64]),
                        ('hMT', [32, 2, 32]), ('hvt', [32, 2, 128]), ('hkt', [32, 2, 128])]:
            T[nm] = P.sb("tk%d_%s" % (g, nm), shp)
        for a_, b_ in [('lgB', 'Pa'), ('Dm', 'PaT'), ('LT', 'Pb'), ('MT', 'PbT'), ('xdt', 'Xa'), ('xdec', 'Xb'),
                       ('ytok', 'Otok')]:
            T[a_] = T[b_]
        tkg.append(T)

    PSB = [P.psum("ps%d" % j, [128, 512]) for j in range(8)]
    psi = [0]

    def ps():
        t = PSB[psi[0] % 8]
        psi[0] += 1
        return t

    wi = [0]
    wbi = [0]

    def wslot():
        t = wsl[wi[0] % NW]
        wi[0] += 1
        return t

    def isap(x):
        return not isinstance(x, (int, float))

    def tt(eng, out, a, b, op):
        P.op(eng, lambda e: e.tensor_tensor(out=out, in0=a, in1=b, op=op), [out], [a, b])

    def ts(eng, out, a, s1, op0, s2=None, op1=None):
        eng = 'dve'
        ins = [a] + [s for s in (s1, s2) if s is not None and isap(s)]
        if op1 is None:
            P.op(eng, lambda e: e.tensor_scalar(out=out, in0=a, scalar1=s1, scalar2=None, op0=op0), [out], ins)
        else:
            P.op(eng, lambda e: e.tensor_scalar(out=out, in0=a, scalar1=s1, scalar2=s2, op0=op0, op1=op1), [out], ins)

    def stt(eng, out, a, s, b, op0, op1):
        eng = 'dve'
        ins = [a, b] + ([s] if isap(s) else [])
        P.op(eng, lambda e: e.scalar_tensor_tensor(out=out, in0=a, scalar=s, in1=b, op0=op0, op1=op1), [out], ins)

    def act(out, in_, func, bias=None, scale=1.0):
        ins = [in_] + ([bias] if bias is not None else []) + ([scale] if isap(scale) else [])
        if bias is None:
            P.op('act', lambda e: e.activation(out=out, in_=in_, func=func, scale=scale), [out], ins)
        else:
            P.op('act', lambda e: e.activation(out=out, in_=in_, func=func, bias=bias, scale=scale), [out], ins)

    def cp(eng, out, in_):
        if eng == 'act':
            P.op('act', lambda e: e.copy(out=out, in_=in_), [out], [in_])
        else:
            P.op(eng, lambda e: e.tensor_copy(out=out, in_=in_), [out], [in_])

    def recip(out, in_):
        P.op('dve', lambda e: e.reciprocal(out=out, in_=in_), [out], [in_])

    def mm(out, lhsT, rhs, start=True, stop=True):
        P.op('pe', lambda e: e.matmul(out, lhsT, rhs, start=start, stop=stop), [out], [lhsT, rhs])

    def tr(out, in_, n_in_part):
        idn = cc[0:n_in_part, CC['ident'][0]:CC['ident'][0] + n_in_part]
        P.op('pe', lambda e: e.transpose(out, in_, idn), [out], [in_, idn])

    def scan(out, d0, d1):
        P.op('dve', lambda e: e.tensor_tensor_scan(out=out, data0=d0, data1=d1, initial=0.0, op0=ALU.mult, op1=ALU.add),
             [out], [d0, d1])

    def memset(eng, out, val):
        P.op(eng, lambda e: e.memset(out, val), [out], [])

    def C(name, rows=128, c0=0, c1=None):
        o, n = CC[name]
        if c1 is None:
            c1 = n
        return cc[0:rows, o + c0:o + c1]

    def SC(i, rows=128):
        o = CC['sc'][0]
        return cc[0:rows, o + i:o + i + 1]

    def PPc(l, name, i=0, n=1, rows=128):
        o = PP_LAYOUT[name][0]
        return pp[l][0:rows, o + i:o + i + n]

    def bc(ap, shape, axis):
        return ap.unsqueeze(axis).broadcast_to(list(shape))

    eng_rr = [0]

    def ve():
        eng_rr[0] += 1
        return 'dve' if eng_rr[0] % 2 else 'pool'

    def rstd(out, in_, scale, eps_ap):
        act(out, in_, AF.Ln, bias=eps_ap, scale=scale)
        act(out, out, AF.Exp, scale=-0.5)

    def sigmoid(out, in_, scale=1.0, nbias=None):
        act(out, in_, AF.Exp, bias=nbias, scale=-scale)
        act(out, out, AF.Ln, bias=SC(1, out.shape[0]))
        act(out, out, AF.Exp, scale=-1.0)

    P.dma('sp', cc[:, :], cc_d[:, :])
    for l in range(DEPTH):
        P.dma('sp', pp[l][:, :], pp_d[l, :, :])
        P.dma('sp', wa2[l][:, :, :], wa2_d[l].rearrange("t p c -> p t c"))
        P.dma('sp', g2[l][:, :], g2_d[l, :, :])
        if 'a' not in KDBG:
            P.dma('sp', wdt32[l][:, :, :], w_in[l, :, 1536:1544].rearrange("(k p) c -> p k c", p=128),
                  allow_slow_non_contiguous=True)
            if TRUNK_BF16:
                P.op('dve', lambda e, l=l: e.tensor_copy(out=wdt[l][:, :, :], in_=wdt32[l][:, :, :]),
                     [wdt[l][:, :, :]], [wdt32[l][:, :, :]])
    for l in range(DEPTH if 'd' not in KDBG else 0):
        ts('dve', dv[l][:, 0:4], PPc(l, 'w0', 0, 4), -1.0, ALU.mult)
        ts('dve', dv[l][:, 4:8], PPc(l, 'a0', 0, 4), -1.0, ALU.mult)
        act(dv[l][:, 8:16], PPc(l, 'alog', 0, 8), AF.Exp)
        ts('dve', dv[l][:, 8:16], dv[l][:, 8:16], -1.0, ALU.mult)
        if l == 0:
            memset('dve', dv[l][:, 16:20], 0.0)
        else:
            tt('dve', dv[l][:, 16:20], PPc(l, 'lg0', 0, 4), PPc(l, 'lg1', 0, 4), ALU.subtract)
            act(dv[l][:, 16:20], dv[l][:, 16:20], AF.Exp)
            ts('dve', dv[l][:, 16:20], dv[l][:, 16:20], 1.0, ALU.add)
            recip(dv[l][:, 16:20], dv[l][:, 16:20])
        ts('dve', dv[l][:, 20:24], dv[l][:, 16:20], -1.0, ALU.mult, 1.0, ALU.add)

    def big_proj(wsrc, nk, c0, ntile, rhs_list, n, evac, width=None):
        banks = [ps() for _ in range(ntile)]
        wcols = ntile * 128 if width is None else width
        if WCACHE:
            wname, l_ = wsrc
            gi = WG[wname].index((c0, ntile))
            cache = wc[(wname, l_)]
            for k0 in range(0, nk, KB):
                kb = min(KB, nk - k0)
                wt = wslot()
                P.dma('sp', wt[:, 0:kb, 0:wcols], cache[gi, :, k0:k0 + kb, 0:wcols])
                for kk in range(kb):
                    k = k0 + kk
                    for j in range(ntile):
                        mm(banks[j][:, 0:n], wt[:, kk, j * 128:(j + 1) * 128], rhs_list[k], start=(k == 0),
                           stop=(k == nk - 1))
            for j in range(ntile):
                evac(j, banks[j][:, 0:n])
            return
        for k in range(nk):
            wt = wslot()
            P.dma('sp', wt[:, 0:wcols], wsrc[k * 128:(k + 1) * 128, c0:c0 + wcols])
            if TRUNK_BF16 and not WCACHE:
                wb = wbl[wbi[0] % NWB]
                wbi[0] += 1
                cp('act' if wbi[0] % 2 else 'dve', wb[:, 0:wcols], wt[:, 0:wcols])
                wt = wb
            for j in range(ntile):
                mm(banks[j][:, 0:n], wt[:, j * 128:(j + 1) * 128], rhs_list[k], start=(k == 0), stop=(k == nk - 1))
        for j in range(ntile):
            evac(j, banks[j][:, 0:n])

    def rmsnorm(l, pname, n, src, dst):
        pss = ps()
        for k in range(8):
            s = sc1[k % 2]
            tt(ve(), s[:, 0:n], src[k][:, 0:n], src[k][:, 0:n], ALU.mult)
            mm(pss[:, 0:n], C('ones'), s[:, 0:n], start=(k == 0), stop=(k == 7))
        rstd(tk['rs'][:, 0:n], pss[:, 0:n], 1.0 / D, SC(0))
        for k in range(8):
            stt(ve(), dst[k][:, 0:n], src[k][:, 0:n], PPc(l, pname, k), tk['rs'][:, 0:n], ALU.mult, ALU.mult)

    def layer(l, n, L, Lh):
        nch = n // L
        nchh = n // Lh
        W_in = ('in', l) if WCACHE else w_in[l]
        rmsnorm(l, 'n1', n, xT, hT)
        hl = [hT[k][:, 0:n] for k in range(8)]

        evc = [0]

        def ev_to(dst_fn):
            def f(j, psap):
                evc[0] += 1
                cp('act' if evc[0] % 2 else 'dve', dst_fn(j), psap)
            return f
        big_proj(W_in, 8, 0, 4, hl, n, ev_to(lambda j: zT[:, j, 0:n]))
        big_proj(W_in, 8, 512, 4, hl, n, ev_to(lambda j: xsT[:, j, 3:3 + n]))
        big_proj(W_in, 8, 1024, 4, hl, n, ev_to(lambda j: bcT[:, j, 3:3 + n]))
        ssd_pro(l, n, L, nch)
        big_proj(W_in, 8, 3336, 4, hl, n, ev_to(lambda j: qT[:, j, 0:n]))
        big_proj(W_in, 8, 3336 + 512, 4, hl, n, ev_to(lambda j: fzT[:, j, 0:n]))
        big_proj(W_in, 8, 3336 + 1024, 4, hl, n, ev_to(lambda j: ivT[:, j, 0:n]))
        big_proj(W_in, 8, 3336 + 1536, 4, hl, n, ev_to(lambda j: ggT[:, j, 0:n]))
        hgrn_pro(l, n, Lh, nchh)
        big_proj(W_in, 8, 1544, 4, hl, n, ev_to(lambda j: rT[:, j, 1:1 + n]))
        big_proj(W_in, 8, 1544 + 512, 4, hl, n, ev_to(lambda j: kT[:, j, 1:1 + n]))
        big_proj(W_in, 8, 1544 + 1024, 4, hl, n, ev_to(lambda j: vT[:, j, 1:1 + n]))
        big_proj(W_in, 8, 1544 + 1536, 2, hl, n, ev_to(lambda j: (x12 if j == 0 else x13)[:, 1:1 + n]))

        if 'proj' not in STAGES:
            return
        interleave(ssd_chain(l, n, L, nch, 0), hgrn_chain(l, n, Lh, nchh, 0),
                   ssd_chain(l, n, L, nch, 1), hgrn_chain(l, n, Lh, nchh, 1))
        ssd_epi(l, n, L, nch)
        hgrn_epi(l, n, Lh, nchh)
        rwkv_pro(l, n, L, nch)
        interleave(rwkv_chain(l, n, L, nch, 0), rwkv_chain(l, n, L, nch, 1))
        rwkv_epi(l, n, L, nch)
        if 'oproj' not in STAGES:
            return

        yl = [Y[j][:, 0:n] for j in range(12)]
        for g in range(2):
            big_proj(('out', l) if WCACHE else w_out[l], 12, g * 512, 4, yl, n,
                     lambda j, psap, g=g: tt('dve', xT[4 * g + j][:, 0:n], xT[4 * g + j][:, 0:n], psap, ALU.add))

        if 'ffn' not in STAGES:
            return
        rmsnorm(l, 'n2', n, xT, hT)
        hl = [hT[k][:, 0:n] for k in range(8)]
        f0 = 0
        while f0 < 22:
            nt_ = min(4, 22 - f0)
            gb = {}

            def ev_gate(j, psap, gb=gb):
                gb[j] = psap
            big_proj(('gate', l) if WCACHE else w_gate[l], 8, f0 * 128, nt_, hl, n, ev_gate)

            def ev_up(j, psap, gb=gb, f0=f0):
                f = f0 + j
                a_ = a16(f, n) if TRUNK_BF16 else SL[f // 4][:, f % 4, 0:n]
                s = sc1[j % 2]
                sigmoid(s[:, 0:n], gb[j])
                tt('dve', s[:, 0:n], s[:, 0:n], gb[j], ALU.mult)
                tt('dve', a_, s[:, 0:n], psap, ALU.mult)
            big_proj(('up', l) if WCACHE else w_up[l], 8, f0 * 128, nt_, hl, n, ev_up)
            f0 += nt_
        al = [(a16(f, n) if TRUNK_BF16 else SL[f // 4][:, f % 4, 0:n]) for f in range(22)]
        for g in range(2):
            big_proj(('down', l) if WCACHE else w_down[l], 22, g * 512, 4, al, n,
                     lambda j, psap, g=g: tt('dve', xT[4 * g + j][:, 0:n], xT[4 * g + j][:, 0:n], psap, ALU.add))

    def ssd_pro(l, n, L, nch):
        cp('pool', xsT[:, :, 0:3], hist[l][:, 0:4, :])
        cp('pool', bcT[:, :, 0:3], hist[l][:, 4:8, :])
        cwo = PP_LAYOUT['cw'][0]
        for i in range(8):
            src = xsT if i < 4 else bcT
            ii = i % 4
            a_ = sc1[i % 2]
            e1 = ve()
            ts(e1, a_[:, 0:n], src[:, ii, 0:n], pp[l][:, cwo + i * 4:cwo + i * 4 + 1], ALU.mult)
            for j in range(1, 4):
                stt(e1, a_[:, 0:n], src[:, ii, j:j + n], pp[l][:, cwo + i * 4 + j:cwo + i * 4 + j + 1], a_[:, 0:n],
                    ALU.mult, ALU.add)
            cp('pool', hist[l][:, i, :], src[:, ii, n:n + 3])
            ts('dve', a_[:, 0:n], a_[:, 0:n], PPc(l, 'cb', i), ALU.add)
            s2 = sc1[2]
            sigmoid(s2[:, 0:n], a_[:, 0:n])
            tt('dve', src[:, ii, 3:3 + n], a_[:, 0:n], s2[:, 0:n], ALU.mult)
        for i in range(4):
            s2 = sc1[i % 2]
            sigmoid(s2[:, 0:n], zT[:, i, 0:n])
            tt('dve', zT[:, i, 0:n], zT[:, i, 0:n], s2[:, 0:n], ALU.mult)

    def ssd_chain(l, n, L, nch, g):
        T = tkg[g]
        yss = SL[8 + g]
        hs4 = slice(4 * g, 4 * g + 4)
        for c in range(nch):
            cs = slice(c * L, (c + 1) * L)
            cs3 = slice(3 + c * L, 3 + (c + 1) * L)
            pdt = ps()
            for k in range(8):
                mm(pdt[0:L, 0:4], hT[k][:, cs], wdt[l][:, k, hs4], start=(k == 0), stop=(k == 7))
            tt('dve', T['t1'][0:L, :], pdt[0:L, 0:4], PPc(l, 'dtb', 4 * g, 4, rows=L), ALU.add)
            act(T['t1'][0:L, :], T['t1'][0:L, :], AF.Exp)
            act(T['dt'][0:L, :], T['t1'][0:L, :], AF.Ln, bias=SC(1, L))
            tt('dve', T['loga'][0:L, :], T['dt'][0:L, :], dv[l][0:L, 8 + 4 * g:12 + 4 * g], ALU.mult)
            yield
            pc = ps()
            mm(pc[0:L, 0:4], C('u1', L, 0, L), T['loga'][0:L, :])
            mm(pc[0:L, 4:8], C('ones', L, 0, L), T['loga'][0:L, :])
            mm(pc[:, 8:12], C('ones', L, 0, 128), T['loga'][0:L, :])
            cp('pool', T['lgB'][0:L, :, 0:L], bc(T['loga'][0:L, :], [L, 4, L], 2))
            cp('dve', T['cum'][0:L, :], pc[0:L, 0:8])
            act(T['et128'][:, :], pc[:, 8:12], AF.Exp)
            act(T['ecum'][0:L, :], T['cum'][0:L, 0:4], AF.Exp)
            tt('dve', T['dec'][0:L, :], T['cum'][0:L, 4:8], T['cum'][0:L, 0:4], ALU.subtract)
            act(T['dec'][0:L, :], T['dec'][0:L, :], AF.Exp)
            yield
            pcb = ps()
            for h in range(4):
                mm(pcb[0:L, h * L:(h + 1) * L], T['lgB'][0:L, h, 0:L], C('u1', L, 0, L))
            pg = ps()
            mm(pg[0:L, 0:L], bcT[:, g, cs3], bcT[:, 2 + g, cs3])
            px = ps()
            for i in range(2):
                tr(px[0:L, i * 128:(i + 1) * 128], xsT[:, 2 * g + i, cs3], 128)
            pb = ps()
            tr(pb[0:L, 0:128], bcT[:, g, cs3], 128)
            pcb3 = pcb[0:L, 0:4 * L].rearrange("p (h i) -> p h i", h=4)
            tt('dve', T['Dm'][0:L, :, 0:L], pcb3, bc(C('mneg', L, 0, L), [L, 4, L], 1), ALU.add)
            tt('pool', T['Dm'][0:L, :, 0:L], T['Dm'][0:L, :, 0:L], bc(T['cum'][0:L, 0:4], [L, 4, L], 2), ALU.subtract)
            act(T['LT'][0:L, :, 0:L], T['Dm'][0:L, :, 0:L], AF.Exp)
            px3 = px[0:L, 0:256].rearrange("p (h d) -> p h d", h=4)
            tt('dve', T['xdt'][0:L, :, :], px3, bc(T['dt'][0:L, :], [L, 4, 64], 2), ALU.mult)
            tt('pool', T['xdec'][0:L, :, :], T['xdt'][0:L, :, :], bc(T['dec'][0:L, :], [L, 4, 64], 2), ALU.mult)
            cp('act', T['Btok'][0:L, :], pb[0:L, 0:128])
            tt('dve', T['MT'][0:L, :, 0:L], T['LT'][0:L, :, 0:L], bc(pg[0:L, 0:L], [L, 4, L], 1), ALU.mult)
            yield
            py = ps()
            for h in range(4):
                mm(py[0:L, h * 64:(h + 1) * 64], T['MT'][0:L, h, 0:L], T['xdt'][0:L, h, :])
            pyi = ps()
            mm(pyi[0:L, 0:256], bcT[:, 2 + g, cs3], hstg[l][g][:, :])
            ph = ps()
            mm(ph[:, 0:256], T['Btok'][0:L, :], T['xdec'][0:L, :, :].rearrange("p h d -> p (h d)"))
            pyi3 = pyi[0:L, 0:256].rearrange("p (h d) -> p h d", h=4)
            py3 = py[0:L, 0:256].rearrange("p (h d) -> p h d", h=4)
            tt('dve', T['ytok'][0:L, :, :], pyi3, bc(T['ecum'][0:L, :], [L, 4, 64], 2), ALU.mult)
            tt('dve', T['ytok'][0:L, :, :], T['ytok'][0:L, :, :], py3, ALU.add)
            h3 = hstg[l][g][:, :].rearrange("p (h d) -> p h d", h=4)
            tt('dve', h3, h3, bc(T['et128'][:, :], [128, 4, 64], 2), ALU.mult)
            tt('dve', hstg[l][g][:, :], hstg[l][g][:, :], ph[:, 0:256], ALU.add)
            yield
            pyt = ps()
            for i in range(2):
                tr(pyt[:, i * L:(i + 1) * L], T['ytok'][0:L, 2 * i:2 * i + 2, :].rearrange("p h d -> p (h d)"), L)
            for i in range(2):
                stt('dve', yss[:, i, cs], xsT[:, 2 * g + i, cs3], PPc(l, 'dsk', 2 * g + i), pyt[:, i * L:(i + 1) * L],
                    ALU.mult, ALU.add)
            yield

    def ssd_epi(l, n, L, nch):
        for g in range(2):
            yss = SL[8 + g]
            for t_ in range(2):
                tt(ve(), yss[:, t_, 0:n], yss[:, t_, 0:n], zT[:, 2 * g + t_, 0:n], ALU.mult)
            pss = ps()
            for t_ in range(2):
                s = sc1[t_]
                tt(ve(), s[:, 0:n], yss[:, t_, 0:n], yss[:, t_, 0:n], ALU.mult)
                mm(pss[:, 0:n], C('ones'), s[:, 0:n], start=(t_ == 0), stop=(t_ == 1))
            rstd(tk['rs2'][:, 0:n], pss[:, 0:n], 1.0 / 256, SC(0))
            for t_ in range(2):
                i = 2 * g + t_
                stt(ve(), Y[i][:, 0:n], yss[:, t_, 0:n], PPc(l, 'snw', i), tk['rs2'][:, 0:n], ALU.mult, ALU.mult)

    def rwkv_pro(l, n, L, nch):
        c1 = slice(1, 1 + n)
        muo = PP_LAYOUT['mu'][0]
        tl = [(rT, 0), (rT, 1), (rT, 2), (rT, 3), (kT, 0), (kT, 1), (kT, 2), (kT, 3), (vT, 0), (vT, 1), (vT, 2), (vT, 3)]
        for g, t3 in enumerate((rT, kT, vT)):
            cp('pool', t3[:, :, 0:1], prev[l][:, 4 * g:4 * g + 4].unsqueeze(2))
        cp('pool', x12[:, 0:1], prev[l][:, 12:13])
        cp('pool', x13[:, 0:1], prev[l][:, 13:14])

        def shift(cur, prv, lastcol, prevdst, mucol):
            d = sc1[mucol % 2]
            e1 = ve()
            tt(e1, d[:, 0:n], prv, cur, ALU.subtract)
            cp('pool', prevdst, lastcol)
            stt(e1, cur, d[:, 0:n], pp[l][:, muo + mucol:muo + mucol + 1], cur, ALU.mult, ALU.add)
        for idx, (t3, i) in enumerate(tl):
            shift(t3[:, i, 1:1 + n], t3[:, i, 0:n], t3[:, i, n:n + 1], prev[l][:, idx:idx + 1], idx)
        shift(x12[:, 1:1 + n], x12[:, 0:n], x12[:, n:n + 1], prev[l][:, 12:13], 12)
        shift(x13[:, 1:1 + n], x13[:, 0:n], x13[:, n:n + 1], prev[l][:, 13:14], 13)
        sigmoid(x12[0:64, c1], x12[0:64, c1], scale=2.0)
        ts('dve', x12[0:64, c1], x12[0:64, c1], 2.0, ALU.mult, -1.0, ALU.add)
        sigmoid(x13[:, c1], x13[:, c1])

        LW, AS, G, CW, KKN, KM, BON, BV, RT_, AT_, BT_, KT_ = SL
        S1, S2 = RT_, AT_
        A4 = slice(0, n)
        for m in range(4):
            p1 = ps()
            mm(p1[:, 0:n], wa2[l][:, 0, m * 128:(m + 1) * 128], x12[:, c1])
            act(LW[:, m, A4], p1[:, 0:n], AF.Exp, bias=dv[l][:, m:m + 1], scale=-1.0)
            p2 = ps()
            mm(p2[:, 0:n], wa2[l][:, 1, m * 128:(m + 1) * 128], x12[:, c1])
            act(AS[:, m, A4], p2[:, 0:n], AF.Exp, bias=dv[l][:, 4 + m:5 + m], scale=-1.0)
            p3 = ps()
            mm(p3[:, 0:n], g2[l][:, m * 128:(m + 1) * 128], x13[:, c1])
            cp('act', G[:, m, A4], p3[:, 0:n])
            ts('dve', KKN[:, m, A4], kT[:, m, c1], PPc(l, 'kk', m), ALU.mult)
        act(LW[:, :, A4], LW[:, :, A4], AF.Ln, bias=SC(1))
        act(LW[:, :, A4], LW[:, :, A4], AF.Exp, scale=-1.0)
        act(AS[:, :, A4], AS[:, :, A4], AF.Ln, bias=SC(1))
        act(AS[:, :, A4], AS[:, :, A4], AF.Exp, scale=-1.0)
        ts('dve', LW[:, :, A4], LW[:, :, A4], -0.6065306597126334, ALU.mult)
        for m in range(4):
            scan(CW[:, m, A4], C('rm64', 128, 0, n), LW[:, m, A4])
        tt('pool', S1[:, :, A4], KKN[:, :, A4], KKN[:, :, A4], ALU.mult)
        for m in range(4):
            p4 = ps()
            mm(p4[:, 0:n], C('bones'), S1[:, m, A4])
            act(S2[:, m, A4], p4[:, 0:n], AF.Ln, bias=SC(3))
        act(S2[:, :, A4], S2[:, :, A4], AF.Exp, scale=-0.5)
        tt('dve', KKN[:, :, A4], KKN[:, :, A4], S2[:, :, A4], ALU.mult)
        for m in range(4):
            ts('dve', KM[:, m, A4], AS[:, m, A4], PPc(l, 'ka', m), ALU.mult, PPc(l, 'ka', m), ALU.subtract)
        stt('dve', KM[:, :, A4], KM[:, :, A4], 1.0, kT[:, :, c1], ALU.add, ALU.mult)
        for m in range(4):
            stt('dve', S1[:, m, A4], rT[:, m, c1], PPc(l, 'rk', m), KM[:, m, A4], ALU.mult, ALU.mult)
            p5 = ps()
            mm(p5[:, 0:n], C('bones'), S1[:, m, A4])
            tt('dve', BON[:, m, A4], p5[:, 0:n], vT[:, m, c1], ALU.mult)
        tt('pool', BV[:, :, A4], KKN[:, :, A4], AS[:, :, A4], ALU.mult)
        Wt = AS
        act(Wt[:, :, 0:n], CW[:, :, 0:n], AF.Exp)
        tt(ve(), RT_[:, :, 0:n], rT[:, :, c1], Wt[:, :, 0:n], ALU.mult)
        cp('pool', tk['wlc'][:, :, 0:nch], Wt[:, :, L - 1:n:L])
        tt(ve(), AT_[:, :, 0:n], CW[:, :, 0:n], LW[:, :, 0:n], ALU.subtract)
        act(AT_[:, :, 0:n], AT_[:, :, 0:n], AF.Exp)
        stt(ve(), AT_[:, :, 0:n], KKN[:, :, 0:n], -1.0, AT_[:, :, 0:n], ALU.mult, ALU.mult)
        En = LW
        act(En[:, :, 0:n], CW[:, :, 0:n], AF.Exp, scale=-1.0)
        tt(ve(), BT_[:, :, 0:n], BV[:, :, 0:n], En[:, :, 0:n], ALU.mult)
        tt(ve(), KT_[:, :, 0:n], KM[:, :, 0:n], En[:, :, 0:n], ALU.mult)
        EB = LW
        for m in range(4):
            cw3 = CW[:, m, 0:n].rearrange("p (c l) -> p c l", l=L)
            eb3 = EB[:, m, 0:n].rearrange("p (c l) -> p c l", l=L)
            tt(ve(), eb3, bc(CW[:, m, L - 1:n:L], [128, nch, L], 2), cw3, ALU.subtract)
        act(EB[:, :, 0:n], EB[:, :, 0:n], AF.Exp)
        BB, KB = KKN, AS
        tt(ve(), BB[:, :, 0:n], BV[:, :, 0:n], EB[:, :, 0:n], ALU.mult)
        tt(ve(), KB[:, :, 0:n], KM[:, :, 0:n], EB[:, :, 0:n], ALU.mult)
        ATo, RTo = KM, BV
        cp(ve(), ATo[:, :, 0:n], AT_[:, :, 0:n])
        cp(ve(), RTo[:, :, 0:n], RT_[:, :, 0:n])
        memset('pool', ATo[0:64, :, 0:n], 0.0)
        memset('pool', RTo[0:64, :, 0:n], 0.0)
        memset('pool', AT_[64:128, :, 0:n], 0.0)
        memset('pool', RT_[64:128, :, 0:n], 0.0)

    def rwkv_chain(l, n, L, nch, g):
        LW, AS, G, CW, KKN, KM, BON, BV, RT_, AT_, BT_, KT_ = SL
        BB, KB = KKN, AS
        ATm, RTm = (AT_, KM), (RT_, BV)
        OT = (CW, LW)[g]
        T = tkg[g]
        ST = rstg[l][g]
        nst = int(np.log2(L))

        def v3(p_):
            return p_[0:L, 0:4 * L].rearrange("p (h i) -> p h i", h=4)

        def x3(p_):
            return p_[0:L, 0:256].rearrange("p (h d) -> p h d", h=4)
        u0b = bc(C('u0', L, 0, L), [L, 4, L], 1)
        u1b = bc(C('u1', L, 0, L), [L, 4, L], 1)
        l0b = bc(C('l0', L, 0, L), [L, 4, L], 1)
        for c in range(nch):
            cs = slice(c * L, (c + 1) * L)
            cs1 = slice(1 + c * L, 1 + (c + 1) * L)
            pN, pNT, pAk, pRb, pRk = ps(), ps(), ps(), ps(), ps()
            for hh in range(4):
                h = 4 * g + hh
                q = h // 2
                hsl = slice(hh * L, (hh + 1) * L)
                am, rm_ = ATm[h % 2], RTm[h % 2]
                mm(pN[0:L, hsl], am[:, q, cs], BT_[:, q, cs])
                mm(pNT[0:L, hsl], BT_[:, q, cs], am[:, q, cs])
                mm(pAk[0:L, hsl], KT_[:, q, cs], am[:, q, cs])
                mm(pRb[0:L, hsl], BT_[:, q, cs], rm_[:, q, cs])
                mm(pRk[0:L, hsl], KT_[:, q, cs], rm_[:, q, cs])
            pv = ps()
            for i in range(2):
                tr(pv[0:L, i * 128:(i + 1) * 128], vT[:, 2 * g + i, cs1], 128)
            pbb, pkb = ps(), ps()
            for i in range(2):
                tr(pbb[0:L, i * 128:(i + 1) * 128], BB[:, 2 * g + i, cs], 128)
                tr(pkb[0:L, i * 128:(i + 1) * 128], KB[:, 2 * g + i, cs], 128)
            Pa, PaT = T['Pa'], T['PaT']
            tt('dve', Pa[0:L, :, 0:L], v3(pN), l0b, ALU.mult)
            tt('dve', PaT[0:L, :, 0:L], v3(pNT), u0b, ALU.mult)
            tt('dve', T['AkT'][0:L, :, 0:L], v3(pAk), u0b, ALU.mult)
            cp('act', T['Vtok'][0:L, :, :], x3(pv))
            tt('dve', T['RbT'][0:L, :, 0:L], v3(pRb), u1b, ALU.mult)
            tt('dve', T['RkT'][0:L, :, 0:L], v3(pRk), u1b, ALU.mult)
            cp('act', T['bbt'][0:L, :, :], pbb[0:L, 0:256].rearrange("p (m d) -> p m d", m=2))
            cp('act', T['kbt'][0:L, :, :], pkb[0:L, 0:256].rearrange("p (m d) -> p m d", m=2))
            yield
            pX = ps()
            for hh in range(4):
                h = 4 * g + hh
                mm(pX[0:L, hh * 64:(hh + 1) * 64], ATm[h % 2][:, h // 2, cs], ST[:, hh // 2, :], start=True, stop=False)
                mm(pX[0:L, hh * 64:(hh + 1) * 64], T['AkT'][0:L, hh, 0:L], T['Vtok'][0:L, hh, :], start=False, stop=True)
            Xc, Xn = T['Xa'], T['Xb']
            cp('act', Xc[0:L, :, :], x3(pX))
            yield
            Pc, PcT, Pn, PnT = Pa, PaT, T['Pb'], T['PbT']
            for st_ in range(nst):
                pU = ps()
                for hh in range(4):
                    mm(pU[0:L, hh * 64:(hh + 1) * 64], PcT[0:L, hh, 0:L], Xc[0:L, hh, :])
                if st_ < nst - 1:
                    pS, pST = ps(), ps()
                    for hh in range(4):
                        hsl = slice(hh * L, (hh + 1) * L)
                        mm(pS[0:L, hsl], PcT[0:L, hh, 0:L], Pc[0:L, hh, 0:L])
                        mm(pST[0:L, hsl], Pc[0:L, hh, 0:L], PcT[0:L, hh, 0:L])
                tt('dve', Xn[0:L, :, :], Xc[0:L, :, :], x3(pU), ALU.add)
                Xc, Xn = Xn, Xc
                if st_ < nst - 1:
                    cp('act', Pn[0:L, :, 0:L], v3(pS))
                    cp('dve', PnT[0:L, :, 0:L], v3(pST))
                    Pc, PcT, Pn, PnT = Pn, PnT, Pc, PcT
                yield
            SA = Xc
            pO = ps()
            for hh in range(4):
                h = 4 * g + hh
                o_ = pO[0:L, hh * 64:(hh + 1) * 64]
                mm(o_, RTm[h % 2][:, h // 2, cs], ST[:, hh // 2, :], start=True, stop=False)
                mm(o_, T['RbT'][0:L, hh, 0:L], SA[0:L, hh, :], start=False, stop=False)
                mm(o_, T['RkT'][0:L, hh, 0:L], T['Vtok'][0:L, hh, :], start=False, stop=True)
            pSt = ps()
            for i in range(2):
                mm(pSt[:, i * 128:(i + 1) * 128], T['bbt'][0:L, i, :],
                   SA[0:L, 2 * i:2 * i + 2, :].rearrange("p h d -> p (h d)"), start=True, stop=False)
                mm(pSt[:, i * 128:(i + 1) * 128], T['kbt'][0:L, i, :],
                   T['Vtok'][0:L, 2 * i:2 * i + 2, :].rearrange("p h d -> p (h d)"), start=False, stop=True)
            cp('act', T['Otok'][0:L, :, :], x3(pO))
            pSt3 = pSt[:, 0:256].rearrange("p (q d) -> p q d", q=2)
            for hh2 in range(2):
                prr = slice(hh2 * 64, hh2 * 64 + 64)
                tt('dve', T['stm'][prr, :, :], ST[prr, :, :], bc(tk['wlc'][prr, 2 * g:2 * g + 2, c], [64, 2, 64], 2),
                   ALU.mult)
                tt('dve', ST[prr, :, :], T['stm'][prr, :, :], pSt3[prr, :, hh2 * 64:(hh2 + 1) * 64], ALU.add)
            yield
            pot = ps()
            for i in range(2):
                tr(pot[:, i * L:(i + 1) * L], T['Otok'][0:L, 2 * i:2 * i + 2, :].rearrange("p h d -> p (h d)"), L)
            cp('act', OT[:, 2 * g:2 * g + 2, cs], pot[:, 0:2 * L].rearrange("p (q i) -> p q i", q=2))
            yield

    def rwkv_epi(l, n, L, nch):
        LW, AS, G, CW, KKN, KM, BON, BV, RT_, AT_, BT_, KT_ = SL
        CEN, SQ = BT_, KT_
        A4 = slice(0, n)
        for m in range(4):
            OT = (CW, LW)[m // 2]
            pm = ps()
            mm(pm[:, 0:n], C('bones'), OT[:, m, A4])
            stt('dve', CEN[:, m, A4], pm[:, 0:n], -1.0 / 64, OT[:, m, A4], ALU.mult, ALU.add)
        tt('pool', SQ[:, :, A4], CEN[:, :, A4], CEN[:, :, A4], ALU.mult)
        for m in range(4):
            pvv = ps()
            mm(pvv[:, 0:n], C('bones'), SQ[:, m, A4])
            act(SQ[:, m, A4], pvv[:, 0:n], AF.Ln, bias=SC(2), scale=1.0 / 64)
        act(SQ[:, :, A4], SQ[:, :, A4], AF.Exp, scale=-0.5)
        tt('dve', CEN[:, :, A4], CEN[:, :, A4], SQ[:, :, A4], ALU.mult)
        for m in range(4):
            ts('dve', CEN[:, m, A4], CEN[:, m, A4], PPc(l, 'lnw', m), ALU.mult, PPc(l, 'lnb', m), ALU.add)
        tt('pool', CEN[:, :, A4], CEN[:, :, A4], BON[:, :, A4], ALU.add)
        for m in range(4):
            tt(ve(), Y[4 + m][:, 0:n], CEN[:, m, A4], G[:, m, A4], ALU.mult)

    def hgrn_pro(l, n, Lh, nchh):
        E_, L1, KG, CU, TM, QT, KT2, QH = SL[0:8]
        mid = Lh // 2
        rmn = 'rm32' if Lh == 32 else 'rm64'
        for m in range(4):
            ts(ve(), fzT[:, m, 0:n], fzT[:, m, 0:n], -60.0, ALU.max)
        act(E_[:, :, 0:n], fzT[:, :, 0:n], AF.Exp, scale=-1.0)
        for m in range(4):
            act(L1[:, m, 0:n], E_[:, m, 0:n], AF.Ln, bias=SC(1), scale=dv[l][:, 16 + m:17 + m])
        act(TM[:, :, 0:n], E_[:, :, 0:n], AF.Ln, bias=SC(1))
        tt(ve(), L1[:, :, 0:n], L1[:, :, 0:n], TM[:, :, 0:n], ALU.subtract)
        act(TM[:, :, 0:n], TM[:, :, 0:n], AF.Exp, scale=-1.0)
        for m in range(4):
            stt(ve(), KG[:, m, 0:n], E_[:, m, 0:n], dv[l][:, 20 + m:21 + m], TM[:, m, 0:n], ALU.mult, ALU.mult)
            scan(CU[:, m, 0:n], C(rmn, 128, 0, n), L1[:, m, 0:n])
        for m in range(4):
            cu3 = CU[:, m, 0:n].rearrange("p (c l) -> p c l", l=Lh)
            tm3 = TM[:, m, 0:n].rearrange("p (c l) -> p c l", l=Lh)
            tt(ve(), tm3, cu3, bc(CU[:, m, mid:n:Lh], [128, nchh, Lh], 2), ALU.subtract)
        ts('dve', TM[:, :, 0:n], TM[:, :, 0:n], 38.0, ALU.min, -38.0, ALU.max)
        act(QT[:, :, 0:n], TM[:, :, 0:n], AF.Exp)
        act(KT2[:, :, 0:n], TM[:, :, 0:n], AF.Exp, scale=-1.0)
        tt(ve(), QT[:, :, 0:n], QT[:, :, 0:n], qT[:, :, 0:n], ALU.mult)
        tt(ve(), KT2[:, :, 0:n], KT2[:, :, 0:n], KG[:, :, 0:n], ALU.mult)
        act(QH[:, :, 0:n], CU[:, :, 0:n], AF.Exp)
        cp('pool', tk['slc'][:, :, 0:nchh], QH[:, :, Lh - 1:n:Lh])
        tt(ve(), QH[:, :, 0:n], QH[:, :, 0:n], qT[:, :, 0:n], ALU.mult)
        KH = E_
        for m in range(4):
            cu3 = CU[:, m, 0:n].rearrange("p (c l) -> p c l", l=Lh)
            kh3 = KH[:, m, 0:n].rearrange("p (c l) -> p c l", l=Lh)
            tt(ve(), kh3, bc(CU[:, m, Lh - 1:n:Lh], [128, nchh, Lh], 2), cu3, ALU.subtract)
        act(KH[:, :, 0:n], KH[:, :, 0:n], AF.Exp)
        tt(ve(), KH[:, :, 0:n], KH[:, :, 0:n], KG[:, :, 0:n], ALU.mult)

    def hgrn_chain(l, n, Lh, nchh, g):
        E_, L1, KG, CU, TM, QT, KT2, QH = SL[0:8]
        KH = E_
        OT = (L1, TM)[g]
        T = tkg[g]
        S = gstg[l][g]
        for c in range(nchh):
            cs = slice(c * Lh, (c + 1) * Lh)
            pA = ps()
            for hh in range(2):
                h = 2 * g + hh
                mm(pA[0:Lh, hh * Lh:(hh + 1) * Lh], KT2[:, h, cs], QT[:, h, cs])
            pv, pk = ps(), ps()
            for hh in range(2):
                h = 2 * g + hh
                tr(pv[0:Lh, hh * 128:(hh + 1) * 128], ivT[:, h, cs], 128)
                tr(pk[0:Lh, hh * 128:(hh + 1) * 128], KH[:, h, cs], 128)
            tt('dve', T['hMT'][0:Lh, :, 0:Lh], pA[0:Lh, 0:2 * Lh].rearrange("p (h i) -> p h i", h=2),
               bc(C('u1', Lh, 0, Lh), [Lh, 2, Lh], 1), ALU.mult)
            cp('act', T['hvt'][0:Lh, :, :], pv[0:Lh, 0:256].rearrange("p (h d) -> p h d", h=2))
            cp('dve', T['hkt'][0:Lh, :, :], pk[0:Lh, 0:256].rearrange("p (h d) -> p h d", h=2))
            yield
            po = ps()
            for hh in range(2):
                h = 2 * g + hh
                o_ = po[:, hh * Lh:(hh + 1) * Lh]
                mm(o_, T['hvt'][0:Lh, hh, :], T['hMT'][0:Lh, hh, 0:Lh], start=True, stop=False)
                mm(o_, S[:, hh, :], QH[:, h, cs], start=False, stop=True)
            pS = ps()
            for hh in range(2):
                mm(pS[:, hh * 128:(hh + 1) * 128], T['hkt'][0:Lh, hh, :], T['hvt'][0:Lh, hh, :])
            cp('act', OT[:, 2 * g:2 * g + 2, cs], po[:, 0:2 * Lh].rearrange("p (h i) -> p h i", h=2))
            for hh in range(2):
                h = 2 * g + hh
                stt('dve', S[:, hh, :], S[:, hh, :], tk['slc'][:, h, c:c + 1], pS[:, hh * 128:(hh + 1) * 128],
                    ALU.mult, ALU.add)
            yield

    def hgrn_epi(l, n, Lh, nchh):
        E_, L1, KG, CU, TM, QT, KT2, QH = SL[0:8]
        for h in range(4):
            OT = (L1, TM)[h // 2]
            sq = sc1[h % 2]
            tt(ve(), sq[:, 0:n], OT[:, h, 0:n], OT[:, h, 0:n], ALU.mult)
            pss = ps()
            mm(pss[:, 0:n], C('ones'), sq[:, 0:n])
            rstd(sq[:, 0:n], pss[:, 0:n], 1.0 / 128, SC(0))
            stt(ve(), OT[:, h, 0:n], OT[:, h, 0:n], PPc(l, 'hnw', h), sq[:, 0:n], ALU.mult, ALU.mult)
            s2 = sc1[2]
            sigmoid(s2[:, 0:n], ggT[:, h, 0:n])
            tt(ve(), s2[:, 0:n], s2[:, 0:n], ggT[:, h, 0:n], ALU.mult)
            tt(ve(), Y[8 + h][:, 0:n], OT[:, h, 0:n], s2[:, 0:n], ALU.mult)

    def interleave(*gens):
        gens = list(gens)
        while gens:
            for g_ in list(gens):
                try:
                    next(g_)
                except StopIteration:
                    gens.remove(g_)


    def run_tile(src, n, L, Lh, dst):
        nb = (n + 127) // 128
        for b in range(nb):
            tb = min(128, n - b * 128)
            xi = xin[0]
            P.dma(DQ, xi[0:tb, :], src[b * 128:b * 128 + tb, :])
            for g in range(2 if 'i' not in KDBG else 0):
                pt = ps()
                for k in range(4):
                    tr(pt[:, k * 128:k * 128 + tb], xi[0:tb, (4 * g + k) * 128:(4 * g + k + 1) * 128], tb)
                for k in range(4):
                    cp('act' if (k % 2 and 'j' not in KDBG) else 'dve', xT[4 * g + k][:, b * 128:b * 128 + tb], pt[:, k * 128:k * 128 + tb])
        for l in range(DEPTH):
            layer(l, n, L, Lh)
        if dst is None or 'g' in KDBG:
            return
        hF = [SL[k // 4][:, k % 4, :] for k in range(8)]
        if 'h' not in KDBG:
            rmsnorm(DEPTH - 1, 'fin', n, xT, hF)
        for b in range(nb):
            tb = min(128, n - b * 128)
            xo = xin[0]
            for g in range(2):
                pt = ps()
                for k in range(4):
                    tr(pt[0:tb, k * 128:(k + 1) * 128], hF[4 * g + k][:, b * 128:b * 128 + tb], 128)
                cp('act' if g else 'dve', xo[0:tb, g * 512:(g + 1) * 512], pt[0:tb, :])
            P.dma(DQ, dst[b * 128:b * 128 + tb, :], xo[0:tb, :])

    def store_states(si):
        for l in range(DEPTH):
            for i in range(4):
                pt = ps()
                tr(pt[:, 0:128], hstg[l][i // 2][:, (i % 2) * 128:(i % 2 + 1) * 128], 128)
                cp('act', stmp[:, i, :], pt[:, 0:128])
            for hh in range(2):
                P.dma(DQ, o_ssm[si, l].rearrange("(i hh) p n -> hh p i n", hh=2)[hh],
                      stmp[hh * 64:(hh + 1) * 64, :, :])
            for i in range(8):
                P.dma(DQ, o_conv[si, l][:, i * 128:(i + 1) * 128].rearrange("j p -> p j"), hist[l][:, i, :],
                      allow_slow_non_contiguous=True)
            P.dma(DQ, o_shift[si, l].rearrange("(i p) -> p i", p=128), prev[l][:, :],
                  allow_slow_non_contiguous=True)
            for q in range(4):
                pt = ps()
                tr(pt[0:64, 0:128], rstg[l][q // 2][:, q % 2, :], 128)
                cp('act', rtmp[:, q, :, :], pt[0:64, 0:128].rearrange("p (hh n) -> p hh n", hh=2))
            P.dma(DQ, o_rwkv[si, l].rearrange("(q hh) v n -> v q hh n", hh=2), rtmp[:, :, :, :])
            for g in range(2):
                P.dma(DQ, o_hgrn[si, l, 2 * g:2 * g + 2].rearrange("h k v -> k h v"), gstg[l][g][:, :, :])

    def load_states():
        for l in range(DEPTH):
            for hh in range(2):
                P.dma(DQ, stmp[hh * 64:(hh + 1) * 64, :, :],
                      st_ssm[l].rearrange("(i hh) p n -> hh p i n", hh=2)[hh])
            for i in range(4):
                pt = ps()
                tr(pt[:, 0:128], stmp[:, i, :], 128)
                cp('act', hstg[l][i // 2][:, (i % 2) * 128:(i % 2 + 1) * 128], pt[:, 0:128])
            for i in range(8):
                P.dma(DQ, hist[l][:, i, :], st_conv[l][:, i * 128:(i + 1) * 128].rearrange("j p -> p j"),
                      allow_slow_non_contiguous=True)
            P.dma(DQ, prev[l][:, :], st_shift[l].rearrange("(i p) -> p i", p=128),
                  allow_slow_non_contiguous=True)
            P.dma(DQ, rtmp[:, :, :, :], st_rwkv[l].rearrange("(q hh) v n -> v q hh n", hh=2))
            for q in range(4):
                pt = ps()
                tr(pt[:, 0:64], rtmp[:, q, :, :].rearrange("p hh n -> p (hh n)"), 64)
                cp('act', rstg[l][q // 2][:, q % 2, :], pt[:, 0:64])
            for g in range(2):
                P.dma(DQ, gstg[l][g][:, :, :], st_hgrn[l, 2 * g:2 * g + 2].rearrange("h k v -> k h v"))

    def zero_states():
        for l in range(DEPTH):
            for g in range(2):
                memset('pool', hstg[l][g][:, :], 0.0)
                memset('pool', rstg[l][g][:, :, :], 0.0)
                memset('pool', gstg[l][g][:, :, :], 0.0)
            memset('pool', hist[l][:, :, :], 0.0)
            memset('pool', prev[l][:, :], 0.0)

    def build_wcache():
        cnt = 0
        for l_ in range(DEPTH):
            for nm_, src in (("in", w_in), ("out", w_out), ("gate", w_gate), ("up", w_up), ("down", w_down)):
                cache = wc[(nm_, l_)]
                for gi, (c0, nt_) in enumerate(WG[nm_]):
                    w_ = nt_ * 128
                    for k in range(WNK[nm_]):
                        land = SL[cnt % 6][:, :, :].rearrange("p a b -> p (a b)")
                        outb = SL[6 + cnt % 6][:, :, :].rearrange("p a b -> p (a b)").bitcast(BF16)
                        P.dma('sp', land[:, 0:w_], src[l_, k * 128:(k + 1) * 128, c0:c0 + w_])
                        cp('dve', outb[:, 0:w_], land[:, 0:w_])
                        P.dma('act', cache[gi, :, k, 0:w_], outb[:, 0:w_])
                        cnt += 1

    if WCACHE:
        build_wcache()

    if 'states' in STAGES:
        load_states()
    else:
        zero_states()
    if 'f' not in KDBG:
        run_tile(xs_in, 64, 64, 32, y_s)
    if 'states' in STAGES:
        store_states(1)
    zero_states()
    if 'b' not in KDBG:
        run_tile(meta, 16, 16, 16, None)
    t0 = 0 if 'c' not in KDBG else SEQ
    while t0 < SEQ:
        n = min(NT, SEQ - t0)
        run_tile(xp[t0:t0 + n, :], n, 64, 32, y_p[t0:t0 + n, :])
        t0 += n
    if 'states' in STAGES:
        store_states(0)

    P.emit()
    return nc, es, P


_CACHE = {}


def kernel(**inp):
    inp = {k: np.asarray(v, dtype=np.float32) for k, v in inp.items()}
    B, SEQ, _ = inp['x_prompt'].shape
    DEPTH = inp['w_in'].shape[0]
    pps = np.stack([_pack_params(inp, l) for l in range(DEPTH)], axis=0)
    ccs = _consts()
    zpad = np.zeros_like(inp['rw_w2'])
    wa2 = np.ascontiguousarray(np.stack([np.concatenate([inp['rw_w2'], zpad], axis=1),
                                         np.concatenate([zpad, inp['rw_a2']], axis=1)], axis=1))
    key = (SEQ, DEPTH)
    if key not in _CACHE:
        _CACHE[key] = build(SEQ, DEPTH=DEPTH)
    nc = _CACHE[key][0]
    in_maps = []
    for c in range(NCORES):
        in_maps.append({
            "xp": np.ascontiguousarray(inp['x_prompt'][c]),
            "xs": np.ascontiguousarray(inp['x_sample'][c]),
            "meta": inp['meta_tokens'],
            "st_ssm": np.ascontiguousarray(inp['state_ssm'][:, c]),
            "st_conv": np.ascontiguousarray(inp['state_conv'][:, c]),
            "st_rwkv": np.ascontiguousarray(inp['state_rwkv'][:, c]),
            "st_shift": np.ascontiguousarray(inp['state_shift'][:, c]),
            "st_hgrn": np.ascontiguousarray(inp['state_hgrn'][:, c]),
            "w_in": inp['w_in'], "w_out": inp['w_out'], "w_gate": inp['w_gate'], "w_up": inp['w_up'],
            "w_down": inp['w_down'], "wa2": wa2, "g2": inp['rw_g2'], "pp": pps, "cc": ccs,
        })
    in_maps = [{"i_" + k: v for k, v in m.items()} for m in in_maps]
    if os.environ.get('KONE'):
        res = run_bass_kernel_spmd(nc, in_maps[:1], core_ids=[0])
        R = [{k[2:]: v for k, v in res.results[0].items()}] * NCORES
    else:
        res = run_bass_kernel_spmd(nc, in_maps, core_ids=list(range(NCORES)))
        R = [{k[2:]: v for k, v in r.items()} for r in res.results]
    y_prompt = np.stack([R[c]["y_p"] for c in range(NCORES)], axis=0)
    y_sample = np.stack([R[c]["y_s"] for c in range(NCORES)], axis=0)

    def st(name, si):
        return np.stack([R[c][name][si] for c in range(NCORES)], axis=1)
    outs = [y_prompt, y_sample]
    for si in (0, 1):
        for name in ("o_ssm", "o_conv", "o_rwkv", "o_shift", "o_hgrn"):
            outs.append(st(name, si))
    return tuple(np.ascontiguousarray(o, dtype=np.float32) for o in outs)
```

```python
import numpy as np
from contextlib import ExitStack
import concourse.bass as bass
import concourse.mybir as mybir
from concourse.bass_utils import run_bass_kernel_spmd

F32 = mybir.dt.float32
BF16 = mybir.dt.bfloat16
AF = mybir.ActivationFunctionType
ALU = mybir.AluOpType

D = 1024
DFF = 2816
NIN = 5384
NCORES = 8
DQ = 'sp'
TRUNK_BF16 = True
import os
KDBG = os.environ.get('KDBG', '')
STAGES = {'states', 'proj', 'ssd', 'rwkv', 'hgrn', 'oproj', 'ffn'}
ENGS = ['pe', 'act', 'dve', 'pool', 'sp']


class Dep:
    __slots__ = ('lw', 'rd')

    def __init__(self):
        self.lw = None
        self.rd = []


class Prog:
    def __init__(self, nc, es):
        self.nc, self.es = nc, es
        self.q = {e: [] for e in ENGS}
        self.deps = {}
        self.dsem = {}
        self.sem = {e: es.enter_context(nc.semaphore("q_" + e)) for e in ENGS}
        self.tensors = {}
        self.psum_names = set()

    def sb(self, name, shape, dtype=F32):
        name = "s_" + name
        t = self.es.enter_context(self.nc.sbuf_tensor(name, list(shape), dtype))
        self.deps[name] = Dep()
        return t

    def psum(self, name, shape):
        t = self.es.enter_context(self.nc.psum_tensor(name, list(shape), F32))
        self.deps[name] = Dep()
        self.psum_names.add(name)
        return t

    def _dep(self, ap):
        nm = ap.tensor.name
        return self.deps.get(nm)

    def _collect(self, eng, outs, ins):
        w = []
        for ap in ins:
            d = self._dep(ap)
            if d is not None and d.lw is not None:
                w.append(d.lw)
            if d is not None and ap.tensor.name in self.psum_names:
                for r in d.rd:
                    if not (r[0] == 'e' and r[1] == eng):
                        w.append(r)
        for ap in outs:
            d = self._dep(ap)
            if d is None:
                continue
            if d.lw is not None:
                w.append(d.lw)
            for r in d.rd:
                w.append(r)
        if eng == 'pe':
            w = [x for x in w if not (x[0] == 'e' and x[1] == 'pe')]
        return w

    def _commit(self, ev, outs, ins):
        for ap in ins:
            d = self._dep(ap)
            if d is not None:
                d.rd.append(ev)
        for ap in outs:
            d = self._dep(ap)
            if d is not None:
                d.lw = ev
                d.rd = []

    def op(self, eng, fn, outs, ins):
        w = self._collect(eng, outs, ins)
        idx = len(self.q[eng])
        self.q[eng].append(dict(fn=fn, waits=w, kind='c'))
        self._commit(('e', eng, idx), outs, ins)

    def dma(self, eng, out, in_, semname=None, **kw):
        outs, ins = [out], [in_]
        w = self._collect(eng, outs, ins)
        if semname is None:
            d = self._dep(out)
            semname = out.tensor.name if d is not None else in_.tensor.name
        if semname not in self.dsem:
            self.dsem[semname] = [self.es.enter_context(self.nc.semaphore("d_" + semname)), 0]
        s = self.dsem[semname]
        s[1] += 16
        self.q[eng].append(dict(fn=lambda e: e.dma_start(out=out, in_=in_, **kw), waits=w, kind='d', sem=semname))
        self._commit(('d', semname, s[1]), outs, ins)

    def emit(self):
        nc = self.nc
        targets = {e: set() for e in ENGS}
        for e in ENGS:
            for ins in self.q[e]:
                for d in ins['waits']:
                    if d[0] == 'e':
                        targets[d[1]].add(d[2])
        semval = {}
        for e in ENGS:
            c = 0
            vals = []
            for i in range(len(self.q[e])):
                if i in targets[e]:
                    c += 1
                vals.append(c)
            semval[e] = vals
        final = [(self.dsem[k][0], self.dsem[k][1]) for k in self.dsem]

        def body_for(e):
            def body(eng):
                waited = {}
                for i, ins in enumerate(self.q[e]):
                    need = {}
                    for d in ins['waits']:
                        if d[0] == 'e':
                            key = ('e', d[1])
                            val = semval[d[1]][d[2]]
                        else:
                            key = ('d', d[1])
                            val = d[2]
                        if val > need.get(key, 0):
                            need[key] = val
                    for key, val in need.items():
                        if waited.get(key, 0) >= val:
                            continue
                        waited[key] = val
                        h = self.sem[key[1]] if key[0] == 'e' else self.dsem[key[1]][0]
                        eng.wait_ge(h, val)
                    r = ins['fn'](eng)
                    if ins['kind'] == 'c':
                        if i in targets[e]:
                            r.then_inc(self.sem[e], 1)
                    else:
                        r.then_inc(self.dsem[ins['sem']][0], 16)
                if e == 'sp':
                    for h, v in final:
                        eng.wait_ge(h, v)
            return body

        with nc.Block() as block:
            block.sync(body_for('sp'))
            block.tensor(body_for('pe'))
            block.scalar(body_for('act'))
            block.vector(body_for('dve'))
            block.gpsimd(body_for('pool'))


def _cols(v):
    v = np.asarray(v, np.float32)
    return v.reshape(-1, 128).T


PP_LAYOUT = {}


def _pack_params(inp, l):
    parts = []
    off = 0

    def add(name, arr):
        nonlocal off
        arr = np.ascontiguousarray(arr, dtype=np.float32)
        PP_LAYOUT[name] = (off, arr.shape[1])
        off += arr.shape[1]
        parts.append(arr)

    add('n1', _cols(inp['norm1_w'][l]))
    add('n2', _cols(inp['norm2_w'][l]))
    cw = inp['conv_w'][l]
    add('cw', np.stack([cw[j].reshape(8, 128).T for j in range(4)], axis=2).reshape(128, 32))
    add('cb', _cols(inp['conv_b'][l]))
    add('dsk', _cols(np.repeat(inp['d_skip'][l], 64)))
    add('snw', _cols(inp['ssd_norm_w'][l]))
    add('dtb', np.tile(inp['dt_bias'][l][None, :], (128, 1)))
    add('alog', np.tile(inp['a_log'][l][None, :], (128, 1)))
    add('mu', _cols(inp['rw_mu'][l]))
    add('w0', _cols(inp['rw_w0'][l]))
    add('a0', _cols(inp['rw_a0'][l]))
    add('kk', _cols(inp['rw_kk'][l]))
    add('ka', _cols(inp['rw_ka'][l]))
    add('rk', _cols(inp['rw_rk'][l]))
    add('lnw', _cols(inp['rw_lnx_w'][l]))
    add('lnb', _cols(inp['rw_lnx_b'][l]))
    add('lg0', _cols(inp['hg_lb_logits'][0]))
    add('lg1', _cols(inp['hg_lb_logits'][1]))
    add('hnw', _cols(inp['hg_norm_w'][l]))
    add('fin', _cols(inp['final_norm_w']))
    PP_LAYOUT['_n'] = off
    return np.concatenate(parts, axis=1)


CC = {}


def _consts():
    parts = []
    off = 0

    def add(name, arr):
        nonlocal off
        arr = np.ascontiguousarray(arr, dtype=np.float32)
        assert arr.shape[0] == 128
        CC[name] = (off, arr.shape[1])
        off += arr.shape[1]
        parts.append(arr)

    i = np.arange(128)
    add('ident', np.eye(128))
    add('ones', np.ones((128, 128)))
    bo = np.zeros((128, 128))
    bo[:64, :64] = 1
    bo[64:, 64:] = 1
    add('bones', bo)
    u1 = (i[:, None] <= i[None, :]).astype(np.float32)
    u0 = (i[:, None] < i[None, :]).astype(np.float32)
    add('u1', u1)
    add('u0', u0)
    add('l0', u0.T)
    add('mneg', (u1 - 1.0) * 30000.0)
    t = np.arange(256)
    add('rm64', np.tile(((t % 64) != 0).astype(np.float32)[None, :], (128, 1)))
    add('rm32', np.tile(((t % 32) != 0).astype(np.float32)[None, :], (128, 1)))
    sc = np.zeros((128, 8), np.float32)
    sc[:, 0] = 1e-6
    sc[:, 1] = 1.0
    sc[:, 2] = 64e-5
    sc[:, 3] = 1e-12
    sc[:, 4] = -0.5
    add('sc', sc)
    CC['_n'] = off
    return np.concatenate(parts, axis=1)


def build(SEQ, NT=int(os.environ.get('KNT', '256')), DEPTH=2):
    nc = bass.Bass("TRN2", target_bir_lowering=False)
    TDT = BF16 if TRUNK_BF16 else F32
    if TRUNK_BF16:
        nc.allow_low_precision("trunk projections use bf16 operands with fp32 accumulation")
    es = ExitStack()
    P = Prog(nc, es)
    NPP = PP_LAYOUT['_n']
    NCC = CC['_n']

    def din(name, shape):
        return nc.dram_tensor("i_" + name, list(shape), F32, kind="ExternalInput").ap()

    def dout(name, shape):
        return nc.dram_tensor("r_" + name, list(shape), F32, kind="ExternalOutput").ap()

    xp = din("xp", [SEQ, D])
    xs_in = din("xs", [64, D])
    meta = din("meta", [16, D])
    st_ssm = din("st_ssm", [DEPTH, 8, 64, 128])
    st_conv = din("st_conv", [DEPTH, 3, 1024])
    st_rwkv = din("st_rwkv", [DEPTH, 8, 64, 64])
    st_shift = din("st_shift", [DEPTH, 1792])
    st_hgrn = din("st_hgrn", [DEPTH, 4, 128, 128])
    w_in = din("w_in", [DEPTH, D, NIN])
    w_out = din("w_out", [DEPTH, 1536, D])
    w_gate = din("w_gate", [DEPTH, D, DFF])
    w_up = din("w_up", [DEPTH, D, DFF])
    w_down = din("w_down", [DEPTH, DFF, D])
    wa2_d = din("wa2", [DEPTH, 2, 128, 512])
    g2_d = din("g2", [DEPTH, 128, 512])
    pp_d = din("pp", [DEPTH, 128, NPP])
    cc_d = din("cc", [128, NCC])

    WCACHE = TRUNK_BF16
    wc = {}
    if WCACHE:
        ffg = [(f * 128, min(4, 22 - f)) for f in range(0, 22, 4)]
        WG = {"in": [(0, 4), (512, 4), (1024, 4), (1544, 4), (2056, 4), (2568, 4), (3080, 2), (3336, 4), (3848, 4),
                     (4360, 4), (4872, 4)],
              "out": [(0, 4), (512, 4)], "gate": ffg, "up": ffg, "down": [(0, 4), (512, 4)]}
        WNK = {"in": 8, "out": 12, "gate": 8, "up": 8, "down": 22}
        for nm_ in WG:
            for l_ in range(DEPTH):
                nm2 = "wc_%s%d" % (nm_, l_)
                wc[(nm_, l_)] = nc.dram_tensor(nm2, [len(WG[nm_]), 128, WNK[nm_], 512], BF16, kind="Internal").ap()
                P.deps[nm2] = Dep()

    y_p = dout("y_p", [SEQ, D])
    y_s = dout("y_s", [64, D])
    o_ssm = dout("o_ssm", [2, DEPTH, 8, 64, 128])
    o_conv = dout("o_conv", [2, DEPTH, 3, 1024])
    o_rwkv = dout("o_rwkv", [2, DEPTH, 8, 64, 64])
    o_shift = dout("o_shift", [2, DEPTH, 1792])
    o_hgrn = dout("o_hgrn", [2, DEPTH, 4, 128, 128])

    cc = P.sb("cc", [128, NCC])
    pp = [P.sb("pp%d" % l, [128, NPP]) for l in range(DEPTH)]
    dv = [P.sb("dv%d" % l, [128, 40]) for l in range(DEPTH)]
    wa2 = [P.sb("wa2_%d" % l, [128, 2, 512]) for l in range(DEPTH)]
    g2 = [P.sb("g2_%d" % l, [128, 512]) for l in range(DEPTH)]
    wdt32 = [P.sb("wdt%d" % l, [128, 8, 8]) for l in range(DEPTH)]
    wdt = [P.sb("wdtb%d" % l, [128, 8, 8], TDT) for l in range(DEPTH)] if TRUNK_BF16 else wdt32

    xT = [P.sb("xT%d" % k, [128, NT]) for k in range(8)]
    hT = [P.sb("hT%d" % k, [128, NT], TDT) for k in range(8)]
    zT = P.sb("zT", [128, 4, NT])
    xsT = P.sb("xsT", [128, 4, NT + 3])
    bcT = P.sb("bcT", [128, 4, NT + 3])
    rT = P.sb("rT", [128, 4, NT + 1])
    kT = P.sb("kT", [128, 4, NT + 1])
    vT = P.sb("vT", [128, 4, NT + 1])
    x12 = P.sb("x12", [128, NT + 1])
    x13 = P.sb("x13", [128, NT + 1])
    qT = P.sb("qT", [128, 4, NT])
    fzT = P.sb("fzT", [128, 4, NT])
    ivT = P.sb("ivT", [128, 4, NT])
    ggT = P.sb("ggT", [128, 4, NT])
    Y = [P.sb("Y%d" % j, [128, NT], TDT) for j in range(12)]

    NSL = 12
    SL = [P.sb("SL%d" % j, [128, 4, NT]) for j in range(NSL)]
    sc1 = [P.sb("sc1_%d" % j, [128, NT]) for j in range(3)]

    def a16(f, n):
        v = SL[f // 8][:, :, :].rearrange("p a b -> p (a b)").bitcast(BF16)
        t = f % 8
        return v[:, t * NT:t * NT + n]
    KB = 4
    NW = 5 if WCACHE else 9
    wsl = [P.sb("wsl%d" % j, [128, KB, 512] if WCACHE else [128, 512], TDT if WCACHE else F32) for j in range(NW)]
    NWB = 4
    wbl = [P.sb("wbl%d" % j, [128, 512], TDT) for j in range(NWB)] if (TRUNK_BF16 and not WCACHE) else None
    xin = [P.sb("xin%d" % j, [128, D]) for j in range(1)]

    hstg = [[P.sb("hst%d_%d" % (l, g), [128, 256]) for g in range(2)] for l in range(DEPTH)]
    hist = [P.sb("hist%d" % l, [128, 8, 3]) for l in range(DEPTH)]
    prev = [P.sb("prev%d" % l, [128, 14]) for l in range(DEPTH)]
    rstg = [[P.sb("rst%d_%d" % (l, g), [128, 2, 64]) for g in range(2)] for l in range(DEPTH)]
    gstg = [[P.sb("gst%d_%d" % (l, g), [128, 2, 128]) for g in range(2)] for l in range(DEPTH)]
    stmp = P.sb("stmp", [128, 4, 128])
    rtmp = P.sb("rtmp", [64, 4, 2, 64])

    tk = {}
    for nm, shp in [('wlc', [128, 4, 4]), ('slc', [128, 4, 8]), ('rs', [128, NT])]:
        tk[nm] = P.sb("tk_" + nm, shp)
    tk['rs2'] = tk['rs']
    tkg = []
    for g in range(2):
        T = {}
        for nm, shp in [('t1', [64, 4]), ('dt', [64, 4]), ('loga', [64, 4]), ('cum', [64, 8]), ('ecum', [64, 4]),
                        ('dec', [64, 4]), ('et128', [128, 4]), ('Btok', [64, 128]),
                        ('AkT', [64, 4, 64]), ('RbT', [64, 4, 64]), ('RkT', [64, 4, 64]),
                        ('Pa', [64, 4, 64]), ('PaT', [64, 4, 64]), ('Pb', [64, 4, 64]), ('PbT', [64, 4, 64]),
                        ('Xa', [64, 4, 64]), ('Xb', [64, 4, 64]), ('Vtok', [64, 4, 64]), ('Otok', [64, 4, 64]),
                        ('bbt', [64, 2, 128]), ('kbt', [64, 2, 128]), ('stm', [128, 2, 64]),
                        ('hMT', [32, 2, 32]), ('hvt', [32, 2, 128]), ('hkt', [32, 2, 128])]:
            T[nm] = P.sb("tk%d_%s" % (g, nm), shp)
        for a_, b_ in [('lgB', 'Pa'), ('Dm', 'PaT'), ('LT', 'Pb'), ('MT', 'PbT'), ('xdt', 'Xa'), ('xdec', 'Xb'),
                       ('ytok', 'Otok')]:
            T[a_] = T[b_]
        tkg.append(T)

    PSB = [P.psum("ps%d" % j, [128, 512]) for j in range(8)]
    psi = [0]

    def ps():
        t = PSB[psi[0] % 8]
        psi[0] += 1
        return t

    wi = [0]
    wbi = [0]

    def wslot():
        t = wsl[wi[0] % NW]
        wi[0] += 1
        return t

    def isap(x):
        return not isinstance(x, (int, float))

    def tt(eng, out, a, b, op):
        P.op(eng, lambda e: e.tensor_tensor(out=out, in0=a, in1=b, op=op), [out], [a, b])

    def ts(eng, out, a, s1, op0, s2=None, op1=None):
        eng = 'dve'
        ins = [a] + [s for s in (s1, s2) if s is not None and isap(s)]
        if op1 is None:
            P.op(eng, lambda e: e.tensor_scalar(out=out, in0=a, scalar1=s1, scalar2=None, op0=op0), [out], ins)
        else:
            P.op(eng, lambda e: e.tensor_scalar(out=out, in0=a, scalar1=s1, scalar2=s2, op0=op0, op1=op1), [out], ins)

    def stt(eng, out, a, s, b, op0, op1):
        eng = 'dve'
        ins = [a, b] + ([s] if isap(s) else [])
        P.op(eng, lambda e: e.scalar_tensor_tensor(out=out, in0=a, scalar=s, in1=b, op0=op0, op1=op1), [out], ins)

    def act(out, in_, func, bias=None, scale=1.0):
        ins = [in_] + ([bias] if bias is not None else []) + ([scale] if isap(scale) else [])
        if bias is None:
            P.op('act', lambda e: e.activation(out=out, in_=in_, func=func, scale=scale), [out], ins)
        else:
            P.op('act', lambda e: e.activation(out=out, in_=in_, func=func, bias=bias, scale=scale), [out], ins)

    def cp(eng, out, in_):
        if eng == 'act':
            P.op('act', lambda e: e.copy(out=out, in_=in_), [out], [in_])
        else:
            P.op(eng, lambda e: e.tensor_copy(out=out, in_=in_), [out], [in_])

    def recip(out, in_):
        P.op('dve', lambda e: e.reciprocal(out=out, in_=in_), [out], [in_])

    def mm(out, lhsT, rhs, start=True, stop=True):
        P.op('pe', lambda e: e.matmul(out, lhsT, rhs, start=start, stop=stop), [out], [lhsT, rhs])

    def tr(out, in_, n_in_part):
        idn = cc[0:n_in_part, CC['ident'][0]:CC['ident'][0] + n_in_part]
        P.op('pe', lambda e: e.transpose(out, in_, idn), [out], [in_, idn])

    def scan(out, d0, d1):
        P.op('dve', lambda e: e.tensor_tensor_scan(out=out, data0=d0, data1=d1, initial=0.0, op0=ALU.mult, op1=ALU.add),
             [out], [d0, d1])

    def memset(eng, out, val):
        P.op(eng, lambda e: e.memset(out, val), [out], [])

    def C(name, rows=128, c0=0, c1=None):
        o, n = CC[name]
        if c1 is None:
            c1 = n
        return cc[0:rows, o + c0:o + c1]

    def SC(i, rows=128):
        o = CC['sc'][0]
        return cc[0:rows, o + i:o + i + 1]

    def PPc(l, name, i=0, n=1, rows=128):
        o = PP_LAYOUT[name][0]
        return pp[l][0:rows, o + i:o + i + n]

    def bc(ap, shape, axis):
        return ap.unsqueeze(axis).broadcast_to(list(shape))

    eng_rr = [0]

    def ve():
        eng_rr[0] += 1
        return 'dve' if eng_rr[0] % 2 else 'pool'

    def rstd(out, in_, scale, eps_ap):
        act(out, in_, AF.Ln, bias=eps_ap, scale=scale)
        act(out, out, AF.Exp, scale=-0.5)

    def sigmoid(out, in_, scale=1.0, nbias=None):
        act(out, in_, AF.Exp, bias=nbias, scale=-scale)
        act(out, out, AF.Ln, bias=SC(1, out.shape[0]))
        act(out, out, AF.Exp, scale=-1.0)

    P.dma('sp', cc[:, :], cc_d[:, :])
    for l in range(DEPTH):
        P.dma('sp', pp[l][:, :], pp_d[l, :, :])
        P.dma('sp', wa2[l][:, :, :], wa2_d[l].rearrange("t p c -> p t c"))
        P.dma('sp', g2[l][:, :], g2_d[l, :, :])
        if 'a' not in KDBG:
            P.dma('sp', wdt32[l][:, :, :], w_in[l, :, 1536:1544].rearrange("(k p) c -> p k c", p=128),
                  allow_slow_non_contiguous=True)
            if TRUNK_BF16:
                P.op('dve', lambda e, l=l: e.tensor_copy(out=wdt[l][:, :, :], in_=wdt32[l][:, :, :]),
                     [wdt[l][:, :, :]], [wdt32[l][:, :, :]])
    for l in range(DEPTH if 'd' not in KDBG else 0):
        ts('dve', dv[l][:, 0:4], PPc(l, 'w0', 0, 4), -1.0, ALU.mult)
        ts('dve', dv[l][:, 4:8], PPc(l, 'a0', 0, 4), -1.0, ALU.mult)
        act(dv[l][:, 8:16], PPc(l, 'alog', 0, 8), AF.Exp)
        ts('dve', dv[l][:, 8:16], dv[l][:, 8:16], -1.0, ALU.mult)
        if l == 0:
            memset('dve', dv[l][:, 16:20], 0.0)
        else:
            tt('dve', dv[l][:, 16:20], PPc(l, 'lg0', 0, 4), PPc(l, 'lg1', 0, 4), ALU.subtract)
            act(dv[l][:, 16:20], dv[l][:, 16:20], AF.Exp)
            ts('dve', dv[l][:, 16:20], dv[l][:, 16:20], 1.0, ALU.add)
            recip(dv[l][:, 16:20], dv[l][:, 16:20])
        ts('dve', dv[l][:, 20:24], dv[l][:, 16:20], -1.0, ALU.mult, 1.0, ALU.add)

    def big_proj(wsrc, nk, c0, ntile, rhs_list, n, evac, width=None):
        banks = [ps() for _ in range(ntile)]
        wcols = ntile * 128 if width is None else width
        if WCACHE:
            wname, l_ = wsrc
            gi = WG[wname].index((c0, ntile))
            cache = wc[(wname, l_)]
            for k0 in range(0, nk, KB):
                kb = min(KB, nk - k0)
                wt = wslot()
                P.dma('sp', wt[:, 0:kb, 0:wcols], cache[gi, :, k0:k0 + kb, 0:wcols])
                for kk in range(kb):
                    k = k0 + kk
                    for j in range(ntile):
                        mm(banks[j][:, 0:n], wt[:, kk, j * 128:(j + 1) * 128], rhs_list[k], start=(k == 0),
                           stop=(k == nk - 1))
            for j in range(ntile):
                evac(j, banks[j][:, 0:n])
            return
        for k in range(nk):
            wt = wslot()
            P.dma('sp', wt[:, 0:wcols], wsrc[k * 128:(k + 1) * 128, c0:c0 + wcols])
            if TRUNK_BF16 and not WCACHE:
                wb = wbl[wbi[0] % NWB]
                wbi[0] += 1
                cp('act' if wbi[0] % 2 else 'dve', wb[:, 0:wcols], wt[:, 0:wcols])
                wt = wb
            for j in range(ntile):
                mm(banks[j][:, 0:n], wt[:, j * 128:(j + 1) * 128], rhs_list[k], start=(k == 0), stop=(k == nk - 1))
        for j in range(ntile):
            evac(j, banks[j][:, 0:n])

    def rmsnorm(l, pname, n, src, dst):
        pss = ps()
        for k in range(8):
            s = sc1[k % 2]
            tt(ve(), s[:, 0:n], src[k][:, 0:n], src[k][:, 0:n], ALU.mult)
            mm(pss[:, 0:n], C('ones'), s[:, 0:n], start=(k == 0), stop=(k == 7))
        rstd(tk['rs'][:, 0:n], pss[:, 0:n], 1.0 / D, SC(0))
        for k in range(8):
            stt(ve(), dst[k][:, 0:n], src[k][:, 0:n], PPc(l, pname, k), tk['rs'][:, 0:n], ALU.mult, ALU.mult)

    def layer(l, n, L, Lh):
        nch = n // L
        nchh = n // Lh
        W_in = ('in', l) if WCACHE else w_in[l]
        rmsnorm(l, 'n1', n, xT, hT)
        hl = [hT[k][:, 0:n] for k in range(8)]

        evc = [0]

        def ev_to(dst_fn):
            def f(j, psap):
                evc[0] += 1
                cp('act' if evc[0] % 2 else 'dve', dst_fn(j), psap)
            return f
        big_proj(W_in, 8, 0, 4, hl, n, ev_to(lambda j: zT[:, j, 0:n]))
        big_proj(W_in, 8, 512, 4, hl, n, ev_to(lambda j: xsT[:, j, 3:3 + n]))
        big_proj(W_in, 8, 1024, 4, hl, n, ev_to(lambda j: bcT[:, j, 3:3 + n]))
        ssd_pro(l, n, L, nch)
        big_proj(W_in, 8, 3336, 4, hl, n, ev_to(lambda j: qT[:, j, 0:n]))
        big_proj(W_in, 8, 3336 + 512, 4, hl, n, ev_to(lambda j: fzT[:, j, 0:n]))
        big_proj(W_in, 8, 3336 + 1024, 4, hl, n, ev_to(lambda j: ivT[:, j, 0:n]))
        big_proj(W_in, 8, 3336 + 1536, 4, hl, n, ev_to(lambda j: ggT[:, j, 0:n]))
        hgrn_pro(l, n, Lh, nchh)
        big_proj(W_in, 8, 1544, 4, hl, n, ev_to(lambda j: rT[:, j, 1:1 + n]))
        big_proj(W_in, 8, 1544 + 512, 4, hl, n, ev_to(lambda j: kT[:, j, 1:1 + n]))
        big_proj(W_in, 8, 1544 + 1024, 4, hl, n, ev_to(lambda j: vT[:, j, 1:1 + n]))
        big_proj(W_in, 8, 1544 + 1536, 2, hl, n, ev_to(lambda j: (x12 if j == 0 else x13)[:, 1:1 + n]))

        if 'proj' not in STAGES:
            return
        interleave(ssd_chain(l, n, L, nch, 0), hgrn_chain(l, n, Lh, nchh, 0),
                   ssd_chain(l, n, L, nch, 1), hgrn_chain(l, n, Lh, nchh, 1))
        ssd_epi(l, n, L, nch)
        hgrn_epi(l, n, Lh, nchh)
        rwkv_pro(l, n, L, nch)
        interleave(rwkv_chain(l, n, L, nch, 0), rwkv_chain(l, n, L, nch, 1))
        rwkv_epi(l, n, L, nch)
        if 'oproj' not in STAGES:
            return

        yl = [Y[j][:, 0:n] for j in range(12)]
        for g in range(2):
            big_proj(('out', l) if WCACHE else w_out[l], 12, g * 512, 4, yl, n,
                     lambda j, psap, g=g: tt('dve', xT[4 * g + j][:, 0:n], xT[4 * g + j][:, 0:n], psap, ALU.add))

        if 'ffn' not in STAGES:
            return
        rmsnorm(l, 'n2', n, xT, hT)
        hl = [hT[k][:, 0:n] for k in range(8)]
        f0 = 0
        while f0 < 22:
            nt_ = min(4, 22 - f0)
            gb = {}

            def ev_gate(j, psap, gb=gb):
                gb[j] = psap
            big_proj(('gate', l) if WCACHE else w_gate[l], 8, f0 * 128, nt_, hl, n, ev_gate)

            def ev_up(j, psap, gb=gb, f0=f0):
                f = f0 + j
                a_ = a16(f, n) if TRUNK_BF16 else SL[f // 4][:, f % 4, 0:n]
                s = sc1[j % 2]
                sigmoid(s[:, 0:n], gb[j])
                tt('dve', s[:, 0:n], s[:, 0:n], gb[j], ALU.mult)
                tt('dve', a_, s[:, 0:n], psap, ALU.mult)
            big_proj(('up', l) if WCACHE else w_up[l], 8, f0 * 128, nt_, hl, n, ev_up)
            f0 += nt_
        al = [(a16(f, n) if TRUNK_BF16 else SL[f // 4][:, f % 4, 0:n]) for f in range(22)]
        for g in range(2):
            big_proj(('down', l) if WCACHE else w_down[l], 22, g * 512, 4, al, n,
                     lambda j, psap, g=g: tt('dve', xT[4 * g + j][:, 0:n], xT[4 * g + j][:, 0:n], psap, ALU.add))

    def ssd_pro(l, n, L, nch):
        cp('pool', xsT[:, :, 0:3], hist[l][:, 0:4, :])
        cp('pool', bcT[:, :, 0:3], hist[l][:, 4:8, :])
        cwo = PP_LAYOUT['cw'][0]
        for half, (src, ACC, SG) in enumerate(((xsT, SL[10], SL[8]), (bcT, SL[11], SL[9]))):
            for ii in range(4):
                i = half * 4 + ii
                ts('dve', ACC[:, ii, 0:n], src[:, ii, 0:n], pp[l][:, cwo + i * 4:cwo + i * 4 + 1], ALU.mult,
                   PPc(l, 'cb', i), ALU.add)
                for j in range(1, 4):
                    stt('dve', ACC[:, ii, 0:n], src[:, ii, j:j + n], pp[l][:, cwo + i * 4 + j:cwo + i * 4 + j + 1],
                        ACC[:, ii, 0:n], ALU.mult, ALU.add)
            cp('pool', hist[l][:, half * 4:half * 4 + 4, :], src[:, :, n:n + 3])
            sigmoid(SG[:, :, 0:n], ACC[:, :, 0:n])
            tt('pool' if half else 'dve', src[:, :, 3:3 + n], ACC[:, :, 0:n], SG[:, :, 0:n], ALU.mult)
        sigmoid(SL[10][:, :, 0:n], zT[:, :, 0:n])
        tt('dve', zT[:, :, 0:n], zT[:, :, 0:n], SL[10][:, :, 0:n], ALU.mult)

    def ssd_chain(l, n, L, nch, g):
        T = tkg[g]
        yss = SL[8 + g]
        hs4 = slice(4 * g, 4 * g + 4)
        for c in range(nch):
            cs = slice(c * L, (c + 1) * L)
            cs3 = slice(3 + c * L, 3 + (c + 1) * L)
            pdt = ps()
            for k in range(8):
                mm(pdt[0:L, 0:4], hT[k][:, cs], wdt[l][:, k, hs4], start=(k == 0), stop=(k == 7))
            tt('dve', T['t1'][0:L, :], pdt[0:L, 0:4], PPc(l, 'dtb', 4 * g, 4, rows=L), ALU.add)
            act(T['t1'][0:L, :], T['t1'][0:L, :], AF.Exp)
            act(T['dt'][0:L, :], T['t1'][0:L, :], AF.Ln, bias=SC(1, L))
            tt('dve', T['loga'][0:L, :], T['dt'][0:L, :], dv[l][0:L, 8 + 4 * g:12 + 4 * g], ALU.mult)
            yield
            pc = ps()
            mm(pc[0:L, 0:4], C('u1', L, 0, L), T['loga'][0:L, :])
            mm(pc[0:L, 4:8], C('ones', L, 0, L), T['loga'][0:L, :])
            mm(pc[:, 8:12], C('ones', L, 0, 128), T['loga'][0:L, :])
            cp('pool', T['lgB'][0:L, :, 0:L], bc(T['loga'][0:L, :], [L, 4, L], 2))
            cp('dve', T['cum'][0:L, :], pc[0:L, 0:8])
            act(T['et128'][:, :], pc[:, 8:12], AF.Exp)
            act(T['ecum'][0:L, :], T['cum'][0:L, 0:4], AF.Exp)
            tt('dve', T['dec'][0:L, :], T['cum'][0:L, 4:8], T['cum'][0:L, 0:4], ALU.subtract)
            act(T['dec'][0:L, :], T['dec'][0:L, :], AF.Exp)
            yield
            pcb = ps()
            for h in range(4):
                mm(pcb[0:L, h * L:(h + 1) * L], T['lgB'][0:L, h, 0:L], C('u1', L, 0, L))
            pg = ps()
            mm(pg[0:L, 0:L], bcT[:, g, cs3], bcT[:, 2 + g, cs3])
            px = ps()
            for i in range(2):
                tr(px[0:L, i * 128:(i + 1) * 128], xsT[:, 2 * g + i, cs3], 128)
            pb = ps()
            tr(pb[0:L, 0:128], bcT[:, g, cs3], 128)
            pcb3 = pcb[0:L, 0:4 * L].rearrange("p (h i) -> p h i", h=4)
            tt('dve', T['Dm'][0:L, :, 0:L], pcb3, bc(C('mneg', L, 0, L), [L, 4, L], 1), ALU.add)
            tt('pool', T['Dm'][0:L, :, 0:L], T['Dm'][0:L, :, 0:L], bc(T['cum'][0:L, 0:4], [L, 4, L], 2), ALU.subtract)
            act(T['LT'][0:L, :, 0:L], T['Dm'][0:L, :, 0:L], AF.Exp)
            px3 = px[0:L, 0:256].rearrange("p (h d) -> p h d", h=4)
            tt('dve', T['xdt'][0:L, :, :], px3, bc(T['dt'][0:L, :], [L, 4, 64], 2), ALU.mult)
            tt('pool', T['xdec'][0:L, :, :], T['xdt'][0:L, :, :], bc(T['dec'][0:L, :], [L, 4, 64], 2), ALU.mult)
            cp('act', T['Btok'][0:L, :], pb[0:L, 0:128])
            tt('dve', T['MT'][0:L, :, 0:L], T['LT'][0:L, :, 0:L], bc(pg[0:L, 0:L], [L, 4, L], 1), ALU.mult)
            yield
            py = ps()
            for h in range(4):
                mm(py[0:L, h * 64:(h + 1) * 64], T['MT'][0:L, h, 0:L], T['xdt'][0:L, h, :])
            pyi = ps()
            mm(pyi[0:L, 0:256], bcT[:, 2 + g, cs3], hstg[l][g][:, :])
            ph = ps()
            mm(ph[:, 0:256], T['Btok'][0:L, :], T['xdec'][0:L, :, :].rearrange("p h d -> p (h d)"))
            pyi3 = pyi[0:L, 0:256].rearrange("p (h d) -> p h d", h=4)
            py3 = py[0:L, 0:256].rearrange("p (h d) -> p h d", h=4)
            tt('dve', T['ytok'][0:L, :, :], pyi3, bc(T['ecum'][0:L, :], [L, 4, 64], 2), ALU.mult)
            tt('dve', T['ytok'][0:L, :, :], T['ytok'][0:L, :, :], py3, ALU.add)
            h3 = hstg[l][g][:, :].rearrange("p (h d) -> p h d", h=4)
            tt('dve', h3, h3, bc(T['et128'][:, :], [128, 4, 64], 2), ALU.mult)
            tt('dve', hstg[l][g][:, :], hstg[l][g][:, :], ph[:, 0:256], ALU.add)
            yield
            pyt = ps()
            for i in range(2):
                tr(pyt[:, i * L:(i + 1) * L], T['ytok'][0:L, 2 * i:2 * i + 2, :].rearrange("p h d -> p (h d)"), L)
            for i in range(2):
                stt('dve', yss[:, i, cs], xsT[:, 2 * g + i, cs3], PPc(l, 'dsk', 2 * g + i), pyt[:, i * L:(i + 1) * L],
                    ALU.mult, ALU.add)
            yield

    def ssd_epi(l, n, L, nch):
        for g in range(2):
            yss = SL[8 + g]
            for t_ in range(2):
                tt(ve(), yss[:, t_, 0:n], yss[:, t_, 0:n], zT[:, 2 * g + t_, 0:n], ALU.mult)
            pss = ps()
            for t_ in range(2):
                s = sc1[t_]
                tt(ve(), s[:, 0:n], yss[:, t_, 0:n], yss[:, t_, 0:n], ALU.mult)
                mm(pss[:, 0:n], C('ones'), s[:, 0:n], start=(t_ == 0), stop=(t_ == 1))
            rstd(tk['rs2'][:, 0:n], pss[:, 0:n], 1.0 / 256, SC(0))
            for t_ in range(2):
                i = 2 * g + t_
                stt(ve(), Y[i][:, 0:n], yss[:, t_, 0:n], PPc(l, 'snw', i), tk['rs2'][:, 0:n], ALU.mult, ALU.mult)

    def rwkv_pro(l, n, L, nch):
        c1 = slice(1, 1 + n)
        muo = PP_LAYOUT['mu'][0]
        tl = [(rT, 0), (rT, 1), (rT, 2), (rT, 3), (kT, 0), (kT, 1), (kT, 2), (kT, 3), (vT, 0), (vT, 1), (vT, 2), (vT, 3)]
        for g, t3 in enumerate((rT, kT, vT)):
            cp('pool', t3[:, :, 0:1], prev[l][:, 4 * g:4 * g + 4].unsqueeze(2))
        cp('pool', x12[:, 0:1], prev[l][:, 12:13])
        cp('pool', x13[:, 0:1], prev[l][:, 13:14])

        def shift(cur, prv, lastcol, prevdst, mucol):
            d = sc1[mucol % 2]
            e1 = ve()
            tt(e1, d[:, 0:n], prv, cur, ALU.subtract)
            cp('pool', prevdst, lastcol)
            stt(e1, cur, d[:, 0:n], pp[l][:, muo + mucol:muo + mucol + 1], cur, ALU.mult, ALU.add)
        for idx, (t3, i) in enumerate(tl):
            shift(t3[:, i, 1:1 + n], t3[:, i, 0:n], t3[:, i, n:n + 1], prev[l][:, idx:idx + 1], idx)
        shift(x12[:, 1:1 + n], x12[:, 0:n], x12[:, n:n + 1], prev[l][:, 12:13], 12)
        shift(x13[:, 1:1 + n], x13[:, 0:n], x13[:, n:n + 1], prev[l][:, 13:14], 13)
        sigmoid(x12[0:64, c1], x12[0:64, c1], scale=2.0)
        ts('dve', x12[0:64, c1], x12[0:64, c1], 2.0, ALU.mult, -1.0, ALU.add)
        sigmoid(x13[:, c1], x13[:, c1])

        LW, AS, G, CW, KKN, KM, BON, BV, RT_, AT_, BT_, KT_ = SL
        S1, S2 = RT_, AT_
        A4 = slice(0, n)
        for m in range(4):
            p1 = ps()
            mm(p1[:, 0:n], wa2[l][:, 0, m * 128:(m + 1) * 128], x12[:, c1])
            act(LW[:, m, A4], p1[:, 0:n], AF.Exp, bias=dv[l][:, m:m + 1], scale=-1.0)
            p2 = ps()
            mm(p2[:, 0:n], wa2[l][:, 1, m * 128:(m + 1) * 128], x12[:, c1])
            act(AS[:, m, A4], p2[:, 0:n], AF.Exp, bias=dv[l][:, 4 + m:5 + m], scale=-1.0)
            p3 = ps()
            mm(p3[:, 0:n], g2[l][:, m * 128:(m + 1) * 128], x13[:, c1])
            cp('act', G[:, m, A4], p3[:, 0:n])
            ts('dve', KKN[:, m, A4], kT[:, m, c1], PPc(l, 'kk', m), ALU.mult)
        act(LW[:, :, A4], LW[:, :, A4], AF.Ln, bias=SC(1))
        act(LW[:, :, A4], LW[:, :, A4], AF.Exp, scale=-1.0)
        act(AS[:, :, A4], AS[:, :, A4], AF.Ln, bias=SC(1))
        act(AS[:, :, A4], AS[:, :, A4], AF.Exp, scale=-1.0)
        ts('dve', LW[:, :, A4], LW[:, :, A4], -0.6065306597126334, ALU.mult)
        for m in range(4):
            scan(CW[:, m, A4], C('rm64', 128, 0, n), LW[:, m, A4])
        tt('pool', S1[:, :, A4], KKN[:, :, A4], KKN[:, :, A4], ALU.mult)
        for m in range(4):
            p4 = ps()
            mm(p4[:, 0:n], C('bones'), S1[:, m, A4])
            act(S2[:, m, A4], p4[:, 0:n], AF.Ln, bias=SC(3))
        act(S2[:, :, A4], S2[:, :, A4], AF.Exp, scale=-0.5)
        tt('dve', KKN[:, :, A4], KKN[:, :, A4], S2[:, :, A4], ALU.mult)
        for m in range(4):
            ts('dve', KM[:, m, A4], AS[:, m, A4], PPc(l, 'ka', m), ALU.mult, PPc(l, 'ka', m), ALU.subtract)
        stt('dve', KM[:, :, A4], KM[:, :, A4], 1.0, kT[:, :, c1], ALU.add, ALU.mult)
        for m in range(4):
            stt('dve', S1[:, m, A4], rT[:, m, c1], PPc(l, 'rk', m), KM[:, m, A4], ALU.mult, ALU.mult)
            p5 = ps()
            mm(p5[:, 0:n], C('bones'), S1[:, m, A4])
            tt('dve', BON[:, m, A4], p5[:, 0:n], vT[:, m, c1], ALU.mult)
        tt('pool', BV[:, :, A4], KKN[:, :, A4], AS[:, :, A4], ALU.mult)
        Wt = AS
        act(Wt[:, :, 0:n], CW[:, :, 0:n], AF.Exp)
        tt(ve(), RT_[:, :, 0:n], rT[:, :, c1], Wt[:, :, 0:n], ALU.mult)
        cp('pool', tk['wlc'][:, :, 0:nch], Wt[:, :, L - 1:n:L])
        tt(ve(), AT_[:, :, 0:n], CW[:, :, 0:n], LW[:, :, 0:n], ALU.subtract)
        act(AT_[:, :, 0:n], AT_[:, :, 0:n], AF.Exp)
        stt(ve(), AT_[:, :, 0:n], KKN[:, :, 0:n], -1.0, AT_[:, :, 0:n], ALU.mult, ALU.mult)
        En = LW
        act(En[:, :, 0:n], CW[:, :, 0:n], AF.Exp, scale=-1.0)
        tt(ve(), BT_[:, :, 0:n], BV[:, :, 0:n], En[:, :, 0:n], ALU.mult)
        tt(ve(), KT_[:, :, 0:n], KM[:, :, 0:n], En[:, :, 0:n], ALU.mult)
        EB = LW
        for m in range(4):
            cw3 = CW[:, m, 0:n].rearrange("p (c l) -> p c l", l=L)
            eb3 = EB[:, m, 0:n].rearrange("p (c l) -> p c l", l=L)
            tt(ve(), eb3, bc(CW[:, m, L - 1:n:L], [128, nch, L], 2), cw3, ALU.subtract)
        act(EB[:, :, 0:n], EB[:, :, 0:n], AF.Exp)
        BB, KB = KKN, AS
        tt(ve(), BB[:, :, 0:n], BV[:, :, 0:n], EB[:, :, 0:n], ALU.mult)
        tt(ve(), KB[:, :, 0:n], KM[:, :, 0:n], EB[:, :, 0:n], ALU.mult)
        ATo, RTo = KM, BV
        cp(ve(), ATo[:, :, 0:n], AT_[:, :, 0:n])
        cp(ve(), RTo[:, :, 0:n], RT_[:, :, 0:n])
        memset('pool', ATo[0:64, :, 0:n], 0.0)
        memset('pool', RTo[0:64, :, 0:n], 0.0)
        memset('pool', AT_[64:128, :, 0:n], 0.0)
        memset('pool', RT_[64:128, :, 0:n], 0.0)

    def rwkv_chain(l, n, L, nch, g):
        LW, AS, G, CW, KKN, KM, BON, BV, RT_, AT_, BT_, KT_ = SL
        BB, KB = KKN, AS
        ATm, RTm = (AT_, KM), (RT_, BV)
        OT = (CW, LW)[g]
        T = tkg[g]
        ST = rstg[l][g]
        nst = int(np.log2(L))

        def v3(p_):
            return p_[0:L, 0:4 * L].rearrange("p (h i) -> p h i", h=4)

        def x3(p_):
            return p_[0:L, 0:256].rearrange("p (h d) -> p h d", h=4)
        u0b = bc(C('u0', L, 0, L), [L, 4, L], 1)
        u1b = bc(C('u1', L, 0, L), [L, 4, L], 1)
        l0b = bc(C('l0', L, 0, L), [L, 4, L], 1)
        for c in range(nch):
            cs = slice(c * L, (c + 1) * L)
            cs1 = slice(1 + c * L, 1 + (c + 1) * L)
            pN, pNT, pAk, pRb, pRk = ps(), ps(), ps(), ps(), ps()
            for hh in range(4):
                h = 4 * g + hh
                q = h // 2
                hsl = slice(hh * L, (hh + 1) * L)
                am, rm_ = ATm[h % 2], RTm[h % 2]
                mm(pN[0:L, hsl], am[:, q, cs], BT_[:, q, cs])
                mm(pNT[0:L, hsl], BT_[:, q, cs], am[:, q, cs])
                mm(pAk[0:L, hsl], KT_[:, q, cs], am[:, q, cs])
                mm(pRb[0:L, hsl], BT_[:, q, cs], rm_[:, q, cs])
                mm(pRk[0:L, hsl], KT_[:, q, cs], rm_[:, q, cs])
            pv = ps()
            for i in range(2):
                tr(pv[0:L, i * 128:(i + 1) * 128], vT[:, 2 * g + i, cs1], 128)
            pbb, pkb = ps(), ps()
            for i in range(2):
                tr(pbb[0:L, i * 128:(i + 1) * 128], BB[:, 2 * g + i, cs], 128)
                tr(pkb[0:L, i * 128:(i + 1) * 128], KB[:, 2 * g + i, cs], 128)
            Pa, PaT = T['Pa'], T['PaT']
            tt('dve', Pa[0:L, :, 0:L], v3(pN), l0b, ALU.mult)
            tt('dve', PaT[0:L, :, 0:L], v3(pNT), u0b, ALU.mult)
            tt('dve', T['AkT'][0:L, :, 0:L], v3(pAk), u0b, ALU.mult)
            cp('act', T['Vtok'][0:L, :, :], x3(pv))
            tt('dve', T['RbT'][0:L, :, 0:L], v3(pRb), u1b, ALU.mult)
            tt('dve', T['RkT'][0:L, :, 0:L], v3(pRk), u1b, ALU.mult)
            cp('act', T['bbt'][0:L, :, :], pbb[0:L, 0:256].rearrange("p (m d) -> p m d", m=2))
            cp('act', T['kbt'][0:L, :, :], pkb[0:L, 0:256].rearrange("p (m d) -> p m d", m=2))
            yield
            pX = ps()
            for hh in range(4):
                h = 4 * g + hh
                mm(pX[0:L, hh * 64:(hh + 1) * 64], ATm[h % 2][:, h // 2, cs], ST[:, hh // 2, :], start=True, stop=False)
                mm(pX[0:L, hh * 64:(hh + 1) * 64], T['AkT'][0:L, hh, 0:L], T['Vtok'][0:L, hh, :], start=False, stop=True)
            Xc, Xn = T['Xa'], T['Xb']
            cp('act', Xc[0:L, :, :], x3(pX))
            yield
            Pc, PcT, Pn, PnT = Pa, PaT, T['Pb'], T['PbT']
            for st_ in range(nst):
                pU = ps()
                for hh in range(4):
                    mm(pU[0:L, hh * 64:(hh + 1) * 64], PcT[0:L, hh, 0:L], Xc[0:L, hh, :])
                if st_ < nst - 1:
                    pS, pST = ps(), ps()
                    for hh in range(4):
                        hsl = slice(hh * L, (hh + 1) * L)
                        mm(pS[0:L, hsl], PcT[0:L, hh, 0:L], Pc[0:L, hh, 0:L])
                        mm(pST[0:L, hsl], Pc[0:L, hh, 0:L], PcT[0:L, hh, 0:L])
                tt('dve', Xn[0:L, :, :], Xc[0:L, :, :], x3(pU), ALU.add)
                Xc, Xn = Xn, Xc
                if st_ < nst - 1:
                    cp('act', Pn[0:L, :, 0:L], v3(pS))
                    cp('dve', PnT[0:L, :, 0:L], v3(pST))
                    Pc, PcT, Pn, PnT = Pn, PnT, Pc, PcT
                yield
            SA = Xc
            pO = ps()
            for hh in range(4):
                h = 4 * g + hh
                o_ = pO[0:L, hh * 64:(hh + 1) * 64]
                mm(o_, RTm[h % 2][:, h // 2, cs], ST[:, hh // 2, :], start=True, stop=False)
                mm(o_, T['RbT'][0:L, hh, 0:L], SA[0:L, hh, :], start=False, stop=False)
                mm(o_, T['RkT'][0:L, hh, 0:L], T['Vtok'][0:L, hh, :], start=False, stop=True)
            pSt = ps()
            for i in range(2):
                mm(pSt[:, i * 128:(i + 1) * 128], T['bbt'][0:L, i, :],
                   SA[0:L, 2 * i:2 * i + 2, :].rearrange("p h d -> p (h d)"), start=True, stop=False)
                mm(pSt[:, i * 128:(i + 1) * 128], T['kbt'][0:L, i, :],
                   T['Vtok'][0:L, 2 * i:2 * i + 2, :].rearrange("p h d -> p (h d)"), start=False, stop=True)
            cp('act', T['Otok'][0:L, :, :], x3(pO))
            pSt3 = pSt[:, 0:256].rearrange("p (q d) -> p q d", q=2)
            for hh2 in range(2):
                prr = slice(hh2 * 64, hh2 * 64 + 64)
                tt('dve', T['stm'][prr, :, :], ST[prr, :, :], bc(tk['wlc'][prr, 2 * g:2 * g + 2, c], [64, 2, 64], 2),
                   ALU.mult)
                tt('dve', ST[prr, :, :], T['stm'][prr, :, :], pSt3[prr, :, hh2 * 64:(hh2 + 1) * 64], ALU.add)
            yield
            pot = ps()
            for i in range(2):
                tr(pot[:, i * L:(i + 1) * L], T['Otok'][0:L, 2 * i:2 * i + 2, :].rearrange("p h d -> p (h d)"), L)
            cp('act', OT[:, 2 * g:2 * g + 2, cs], pot[:, 0:2 * L].rearrange("p (q i) -> p q i", q=2))
            yield

    def rwkv_epi(l, n, L, nch):
        LW, AS, G, CW, KKN, KM, BON, BV, RT_, AT_, BT_, KT_ = SL
        CEN, SQ = BT_, KT_
        A4 = slice(0, n)
        for m in range(4):
            OT = (CW, LW)[m // 2]
            pm = ps()
            mm(pm[:, 0:n], C('bones'), OT[:, m, A4])
            stt('dve', CEN[:, m, A4], pm[:, 0:n], -1.0 / 64, OT[:, m, A4], ALU.mult, ALU.add)
        tt('pool', SQ[:, :, A4], CEN[:, :, A4], CEN[:, :, A4], ALU.mult)
        for m in range(4):
            pvv = ps()
            mm(pvv[:, 0:n], C('bones'), SQ[:, m, A4])
            act(SQ[:, m, A4], pvv[:, 0:n], AF.Ln, bias=SC(2), scale=1.0 / 64)
        act(SQ[:, :, A4], SQ[:, :, A4], AF.Exp, scale=-0.5)
        tt('dve', CEN[:, :, A4], CEN[:, :, A4], SQ[:, :, A4], ALU.mult)
        for m in range(4):
            ts('dve', CEN[:, m, A4], CEN[:, m, A4], PPc(l, 'lnw', m), ALU.mult, PPc(l, 'lnb', m), ALU.add)
        tt('pool', CEN[:, :, A4], CEN[:, :, A4], BON[:, :, A4], ALU.add)
        for m in range(4):
            tt(ve(), Y[4 + m][:, 0:n], CEN[:, m, A4], G[:, m, A4], ALU.mult)

    def hgrn_pro(l, n, Lh, nchh):
        E_, L1, KG, CU, TM, QT, KT2, QH = SL[0:8]
        mid = Lh // 2
        rmn = 'rm32' if Lh == 32 else 'rm64'
        for m in range(4):
            ts(ve(), fzT[:, m, 0:n], fzT[:, m, 0:n], -60.0, ALU.max)
        act(E_[:, :, 0:n], fzT[:, :, 0:n], AF.Exp, scale=-1.0)
        for m in range(4):
            act(L1[:, m, 0:n], E_[:, m, 0:n], AF.Ln, bias=SC(1), scale=dv[l][:, 16 + m:17 + m])
        act(TM[:, :, 0:n], E_[:, :, 0:n], AF.Ln, bias=SC(1))
        tt(ve(), L1[:, :, 0:n], L1[:, :, 0:n], TM[:, :, 0:n], ALU.subtract)
        act(TM[:, :, 0:n], TM[:, :, 0:n], AF.Exp, scale=-1.0)
        for m in range(4):
            stt(ve(), KG[:, m, 0:n], E_[:, m, 0:n], dv[l][:, 20 + m:21 + m], TM[:, m, 0:n], ALU.mult, ALU.mult)
            scan(CU[:, m, 0:n], C(rmn, 128, 0, n), L1[:, m, 0:n])
        for m in range(4):
            cu3 = CU[:, m, 0:n].rearrange("p (c l) -> p c l", l=Lh)
            tm3 = TM[:, m, 0:n].rearrange("p (c l) -> p c l", l=Lh)
            tt(ve(), tm3, cu3, bc(CU[:, m, mid:n:Lh], [128, nchh, Lh], 2), ALU.subtract)
        ts('dve', TM[:, :, 0:n], TM[:, :, 0:n], 38.0, ALU.min, -38.0, ALU.max)
        act(QT[:, :, 0:n], TM[:, :, 0:n], AF.Exp)
        act(KT2[:, :, 0:n], TM[:, :, 0:n], AF.Exp, scale=-1.0)
        tt(ve(), QT[:, :, 0:n], QT[:, :, 0:n], qT[:, :, 0:n], ALU.mult)
        tt(ve(), KT2[:, :, 0:n], KT2[:, :, 0:n], KG[:, :, 0:n], ALU.mult)
        act(QH[:, :, 0:n], CU[:, :, 0:n], AF.Exp)
        cp('pool', tk['slc'][:, :, 0:nchh], QH[:, :, Lh - 1:n:Lh])
        tt(ve(), QH[:, :, 0:n], QH[:, :, 0:n], qT[:, :, 0:n], ALU.mult)
        KH = E_
        for m in range(4):
            cu3 = CU[:, m, 0:n].rearrange("p (c l) -> p c l", l=Lh)
            kh3 = KH[:, m, 0:n].rearrange("p (c l) -> p c l", l=Lh)
            tt(ve(), kh3, bc(CU[:, m, Lh - 1:n:Lh], [128, nchh, Lh], 2), cu3, ALU.subtract)
        act(KH[:, :, 0:n], KH[:, :, 0:n], AF.Exp)
        tt(ve(), KH[:, :, 0:n], KH[:, :, 0:n], KG[:, :, 0:n], ALU.mult)

    def hgrn_chain(l, n, Lh, nchh, g):
        E_, L1, KG, CU, TM, QT, KT2, QH = SL[0:8]
        KH = E_
        OT = (L1, TM)[g]
        T = tkg[g]
        S = gstg[l][g]
        for c in range(nchh):
            cs = slice(c * Lh, (c + 1) * Lh)
            pA = ps()
            for hh in range(2):
                h = 2 * g + hh
                mm(pA[0:Lh, hh * Lh:(hh + 1) * Lh], KT2[:, h, cs], QT[:, h, cs])
            pv, pk = ps(), ps()
            for hh in range(2):
                h = 2 * g + hh
                tr(pv[0:Lh, hh * 128:(hh + 1) * 128], ivT[:, h, cs], 128)
                tr(pk[0:Lh, hh * 128:(hh + 1) * 128], KH[:, h, cs], 128)
            tt('dve', T['hMT'][0:Lh, :, 0:Lh], pA[0:Lh, 0:2 * Lh].rearrange("p (h i) -> p h i", h=2),
               bc(C('u1', Lh, 0, Lh), [Lh, 2, Lh], 1), ALU.mult)
            cp('act', T['hvt'][0:Lh, :, :], pv[0:Lh, 0:256].rearrange("p (h d) -> p h d", h=2))
            cp('dve', T['hkt'][0:Lh, :, :], pk[0:Lh, 0:256].rearrange("p (h d) -> p h d", h=2))
            yield
            po = ps()
            for hh in range(2):
                h = 2 * g + hh
                o_ = po[:, hh * Lh:(hh + 1) * Lh]
                mm(o_, T['hvt'][0:Lh, hh, :], T['hMT'][0:Lh, hh, 0:Lh], start=True, stop=False)
                mm(o_, S[:, hh, :], QH[:, h, cs], start=False, stop=True)
            pS = ps()
            for hh in range(2):
                mm(pS[:, hh * 128:(hh + 1) * 128], T['hkt'][0:Lh, hh, :], T['hvt'][0:Lh, hh, :])
            cp('act', OT[:, 2 * g:2 * g + 2, cs], po[:, 0:2 * Lh].rearrange("p (h i) -> p h i", h=2))
            for hh in range(2):
                h = 2 * g + hh
                stt('dve', S[:, hh, :], S[:, hh, :], tk['slc'][:, h, c:c + 1], pS[:, hh * 128:(hh + 1) * 128],
                    ALU.mult, ALU.add)
            yield

    def hgrn_epi(l, n, Lh, nchh):
        E_, L1, KG, CU, TM, QT, KT2, QH = SL[0:8]
        SQ, SG = KG, CU
        OTs = (L1, TM)
        for g in range(2):
            tt(ve(), SQ[:, 2 * g:2 * g + 2, 0:n], OTs[g][:, 2 * g:2 * g + 2, 0:n], OTs[g][:, 2 * g:2 * g + 2, 0:n],
               ALU.mult)
        for h in range(4):
            pss = ps()
            mm(pss[:, 0:n], C('ones'), SQ[:, h, 0:n])
            act(SQ[:, h, 0:n], pss[:, 0:n], AF.Ln, bias=SC(0), scale=1.0 / 128)
        act(SQ[:, :, 0:n], SQ[:, :, 0:n], AF.Exp, scale=-0.5)
        sigmoid(SG[:, :, 0:n], ggT[:, :, 0:n])
        tt('pool', SG[:, :, 0:n], SG[:, :, 0:n], ggT[:, :, 0:n], ALU.mult)
        tt('dve', SG[:, :, 0:n], SG[:, :, 0:n], SQ[:, :, 0:n], ALU.mult)
        for h in range(4):
            stt('dve', Y[8 + h][:, 0:n], OTs[h // 2][:, h, 0:n], PPc(l, 'hnw', h), SG[:, h, 0:n], ALU.mult, ALU.mult)

    def interleave(*gens):
        gens = list(gens)
        while gens:
            for g_ in list(gens):
                try:
                    next(g_)
                except StopIteration:
                    gens.remove(g_)


    def run_tile(src, n, L, Lh, dst):
        nb = (n + 127) // 128
        for b in range(nb):
            tb = min(128, n - b * 128)
            xi = xin[0]
            P.dma(DQ, xi[0:tb, :], src[b * 128:b * 128 + tb, :])
            for g in range(2 if 'i' not in KDBG else 0):
                pt = ps()
                for k in range(4):
                    tr(pt[:, k * 128:k * 128 + tb], xi[0:tb, (4 * g + k) * 128:(4 * g + k + 1) * 128], tb)
                for k in range(4):
                    cp('act' if (k % 2 and 'j' not in KDBG) else 'dve', xT[4 * g + k][:, b * 128:b * 128 + tb], pt[:, k * 128:k * 128 + tb])
        for l in range(DEPTH):
            layer(l, n, L, Lh)
        if dst is None or 'g' in KDBG:
            return
        hF = [SL[k // 4][:, k % 4, :] for k in range(8)]
        if 'h' not in KDBG:
            rmsnorm(DEPTH - 1, 'fin', n, xT, hF)
        for b in range(nb):
            tb = min(128, n - b * 128)
            xo = xin[0]
            for g in range(2):
                pt = ps()
                for k in range(4):
                    tr(pt[0:tb, k * 128:(k + 1) * 128], hF[4 * g + k][:, b * 128:b * 128 + tb], 128)
                cp('act' if g else 'dve', xo[0:tb, g * 512:(g + 1) * 512], pt[0:tb, :])
            P.dma(DQ, dst[b * 128:b * 128 + tb, :], xo[0:tb, :])

    def store_states(si):
        for l in range(DEPTH):
            for i in range(4):
                pt = ps()
                tr(pt[:, 0:128], hstg[l][i // 2][:, (i % 2) * 128:(i % 2 + 1) * 128], 128)
                cp('act', stmp[:, i, :], pt[:, 0:128])
            for hh in range(2):
                P.dma(DQ, o_ssm[si, l].rearrange("(i hh) p n -> hh p i n", hh=2)[hh],
                      stmp[hh * 64:(hh + 1) * 64, :, :])
            for i in range(8):
                P.dma(DQ, o_conv[si, l][:, i * 128:(i + 1) * 128].rearrange("j p -> p j"), hist[l][:, i, :],
                      allow_slow_non_contiguous=True)
            P.dma(DQ, o_shift[si, l].rearrange("(i p) -> p i", p=128), prev[l][:, :],
                  allow_slow_non_contiguous=True)
            for q in range(4):
                pt = ps()
                tr(pt[0:64, 0:128], rstg[l][q // 2][:, q % 2, :], 128)
                cp('act', rtmp[:, q, :, :], pt[0:64, 0:128].rearrange("p (hh n) -> p hh n", hh=2))
            P.dma(DQ, o_rwkv[si, l].rearrange("(q hh) v n -> v q hh n", hh=2), rtmp[:, :, :, :])
            for g in range(2):
                P.dma(DQ, o_hgrn[si, l, 2 * g:2 * g + 2].rearrange("h k v -> k h v"), gstg[l][g][:, :, :])

    def load_states():
        for l in range(DEPTH):
            for hh in range(2):
                P.dma(DQ, stmp[hh * 64:(hh + 1) * 64, :, :],
                      st_ssm[l].rearrange("(i hh) p n -> hh p i n", hh=2)[hh])
            for i in range(4):
                pt = ps()
                tr(pt[:, 0:128], stmp[:, i, :], 128)
                cp('act', hstg[l][i // 2][:, (i % 2) * 128:(i % 2 + 1) * 128], pt[:, 0:128])
            for i in range(8):
                P.dma(DQ, hist[l][:, i, :], st_conv[l][:, i * 128:(i + 1) * 128].rearrange("j p -> p j"),
                      allow_slow_non_contiguous=True)
            P.dma(DQ, prev[l][:, :], st_shift[l].rearrange("(i p) -> p i", p=128),
                  allow_slow_non_contiguous=True)
            P.dma(DQ, rtmp[:, :, :, :], st_rwkv[l].rearrange("(q hh) v n -> v q hh n", hh=2))
            for q in range(4):
                pt = ps()
                tr(pt[:, 0:64], rtmp[:, q, :, :].rearrange("p hh n -> p (hh n)"), 64)
                cp('act', rstg[l][q // 2][:, q % 2, :], pt[:, 0:64])
            for g in range(2):
                P.dma(DQ, gstg[l][g][:, :, :], st_hgrn[l, 2 * g:2 * g + 2].rearrange("h k v -> k h v"))

    def zero_states():
        for l in range(DEPTH):
            for g in range(2):
                memset('pool', hstg[l][g][:, :], 0.0)
                memset('pool', rstg[l][g][:, :, :], 0.0)
                memset('pool', gstg[l][g][:, :, :], 0.0)
            memset('pool', hist[l][:, :, :], 0.0)
            memset('pool', prev[l][:, :], 0.0)

    def build_wcache():
        cnt = 0
        for l_ in range(DEPTH):
            for nm_, src in (("in", w_in), ("out", w_out), ("gate", w_gate), ("up", w_up), ("down", w_down)):
                cache = wc[(nm_, l_)]
                gl = WG[nm_]
                blocks = []
                gi = 0
                while gi < len(gl):
                    if gi + 1 < len(gl) and gl[gi][1] == 4 and gl[gi + 1][0] == gl[gi][0] + 512:
                        blocks.append((gl[gi][0], [(gi, 0, 512), (gi + 1, 512, gl[gi + 1][1] * 128)]))
                        gi += 2
                    else:
                        blocks.append((gl[gi][0], [(gi, 0, gl[gi][1] * 128)]))
                        gi += 1
                for c0, parts in blocks:
                    w_ = parts[-1][1] + parts[-1][2]
                    for k in range(WNK[nm_]):
                        land = SL[cnt % 6][:, :, :].rearrange("p a b -> p (a b)")
                        outb = SL[6 + cnt % 6][:, :, :].rearrange("p a b -> p (a b)").bitcast(BF16)
                        P.dma('sp', land[:, 0:w_], src[l_, k * 128:(k + 1) * 128, c0:c0 + w_])
                        cp('dve', outb[:, 0:w_], land[:, 0:w_])
                        for gidx, off, gw in parts:
                            P.dma('act', cache[gidx, :, k, 0:gw], outb[:, off:off + gw])
                        cnt += 1

    if WCACHE:
        build_wcache()

    if 'states' in STAGES:
        load_states()
    else:
        zero_states()
    if 'f' not in KDBG:
        run_tile(xs_in, 64, 64, 32, y_s)
    if 'states' in STAGES:
        store_states(1)
    zero_states()
    if 'b' not in KDBG:
        run_tile(meta, 16, 16, 16, None)
    t0 = 0 if 'c' not in KDBG else SEQ
    while t0 < SEQ:
        n = min(NT, SEQ - t0)
        run_tile(xp[t0:t0 + n, :], n, 64, 32, y_p[t0:t0 + n, :])
        t0 += n
    if 'states' in STAGES:
        store_states(0)

    P.emit()
    return nc, es, P


_CACHE = {}


def kernel(**inp):
    inp = {k: np.asarray(v, dtype=np.float32) for k, v in inp.items()}
    B, SEQ, _ = inp['x_prompt'].shape
    DEPTH = inp['w_in'].shape[0]
    pps = np.stack([_pack_params(inp, l) for l in range(DEPTH)], axis=0)
    ccs = _consts()
    zpad = np.zeros_like(inp['rw_w2'])
    wa2 = np.ascontiguousarray(np.stack([np.concatenate([inp['rw_w2'], zpad], axis=1),
                                         np.concatenate([zpad, inp['rw_a2']], axis=1)], axis=1))
    key = (SEQ, DEPTH)
    if key not in _CACHE:
        _CACHE[key] = build(SEQ, DEPTH=DEPTH)
    nc = _CACHE[key][0]
    in_maps = []
    for c in range(NCORES):
        in_maps.append({
            "xp": np.ascontiguousarray(inp['x_prompt'][c]),
            "xs": np.ascontiguousarray(inp['x_sample'][c]),
            "meta": inp['meta_tokens'],
            "st_ssm": np.ascontiguousarray(inp['state_ssm'][:, c]),
            "st_conv": np.ascontiguousarray(inp['state_conv'][:, c]),
            "st_rwkv": np.ascontiguousarray(inp['state_rwkv'][:, c]),
            "st_shift": np.ascontiguousarray(inp['state_shift'][:, c]),
            "st_hgrn": np.ascontiguousarray(inp['state_hgrn'][:, c]),
            "w_in": inp['w_in'], "w_out": inp['w_out'], "w_gate": inp['w_gate'], "w_up": inp['w_up'],
            "w_down": inp['w_down'], "wa2": wa2, "g2": inp['rw_g2'], "pp": pps, "cc": ccs,
        })
    in_maps = [{"i_" + k: v for k, v in m.items()} for m in in_maps]
    if os.environ.get('KONE'):
        res = run_bass_kernel_spmd(nc, in_maps[:1], core_ids=[0])
        R = [{k[2:]: v for k, v in res.results[0].items()}] * NCORES
    else:
        res = run_bass_kernel_spmd(nc, in_maps, core_ids=list(range(NCORES)))
        R = [{k[2:]: v for k, v in r.items()} for r in res.results]
    y_prompt = np.stack([R[c]["y_p"] for c in range(NCORES)], axis=0)
    y_sample = np.stack([R[c]["y_s"] for c in range(NCORES)], axis=0)

    def st(name, si):
        return np.stack([R[c][name][si] for c in range(NCORES)], axis=1)
    outs = [y_prompt, y_sample]
    for si in (0, 1):
        for name in ("o_ssm", "o_conv", "o_rwkv", "o_shift", "o_hgrn"):
            outs.append(st(name, si))
    return tuple(np.ascontiguousarray(o, dtype=np.float32) for o in outs)
```

```python
import numpy as np
from contextlib import ExitStack
import concourse.bass as bass
import concourse.mybir as mybir
from concourse.bass_utils import run_bass_kernel_spmd

F32 = mybir.dt.float32
BF16 = mybir.dt.bfloat16
AF = mybir.ActivationFunctionType
ALU = mybir.AluOpType

D = 1024
DFF = 2816
NIN = 5384
NCORES = 8
DQ = 'sp'
TRUNK_BF16 = True
import os
KDBG = os.environ.get('KDBG', '')
STAGES = {'states', 'proj', 'ssd', 'rwkv', 'hgrn', 'oproj', 'ffn'}
ENGS = ['pe', 'act', 'dve', 'pool', 'sp']


class Dep:
    __slots__ = ('lw', 'rd', 'lws')

    def __init__(self):
        self.lw = None
        self.rd = []
        self.lws = {}


class Prog:
    def __init__(self, nc, es):
        self.nc, self.es = nc, es
        self.q = {e: [] for e in ENGS}
        self.deps = {}
        self.dsem = {}
        self.sem = {e: es.enter_context(nc.semaphore("q_" + e)) for e in ENGS}
        self.tensors = {}
        self.psum_names = set()
        self.nowaw = set()

    def sb(self, name, shape, dtype=F32):
        name = "s_" + name
        t = self.es.enter_context(self.nc.sbuf_tensor(name, list(shape), dtype))
        self.deps[name] = Dep()
        return t

    def psum(self, name, shape):
        t = self.es.enter_context(self.nc.psum_tensor(name, list(shape), F32))
        self.deps[name] = Dep()
        self.psum_names.add(name)
        return t

    def _dep(self, ap):
        nm = ap.tensor.name
        return self.deps.get(nm)

    def _collect(self, eng, outs, ins):
        w = []
        for ap in ins:
            d = self._dep(ap)
            if d is not None and d.lw is not None:
                w.append(d.lw)
            if d is not None and d.lws:
                w.extend(d.lws.values())
            if d is not None and ap.tensor.name in self.psum_names:
                for r in d.rd:
                    if not (r[0] == 'e' and r[1] == eng):
                        w.append(r)
        for ap in outs:
            d = self._dep(ap)
            if d is None:
                continue
            if d.lw is not None and ap.tensor.name not in self.nowaw:
                w.append(d.lw)
            for r in d.rd:
                w.append(r)
        if eng == 'pe':
            w = [x for x in w if not (x[0] == 'e' and x[1] == 'pe')]
        return w

    def _commit(self, ev, outs, ins):
        for ap in ins:
            d = self._dep(ap)
            if d is not None:
                d.rd.append(ev)
        for ap in outs:
            d = self._dep(ap)
            if d is not None:
                d.lw = ev
                d.rd = []
                if ap.tensor.name in self.nowaw and ev[0] == 'd':
                    d.lws[ev[1]] = ev

    def op(self, eng, fn, outs, ins):
        w = self._collect(eng, outs, ins)
        idx = len(self.q[eng])
        self.q[eng].append(dict(fn=fn, waits=w, kind='c'))
        self._commit(('e', eng, idx), outs, ins)

    def dma(self, eng, out, in_, semname=None, **kw):
        outs, ins = [out], [in_]
        w = self._collect(eng, outs, ins)
        if semname is None:
            d = self._dep(out)
            semname = out.tensor.name if d is not None else in_.tensor.name
        if semname not in self.dsem:
            self.dsem[semname] = [self.es.enter_context(self.nc.semaphore("d_" + semname)), 0]
        s = self.dsem[semname]
        s[1] += 16
        self.q[eng].append(dict(fn=lambda e: e.dma_start(out=out, in_=in_, **kw), waits=w, kind='d', sem=semname))
        self._commit(('d', semname, s[1]), outs, ins)

    def emit(self):
        nc = self.nc
        targets = {e: set() for e in ENGS}
        for e in ENGS:
            for ins in self.q[e]:
                for d in ins['waits']:
                    if d[0] == 'e':
                        targets[d[1]].add(d[2])
        semval = {}
        for e in ENGS:
            c = 0
            vals = []
            for i in range(len(self.q[e])):
                if i in targets[e]:
                    c += 1
                vals.append(c)
            semval[e] = vals
        final = [(self.dsem[k][0], self.dsem[k][1]) for k in self.dsem]

        def body_for(e):
            def body(eng):
                waited = {}
                for i, ins in enumerate(self.q[e]):
                    need = {}
                    for d in ins['waits']:
                        if d[0] == 'e':
                            key = ('e', d[1])
                            val = semval[d[1]][d[2]]
                        else:
                            key = ('d', d[1])
                            val = d[2]
                        if val > need.get(key, 0):
                            need[key] = val
                    for key, val in need.items():
                        if waited.get(key, 0) >= val:
                            continue
                        waited[key] = val
                        h = self.sem[key[1]] if key[0] == 'e' else self.dsem[key[1]][0]
                        eng.wait_ge(h, val)
                    r = ins['fn'](eng)
                    if ins['kind'] == 'c':
                        if i in targets[e]:
                            r.then_inc(self.sem[e], 1)
                    else:
                        r.then_inc(self.dsem[ins['sem']][0], 16)
                if e == 'sp':
                    for h, v in final:
                        eng.wait_ge(h, v)
            return body

        with nc.Block() as block:
            block.sync(body_for('sp'))
            block.tensor(body_for('pe'))
            block.scalar(body_for('act'))
            block.vector(body_for('dve'))
            block.gpsimd(body_for('pool'))


def _cols(v):
    v = np.asarray(v, np.float32)
    return v.reshape(-1, 128).T


PP_LAYOUT = {}


def _pack_params(inp, l):
    parts = []
    off = 0

    def add(name, arr):
        nonlocal off
        arr = np.ascontiguousarray(arr, dtype=np.float32)
        PP_LAYOUT[name] = (off, arr.shape[1])
        off += arr.shape[1]
        parts.append(arr)

    add('n1', _cols(inp['norm1_w'][l]))
    add('n2', _cols(inp['norm2_w'][l]))
    cw = inp['conv_w'][l]
    add('cw', np.stack([cw[j].reshape(8, 128).T for j in range(4)], axis=2).reshape(128, 32))
    add('cb', _cols(inp['conv_b'][l]))
    add('dsk', _cols(np.repeat(inp['d_skip'][l], 64)))
    add('snw', _cols(inp['ssd_norm_w'][l]))
    add('dtb', np.tile(inp['dt_bias'][l][None, :], (128, 1)))
    add('alog', np.tile(inp['a_log'][l][None, :], (128, 1)))
    add('mu', _cols(inp['rw_mu'][l]))
    add('w0', _cols(inp['rw_w0'][l]))
    add('a0', _cols(inp['rw_a0'][l]))
    add('kk', _cols(inp['rw_kk'][l]))
    add('ka', _cols(inp['rw_ka'][l]))
    add('rk', _cols(inp['rw_rk'][l]))
    add('lnw', _cols(inp['rw_lnx_w'][l]))
    add('lnb', _cols(inp['rw_lnx_b'][l]))
    add('lg0', _cols(inp['hg_lb_logits'][0]))
    add('lg1', _cols(inp['hg_lb_logits'][1]))
    add('hnw', _cols(inp['hg_norm_w'][l]))
    add('fin', _cols(inp['final_norm_w']))
    PP_LAYOUT['_n'] = off
    return np.concatenate(parts, axis=1)


CC = {}


def _consts():
    parts = []
    off = 0

    def add(name, arr):
        nonlocal off
        arr = np.ascontiguousarray(arr, dtype=np.float32)
        assert arr.shape[0] == 128
        CC[name] = (off, arr.shape[1])
        off += arr.shape[1]
        parts.append(arr)

    i = np.arange(128)
    add('ident', np.eye(128))
    add('ones', np.ones((128, 128)))
    bo = np.zeros((128, 128))
    bo[:64, :64] = 1
    bo[64:, 64:] = 1
    add('bones', bo)
    u1 = (i[:, None] <= i[None, :]).astype(np.float32)
    u0 = (i[:, None] < i[None, :]).astype(np.float32)
    add('u1', u1)
    add('u0', u0)
    add('l0', u0.T)
    add('mneg', (u1 - 1.0) * 30000.0)
    t = np.arange(256)
    add('rm64', np.tile(((t % 64) != 0).astype(np.float32)[None, :], (128, 1)))
    add('rm32', np.tile(((t % 32) != 0).astype(np.float32)[None, :], (128, 1)))
    sc = np.zeros((128, 8), np.float32)
    sc[:, 0] = 1e-6
    sc[:, 1] = 1.0
    sc[:, 2] = 64e-5
    sc[:, 3] = 1e-12
    sc[:, 4] = -0.5
    add('sc', sc)
    CC['_n'] = off
    return np.concatenate(parts, axis=1)


def build(SEQ, NT=int(os.environ.get('KNT', '256')), DEPTH=2):
    nc = bass.Bass("TRN2", target_bir_lowering=False)
    TDT = BF16 if TRUNK_BF16 else F32
    if TRUNK_BF16:
        nc.allow_low_precision("trunk projections use bf16 operands with fp32 accumulation")
    es = ExitStack()
    P = Prog(nc, es)
    NPP = PP_LAYOUT['_n']
    NCC = CC['_n']

    def din(name, shape):
        return nc.dram_tensor("i_" + name, list(shape), F32, kind="ExternalInput").ap()

    def dout(name, shape):
        return nc.dram_tensor("r_" + name, list(shape), F32, kind="ExternalOutput").ap()

    xp = din("xp", [SEQ, D])
    xs_in = din("xs", [64, D])
    meta = din("meta", [16, D])
    st_ssm = din("st_ssm", [DEPTH, 8, 64, 128])
    st_conv = din("st_conv", [DEPTH, 3, 1024])
    st_rwkv = din("st_rwkv", [DEPTH, 8, 64, 64])
    st_shift = din("st_shift", [DEPTH, 1792])
    st_hgrn = din("st_hgrn", [DEPTH, 4, 128, 128])
    w_in = din("w_in", [DEPTH, D, NIN])
    w_out = din("w_out", [DEPTH, 1536, D])
    w_gate = din("w_gate", [DEPTH, D, DFF])
    w_up = din("w_up", [DEPTH, D, DFF])
    w_down = din("w_down", [DEPTH, DFF, D])
    wa2_d = din("wa2", [DEPTH, 2, 128, 512])
    g2_d = din("g2", [DEPTH, 128, 512])
    pp_d = din("pp", [DEPTH, 128, NPP])
    cc_d = din("cc", [128, NCC])

    WCACHE = TRUNK_BF16
    wc = {}
    if WCACHE:
        ffg = [(f * 128, min(4, 22 - f)) for f in range(0, 22, 4)]
        WG = {"in": [(0, 4), (512, 4), (1024, 4), (1544, 4), (2056, 4), (2568, 4), (3080, 2), (3336, 4), (3848, 4),
                     (4360, 4), (4872, 4)],
              "out": [(0, 4), (512, 4)], "gate": ffg, "up": ffg, "down": [(0, 4), (512, 4)]}
        WNK = {"in": 8, "out": 12, "gate": 8, "up": 8, "down": 22}
        for nm_ in WG:
            for l_ in range(DEPTH):
                nm2 = "wc_%s%d" % (nm_, l_)
                wc[(nm_, l_)] = nc.dram_tensor(nm2, [len(WG[nm_]), 128, WNK[nm_], 512], BF16, kind="Internal").ap()
                P.deps[nm2] = Dep()
                P.nowaw.add(nm2)

    y_p = dout("y_p", [SEQ, D])
    y_s = dout("y_s", [64, D])
    o_ssm = dout("o_ssm", [2, DEPTH, 8, 64, 128])
    o_conv = dout("o_conv", [2, DEPTH, 3, 1024])
    o_rwkv = dout("o_rwkv", [2, DEPTH, 8, 64, 64])
    o_shift = dout("o_shift", [2, DEPTH, 1792])
    o_hgrn = dout("o_hgrn", [2, DEPTH, 4, 128, 128])

    cc = P.sb("cc", [128, NCC])
    pp = [P.sb("pp%d" % l, [128, NPP]) for l in range(DEPTH)]
    dv = [P.sb("dv%d" % l, [128, 40]) for l in range(DEPTH)]
    wa2 = [P.sb("wa2_%d" % l, [128, 2, 512]) for l in range(DEPTH)]
    g2 = [P.sb("g2_%d" % l, [128, 512]) for l in range(DEPTH)]
    wdt32 = [P.sb("wdt%d" % l, [128, 8, 8]) for l in range(DEPTH)]
    wdt = [P.sb("wdtb%d" % l, [128, 8, 8], TDT) for l in range(DEPTH)] if TRUNK_BF16 else wdt32

    xT = [P.sb("xT%d" % k, [128, NT]) for k in range(8)]
    hT = [P.sb("hT%d" % k, [128, NT], TDT) for k in range(8)]
    zT = P.sb("zT", [128, 4, NT])
    xsT = P.sb("xsT", [128, 4, NT + 3])
    bcT = P.sb("bcT", [128, 4, NT + 3])
    rT = P.sb("rT", [128, 4, NT + 1])
    kT = P.sb("kT", [128, 4, NT + 1])
    vT = P.sb("vT", [128, 4, NT + 1])
    x12 = P.sb("x12", [128, NT + 1])
    x13 = P.sb("x13", [128, NT + 1])
    qT = P.sb("qT", [128, 4, NT])
    fzT = P.sb("fzT", [128, 4, NT])
    ivT = P.sb("ivT", [128, 4, NT])
    ggT = P.sb("ggT", [128, 4, NT])
    Y = [P.sb("Y%d" % j, [128, NT], TDT) for j in range(12)]

    NSL = 12
    SL = [P.sb("SL%d" % j, [128, 4, NT]) for j in range(NSL)]
    sc1 = [P.sb("sc1_%d" % j, [128, NT]) for j in range(3)]

    def a16(f, n):
        v = SL[f // 8][:, :, :].rearrange("p a b -> p (a b)").bitcast(BF16)
        t = f % 8
        return v[:, t * NT:t * NT + n]
    KB = 4
    NW = 5 if WCACHE else 9
    wsl = [P.sb("wsl%d" % j, [128, KB, 512] if WCACHE else [128, 512], TDT if WCACHE else F32) for j in range(NW)]
    NWB = 4
    wbl = [P.sb("wbl%d" % j, [128, 512], TDT) for j in range(NWB)] if (TRUNK_BF16 and not WCACHE) else None
    xin = [P.sb("xin%d" % j, [128, D]) for j in range(1)]

    hstg = [[P.sb("hst%d_%d" % (l, g), [128, 256]) for g in range(2)] for l in range(DEPTH)]
    hist = [P.sb("hist%d" % l, [128, 8, 3]) for l in range(DEPTH)]
    prev = [P.sb("prev%d" % l, [128, 14]) for l in range(DEPTH)]
    rstg = [[P.sb("rst%d_%d" % (l, g), [128, 2, 64]) for g in range(2)] for l in range(DEPTH)]
    gstg = [[P.sb("gst%d_%d" % (l, g), [128, 2, 128]) for g in range(2)] for l in range(DEPTH)]
    stmp = P.sb("stmp", [128, 4, 128])
    rtmp = P.sb("rtmp", [64, 4, 2, 64])

    tk = {}
    for nm, shp in [('wlc', [128, 4, 4]), ('slc', [128, 4, 8]), ('rs', [128, NT])]:
        tk[nm] = P.sb("tk_" + nm, shp)
    tk['rs2'] = tk['rs']
    tkg = []
    for g in range(2):
        T = {}
        for nm, shp in [('t1', [64, 4]), ('dt', [64, 4]), ('loga', [64, 4]), ('cum', [64, 8]), ('ecum', [64, 4]),
                        ('dec', [64, 4]), ('et128', [128, 4]), ('Btok', [64, 128]),
                        ('AkT', [64, 4, 64]), ('RbT', [64, 4, 64]), ('RkT', [64, 4, 64]),
                        ('Pa', [64, 4, 64]), ('PaT', [64, 4, 64]), ('Pb', [64, 4, 64]), ('PbT', [64, 4, 64]),
                        ('Xa', [64, 4, 64]), ('Xb', [64, 4, 64]), ('Vtok', [64, 4, 64]), ('Otok', [64, 4, 64]),
                        ('bbt', [64, 2, 128]), ('kbt', [64, 2, 128]), ('stm', [128, 2, 64]),
                        ('hMT', [32, 2, 32]), ('hvt', [32, 2, 128]), ('hkt', [32, 2, 128])]:
            T[nm] = P.sb("tk%d_%s" % (g, nm), shp)
        for a_, b_ in [('lgB', 'Pa'), ('Dm', 'PaT'), ('LT', 'Pb'), ('MT', 'PbT'), ('xdt', 'Xa'), ('xdec', 'Xb'),
                       ('ytok', 'Otok')]:
            T[a_] = T[b_]
        tkg.append(T)

    PSB = [P.psum("ps%d" % j, [128, 512]) for j in range(8)]
    psi = [0]

    def ps():
        t = PSB[psi[0] % 8]
        psi[0] += 1
        return t

    wi = [0]
    wbi = [0]

    def wslot():
        t = wsl[wi[0] % NW]
        wi[0] += 1
        return t

    def isap(x):
        return not isinstance(x, (int, float))

    def tt(eng, out, a, b, op):
        P.op(eng, lambda e: e.tensor_tensor(out=out, in0=a, in1=b, op=op), [out], [a, b])

    def ts(eng, out, a, s1, op0, s2=None, op1=None):
        eng = 'dve'
        ins = [a] + [s for s in (s1, s2) if s is not None and isap(s)]
        if op1 is None:
            P.op(eng, lambda e: e.tensor_scalar(out=out, in0=a, scalar1=s1, scalar2=None, op0=op0), [out], ins)
        else:
            P.op(eng, lambda e: e.tensor_scalar(out=out, in0=a, scalar1=s1, scalar2=s2, op0=op0, op1=op1), [out], ins)

    def stt(eng, out, a, s, b, op0, op1):
        eng = 'dve'
        ins = [a, b] + ([s] if isap(s) else [])
        P.op(eng, lambda e: e.scalar_tensor_tensor(out=out, in0=a, scalar=s, in1=b, op0=op0, op1=op1), [out], ins)

    def act(out, in_, func, bias=None, scale=1.0):
        ins = [in_] + ([bias] if bias is not None else []) + ([scale] if isap(scale) else [])
        if bias is None:
            P.op('act', lambda e: e.activation(out=out, in_=in_, func=func, scale=scale), [out], ins)
        else:
            P.op('act', lambda e: e.activation(out=out, in_=in_, func=func, bias=bias, scale=scale), [out], ins)

    def cp(eng, out, in_):
        if eng == 'act':
            P.op('act', lambda e: e.copy(out=out, in_=in_), [out], [in_])
        else:
            P.op(eng, lambda e: e.tensor_copy(out=out, in_=in_), [out], [in_])

    def recip(out, in_):
        P.op('dve', lambda e: e.reciprocal(out=out, in_=in_), [out], [in_])

    def mm(out, lhsT, rhs, start=True, stop=True):
        P.op('pe', lambda e: e.matmul(out, lhsT, rhs, start=start, stop=stop), [out], [lhsT, rhs])

    def tr(out, in_, n_in_part):
        idn = cc[0:n_in_part, CC['ident'][0]:CC['ident'][0] + n_in_part]
        P.op('pe', lambda e: e.transpose(out, in_, idn), [out], [in_, idn])

    def scan(out, d0, d1):
        P.op('dve', lambda e: e.tensor_tensor_scan(out=out, data0=d0, data1=d1, initial=0.0, op0=ALU.mult, op1=ALU.add),
             [out], [d0, d1])

    def memset(eng, out, val):
        P.op(eng, lambda e: e.memset(out, val), [out], [])

    def C(name, rows=128, c0=0, c1=None):
        o, n = CC[name]
        if c1 is None:
            c1 = n
        return cc[0:rows, o + c0:o + c1]

    def SC(i, rows=128):
        o = CC['sc'][0]
        return cc[0:rows, o + i:o + i + 1]

    def PPc(l, name, i=0, n=1, rows=128):
        o = PP_LAYOUT[name][0]
        return pp[l][0:rows, o + i:o + i + n]

    def bc(ap, shape, axis):
        return ap.unsqueeze(axis).broadcast_to(list(shape))

    eng_rr = [0]

    def ve():
        eng_rr[0] += 1
        return 'dve' if eng_rr[0] % 2 else 'pool'

    def rstd(out, in_, scale, eps_ap):
        act(out, in_, AF.Ln, bias=eps_ap, scale=scale)
        act(out, out, AF.Exp, scale=-0.5)

    def sigmoid(out, in_, scale=1.0, nbias=None):
        act(out, in_, AF.Exp, bias=nbias, scale=-scale)
        act(out, out, AF.Ln, bias=SC(1, out.shape[0]))
        act(out, out, AF.Exp, scale=-1.0)

    P.dma('sp', cc[:, :], cc_d[:, :])
    for l in range(DEPTH):
        P.dma('sp', pp[l][:, :], pp_d[l, :, :])
        P.dma('sp', wa2[l][:, :, :], wa2_d[l].rearrange("t p c -> p t c"))
        P.dma('sp', g2[l][:, :], g2_d[l, :, :])
        if 'a' not in KDBG:
            P.dma('sp', wdt32[l][:, :, :], w_in[l, :, 1536:1544].rearrange("(k p) c -> p k c", p=128),
                  allow_slow_non_contiguous=True)
            if TRUNK_BF16:
                P.op('dve', lambda e, l=l: e.tensor_copy(out=wdt[l][:, :, :], in_=wdt32[l][:, :, :]),
                     [wdt[l][:, :, :]], [wdt32[l][:, :, :]])
    for l in range(DEPTH if 'd' not in KDBG else 0):
        ts('dve', dv[l][:, 0:4], PPc(l, 'w0', 0, 4), -1.0, ALU.mult)
        ts('dve', dv[l][:, 4:8], PPc(l, 'a0', 0, 4), -1.0, ALU.mult)
        act(dv[l][:, 8:16], PPc(l, 'alog', 0, 8), AF.Exp)
        ts('dve', dv[l][:, 8:16], dv[l][:, 8:16], -1.0, ALU.mult)
        if l == 0:
            memset('dve', dv[l][:, 16:20], 0.0)
        else:
            tt('dve', dv[l][:, 16:20], PPc(l, 'lg0', 0, 4), PPc(l, 'lg1', 0, 4), ALU.subtract)
            act(dv[l][:, 16:20], dv[l][:, 16:20], AF.Exp)
            ts('dve', dv[l][:, 16:20], dv[l][:, 16:20], 1.0, ALU.add)
            recip(dv[l][:, 16:20], dv[l][:, 16:20])
        ts('dve', dv[l][:, 20:24], dv[l][:, 16:20], -1.0, ALU.mult, 1.0, ALU.add)

    def big_proj(wsrc, nk, c0, ntile, rhs_list, n, evac, width=None):
        banks = [ps() for _ in range(ntile)]
        wcols = ntile * 128 if width is None else width
        if WCACHE:
            wname, l_ = wsrc
            gi = WG[wname].index((c0, ntile))
            cache = wc[(wname, l_)]
            for k0 in range(0, nk, KB):
                kb = min(KB, nk - k0)
                wt = wslot()
                P.dma('sp', wt[:, 0:kb, 0:wcols], cache[gi, :, k0:k0 + kb, 0:wcols])
                for kk in range(kb):
                    k = k0 + kk
                    for j in range(ntile):
                        mm(banks[j][:, 0:n], wt[:, kk, j * 128:(j + 1) * 128], rhs_list[k], start=(k == 0),
                           stop=(k == nk - 1))
            for j in range(ntile):
                evac(j, banks[j][:, 0:n])
            return
        for k in range(nk):
            wt = wslot()
            P.dma('sp', wt[:, 0:wcols], wsrc[k * 128:(k + 1) * 128, c0:c0 + wcols])
            if TRUNK_BF16 and not WCACHE:
                wb = wbl[wbi[0] % NWB]
                wbi[0] += 1
                cp('act' if wbi[0] % 2 else 'dve', wb[:, 0:wcols], wt[:, 0:wcols])
                wt = wb
            for j in range(ntile):
                mm(banks[j][:, 0:n], wt[:, j * 128:(j + 1) * 128], rhs_list[k], start=(k == 0), stop=(k == nk - 1))
        for j in range(ntile):
            evac(j, banks[j][:, 0:n])

    def rmsnorm(l, pname, n, src, dst):
        pss = ps()
        for k in range(8):
            s = sc1[k % 2]
            tt(ve(), s[:, 0:n], src[k][:, 0:n], src[k][:, 0:n], ALU.mult)
            mm(pss[:, 0:n], C('ones'), s[:, 0:n], start=(k == 0), stop=(k == 7))
        rstd(tk['rs'][:, 0:n], pss[:, 0:n], 1.0 / D, SC(0))
        for k in range(8):
            stt(ve(), dst[k][:, 0:n], src[k][:, 0:n], PPc(l, pname, k), tk['rs'][:, 0:n], ALU.mult, ALU.mult)

    def layer(l, n, L, Lh):
        nch = n // L
        nchh = n // Lh
        W_in = ('in', l) if WCACHE else w_in[l]
        rmsnorm(l, 'n1', n, xT, hT)
        hl = [hT[k][:, 0:n] for k in range(8)]

        evc = [0]

        def ev_to(dst_fn):
            def f(j, psap):
                evc[0] += 1
                cp('act' if evc[0] % 2 else 'dve', dst_fn(j), psap)
            return f
        big_proj(W_in, 8, 0, 4, hl, n, ev_to(lambda j: zT[:, j, 0:n]))
        big_proj(W_in, 8, 512, 4, hl, n, ev_to(lambda j: xsT[:, j, 3:3 + n]))
        big_proj(W_in, 8, 1024, 4, hl, n, ev_to(lambda j: bcT[:, j, 3:3 + n]))
        ssd_pro(l, n, L, nch)
        big_proj(W_in, 8, 3336, 4, hl, n, ev_to(lambda j: qT[:, j, 0:n]))
        big_proj(W_in, 8, 3336 + 512, 4, hl, n, ev_to(lambda j: fzT[:, j, 0:n]))
        big_proj(W_in, 8, 3336 + 1024, 4, hl, n, ev_to(lambda j: ivT[:, j, 0:n]))
        big_proj(W_in, 8, 3336 + 1536, 4, hl, n, ev_to(lambda j: ggT[:, j, 0:n]))
        hgrn_pro(l, n, Lh, nchh)
        big_proj(W_in, 8, 1544, 4, hl, n, ev_to(lambda j: rT[:, j, 1:1 + n]))
        big_proj(W_in, 8, 1544 + 512, 4, hl, n, ev_to(lambda j: kT[:, j, 1:1 + n]))
        big_proj(W_in, 8, 1544 + 1024, 4, hl, n, ev_to(lambda j: vT[:, j, 1:1 + n]))
        big_proj(W_in, 8, 1544 + 1536, 2, hl, n, ev_to(lambda j: (x12 if j == 0 else x13)[:, 1:1 + n]))

        if 'proj' not in STAGES:
            return
        interleave(ssd_chain(l, n, L, nch, 0), hgrn_chain(l, n, Lh, nchh, 0),
                   ssd_chain(l, n, L, nch, 1), hgrn_chain(l, n, Lh, nchh, 1))
        ssd_epi(l, n, L, nch)
        hgrn_epi(l, n, Lh, nchh)
        rwkv_pro(l, n, L, nch)
        interleave(rwkv_chain(l, n, L, nch, 0), rwkv_chain(l, n, L, nch, 1))
        rwkv_epi(l, n, L, nch)
        if 'oproj' not in STAGES:
            return

        yl = [Y[j][:, 0:n] for j in range(12)]
        for g in range(2):
            big_proj(('out', l) if WCACHE else w_out[l], 12, g * 512, 4, yl, n,
                     lambda j, psap, g=g: tt('dve', xT[4 * g + j][:, 0:n], xT[4 * g + j][:, 0:n], psap, ALU.add))

        if 'ffn' not in STAGES:
            return
        rmsnorm(l, 'n2', n, xT, hT)
        hl = [hT[k][:, 0:n] for k in range(8)]
        f0 = 0
        while f0 < 22:
            nt_ = min(4, 22 - f0)
            gb = {}

            def ev_gate(j, psap, gb=gb):
                gb[j] = psap
            big_proj(('gate', l) if WCACHE else w_gate[l], 8, f0 * 128, nt_, hl, n, ev_gate)

            def ev_up(j, psap, gb=gb, f0=f0):
                f = f0 + j
                a_ = a16(f, n) if TRUNK_BF16 else SL[f // 4][:, f % 4, 0:n]
                s = sc1[j % 2]
                sigmoid(s[:, 0:n], gb[j])
                tt('dve', s[:, 0:n], s[:, 0:n], gb[j], ALU.mult)
                tt('dve', a_, s[:, 0:n], psap, ALU.mult)
            big_proj(('up', l) if WCACHE else w_up[l], 8, f0 * 128, nt_, hl, n, ev_up)
            f0 += nt_
        al = [(a16(f, n) if TRUNK_BF16 else SL[f // 4][:, f % 4, 0:n]) for f in range(22)]
        for g in range(2):
            big_proj(('down', l) if WCACHE else w_down[l], 22, g * 512, 4, al, n,
                     lambda j, psap, g=g: tt('dve', xT[4 * g + j][:, 0:n], xT[4 * g + j][:, 0:n], psap, ALU.add))

    def ssd_pro(l, n, L, nch):
        cp('pool', xsT[:, :, 0:3], hist[l][:, 0:4, :])
        cp('pool', bcT[:, :, 0:3], hist[l][:, 4:8, :])
        cwo = PP_LAYOUT['cw'][0]
        for half, (src, ACC, SG) in enumerate(((xsT, SL[10], SL[8]), (bcT, SL[11], SL[9]))):
            for ii in range(4):
                i = half * 4 + ii
                ts('dve', ACC[:, ii, 0:n], src[:, ii, 0:n], pp[l][:, cwo + i * 4:cwo + i * 4 + 1], ALU.mult,
                   PPc(l, 'cb', i), ALU.add)
                for j in range(1, 4):
                    stt('dve', ACC[:, ii, 0:n], src[:, ii, j:j + n], pp[l][:, cwo + i * 4 + j:cwo + i * 4 + j + 1],
                        ACC[:, ii, 0:n], ALU.mult, ALU.add)
            cp('pool', hist[l][:, half * 4:half * 4 + 4, :], src[:, :, n:n + 3])
            sigmoid(SG[:, :, 0:n], ACC[:, :, 0:n])
            tt('pool' if half else 'dve', src[:, :, 3:3 + n], ACC[:, :, 0:n], SG[:, :, 0:n], ALU.mult)
        sigmoid(SL[10][:, :, 0:n], zT[:, :, 0:n])
        tt('dve', zT[:, :, 0:n], zT[:, :, 0:n], SL[10][:, :, 0:n], ALU.mult)

    def ssd_chain(l, n, L, nch, g):
        T = tkg[g]
        yss = SL[8 + g]
        hs4 = slice(4 * g, 4 * g + 4)
        for c in range(nch):
            cs = slice(c * L, (c + 1) * L)
            cs3 = slice(3 + c * L, 3 + (c + 1) * L)
            pdt = ps()
            for k in range(8):
                mm(pdt[0:L, 0:4], hT[k][:, cs], wdt[l][:, k, hs4], start=(k == 0), stop=(k == 7))
            tt('dve', T['t1'][0:L, :], pdt[0:L, 0:4], PPc(l, 'dtb', 4 * g, 4, rows=L), ALU.add)
            act(T['t1'][0:L, :], T['t1'][0:L, :], AF.Exp)
            act(T['dt'][0:L, :], T['t1'][0:L, :], AF.Ln, bias=SC(1, L))
            tt('dve', T['loga'][0:L, :], T['dt'][0:L, :], dv[l][0:L, 8 + 4 * g:12 + 4 * g], ALU.mult)
            yield
            pc = ps()
            mm(pc[0:L, 0:4], C('u1', L, 0, L), T['loga'][0:L, :])
            mm(pc[0:L, 4:8], C('ones', L, 0, L), T['loga'][0:L, :])
            mm(pc[:, 8:12], C('ones', L, 0, 128), T['loga'][0:L, :])
            cp('pool', T['lgB'][0:L, :, 0:L], bc(T['loga'][0:L, :], [L, 4, L], 2))
            cp('dve', T['cum'][0:L, :], pc[0:L, 0:8])
            act(T['et128'][:, :], pc[:, 8:12], AF.Exp)
            act(T['ecum'][0:L, :], T['cum'][0:L, 0:4], AF.Exp)
            tt('dve', T['dec'][0:L, :], T['cum'][0:L, 4:8], T['cum'][0:L, 0:4], ALU.subtract)
            act(T['dec'][0:L, :], T['dec'][0:L, :], AF.Exp)
            yield
            pcb = ps()
            for h in range(4):
                mm(pcb[0:L, h * L:(h + 1) * L], T['lgB'][0:L, h, 0:L], C('u1', L, 0, L))
            pg = ps()
            mm(pg[0:L, 0:L], bcT[:, g, cs3], bcT[:, 2 + g, cs3])
            px = ps()
            for i in range(2):
                tr(px[0:L, i * 128:(i + 1) * 128], xsT[:, 2 * g + i, cs3], 128)
            pb = ps()
            tr(pb[0:L, 0:128], bcT[:, g, cs3], 128)
            pcb3 = pcb[0:L, 0:4 * L].rearrange("p (h i) -> p h i", h=4)
            tt('dve', T['Dm'][0:L, :, 0:L], pcb3, bc(C('mneg', L, 0, L), [L, 4, L], 1), ALU.add)
            tt('pool', T['Dm'][0:L, :, 0:L], T['Dm'][0:L, :, 0:L], bc(T['cum'][0:L, 0:4], [L, 4, L], 2), ALU.subtract)
            act(T['LT'][0:L, :, 0:L], T['Dm'][0:L, :, 0:L], AF.Exp)
            px3 = px[0:L, 0:256].rearrange("p (h d) -> p h d", h=4)
            tt('dve', T['xdt'][0:L, :, :], px3, bc(T['dt'][0:L, :], [L, 4, 64], 2), ALU.mult)
            tt('pool', T['xdec'][0:L, :, :], T['xdt'][0:L, :, :], bc(T['dec'][0:L, :], [L, 4, 64], 2), ALU.mult)
            cp('act', T['Btok'][0:L, :], pb[0:L, 0:128])
            tt('dve', T['MT'][0:L, :, 0:L], T['LT'][0:L, :, 0:L], bc(pg[0:L, 0:L], [L, 4, L], 1), ALU.mult)
            yield
            py = ps()
            for h in range(4):
                mm(py[0:L, h * 64:(h + 1) * 64], T['MT'][0:L, h, 0:L], T['xdt'][0:L, h, :])
            pyi = ps()
            mm(pyi[0:L, 0:256], bcT[:, 2 + g, cs3], hstg[l][g][:, :])
            ph = ps()
            mm(ph[:, 0:256], T['Btok'][0:L, :], T['xdec'][0:L, :, :].rearrange("p h d -> p (h d)"))
            pyi3 = pyi[0:L, 0:256].rearrange("p (h d) -> p h d", h=4)
            py3 = py[0:L, 0:256].rearrange("p (h d) -> p h d", h=4)
            tt('dve', T['ytok'][0:L, :, :], pyi3, bc(T['ecum'][0:L, :], [L, 4, 64], 2), ALU.mult)
            tt('dve', T['ytok'][0:L, :, :], T['ytok'][0:L, :, :], py3, ALU.add)
            h3 = hstg[l][g][:, :].rearrange("p (h d) -> p h d", h=4)
            tt('dve', h3, h3, bc(T['et128'][:, :], [128, 4, 64], 2), ALU.mult)
            tt('dve', hstg[l][g][:, :], hstg[l][g][:, :], ph[:, 0:256], ALU.add)
            yield
            pyt = ps()
            for i in range(2):
                tr(pyt[:, i * L:(i + 1) * L], T['ytok'][0:L, 2 * i:2 * i + 2, :].rearrange("p h d -> p (h d)"), L)
            for i in range(2):
                stt('dve', yss[:, i, cs], xsT[:, 2 * g + i, cs3], PPc(l, 'dsk', 2 * g + i), pyt[:, i * L:(i + 1) * L],
                    ALU.mult, ALU.add)
            yield

    def ssd_epi(l, n, L, nch):
        for g in range(2):
            yss = SL[8 + g]
            for t_ in range(2):
                tt(ve(), yss[:, t_, 0:n], yss[:, t_, 0:n], zT[:, 2 * g + t_, 0:n], ALU.mult)
            pss = ps()
            for t_ in range(2):
                s = sc1[t_]
                tt(ve(), s[:, 0:n], yss[:, t_, 0:n], yss[:, t_, 0:n], ALU.mult)
                mm(pss[:, 0:n], C('ones'), s[:, 0:n], start=(t_ == 0), stop=(t_ == 1))
            rstd(tk['rs2'][:, 0:n], pss[:, 0:n], 1.0 / 256, SC(0))
            for t_ in range(2):
                i = 2 * g + t_
                stt(ve(), Y[i][:, 0:n], yss[:, t_, 0:n], PPc(l, 'snw', i), tk['rs2'][:, 0:n], ALU.mult, ALU.mult)

    def rwkv_pro(l, n, L, nch):
        c1 = slice(1, 1 + n)
        muo = PP_LAYOUT['mu'][0]
        tl = [(rT, 0), (rT, 1), (rT, 2), (rT, 3), (kT, 0), (kT, 1), (kT, 2), (kT, 3), (vT, 0), (vT, 1), (vT, 2), (vT, 3)]
        for g, t3 in enumerate((rT, kT, vT)):
            cp('pool', t3[:, :, 0:1], prev[l][:, 4 * g:4 * g + 4].unsqueeze(2))
        cp('pool', x12[:, 0:1], prev[l][:, 12:13])
        cp('pool', x13[:, 0:1], prev[l][:, 13:14])

        def shift(cur, prv, lastcol, prevdst, mucol):
            d = sc1[mucol % 2]
            e1 = ve()
            tt(e1, d[:, 0:n], prv, cur, ALU.subtract)
            cp('pool', prevdst, lastcol)
            stt(e1, cur, d[:, 0:n], pp[l][:, muo + mucol:muo + mucol + 1], cur, ALU.mult, ALU.add)
        for idx, (t3, i) in enumerate(tl):
            shift(t3[:, i, 1:1 + n], t3[:, i, 0:n], t3[:, i, n:n + 1], prev[l][:, idx:idx + 1], idx)
        shift(x12[:, 1:1 + n], x12[:, 0:n], x12[:, n:n + 1], prev[l][:, 12:13], 12)
        shift(x13[:, 1:1 + n], x13[:, 0:n], x13[:, n:n + 1], prev[l][:, 13:14], 13)
        sigmoid(x12[0:64, c1], x12[0:64, c1], scale=2.0)
        ts('dve', x12[0:64, c1], x12[0:64, c1], 2.0, ALU.mult, -1.0, ALU.add)
        sigmoid(x13[:, c1], x13[:, c1])

        LW, AS, G, CW, KKN, KM, BON, BV, RT_, AT_, BT_, KT_ = SL
        S1, S2 = RT_, AT_
        A4 = slice(0, n)
        for m in range(4):
            p1 = ps()
            mm(p1[:, 0:n], wa2[l][:, 0, m * 128:(m + 1) * 128], x12[:, c1])
            act(LW[:, m, A4], p1[:, 0:n], AF.Exp, bias=dv[l][:, m:m + 1], scale=-1.0)
            p2 = ps()
            mm(p2[:, 0:n], wa2[l][:, 1, m * 128:(m + 1) * 128], x12[:, c1])
            act(AS[:, m, A4], p2[:, 0:n], AF.Exp, bias=dv[l][:, 4 + m:5 + m], scale=-1.0)
            p3 = ps()
            mm(p3[:, 0:n], g2[l][:, m * 128:(m + 1) * 128], x13[:, c1])
            cp('act', G[:, m, A4], p3[:, 0:n])
            ts('dve', KKN[:, m, A4], kT[:, m, c1], PPc(l, 'kk', m), ALU.mult)
        act(LW[:, :, A4], LW[:, :, A4], AF.Ln, bias=SC(1))
        act(LW[:, :, A4], LW[:, :, A4], AF.Exp, scale=-1.0)
        act(AS[:, :, A4], AS[:, :, A4], AF.Ln, bias=SC(1))
        act(AS[:, :, A4], AS[:, :, A4], AF.Exp, scale=-1.0)
        ts('dve', LW[:, :, A4], LW[:, :, A4], -0.6065306597126334, ALU.mult)
        for m in range(4):
            scan(CW[:, m, A4], C('rm64', 128, 0, n), LW[:, m, A4])
        tt('pool', S1[:, :, A4], KKN[:, :, A4], KKN[:, :, A4], ALU.mult)
        for m in range(4):
            p4 = ps()
            mm(p4[:, 0:n], C('bones'), S1[:, m, A4])
            act(S2[:, m, A4], p4[:, 0:n], AF.Ln, bias=SC(3))
        act(S2[:, :, A4], S2[:, :, A4], AF.Exp, scale=-0.5)
        tt('dve', KKN[:, :, A4], KKN[:, :, A4], S2[:, :, A4], ALU.mult)
        for m in range(4):
            ts('dve', KM[:, m, A4], AS[:, m, A4], PPc(l, 'ka', m), ALU.mult, PPc(l, 'ka', m), ALU.subtract)
        stt('dve', KM[:, :, A4], KM[:, :, A4], 1.0, kT[:, :, c1], ALU.add, ALU.mult)
        for m in range(4):
            stt('dve', S1[:, m, A4], rT[:, m, c1], PPc(l, 'rk', m), KM[:, m, A4], ALU.mult, ALU.mult)
            p5 = ps()
            mm(p5[:, 0:n], C('bones'), S1[:, m, A4])
            tt('dve', BON[:, m, A4], p5[:, 0:n], vT[:, m, c1], ALU.mult)
        tt('pool', BV[:, :, A4], KKN[:, :, A4], AS[:, :, A4], ALU.mult)
        Wt = AS
        act(Wt[:, :, 0:n], CW[:, :, 0:n], AF.Exp)
        tt(ve(), RT_[:, :, 0:n], rT[:, :, c1], Wt[:, :, 0:n], ALU.mult)
        cp('pool', tk['wlc'][:, :, 0:nch], Wt[:, :, L - 1:n:L])
        tt(ve(), AT_[:, :, 0:n], CW[:, :, 0:n], LW[:, :, 0:n], ALU.subtract)
        act(AT_[:, :, 0:n], AT_[:, :, 0:n], AF.Exp)
        stt(ve(), AT_[:, :, 0:n], KKN[:, :, 0:n], -1.0, AT_[:, :, 0:n], ALU.mult, ALU.mult)
        En = LW
        act(En[:, :, 0:n], CW[:, :, 0:n], AF.Exp, scale=-1.0)
        tt(ve(), BT_[:, :, 0:n], BV[:, :, 0:n], En[:, :, 0:n], ALU.mult)
        tt(ve(), KT_[:, :, 0:n], KM[:, :, 0:n], En[:, :, 0:n], ALU.mult)
        EB = LW
        for m in range(4):
            cw3 = CW[:, m, 0:n].rearrange("p (c l) -> p c l", l=L)
            eb3 = EB[:, m, 0:n].rearrange("p (c l) -> p c l", l=L)
            tt(ve(), eb3, bc(CW[:, m, L - 1:n:L], [128, nch, L], 2), cw3, ALU.subtract)
        act(EB[:, :, 0:n], EB[:, :, 0:n], AF.Exp)
        BB, KB = KKN, AS
        tt(ve(), BB[:, :, 0:n], BV[:, :, 0:n], EB[:, :, 0:n], ALU.mult)
        tt(ve(), KB[:, :, 0:n], KM[:, :, 0:n], EB[:, :, 0:n], ALU.mult)
        ATo, RTo = KM, BV
        cp(ve(), ATo[:, :, 0:n], AT_[:, :, 0:n])
        cp(ve(), RTo[:, :, 0:n], RT_[:, :, 0:n])
        memset('pool', ATo[0:64, :, 0:n], 0.0)
        memset('pool', RTo[0:64, :, 0:n], 0.0)
        memset('pool', AT_[64:128, :, 0:n], 0.0)
        memset('pool', RT_[64:128, :, 0:n], 0.0)

    def rwkv_chain(l, n, L, nch, g):
        LW, AS, G, CW, KKN, KM, BON, BV, RT_, AT_, BT_, KT_ = SL
        BB, KB = KKN, AS
        ATm, RTm = (AT_, KM), (RT_, BV)
        OT = (CW, LW)[g]
        T = tkg[g]
        ST = rstg[l][g]
        nst = int(np.log2(L))

        def v3(p_):
            return p_[0:L, 0:4 * L].rearrange("p (h i) -> p h i", h=4)

        def x3(p_):
            return p_[0:L, 0:256].rearrange("p (h d) -> p h d", h=4)
        u0b = bc(C('u0', L, 0, L), [L, 4, L], 1)
        u1b = bc(C('u1', L, 0, L), [L, 4, L], 1)
        l0b = bc(C('l0', L, 0, L), [L, 4, L], 1)
        for c in range(nch):
            cs = slice(c * L, (c + 1) * L)
            cs1 = slice(1 + c * L, 1 + (c + 1) * L)
            pN, pNT, pAk, pRb, pRk = ps(), ps(), ps(), ps(), ps()
            for hh in range(4):
                h = 4 * g + hh
                q = h // 2
                hsl = slice(hh * L, (hh + 1) * L)
                am, rm_ = ATm[h % 2], RTm[h % 2]
                mm(pN[0:L, hsl], am[:, q, cs], BT_[:, q, cs])
                mm(pNT[0:L, hsl], BT_[:, q, cs], am[:, q, cs])
                mm(pAk[0:L, hsl], KT_[:, q, cs], am[:, q, cs])
                mm(pRb[0:L, hsl], BT_[:, q, cs], rm_[:, q, cs])
                mm(pRk[0:L, hsl], KT_[:, q, cs], rm_[:, q, cs])
            pv = ps()
            for i in range(2):
                tr(pv[0:L, i * 128:(i + 1) * 128], vT[:, 2 * g + i, cs1], 128)
            pbb, pkb = ps(), ps()
            for i in range(2):
                tr(pbb[0:L, i * 128:(i + 1) * 128], BB[:, 2 * g + i, cs], 128)
                tr(pkb[0:L, i * 128:(i + 1) * 128], KB[:, 2 * g + i, cs], 128)
            Pa, PaT = T['Pa'], T['PaT']
            tt('dve', Pa[0:L, :, 0:L], v3(pN), l0b, ALU.mult)
            tt('dve', PaT[0:L, :, 0:L], v3(pNT), u0b, ALU.mult)
            tt('dve', T['AkT'][0:L, :, 0:L], v3(pAk), u0b, ALU.mult)
            cp('act', T['Vtok'][0:L, :, :], x3(pv))
            tt('dve', T['RbT'][0:L, :, 0:L], v3(pRb), u1b, ALU.mult)
            tt('dve', T['RkT'][0:L, :, 0:L], v3(pRk), u1b, ALU.mult)
            cp('act', T['bbt'][0:L, :, :], pbb[0:L, 0:256].rearrange("p (m d) -> p m d", m=2))
            cp('act', T['kbt'][0:L, :, :], pkb[0:L, 0:256].rearrange("p (m d) -> p m d", m=2))
            yield
            pX = ps()
            for hh in range(4):
                h = 4 * g + hh
                mm(pX[0:L, hh * 64:(hh + 1) * 64], ATm[h % 2][:, h // 2, cs], ST[:, hh // 2, :], start=True, stop=False)
                mm(pX[0:L, hh * 64:(hh + 1) * 64], T['AkT'][0:L, hh, 0:L], T['Vtok'][0:L, hh, :], start=False, stop=True)
            Xc, Xn = T['Xa'], T['Xb']
            cp('act', Xc[0:L, :, :], x3(pX))
            yield
            Pc, PcT, Pn, PnT = Pa, PaT, T['Pb'], T['PbT']
            for st_ in range(nst):
                pU = ps()
                for hh in range(4):
                    mm(pU[0:L, hh * 64:(hh + 1) * 64], PcT[0:L, hh, 0:L], Xc[0:L, hh, :])
                if st_ < nst - 1:
                    pS, pST = ps(), ps()
                    for hh in range(4):
                        hsl = slice(hh * L, (hh + 1) * L)
                        mm(pS[0:L, hsl], PcT[0:L, hh, 0:L], Pc[0:L, hh, 0:L])
                        mm(pST[0:L, hsl], Pc[0:L, hh, 0:L], PcT[0:L, hh, 0:L])
                tt('dve', Xn[0:L, :, :], Xc[0:L, :, :], x3(pU), ALU.add)
                Xc, Xn = Xn, Xc
                if st_ < nst - 1:
                    cp('act', Pn[0:L, :, 0:L], v3(pS))
                    cp('dve', PnT[0:L, :, 0:L], v3(pST))
                    Pc, PcT, Pn, PnT = Pn, PnT, Pc, PcT
                yield
            SA = Xc
            pO = ps()
            for hh in range(4):
                h = 4 * g + hh
                o_ = pO[0:L, hh * 64:(hh + 1) * 64]
                mm(o_, RTm[h % 2][:, h // 2, cs], ST[:, hh // 2, :], start=True, stop=False)
                mm(o_, T['RbT'][0:L, hh, 0:L], SA[0:L, hh, :], start=False, stop=False)
                mm(o_, T['RkT'][0:L, hh, 0:L], T['Vtok'][0:L, hh, :], start=False, stop=True)
            pSt = ps()
            for i in range(2):
                mm(pSt[:, i * 128:(i + 1) * 128], T['bbt'][0:L, i, :],
                   SA[0:L, 2 * i:2 * i + 2, :].rearrange("p h d -> p (h d)"), start=True, stop=False)
                mm(pSt[:, i * 128:(i + 1) * 128], T['kbt'][0:L, i, :],
                   T['Vtok'][0:L, 2 * i:2 * i + 2, :].rearrange("p h d -> p (h d)"), start=False, stop=True)
            cp('act', T['Otok'][0:L, :, :], x3(pO))
            pSt3 = pSt[:, 0:256].rearrange("p (q d) -> p q d", q=2)
            for hh2 in range(2):
                prr = slice(hh2 * 64, hh2 * 64 + 64)
                tt('dve', T['stm'][prr, :, :], ST[prr, :, :], bc(tk['wlc'][prr, 2 * g:2 * g + 2, c], [64, 2, 64], 2),
                   ALU.mult)
                tt('dve', ST[prr, :, :], T['stm'][prr, :, :], pSt3[prr, :, hh2 * 64:(hh2 + 1) * 64], ALU.add)
            yield
            pot = ps()
            for i in range(2):
                tr(pot[:, i * L:(i + 1) * L], T['Otok'][0:L, 2 * i:2 * i + 2, :].rearrange("p h d -> p (h d)"), L)
            cp('act', OT[:, 2 * g:2 * g + 2, cs], pot[:, 0:2 * L].rearrange("p (q i) -> p q i", q=2))
            yield

    def rwkv_epi(l, n, L, nch):
        LW, AS, G, CW, KKN, KM, BON, BV, RT_, AT_, BT_, KT_ = SL
        CEN, SQ = BT_, KT_
        A4 = slice(0, n)
        for m in range(4):
            OT = (CW, LW)[m // 2]
            pm = ps()
            mm(pm[:, 0:n], C('bones'), OT[:, m, A4])
            stt('dve', CEN[:, m, A4], pm[:, 0:n], -1.0 / 64, OT[:, m, A4], ALU.mult, ALU.add)
        tt('pool', SQ[:, :, A4], CEN[:, :, A4], CEN[:, :, A4], ALU.mult)
        for m in range(4):
            pvv = ps()
            mm(pvv[:, 0:n], C('bones'), SQ[:, m, A4])
            act(SQ[:, m, A4], pvv[:, 0:n], AF.Ln, bias=SC(2), scale=1.0 / 64)
        act(SQ[:, :, A4], SQ[:, :, A4], AF.Exp, scale=-0.5)
        tt('dve', CEN[:, :, A4], CEN[:, :, A4], SQ[:, :, A4], ALU.mult)
        for m in range(4):
            ts('dve', CEN[:, m, A4], CEN[:, m, A4], PPc(l, 'lnw', m), ALU.mult, PPc(l, 'lnb', m), ALU.add)
        tt('pool', CEN[:, :, A4], CEN[:, :, A4], BON[:, :, A4], ALU.add)
        for m in range(4):
            tt(ve(), Y[4 + m][:, 0:n], CEN[:, m, A4], G[:, m, A4], ALU.mult)

    def hgrn_pro(l, n, Lh, nchh):
        E_, L1, KG, CU, TM, QT, KT2, QH = SL[0:8]
        mid = Lh // 2
        rmn = 'rm32' if Lh == 32 else 'rm64'
        for m in range(4):
            ts(ve(), fzT[:, m, 0:n], fzT[:, m, 0:n], -60.0, ALU.max)
        act(E_[:, :, 0:n], fzT[:, :, 0:n], AF.Exp, scale=-1.0)
        for m in range(4):
            act(L1[:, m, 0:n], E_[:, m, 0:n], AF.Ln, bias=SC(1), scale=dv[l][:, 16 + m:17 + m])
        act(TM[:, :, 0:n], E_[:, :, 0:n], AF.Ln, bias=SC(1))
        tt(ve(), L1[:, :, 0:n], L1[:, :, 0:n], TM[:, :, 0:n], ALU.subtract)
        act(TM[:, :, 0:n], TM[:, :, 0:n], AF.Exp, scale=-1.0)
        for m in range(4):
            stt(ve(), KG[:, m, 0:n], E_[:, m, 0:n], dv[l][:, 20 + m:21 + m], TM[:, m, 0:n], ALU.mult, ALU.mult)
            scan(CU[:, m, 0:n], C(rmn, 128, 0, n), L1[:, m, 0:n])
        for m in range(4):
            cu3 = CU[:, m, 0:n].rearrange("p (c l) -> p c l", l=Lh)
            tm3 = TM[:, m, 0:n].rearrange("p (c l) -> p c l", l=Lh)
            tt(ve(), tm3, cu3, bc(CU[:, m, mid:n:Lh], [128, nchh, Lh], 2), ALU.subtract)
        ts('dve', TM[:, :, 0:n], TM[:, :, 0:n], 38.0, ALU.min, -38.0, ALU.max)
        act(QT[:, :, 0:n], TM[:, :, 0:n], AF.Exp)
        act(KT2[:, :, 0:n], TM[:, :, 0:n], AF.Exp, scale=-1.0)
        tt(ve(), QT[:, :, 0:n], QT[:, :, 0:n], qT[:, :, 0:n], ALU.mult)
        tt(ve(), KT2[:, :, 0:n], KT2[:, :, 0:n], KG[:, :, 0:n], ALU.mult)
        act(QH[:, :, 0:n], CU[:, :, 0:n], AF.Exp)
        cp('pool', tk['slc'][:, :, 0:nchh], QH[:, :, Lh - 1:n:Lh])
        tt(ve(), QH[:, :, 0:n], QH[:, :, 0:n], qT[:, :, 0:n], ALU.mult)
        KH = E_
        for m in range(4):
            cu3 = CU[:, m, 0:n].rearrange("p (c l) -> p c l", l=Lh)
            kh3 = KH[:, m, 0:n].rearrange("p (c l) -> p c l", l=Lh)
            tt(ve(), kh3, bc(CU[:, m, Lh - 1:n:Lh], [128, nchh, Lh], 2), cu3, ALU.subtract)
        act(KH[:, :, 0:n], KH[:, :, 0:n], AF.Exp)
        tt(ve(), KH[:, :, 0:n], KH[:, :, 0:n], KG[:, :, 0:n], ALU.mult)

    def hgrn_chain(l, n, Lh, nchh, g):
        E_, L1, KG, CU, TM, QT, KT2, QH = SL[0:8]
        KH = E_
        OT = (L1, TM)[g]
        T = tkg[g]
        S = gstg[l][g]
        for c in range(nchh):
            cs = slice(c * Lh, (c + 1) * Lh)
            pA = ps()
            for hh in range(2):
                h = 2 * g + hh
                mm(pA[0:Lh, hh * Lh:(hh + 1) * Lh], KT2[:, h, cs], QT[:, h, cs])
            pv, pk = ps(), ps()
            for hh in range(2):
                h = 2 * g + hh
                tr(pv[0:Lh, hh * 128:(hh + 1) * 128], ivT[:, h, cs], 128)
                tr(pk[0:Lh, hh * 128:(hh + 1) * 128], KH[:, h, cs], 128)
            tt('dve', T['hMT'][0:Lh, :, 0:Lh], pA[0:Lh, 0:2 * Lh].rearrange("p (h i) -> p h i", h=2),
               bc(C('u1', Lh, 0, Lh), [Lh, 2, Lh], 1), ALU.mult)
            cp('act', T['hvt'][0:Lh, :, :], pv[0:Lh, 0:256].rearrange("p (h d) -> p h d", h=2))
            cp('dve', T['hkt'][0:Lh, :, :], pk[0:Lh, 0:256].rearrange("p (h d) -> p h d", h=2))
            yield
            po = ps()
            for hh in range(2):
                h = 2 * g + hh
                o_ = po[:, hh * Lh:(hh + 1) * Lh]
                mm(o_, T['hvt'][0:Lh, hh, :], T['hMT'][0:Lh, hh, 0:Lh], start=True, stop=False)
                mm(o_, S[:, hh, :], QH[:, h, cs], start=False, stop=True)
            pS = ps()
            for hh in range(2):
                mm(pS[:, hh * 128:(hh + 1) * 128], T['hkt'][0:Lh, hh, :], T['hvt'][0:Lh, hh, :])
            cp('act', OT[:, 2 * g:2 * g + 2, cs], po[:, 0:2 * Lh].rearrange("p (h i) -> p h i", h=2))
            for hh in range(2):
                h = 2 * g + hh
                stt('dve', S[:, hh, :], S[:, hh, :], tk['slc'][:, h, c:c + 1], pS[:, hh * 128:(hh + 1) * 128],
                    ALU.mult, ALU.add)
            yield

    def hgrn_epi(l, n, Lh, nchh):
        E_, L1, KG, CU, TM, QT, KT2, QH = SL[0:8]
        SQ, SG = KG, CU
        OTs = (L1, TM)
        for g in range(2):
            tt(ve(), SQ[:, 2 * g:2 * g + 2, 0:n], OTs[g][:, 2 * g:2 * g + 2, 0:n], OTs[g][:, 2 * g:2 * g + 2, 0:n],
               ALU.mult)
        for h in range(4):
            pss = ps()
            mm(pss[:, 0:n], C('ones'), SQ[:, h, 0:n])
            act(SQ[:, h, 0:n], pss[:, 0:n], AF.Ln, bias=SC(0), scale=1.0 / 128)
        act(SQ[:, :, 0:n], SQ[:, :, 0:n], AF.Exp, scale=-0.5)
        sigmoid(SG[:, :, 0:n], ggT[:, :, 0:n])
        tt('pool', SG[:, :, 0:n], SG[:, :, 0:n], ggT[:, :, 0:n], ALU.mult)
        tt('dve', SG[:, :, 0:n], SG[:, :, 0:n], SQ[:, :, 0:n], ALU.mult)
        for h in range(4):
            stt('dve', Y[8 + h][:, 0:n], OTs[h // 2][:, h, 0:n], PPc(l, 'hnw', h), SG[:, h, 0:n], ALU.mult, ALU.mult)

    def interleave(*gens):
        gens = list(gens)
        while gens:
            for g_ in list(gens):
                try:
                    next(g_)
                except StopIteration:
                    gens.remove(g_)


    def run_tile(src, n, L, Lh, dst):
        nb = (n + 127) // 128
        for b in range(nb):
            tb = min(128, n - b * 128)
            xi = xin[0]
            P.dma(DQ, xi[0:tb, :], src[b * 128:b * 128 + tb, :])
            for g in range(2 if 'i' not in KDBG else 0):
                pt = ps()
                for k in range(4):
                    tr(pt[:, k * 128:k * 128 + tb], xi[0:tb, (4 * g + k) * 128:(4 * g + k + 1) * 128], tb)
                for k in range(4):
                    cp('act' if (k % 2 and 'j' not in KDBG) else 'dve', xT[4 * g + k][:, b * 128:b * 128 + tb], pt[:, k * 128:k * 128 + tb])
        for l in range(DEPTH):
            layer(l, n, L, Lh)
        if dst is None or 'g' in KDBG:
            return
        hF = [SL[k // 4][:, k % 4, :] for k in range(8)]
        if 'h' not in KDBG:
            rmsnorm(DEPTH - 1, 'fin', n, xT, hF)
        for b in range(nb):
            tb = min(128, n - b * 128)
            xo = xin[0]
            for g in range(2):
                pt = ps()
                for k in range(4):
                    tr(pt[0:tb, k * 128:(k + 1) * 128], hF[4 * g + k][:, b * 128:b * 128 + tb], 128)
                cp('act' if g else 'dve', xo[0:tb, g * 512:(g + 1) * 512], pt[0:tb, :])
            P.dma(DQ, dst[b * 128:b * 128 + tb, :], xo[0:tb, :])

    def store_states(si):
        for l in range(DEPTH):
            for i in range(4):
                pt = ps()
                tr(pt[:, 0:128], hstg[l][i // 2][:, (i % 2) * 128:(i % 2 + 1) * 128], 128)
                cp('act', stmp[:, i, :], pt[:, 0:128])
            for hh in range(2):
                P.dma(DQ, o_ssm[si, l].rearrange("(i hh) p n -> hh p i n", hh=2)[hh],
                      stmp[hh * 64:(hh + 1) * 64, :, :])
            for i in range(8):
                P.dma(DQ, o_conv[si, l][:, i * 128:(i + 1) * 128].rearrange("j p -> p j"), hist[l][:, i, :],
                      allow_slow_non_contiguous=True)
            P.dma(DQ, o_shift[si, l].rearrange("(i p) -> p i", p=128), prev[l][:, :],
                  allow_slow_non_contiguous=True)
            for q in range(4):
                pt = ps()
                tr(pt[0:64, 0:128], rstg[l][q // 2][:, q % 2, :], 128)
                cp('act', rtmp[:, q, :, :], pt[0:64, 0:128].rearrange("p (hh n) -> p hh n", hh=2))
            P.dma(DQ, o_rwkv[si, l].rearrange("(q hh) v n -> v q hh n", hh=2), rtmp[:, :, :, :])
            for g in range(2):
                P.dma(DQ, o_hgrn[si, l, 2 * g:2 * g + 2].rearrange("h k v -> k h v"), gstg[l][g][:, :, :])

    def load_states():
        for l in range(DEPTH):
            for hh in range(2):
                P.dma(DQ, stmp[hh * 64:(hh + 1) * 64, :, :],
                      st_ssm[l].rearrange("(i hh) p n -> hh p i n", hh=2)[hh])
            for i in range(4):
                pt = ps()
                tr(pt[:, 0:128], stmp[:, i, :], 128)
                cp('act', hstg[l][i // 2][:, (i % 2) * 128:(i % 2 + 1) * 128], pt[:, 0:128])
            for i in range(8):
                P.dma(DQ, hist[l][:, i, :], st_conv[l][:, i * 128:(i + 1) * 128].rearrange("j p -> p j"),
                      allow_slow_non_contiguous=True)
            P.dma(DQ, prev[l][:, :], st_shift[l].rearrange("(i p) -> p i", p=128),
                  allow_slow_non_contiguous=True)
            P.dma(DQ, rtmp[:, :, :, :], st_rwkv[l].rearrange("(q hh) v n -> v q hh n", hh=2))
            for q in range(4):
                pt = ps()
                tr(pt[:, 0:64], rtmp[:, q, :, :].rearrange("p hh n -> p (hh n)"), 64)
                cp('act', rstg[l][q // 2][:, q % 2, :], pt[:, 0:64])
            for g in range(2):
                P.dma(DQ, gstg[l][g][:, :, :], st_hgrn[l, 2 * g:2 * g + 2].rearrange("h k v -> k h v"))

    def zero_states():
        for l in range(DEPTH):
            for g in range(2):
                memset('pool', hstg[l][g][:, :], 0.0)
                memset('pool', rstg[l][g][:, :, :], 0.0)
                memset('pool', gstg[l][g][:, :, :], 0.0)
            memset('pool', hist[l][:, :, :], 0.0)
            memset('pool', prev[l][:, :], 0.0)

    def build_wcache():
        cnt = 0
        for l_ in range(DEPTH):
            for nm_, src in (("in", w_in), ("out", w_out), ("gate", w_gate), ("up", w_up), ("down", w_down)):
                cache = wc[(nm_, l_)]
                gl = WG[nm_]
                blocks = []
                gi = 0
                while gi < len(gl):
                    if gi + 1 < len(gl) and gl[gi][1] == 4 and gl[gi + 1][0] == gl[gi][0] + 512:
                        blocks.append((gl[gi][0], [(gi, 0, 512), (gi + 1, 512, gl[gi + 1][1] * 128)]))
                        gi += 2
                    else:
                        blocks.append((gl[gi][0], [(gi, 0, gl[gi][1] * 128)]))
                        gi += 1
                for c0, parts in blocks:
                    w_ = parts[-1][1] + parts[-1][2]
                    for k in range(WNK[nm_]):
                        land = SL[cnt % 6][:, :, :].rearrange("p a b -> p (a b)")
                        outb = SL[6 + cnt % 6][:, :, :].rearrange("p a b -> p (a b)").bitcast(BF16)
                        P.dma('sp', land[:, 0:w_], src[l_, k * 128:(k + 1) * 128, c0:c0 + w_])
                        cp('dve', outb[:, 0:w_], land[:, 0:w_])
                        for gidx, off, gw in parts:
                            P.dma('act', cache[gidx, :, k, 0:gw], outb[:, off:off + gw],
                                  semname="wcw%d" % (cnt % 6))
                        cnt += 1

    if WCACHE:
        build_wcache()

    if 'states' in STAGES:
        load_states()
    else:
        zero_states()
    if 'f' not in KDBG:
        run_tile(xs_in, 64, 64, 32, y_s)
    if 'states' in STAGES:
        store_states(1)
    zero_states()
    if 'b' not in KDBG:
        run_tile(meta, 16, 16, 16, None)
    t0 = 0 if 'c' not in KDBG else SEQ
    while t0 < SEQ:
        n = min(NT, SEQ - t0)
        run_tile(xp[t0:t0 + n, :], n, 64, 32, y_p[t0:t0 + n, :])
        t0 += n
    if 'states' in STAGES:
        store_states(0)

    P.emit()
    return nc, es, P


_CACHE = {}


def kernel(**inp):
    inp = {k: np.asarray(v, dtype=np.float32) for k, v in inp.items()}
    B, SEQ, _ = inp['x_prompt'].shape
    DEPTH = inp['w_in'].shape[0]
    pps = np.stack([_pack_params(inp, l) for l in range(DEPTH)], axis=0)
    ccs = _consts()
    zpad = np.zeros_like(inp['rw_w2'])
    wa2 = np.ascontiguousarray(np.stack([np.concatenate([inp['rw_w2'], zpad], axis=1),
                                         np.concatenate([zpad, inp['rw_a2']], axis=1)], axis=1))
    key = (SEQ, DEPTH)
    if key not in _CACHE:
        _CACHE[key] = build(SEQ, DEPTH=DEPTH)
    nc = _CACHE[key][0]
    in_maps = []
    for c in range(NCORES):
        in_maps.append({
            "xp": np.ascontiguousarray(inp['x_prompt'][c]),
            "xs": np.ascontiguousarray(inp['x_sample'][c]),
            "meta": inp['meta_tokens'],
            "st_ssm": np.ascontiguousarray(inp['state_ssm'][:, c]),
            "st_conv": np.ascontiguousarray(inp['state_conv'][:, c]),
            "st_rwkv": np.ascontiguousarray(inp['state_rwkv'][:, c]),
            "st_shift": np.ascontiguousarray(inp['state_shift'][:, c]),
            "st_hgrn": np.ascontiguousarray(inp['state_hgrn'][:, c]),
            "w_in": inp['w_in'], "w_out": inp['w_out'], "w_gate": inp['w_gate'], "w_up": inp['w_up'],
            "w_down": inp['w_down'], "wa2": wa2, "g2": inp['rw_g2'], "pp": pps, "cc": ccs,
        })
    in_maps = [{"i_" + k: v for k, v in m.items()} for m in in_maps]
    if os.environ.get('KONE'):
        res = run_bass_kernel_spmd(nc, in_maps[:1], core_ids=[0])
        R = [{k[2:]: v for k, v in res.results[0].items()}] * NCORES
    else:
        res = run_bass_kernel_spmd(nc, in_maps, core_ids=list(range(NCORES)))
        R = [{k[2:]: v for k, v in r.items()} for r in res.results]
    y_prompt = np.stack([R[c]["y_p"] for c in range(NCORES)], axis=0)
    y_sample = np.stack([R[c]["y_s"] for c in range(NCORES)], axis=0)

    def st(name, si):
        return np.stack([R[c][name][si] for c in range(NCORES)], axis=1)
    outs = [y_prompt, y_sample]
    for si in (0, 1):
        for name in ("o_ssm", "o_conv", "o_rwkv", "o_shift", "o_hgrn"):
            outs.append(st(name, si))
    return tuple(np.ascontiguousarray(o, dtype=np.float32) for o in outs)
```
